# Optimizing a Trainium2 kernel written in Bass

```python
import jax, jax.numpy as jnp
from jax import lax
import numpy as np

D_MODEL = 1024
BATCH = 4
SEQ = 4096
DEPTH = 1

D_PLE = 256
HEAD_DIM = 64
RET_WIDTH = D_MODEL // 2
RWKV_WIDTH = D_MODEL - RET_WIDTH
RET_HEADS = RET_WIDTH // HEAD_DIM
RWKV_HEADS = RWKV_WIDTH // HEAD_DIM
MIX_WIDTH = RET_WIDTH + RWKV_WIDTH
RET_CHUNK = 128
ROPE_BASE = 10000.0
DECAY_LORA = 64
AAA_LORA = 64
GATE_LORA = 160
D_FF = 2816
LN_EPS = 1e-5
RET_GN_EPS = 1e-5
RWKV_GN_EPS = 64e-5
DEEPNORM_ALPHA = (2.0 * DEPTH) ** 0.25
DEEPNORM_BETA = (8.0 * DEPTH) ** -0.25

RET_COLS = 4 * RET_WIDTH
RW_COLS = 3 * RWKV_WIDTH + DECAY_LORA + AAA_LORA + GATE_LORA
IN_COLS = RET_COLS + RW_COLS
RW_SPLITS = (RWKV_WIDTH, 2 * RWKV_WIDTH, 3 * RWKV_WIDTH,
             3 * RWKV_WIDTH + DECAY_LORA, 3 * RWKV_WIDTH + DECAY_LORA + AAA_LORA)

kernel_name = 'hymba_retnet_rwkv7_macaron_deepnorm'


def _layer_norm(x, g, b):
    xf = x.astype(jnp.float32)
    mu = jnp.mean(xf, axis=-1, keepdims=True)
    var = jnp.mean(jnp.square(xf - mu), axis=-1, keepdims=True)
    y = (xf - mu) * lax.rsqrt(var + LN_EPS) * g.astype(jnp.float32) + b.astype(jnp.float32)
    return y.astype(x.dtype)


def _head_group_norm(y, g, b, eps):
    mu = jnp.mean(y, axis=-1, keepdims=True)
    var = jnp.mean(jnp.square(y - mu), axis=-1, keepdims=True)
    bsz, s, h, d = y.shape
    yn = ((y - mu) * lax.rsqrt(var + eps)).reshape(bsz, s, h * d)
    return yn * g.astype(jnp.float32) + b.astype(jnp.float32)


def _swiglu(x, w_gu, w_down):
    hdn = jnp.einsum('bsd,df->bsf', x, w_gu)
    gate, up = jnp.split(hdn, 2, axis=-1)
    return jnp.einsum('bsf,fd->bsd', jax.nn.silu(gate) * up, w_down)


def _rotary(t):
    s, d = t.shape[1], t.shape[-1]
    pos = jnp.arange(s, dtype=jnp.float32)
    inv_freq = ROPE_BASE ** (-jnp.arange(0, d, 2, dtype=jnp.float32) / d)
    ang = pos[:, None] * inv_freq[None, :]
    cos = jnp.cos(ang)[None, :, None, :]
    sin = jnp.sin(ang)[None, :, None, :]
    t1, t2 = jnp.split(t, 2, axis=-1)
    return jnp.concatenate([t1 * cos - t2 * sin, t1 * sin + t2 * cos], axis=-1)


def _retention_chunkwise(q, k, v):
    bsz, s, h, dk = q.shape
    dv = v.shape[-1]
    c = RET_CHUNK
    n = s // c
    log_gamma = jnp.log1p(-jnp.exp2(-5.0 - jnp.arange(h, dtype=jnp.float32)))
    idx = jnp.arange(c, dtype=jnp.float32)
    rel = idx[:, None] - idx[None, :]
    causal = rel >= 0
    intra = jnp.where(causal[None], jnp.exp(jnp.where(causal, rel, 0.0)[None] * log_gamma[:, None, None]), 0.0)
    q_decay = jnp.exp((idx[None, :] + 1.0) * log_gamma[:, None])
    k_decay = jnp.exp((c - 1.0 - idx[None, :]) * log_gamma[:, None])
    chunk_decay = jnp.exp(c * log_gamma)
    qc = q.reshape(bsz, n, c, h, dk)
    kc = k.reshape(bsz, n, c, h, dk)
    vc = v.reshape(bsz, n, c, h, dv)
    scores = jnp.einsum('bnihd,bnjhd->bnhij', qc, kc) * intra
    inner = jnp.einsum('bnhij,bnjhe->bnihe', scores, vc)
    kv = jnp.einsum('bnjhd,bnjhe,hj->nbhde', kc, vc, k_decay)

    def step(state, kv_n):
        return chunk_decay[None, :, None, None] * state + kv_n, state

    _, prev_states = lax.scan(step, jnp.zeros((bsz, h, dk, dv), jnp.float32), kv)
    cross = jnp.einsum('bnihd,nbhde,hi->bnihe', qc, prev_states, q_decay)
    return (inner + cross).reshape(bsz, s, h, dv)


def _rwkv7_scan(r, w, k, v, kk, a):
    bsz, s, h, nd = r.shape
    b_vec = kk * a
    xs = tuple(jnp.moveaxis(t, 1, 0) for t in (r, w, k, v, kk, b_vec))

    def step(state, inp):
        r_t, w_t, k_t, v_t, kk_t, b_t = inp
        sa = jnp.einsum('bhvk,bhk->bhv', state, kk_t)
        state = (state * w_t[:, :, None, :] - sa[..., None] * b_t[:, :, None, :]
                 + v_t[..., None] * k_t[:, :, None, :])
        return state, jnp.einsum('bhvk,bhk->bhv', state, r_t)

    _, y = lax.scan(step, jnp.zeros((bsz, h, nd, nd), jnp.float32), xs)
    return jnp.moveaxis(y, 0, 1)


def _token_mixer(hin, w_in, ret_gn_g, ret_gn_b, rw_mu, rw_w0, rw_w_up, rw_a0, rw_a_up,
                 rw_g_up, rw_k_k, rw_k_a, rw_r_k, rw_gn_g, rw_gn_b, w_out):
    bsz, s, _ = hin.shape
    z = jnp.einsum('bsd,dc->bsc', hin, w_in).astype(jnp.float32)
    z_ret, z_rw = z[..., :RET_COLS], z[..., RET_COLS:]

    q, k, v, g = jnp.split(z_ret, 4, axis=-1)
    q = _rotary(q.reshape(bsz, s, RET_HEADS, HEAD_DIM))
    k = _rotary(k.reshape(bsz, s, RET_HEADS, HEAD_DIM)) * (HEAD_DIM ** -0.5)
    v = v.reshape(bsz, s, RET_HEADS, HEAD_DIM)
    ret = _retention_chunkwise(q, k, v)
    ret_out = jax.nn.silu(g) * _head_group_norm(ret, ret_gn_g, ret_gn_b, RET_GN_EPS)

    z_prev = jnp.pad(z_rw, ((0, 0), (1, 0), (0, 0)))[:, :-1]
    z_rw = z_rw + (z_prev - z_rw) * rw_mu.astype(jnp.float32)
    r, kr, vr, wd, ad, gd = jnp.split(z_rw, RW_SPLITS, axis=-1)
    w_log = -jax.nn.softplus(-(rw_w0 + jnp.tanh(wd) @ rw_w_up)) - 0.5
    decay = jnp.exp(-jnp.exp(w_log))
    a = jax.nn.sigmoid(rw_a0 + ad @ rw_a_up)
    gate = jax.nn.sigmoid(gd) @ rw_g_up
    heads = (bsz, s, RWKV_HEADS, HEAD_DIM)
    kk = (kr * rw_k_k).reshape(heads)
    kk = kk / jnp.maximum(jnp.sqrt(jnp.sum(jnp.square(kk), axis=-1, keepdims=True)), 1e-12)
    kr = kr * (1.0 + (a - 1.0) * rw_k_a)
    r_h, k_h, v_h = r.reshape(heads), kr.reshape(heads), vr.reshape(heads)
    y = _rwkv7_scan(r_h, decay.reshape(heads), k_h, v_h, kk, a.reshape(heads))
    bonus = jnp.sum(r_h * k_h * rw_r_k.astype(jnp.float32), axis=-1, keepdims=True) * v_h
    rw_out = (_head_group_norm(y, rw_gn_g, rw_gn_b, RWKV_GN_EPS) + bonus.reshape(bsz, s, RWKV_WIDTH)) * gate

    mixed = jnp.concatenate([ret_out, rw_out], axis=-1).astype(hin.dtype)
    return jnp.einsum('bsc,cd->bsd', mixed, w_out)


def setup_inputs(seed: int = 0) -> dict:
    key = jax.random.key(seed)
    ks = jax.random.split(key, 32)
    f32 = jnp.float32

    def nrm(k_, shape, scale):
        return jax.random.normal(k_, shape, f32) * scale

    def gain(k_, shape):
        return 1.0 + 0.02 * jax.random.normal(k_, shape, f32)

    w0_base = jnp.tile(jnp.linspace(-6.5, -1.5, HEAD_DIM, dtype=f32), RWKV_HEADS)
    return {
        'x': jax.random.normal(ks[0], (BATCH, SEQ, D_MODEL), f32),
        'p': jax.random.normal(ks[1], (DEPTH, BATCH, SEQ, D_PLE), f32),
        'ffn1_w_gu': nrm(ks[2], (DEPTH, D_MODEL, 2 * D_FF), D_MODEL ** -0.5),
        'ffn1_w_down': nrm(ks[3], (DEPTH, D_FF, D_MODEL), DEEPNORM_BETA * D_FF ** -0.5),
        'ln1_g': gain(ks[4], (DEPTH, D_MODEL)),
        'ln1_b': nrm(ks[5], (DEPTH, D_MODEL), 0.02),
        'w_in': nrm(ks[6], (DEPTH, D_MODEL, IN_COLS), D_MODEL ** -0.5),
        'ret_gn_g': gain(ks[7], (DEPTH, RET_WIDTH)),
        'ret_gn_b': nrm(ks[8], (DEPTH, RET_WIDTH), 0.02),
        'rw_mu': jax.random.uniform(ks[9], (DEPTH, RW_COLS), f32),
        'rw_w0': w0_base[None] + nrm(ks[10], (DEPTH, RWKV_WIDTH), 0.1),
        'rw_w_up': nrm(ks[11], (DEPTH, DECAY_LORA, RWKV_WIDTH), DECAY_LORA ** -0.5),
        'rw_a0': nrm(ks[12], (DEPTH, RWKV_WIDTH), 0.1),
        'rw_a_up': nrm(ks[13], (DEPTH, AAA_LORA, RWKV_WIDTH), AAA_LORA ** -0.5),
        'rw_g_up': nrm(ks[14], (DEPTH, GATE_LORA, RWKV_WIDTH), GATE_LORA ** -0.5),
        'rw_k_k': 0.85 + nrm(ks[15], (DEPTH, RWKV_WIDTH), 0.02),
        'rw_k_a': 1.0 + nrm(ks[16], (DEPTH, RWKV_WIDTH), 0.02),
        'rw_r_k': nrm(ks[17], (DEPTH, RWKV_HEADS, HEAD_DIM), 0.1),
        'rw_gn_g': gain(ks[18], (DEPTH, RWKV_WIDTH)),
        'rw_gn_b': nrm(ks[19], (DEPTH, RWKV_WIDTH), 0.02),
        'w_out': nrm(ks[20], (DEPTH, MIX_WIDTH, D_MODEL), DEEPNORM_BETA * MIX_WIDTH ** -0.5),
        'ln2_g': gain(ks[21], (DEPTH, D_MODEL)),
        'ln2_b': nrm(ks[22], (DEPTH, D_MODEL), 0.02),
        'ffn2_w_gu': nrm(ks[23], (DEPTH, D_MODEL, 2 * D_FF), D_MODEL ** -0.5),
        'ffn2_w_down': nrm(ks[24], (DEPTH, D_FF, D_MODEL), DEEPNORM_BETA * D_FF ** -0.5),
        'ln3_g': gain(ks[25], (DEPTH, D_MODEL)),
        'ln3_b': nrm(ks[26], (DEPTH, D_MODEL), 0.02),
        'ple_w_proj': nrm(ks[27], (DEPTH, D_PLE, D_MODEL), D_PLE ** -0.5),
        'ple_w_gate': nrm(ks[28], (DEPTH, D_MODEL, D_MODEL), D_MODEL ** -0.5),
        'ple_b_gate': nrm(ks[29], (DEPTH, D_MODEL), 0.02),
    }


def reference(x, p, ffn1_w_gu, ffn1_w_down, ln1_g, ln1_b, w_in, ret_gn_g, ret_gn_b, rw_mu,
              rw_w0, rw_w_up, rw_a0, rw_a_up, rw_g_up, rw_k_k, rw_k_a, rw_r_k, rw_gn_g, rw_gn_b,
              w_out, ln2_g, ln2_b, ffn2_w_gu, ffn2_w_down, ln3_g, ln3_b,
              ple_w_proj, ple_w_gate, ple_b_gate):
    for i in range(DEPTH):
        x = _layer_norm(DEEPNORM_ALPHA * x + 0.5 * _swiglu(x, ffn1_w_gu[i], ffn1_w_down[i]), ln1_g[i], ln1_b[i])
        mix = _token_mixer(x, w_in[i], ret_gn_g[i], ret_gn_b[i], rw_mu[i], rw_w0[i], rw_w_up[i],
                           rw_a0[i], rw_a_up[i], rw_g_up[i], rw_k_k[i], rw_k_a[i], rw_r_k[i],
                           rw_gn_g[i], rw_gn_b[i], w_out[i])
        x = _layer_norm(DEEPNORM_ALPHA * x + mix, ln2_g[i], ln2_b[i])
        x = _layer_norm(DEEPNORM_ALPHA * x + 0.5 * _swiglu(x, ffn2_w_gu[i], ffn2_w_down[i]), ln3_g[i], ln3_b[i])
        gate = jax.nn.sigmoid(jnp.einsum('bsd,de->bse', x, ple_w_gate[i]) + ple_b_gate[i])
        x = x + gate * jnp.einsum('bsp,pd->bsd', p[i], ple_w_proj[i])
    return x
```

```python
import os
import numpy as np
import concourse.bass as bass
import concourse.mybir as mybir
from concourse.bass_utils import run_bass_kernel_spmd

F32 = mybir.dt.float32
BF16 = mybir.dt.bfloat16
AF = mybir.ActivationFunctionType
ALU = mybir.AluOpType
AX = mybir.AxisListType

D = 1024
DFF = 2816
NJ = DFF // 128
S_HALF = 2048
NT_HALF = S_HALF // 128
RETC = 2048
RWC = 1824
INC = RETC + RWC
ALPHA = 2.0 ** 0.25
LN_EPS = 1e-5
EDEC = float(np.exp(-0.5))

STRICT = True


class _Rec:
    def __init__(self):
        self.calls = []

    def __getattr__(self, name):
        def f(*a, **k):
            self.calls.append((name, a, k))
            return self
        return f


def _capture(fn):
    r = _Rec()
    fn(r)
    assert len(r.calls) == 1, r.calls
    name, a, k = r.calls[0]
    return lambda e: getattr(e, name)(*a, **k)


class Sched:
    def __init__(self):
        self.engs = ['pe', 'act', 'dve', 'pool', 'sp']
        self.lists = {e: [] for e in self.engs}
        self.cnt = {e: 0 for e in self.engs}
        self.lastw = {}
        self.readers = {}
        self.waited = {e: {} for e in self.engs}
        self.dmacnt = {}
        self.semkeys = set(self.engs)
        self.alltok = {}

    def _deps(self, eng, R, W, is_dma):
        deps = []
        raw = set()
        for k in R:
            t = self.lastw.get(k)
            if t:
                deps.append(t)
                raw.add(t)
            if k.startswith('ps'):
                deps.extend(tk for tk in self.readers.get(k, ()) if tk[0] != eng)
        for k in W:
            t = self.lastw.get(k)
            if t:
                deps.append(t)
            deps.extend(self.readers.get(k, ()))
        waits = {}
        for (sk, v) in deps:
            if sk == eng and not is_dma and (eng == 'pe' or not STRICT or (sk, v) not in raw):
                continue
            if self.waited[eng].get(sk, 0) >= v:
                continue
            waits[sk] = max(waits.get(sk, 0), v)
        for sk, v in waits.items():
            self.waited[eng][sk] = v
        return list(waits.items())

    def _commit(self, tok, R, W):
        for k in W:
            self.lastw[k] = tok
            self.readers[k] = []
        for k in R:
            if k not in W:
                self.readers.setdefault(k, []).append(tok)
        self.alltok[tok[0]] = max(self.alltok.get(tok[0], 0), tok[1])

    def op(self, eng, fn, R=(), W=(), inc=True):
        self._clean = False
        waits = self._deps(eng, R, W, False)
        if inc:
            self.cnt[eng] += 1
            tok = (eng, self.cnt[eng])
        else:
            tok = (eng, self.cnt[eng] + 1)
        self.lists[eng].append((waits, _capture(fn), (eng, 1) if inc else None))
        self._commit(tok, R, W)

    def dma(self, eng, fn, R, W, sk=None):
        self._clean = False
        waits = self._deps(eng, R, W, True)
        sk = 'd:' + (sk or W[0])
        self.semkeys.add(sk)
        self.dmacnt[sk] = self.dmacnt.get(sk, 0) + 16
        tok = (sk, self.dmacnt[sk])
        self.lists[eng].append((waits, _capture(fn), (sk, 16)))
        self._commit(tok, R, W)

    def barrier(self):
        if getattr(self, '_clean', False):
            return
        self._clean = True
        for e in ['pe', 'act', 'dve', 'pool']:
            if self.lists[e] and self.lists[e][-1][2] is not None and self.lists[e][-1][2][0] == e:
                continue
            self.cnt[e] += 1
            self.alltok[e] = self.cnt[e]
            self.lists[e].append(([], 'nop', (e, 1)))
        for e in self.engs:
            waits = []
            for sk, v in self.alltok.items():
                if self.waited[e].get(sk, 0) >= v:
                    continue
                if sk == e:
                    continue
                waits.append((sk, v))
                self.waited[e][sk] = v
            self.lists[e].append((waits, None, None))
        self.lastw = {}
        self.readers = {}

    def emit(self, nc, block):
        sems = {sk: nc.alloc_semaphore(name=("s_" + sk.replace(':', '_').replace('.', '_'))[:40]) for sk in sorted(self.semkeys)}
        engobj = {'pe': 'tensor', 'act': 'scalar', 'dve': 'vector', 'pool': 'gpsimd', 'sp': 'sync'}

        def make(ename):
            lst = self.lists[ename]

            def body(e):
                for (waits, fn, inc) in lst:
                    for (sk, v) in waits:
                        e.wait_ge(sems[sk], v)
                    if fn is None:
                        continue
                    if fn == 'nop':
                        ins = e.nop()
                    else:
                        ins = fn(e)
                    if inc is not None:
                        ins.then_inc(sems[inc[0]], inc[1])
            return body

        for ename in self.engs:
            getattr(block, engobj[ename])(make(ename))


def gammas():
    return 1.0 - 2.0 ** (-5.0 - np.arange(8, dtype=np.float64))


def host_consts():
    g = gammas()
    i = np.arange(128)
    c = {}
    s_le_t = (i[:, None] <= i[None, :]).astype(np.float32)
    s_lt_t = (i[:, None] < i[None, :]).astype(np.float32)
    c['tri_incl'] = -EDEC * s_le_t
    c['tri_strict'] = -EDEC * s_lt_t
    c['negcol'] = np.full((128, 1), -EDEC, np.float32)
    c['ones'] = np.ones((128, 128), np.float32)
    rel = (i[None, :] - i[:, None]).astype(np.float64)
    dm = np.zeros((128, 8, 128), np.float64)
    for h in range(8):
        dm[:, h, :] = np.where(rel >= 0, 0.125 * np.exp(np.where(rel >= 0, rel, 0) * np.log(g[h])), 0.0)
    cr = {}
    cr['dmask'] = dm.reshape(128, 1024).astype(np.float32)
    kd = np.zeros((128, 8), np.float64)
    for h in range(8):
        kd[:, h] = 0.125 * g[h] ** (127.0 - i)
    c['kdec'] = kd.astype(np.float32)
    qd = np.zeros((128, 4, 128), np.float64)
    gm = np.zeros((128, 4), np.float64)
    for pr in range(4):
        for hp in range(2):
            h = 2 * pr + hp
            qd[hp * 64:(hp + 1) * 64, pr, :] = (g[h] ** (i + 1.0))[None, :]
            gm[hp * 64:(hp + 1) * 64, pr] = g[h] ** 128.0
    cr['qdec'] = qd.reshape(128, 512).astype(np.float32)
    cr['kdecf'] = np.repeat(kd, 64, axis=1).astype(np.float32)
    cr['gam128f'] = np.repeat(gm, 64, axis=1).astype(np.float32)
    c['gam128'] = gm.astype(np.float32)
    b = {}
    b['ident'] = np.eye(128, dtype=np.float32)
    m1 = np.concatenate([-s_lt_t, s_le_t], axis=1)
    m2 = np.concatenate([s_lt_t, s_le_t], axis=1)
    b['m1'] = np.concatenate([m1, m1], axis=1)
    b['m2'] = np.concatenate([m2, m2], axis=1)
    m3 = -(i[:, None] > i[None, :]).astype(np.float32)
    b['m3'] = np.concatenate([m3, m3], axis=1)
    return c, cr, b


def rope_tables(pos):
    inv = 10000.0 ** (-np.arange(0, 64, 2, dtype=np.float32) / 64.0)
    ang = pos.astype(np.float32)[:, None] * inv[None, :]
    cos = np.cos(ang).astype(np.float32)
    sin = np.sin(ang).astype(np.float32)
    n = pos.shape[0] // 128
    cos = cos.reshape(n, 128, 32).transpose(1, 0, 2).reshape(128, n * 32)
    sin = sin.reshape(n, 128, 32).transpose(1, 0, 2).reshape(128, n * 32)
    return np.ascontiguousarray(cos), np.ascontiguousarray(sin)


C32, C32R, CB = host_consts()
C32_OFF = {}
_o = 0
for _k, _v in C32.items():
    C32_OFF[_k] = (_o, _v.shape[1])
    _o += _v.shape[1]
C32_N = _o
C32R_OFF = {}
_o = 0
for _k, _v in C32R.items():
    C32R_OFF[_k] = (_o, _v.shape[1])
    _o += _v.shape[1]
C32R_N = _o
CB_OFF = {}
_o = 0
for _k, _v in CB.items():
    CB_OFF[_k] = (_o, _v.shape[1])
    _o += _v.shape[1]
CB_N = _o

WEIGHT_NAMES = ['ffn1_w_gu', 'ffn1_w_down', 'ln1_g', 'ln1_b', 'w_in', 'ret_gn_g', 'ret_gn_b', 'rw_mu',
                'rw_w0', 'rw_w_up', 'rw_a0', 'rw_a_up', 'rw_g_up', 'rw_k_k', 'rw_k_a', 'rw_r_k', 'rw_gn_g',
                'rw_gn_b', 'w_out', 'ln2_g', 'ln2_b', 'ffn2_w_gu', 'ffn2_w_down', 'ln3_g', 'ln3_b',
                'ple_w_proj', 'ple_w_gate', 'ple_b_gate']
WSHAPES = {'ffn1_w_gu': [D, 2 * DFF], 'ffn1_w_down': [DFF, D], 'ln1_g': [1, D], 'ln1_b': [1, D], 'w_in': [D, INC],
           'ret_gn_g': [1, 512], 'ret_gn_b': [1, 512], 'rw_mu': [1, RWC], 'rw_w0': [1, 512], 'rw_w_up': [64, 512],
           'rw_a0': [1, 512], 'rw_a_up': [64, 512], 'rw_g_up': [160, 512], 'rw_k_k': [1, 512], 'rw_k_a': [1, 512],
           'rw_r_k': [1, 512], 'rw_gn_g': [1, 512], 'rw_gn_b': [1, 512], 'w_out': [D, D], 'ln2_g': [1, D],
           'ln2_b': [1, D], 'ffn2_w_gu': [D, 2 * DFF], 'ffn2_w_down': [DFF, D], 'ln3_g': [1, D], 'ln3_b': [1, D],
           'ple_w_proj': [256, D], 'ple_w_gate': [D, D], 'ple_b_gate': [1, D]}


def build_program(plan=None, dbg=False):
    nc = bass.Bass("TRN2", target_bir_lowering=False)
    dr = {}
    dr['xs'] = nc.dram_tensor("xs", [2 * S_HALF, D], F32, kind="ExternalInput").ap()
    dr['p'] = nc.dram_tensor("p", [S_HALF, 256], F32, kind="ExternalInput").ap()
    dr['hmask'] = nc.dram_tensor("hmask", [128, 2], F32, kind="ExternalInput").ap()
    dr['cos'] = nc.dram_tensor("cos", [128, 1024], F32, kind="ExternalInput").ap()
    dr['sin'] = nc.dram_tensor("sin", [128, 1024], F32, kind="ExternalInput").ap()
    dr['c32'] = nc.dram_tensor("c32", [128, C32_N], F32, kind="ExternalInput").ap()
    dr['cb'] = nc.dram_tensor("cb", [128, CB_N], F32, kind="ExternalInput").ap()
    dr['c32r'] = nc.dram_tensor("c32r", [128, C32R_N], F32, kind="ExternalInput").ap()
    for n in WEIGHT_NAMES:
        dr[n] = nc.dram_tensor(n, WSHAPES[n], F32, kind="ExternalInput").ap()
    out = nc.dram_tensor("out", [S_HALF, D], F32, kind="ExternalOutput").ap()
    skind = "ExternalOutput" if dbg else "Internal"
    x1s = nc.dram_tensor("x1s", [S_HALF, D], F32, kind=skind).ap()
    x2s = nc.dram_tensor("x2s", [S_HALF, D], F32, kind=skind).ap()
    x3s = nc.dram_tensor("x3s", [S_HALF, D], F32, kind=skind).ap()
    if plan is None:
        plan = ['ffn1_0', 'mix_0', 'ffn1_1', 'mix_1', 'wout', 'ffn2', 'ple']

    S = Sched()
    dumped = {}

    def dump(name, ap, keys):
        if not dbg or name in dumped:
            return
        shp = list(ap.shape)
        d_ = nc.dram_tensor("dbg_" + name, shp, F32, kind="ExternalOutput").ap()
        dumped[name] = d_
        S.dma('pool', lambda e: e.dma_start(out=d_, in_=ap), R=list(keys), W=['dbg_' + name])
    ARENA_W = int(os.environ.get('ARENA_W', '53200'))
    from contextlib import ExitStack
    es = ExitStack()
    arena = es.enter_context(nc.sbuf_tensor("arena", [128, ARENA_W], F32))
    psf = [es.enter_context(nc.psum_tensor("ps%d" % i, [128, 512], F32)) for i in range(8)]

    class Alloc:
        def __init__(self):
            self.p = 0
            self.marks = []

        def f32(self, n, parts=(0, 128)):
            if os.environ.get('DRY'):
                self.p += n
                self.hi = max(getattr(self, 'hi', 0), self.p)
                return arena[parts[0]:parts[1], 0:n]
            a = arena[parts[0]:parts[1], self.p:self.p + n]
            self.p += n
            self.hi = max(getattr(self, 'hi', 0), self.p)
            assert self.p <= ARENA_W, ("arena overflow", self.p)
            return a

        def bf(self, n, parts=(0, 128)):
            w = (n + 1) // 2
            if os.environ.get('DRY'):
                self.p += w
                self.hi = max(getattr(self, 'hi', 0), self.p)
                return arena[parts[0]:parts[1], 0:w].bitcast(BF16)
            a = arena[parts[0]:parts[1], self.p:self.p + w].bitcast(BF16)
            self.p += w
            self.hi = max(getattr(self, 'hi', 0), self.p)
            assert self.p <= ARENA_W, ("arena overflow", self.p)
            return a

        def mark(self):
            self.marks.append(self.p)

        def release(self):
            self.p = self.marks.pop()
            S.barrier()

    A = Alloc()
    bankctr = [0]
    bankgen = [0] * 8

    class Bk(int):
        pass

    def bank():
        b = Bk(bankctr[0] % 8)
        bankctr[0] += 1
        bankgen[int(b)] = bankctr[0]
        b.gen = bankctr[0]
        return b

    def PS(b):
        return psf[int(b)][:, :]

    def PSB(b):
        return psf[int(b)][:, :].bitcast(BF16)

    def PK(b):
        assert bankgen[int(b)] == b.gen, "stale PSUM bank use"
        return 'ps%d' % int(b)

    c32 = A.f32(C32_N)
    cbt = A.bf(CB_N)
    S.dma('sp', lambda e: e.dma_start(out=c32, in_=dr['c32']), R=[], W=['c32'])
    S.dma('pool', lambda e: e.dma_start(out=cbt, in_=dr['cb']), R=[], W=['cb'])

    def c32v(name, parts=(0, 128)):
        o, n = C32_OFF[name]
        return c32[parts[0]:parts[1], o:o + n]

    def cbv(name):
        o, n = CB_OFF[name]
        return cbt[:, o:o + n]

    ident = cbv('ident')
    hmask = A.f32(2)
    S.dma('sp', lambda e: e.dma_start(out=hmask, in_=dr['hmask']), R=[], W=['hmask'])
    ptile = {}
    LN = {}
    retS = A.f32(256)
    retSb = A.bf(256)
    rwA = A.f32(256)
    rwAb = A.bf(256)
    x1T = A.bf(8 * (S_HALF + 8))
    x1Tv = x1T.rearrange("p (c t) -> p c t", c=8)
    XO = 8
    mixT = A.bf(8 * S_HALF)
    mixTv = mixT.rearrange("p (c t) -> p c t", c=8)
    x2Tv = x1Tv

    def load_ln(gn, bn):
        lng = A.f32(1024)
        lnb = A.f32(1024)
        LN['g'] = lng
        LN['b'] = lnb
        S.dma('sp', lambda e: e.dma_start(out=lng, in_=dr[gn].partition_broadcast(128)), R=[], W=['lng'])
        S.dma('sp', lambda e: e.dma_start(out=lnb, in_=dr[bn].partition_broadcast(128)), R=[], W=['lnb'])

    def load_ptiles(names):
        for n in names:
            t = A.f32(512)
            ptile[n] = t
            S.dma('sp', (lambda t, n: lambda e: e.dma_start(out=t, in_=dr[n].partition_broadcast(128)))(t, n), R=[], W=['pt_' + n])

    def transpose_to(src_bf, src_keys, n_blocks, dst_view, dst_keys, evac_eng='act'):
        b = bank()
        pk = PK(b)
        for i in range(n_blocks):
            S.op('pe', (lambda i, b: lambda e: e.transpose(out=PSB(b)[:, i * 128:(i + 1) * 128],
                                                         in_=src_bf[:, i * 128:(i + 1) * 128], identity=ident))(i, b),
                 R=list(src_keys) + ['cb'], W=[pk], inc=(i == n_blocks - 1))
        src = PSB(b)[:, 0:n_blocks * 128].rearrange("p (c t) -> p c t", c=n_blocks)
        if evac_eng == 'act':
            S.op('act', lambda e: e.activation(out=dst_view, in_=src, func=AF.Copy), R=[pk], W=list(dst_keys))
        else:
            S.op(evac_eng, lambda e: e.tensor_copy(out=dst_view, in_=src), R=[pk], W=list(dst_keys))

    def layer_norm_group(tiles, eps, junk, st):
        n_ = len(tiles)
        sl = lambda i, c: st[:, i * 8 + c:i * 8 + c + 1]
        k = lambda i, c: 'lnst%d_%d' % (i, c)
        for i, (y, yk) in enumerate(tiles):
            S.op('act', (lambda i, y: lambda e: e.activation(out=junk, in_=y, func=AF.Square, accum_out=sl(i, 0)))(i, y), R=[yk], W=['lnjunk', k(i, 0)])
            S.op('act', (lambda i, y: lambda e: e.activation(out=junk, in_=y, func=AF.Identity, accum_out=sl(i, 1)))(i, y), R=[yk], W=['lnjunk', k(i, 1)])
        for i in range(n_):
            S.op('dve', (lambda i: lambda e: e.tensor_scalar(out=sl(i, 2), in0=sl(i, 1), scalar1=1.0 / 1024, scalar2=None, op0=ALU.mult))(i), R=[k(i, 1)], W=[k(i, 2)])
        for i in range(n_):
            S.op('dve', (lambda i: lambda e: e.tensor_tensor(out=sl(i, 3), in0=sl(i, 2), in1=sl(i, 2), op=ALU.mult))(i), R=[k(i, 2)], W=[k(i, 3)])
        for i in range(n_):
            S.op('dve', (lambda i: lambda e: e.scalar_tensor_tensor(out=sl(i, 4), in0=sl(i, 0), scalar=1.0 / 1024, in1=sl(i, 3), op0=ALU.mult, op1=ALU.subtract))(i), R=[k(i, 0), k(i, 3)], W=[k(i, 4)])
        for i in range(n_):
            S.op('dve', (lambda i: lambda e: e.tensor_scalar(out=sl(i, 4), in0=sl(i, 4), scalar1=float(eps), scalar2=None, op0=ALU.add))(i), R=[k(i, 4)], W=[k(i, 4)])
        for i in range(n_):
            S.op('act', (lambda i: lambda e: e.activation(out=sl(i, 5), in_=sl(i, 4), func=AF.Sqrt))(i), R=[k(i, 4)], W=[k(i, 5)])
        for i in range(n_):
            S.op('dve', (lambda i: lambda e: e.reciprocal(out=sl(i, 6), in_=sl(i, 5)))(i), R=[k(i, 5)], W=[k(i, 6)])
        for i in range(n_):
            S.op('dve', (lambda i: lambda e: e.scalar_tensor_tensor(out=sl(i, 7), in0=sl(i, 2), scalar=-1.0, in1=sl(i, 6), op0=ALU.mult, op1=ALU.mult))(i), R=[k(i, 2), k(i, 6)], W=[k(i, 7)])
        for i, (y, yk) in enumerate(tiles):
            S.op('act', (lambda i, y: lambda e: e.activation(out=y, in_=y, func=AF.Identity, scale=sl(i, 6), bias=sl(i, 7)))(i, y), R=[yk, k(i, 6), k(i, 7)], W=[yk])
        for i, (y, yk) in enumerate(tiles):
            S.op('dve', (lambda y: lambda e: e.tensor_tensor(out=y, in0=y, in1=LN['g'], op=ALU.mult))(y), R=[yk, 'lng'], W=[yk])
            S.op('dve', (lambda y: lambda e: e.tensor_tensor(out=y, in0=y, in1=LN['b'], op=ALU.add))(y), R=[yk, 'lnb'], W=[yk])

    def layer_norm_tile(y, ykey, eps, outs, tmp_sq, st):
        S.op('act', lambda e: e.activation(out=tmp_sq, in_=y, func=AF.Square), R=[ykey], W=['lnsq'])
        S.op('dve', lambda e: e.reduce_sum(out=st[:, 0:1], in_=tmp_sq, axis=AX.X), R=['lnsq'], W=['lnst0'])
        S.op('dve', lambda e: e.reduce_sum(out=st[:, 1:2], in_=y, axis=AX.X), R=[ykey], W=['lnst1'])
        S.op('dve', lambda e: e.tensor_scalar(out=st[:, 2:3], in0=st[:, 1:2], scalar1=1.0 / 1024, scalar2=None, op0=ALU.mult), R=['lnst1'], W=['lnst2'])
        S.op('dve', lambda e: e.tensor_tensor(out=st[:, 3:4], in0=st[:, 2:3], in1=st[:, 2:3], op=ALU.mult), R=['lnst2'], W=['lnst3'])
        S.op('dve', lambda e: e.scalar_tensor_tensor(out=st[:, 4:5], in0=st[:, 0:1], scalar=1.0 / 1024, in1=st[:, 3:4], op0=ALU.mult, op1=ALU.subtract), R=['lnst0', 'lnst3'], W=['lnst4'])
        S.op('dve', lambda e: e.tensor_scalar(out=st[:, 4:5], in0=st[:, 4:5], scalar1=float(eps), scalar2=None, op0=ALU.add), R=['lnst4'], W=['lnst4'])
        S.op('act', lambda e: e.activation(out=st[:, 5:6], in_=st[:, 4:5], func=AF.Sqrt), R=['lnst4'], W=['lnst5'])
        S.op('dve', lambda e: e.reciprocal(out=st[:, 6:7], in_=st[:, 5:6]), R=['lnst5'], W=['lnst6'])
        S.op('dve', lambda e: e.scalar_tensor_tensor(out=st[:, 7:8], in0=st[:, 2:3], scalar=-1.0, in1=st[:, 6:7], op0=ALU.mult, op1=ALU.mult), R=['lnst2', 'lnst6'], W=['lnst7'])
        S.op('act', lambda e: e.activation(out=y, in_=y, func=AF.Identity, scale=st[:, 6:7], bias=st[:, 7:8]), R=[ykey, 'lnst6', 'lnst7'], W=[ykey])
        S.op('dve', lambda e: e.tensor_tensor(out=y, in0=y, in1=LN['g'], op=ALU.mult), R=[ykey, 'lng'], W=[ykey])
        S.op('dve', lambda e: e.tensor_tensor(out=y, in0=y, in1=LN['b'], op=ALU.add), R=[ykey, 'lnb'], W=[ykey])

    def ffn(xTv_src, xoff, xkey_fn, ntok, wgu, wdown, res_dram, eps, consumer, tagp):
        A.mark()
        NB = 1024
        hhT = A.bf(NJ * NB)
        hhv = hhT.rearrange("p (j t) -> p j t", j=NJ)
        wg = [A.bf(8 * 512) for _ in range(2)]
        wd = [A.bf(1024) for _ in range(3)]
        sg = [A.f32(512) for _ in range(2)]
        ybuf = [A.f32(1024) for _ in range(8)]
        tmp_sq = A.f32(1024)
        st = A.f32(32)
        wgu_v = wgu.rearrange("(c p) f -> p c f", p=128)
        deferred = []

        def run_deferred():
            for f_ in deferred:
                f_()
            del deferred[:]
        wi = 0
        di = 0
        for blk in range(ntok // NB):
            t0 = blk * NB
            for jg in range(NJ // 2):
                w = wg[wi % 2]
                wk = 'wg%d' % (wi % 2)
                wv = w.rearrange("p (c f) -> p c f", c=8)
                wi += 1
                S.dma('pool', (lambda wv, jg: lambda e: e.dma_start(out=wv[:, :, 0:256], in_=wgu_v[:, :, jg * 256:(jg + 1) * 256]))(wv, jg), R=[], W=[wk + 'g'])
                S.dma('pool', (lambda wv, jg: lambda e: e.dma_start(out=wv[:, :, 256:512], in_=wgu_v[:, :, DFF + jg * 256:DFF + (jg + 1) * 256]))(wv, jg), R=[], W=[wk + 'u'])
                for jj in range(2):
                    j = jg * 2 + jj
                    for sb in range(NB // 512):
                        bg = bank()
                        bu = bank()
                        c0 = xoff + t0 + sb * 512
                        xk = xkey_fn(t0 + sb * 512, 512)
                        for dc in range(8):
                            S.op('pe', (lambda wv, dc, jj, bg, c0: lambda e: e.matmul(PS(bg), lhsT=wv[:, dc, jj * 128:(jj + 1) * 128], rhs=xTv_src[:, dc, c0:c0 + 512], start=(dc == 0), stop=(dc == 7)))(wv, dc, jj, bg, c0),
                                 R=[wk + 'g'] + xk, W=[PK(bg)], inc=(dc == 7))
                        for dc in range(8):
                            S.op('pe', (lambda wv, dc, jj, bu, c0: lambda e: e.matmul(PS(bu), lhsT=wv[:, dc, 256 + jj * 128:256 + (jj + 1) * 128], rhs=xTv_src[:, dc, c0:c0 + 512], start=(dc == 0), stop=(dc == 7)))(wv, dc, jj, bu, c0),
                                 R=[wk + 'u'] + xk, W=[PK(bu)], inc=(dc == 7))
                        sgt = sg[(j * 2 + sb) % 2]
                        sgk = 'sg%d' % ((j * 2 + sb) % 2)
                        S.op('act', (lambda sgt, bg: lambda e: e.activation(out=sgt, in_=PS(bg), func=AF.Silu))(sgt, bg), R=[PK(bg)], W=[sgk])
                        S.op('dve', (lambda sgt, bu, j, sb: lambda e: e.tensor_tensor(out=hhv[:, j, sb * 512:(sb + 1) * 512], in0=sgt, in1=PS(bu), op=ALU.mult))(sgt, bu, j, sb),
                             R=[sgk, PK(bu)], W=['hh%d_%d' % (j, sb)])
            if blk == 0:
                dump(tagp + '_hh0', hhv[:, 0, :], ['hh0_0', 'hh0_1'])
                dump(tagp + '_hh21', hhv[:, 21, :], ['hh21_0', 'hh21_1'])
                dump(tagp + '_xT0', xTv_src[:, 0, xoff:xoff + 512], xkey_fn(0, 512))
            if os.environ.get('FFN_STOP') == 'up':
                continue
            for rnd in range(NB // 512):
                banks = [[bank(), bank()] for _ in range(4)]
                for j in range(NJ):
                    w = wd[di % 3]
                    wk = 'wd%d' % (di % 3)
                    di += 1
                    S.dma('pool', (lambda w, j: lambda e: e.dma_start(out=w, in_=wdown[j * 128:(j + 1) * 128, :]))(w, j), R=[], W=[wk])
                    for tt in range(4):
                        for dh in range(2):
                            b = banks[tt][dh]
                            S.op('pe', (lambda w, j, tt, dh, b, rnd: lambda e: e.matmul(PS(b), lhsT=hhv[:, j, rnd * 512 + tt * 128:rnd * 512 + (tt + 1) * 128], rhs=w[:, dh * 512:(dh + 1) * 512], start=(j == 0), stop=(j == NJ - 1)))(w, j, tt, dh, b, rnd),
                                 R=[wk, 'hh%d_%d' % (j, rnd)], W=[PK(b)], inc=(j == NJ - 1 or (tt == 3 and dh == 1)))
                cur = []
                for tt in range(4):
                    ti = (t0 + rnd * 512) // 128 + tt
                    y = ybuf[ti % 8]
                    yk = 'y%d' % (ti % 8)
                    S.dma('sp', (lambda y, ti: lambda e: e.dma_start(out=y, in_=res_dram[ti * 128:(ti + 1) * 128, :]))(y, ti), R=[], W=[yk])
                    for dh in range(2):
                        b = banks[tt][dh]
                        S.op('dve', (lambda y, b, dh: lambda e: e.scalar_tensor_tensor(out=y[:, dh * 512:(dh + 1) * 512], in0=PS(b), scalar=0.5 / ALPHA, in1=y[:, dh * 512:(dh + 1) * 512], op0=ALU.mult, op1=ALU.add))(y, b, dh),
                             R=[PK(b), yk], W=[yk])
                    cur.append((ti, y, yk))
                run_deferred()
                layer_norm_group([(y, yk) for (ti, y, yk) in cur], eps, tmp_sq, st)
                for (ti, y, yk) in cur:
                    later = consumer(ti, y, yk)
                    if later is not None:
                        deferred.append(later)
        run_deferred()
        A.release()

    def prep_xT(src_dram, ntok, dstv, doff, keyfn):
        A.mark()
        xb = [A.bf(1024) for _ in range(2)]
        for ti in range(ntok // 128):
            b = xb[ti % 2]
            bk = 'xb%d' % (ti % 2)
            S.dma('pool', (lambda b, ti: lambda e: e.dma_start(out=b, in_=src_dram[ti * 128:(ti + 1) * 128, :]))(b, ti), R=[], W=[bk])
            transpose_to(b, [bk], 8, dstv[:, :, doff + ti * 128:doff + (ti + 1) * 128], keyfn(ti * 128, 128))
        A.release()

    def mixer(half, full):
        A.mark()
        w_in = dr['w_in'].rearrange("(c p) f -> p c f", p=128)
        A.mark()
        wret = A.bf(8 * RETC)
        wretv = wret.rearrange("p (c f) -> p c f", c=8)
        c32r = A.f32(C32R_N)
        S.dma('sp', lambda e: e.dma_start(out=c32r, in_=dr['c32r']), R=[], W=['c32r'])

        def c32rv(name):
            o, n = C32R_OFF[name]
            return c32r[:, o:o + n]
        cos_t = A.f32(512)
        sin_t = A.f32(512)
        S.dma('sp', lambda e: e.dma_start(out=cos_t, in_=dr['cos'][:, half * 512:(half + 1) * 512]), R=[], W=['cos'])
        S.dma('sp', lambda e: e.dma_start(out=sin_t, in_=dr['sin'][:, half * 512:(half + 1) * 512]), R=[], W=['sin'])
        load_ptiles(['ret_gn_g', 'ret_gn_b'])
        for q4 in range(4):
            S.dma('pool', (lambda q4: lambda e: e.dma_start(out=wretv[:, :, q4 * 512:(q4 + 1) * 512], in_=w_in[:, :, q4 * 512:(q4 + 1) * 512]))(q4), R=[], W=['wret%d' % q4])
        qr = A.bf(512)
        kr = A.bf(512)
        vb = A.bf(512)
        vdb = A.bf(512)
        sgg = A.f32(512)
        sgg_b = A.f32(512)
        t1 = A.f32(256)
        t2 = A.f32(256)
        t3 = A.f32(256)
        t4 = A.f32(256)
        qT = A.bf(512)
        qsTm = A.bf(1024)
        kTm = A.bf(1024)
        S.op('pool', lambda e: e.memset(qsTm, 0.0), R=[], W=['qsTm'])
        S.op('pool', lambda e: e.memset(kTm, 0.0), R=[], W=['kTm'])
        scm = A.bf(1024)
        o_sb = A.f32(512)
        o_sq = A.f32(512)
        gst = A.f32(64)
        mixr = A.bf(512)
        ret_deferred = []
        kdec = c32v('kdec')
        sggs = [sgg, sgg_b]

        def ret_head(ti):
            c0 = XO + ti * 128
            xk = ['x1T_%d' % ti]
            cosv = cos_t[:, ti * 32:(ti + 1) * 32].unsqueeze(1).broadcast_to([128, 8, 32])
            sinv = sin_t[:, ti * 32:(ti + 1) * 32].unsqueeze(1).broadcast_to([128, 8, 32])
            pb = {}
            for nm in (['q', 'k', 'v', 'g'] if full else ['k', 'v']):
                q4 = ['q', 'k', 'v', 'g'].index(nm)
                b = bank()
                pb[nm] = b
                for dc in range(8):
                    S.op('pe', (lambda dc, b, q4, c0: lambda e: e.matmul(PS(b), lhsT=x1Tv[:, dc, c0:c0 + 128], rhs=wretv[:, dc, q4 * 512:(q4 + 1) * 512], start=(dc == 0), stop=(dc == 7)))(dc, b, q4, c0),
                         R=xk + ['wret%d' % q4], W=[PK(b)], inc=(dc == 7))

            def rotary(b, dst, dkey):
                src = PS(b).rearrange("p (h d) -> p h d", h=8)
                dv = dst.rearrange("p (h d) -> p h d", h=8)
                a1 = t1.rearrange("p (h d) -> p h d", h=8)
                a2 = t2.rearrange("p (h d) -> p h d", h=8)
                a3 = t3.rearrange("p (h d) -> p h d", h=8)
                a4 = t4.rearrange("p (h d) -> p h d", h=8)
                pk = PK(b)
                S.op('dve', lambda e: e.tensor_tensor(out=a1, in0=src[:, :, 0:32], in1=cosv, op=ALU.mult), R=[pk, 'cos'], W=['rt1'])
                S.op('dve', lambda e: e.tensor_tensor(out=a2, in0=src[:, :, 32:64], in1=sinv, op=ALU.mult), R=[pk, 'sin'], W=['rt2'])
                S.op('dve', lambda e: e.tensor_tensor(out=a3, in0=src[:, :, 0:32], in1=sinv, op=ALU.mult), R=[pk, 'sin'], W=['rt3'])
                S.op('dve', lambda e: e.tensor_tensor(out=a4, in0=src[:, :, 32:64], in1=cosv, op=ALU.mult), R=[pk, 'cos'], W=['rt4'])
                S.op('pool', lambda e: e.tensor_tensor(out=dv[:, :, 0:32], in0=a1, in1=a2, op=ALU.subtract), R=['rt1', 'rt2'], W=[dkey])
                S.op('pool', lambda e: e.tensor_tensor(out=dv[:, :, 32:64], in0=a3, in1=a4, op=ALU.add), R=['rt3', 'rt4'], W=[dkey])

            rotary(pb['k'], kr, 'kr')
            if full:
                rotary(pb['q'], qr, 'qr')
            bv = pb['v']
            S.op('act', lambda e: e.activation(out=vb, in_=PS(bv), func=AF.Copy), R=[PK(bv)], W=['vb'])
            S.op('dve', lambda e: e.tensor_tensor(out=vdb, in0=PS(bv), in1=c32rv('kdecf'), op=ALU.mult), R=[PK(bv), 'c32r', 'vb'], W=['vdb'])
            if full:
                bgg = pb['g']
                sg_ = sggs[ti % 2]
                S.op('act', lambda e: e.activation(out=sg_, in_=PS(bgg), func=AF.Silu), R=[PK(bgg)], W=['sgg%d' % (ti % 2)])

        def ret_mid(ti):
            if full:
                b = bank()
                pk = PK(b)
                for i in range(4):
                    S.op('pe', (lambda i, b: lambda e: e.transpose(out=PSB(b)[:, i * 128:(i + 1) * 128], in_=qr[:, i * 128:(i + 1) * 128], identity=ident))(i, b), R=['qr', 'cb'], W=[pk], inc=False)
                for i in range(4):
                    S.op('pe', (lambda i, b: lambda e: e.transpose(out=PSB(b)[:, 512 + i * 128:512 + (i + 1) * 128], in_=kr[:, i * 128:(i + 1) * 128], identity=ident))(i, b), R=['kr', 'cb'], W=[pk], inc=(i == 3))
                S.op('act', lambda e: e.activation(out=qT, in_=PSB(b)[:, 0:512], func=AF.Copy), R=[pk], W=['qT'])
                for hp in range(2):
                    rs_ = slice(hp * 64, (hp + 1) * 64)
                    S.op('dve', lambda e: e.tensor_tensor(out=qsTm[rs_, :].rearrange("p (a c t) -> p a c t", a=4, c=2)[:, :, hp, :],
                                                          in0=PSB(b)[rs_, 0:512].rearrange("p (a t) -> p a t", a=4),
                                                          in1=c32rv('qdec')[rs_, :].rearrange("p (a t) -> p a t", a=4), op=ALU.mult), R=[pk, 'c32r'], W=['qsTm'])
                    S.op('act', lambda e: e.activation(out=kTm[rs_, :].rearrange("p (a c t) -> p a c t", a=4, c=2)[:, :, hp, :],
                                                       in_=PSB(b)[rs_, 512:1024].rearrange("p (a t) -> p a t", a=4), func=AF.Copy), R=[pk], W=['kTm'])
                sb_ = [bank(), bank()]
                for h in range(8):
                    pr = h // 2
                    bs = sb_[h // 4]
                    S.op('pe', lambda e: e.matmul(PS(bs)[:, (h % 4) * 128:(h % 4 + 1) * 128], lhsT=kTm[:, h * 128:(h + 1) * 128], rhs=qT[:, pr * 128:(pr + 1) * 128], start=True, stop=True),
                         R=['kTm', 'qT'], W=[PK(bs)], inc=(h % 4 == 3))
                for hb in range(2):
                    bs = sb_[hb]
                    S.op('dve', lambda e: e.tensor_tensor(out=scm[:, hb * 512:(hb + 1) * 512], in0=PS(bs), in1=c32rv('dmask')[:, hb * 512:(hb + 1) * 512], op=ALU.mult),
                         R=[PK(bs), 'c32r'], W=['scm%d' % hb])
                bo = bank()
                for h in range(8):
                    pr = h // 2
                    S.op('pe', lambda e: e.matmul(PS(bo)[:, h * 64:(h + 1) * 64], lhsT=scm[:, h * 128:(h + 1) * 128], rhs=vb[:, h * 64:(h + 1) * 64], start=True, stop=False),
                         R=['scm%d' % (h // 4), 'vb'], W=[PK(bo)], inc=False)
                    S.op('pe', lambda e: e.matmul(PS(bo)[:, h * 64:(h + 1) * 64], lhsT=qsTm[:, h * 128:(h + 1) * 128], rhs=retSb[:, pr * 64:(pr + 1) * 64], start=False, stop=True),
                         R=['qsTm', 'retSb'], W=[PK(bo)], inc=(h == 7))
                S.op('act', lambda e: e.activation(out=o_sb, in_=PS(bo), func=AF.Copy), R=[PK(bo)], W=['o_sb'])
            bk_ = bank()
            for pr in range(4):
                S.op('pe', lambda e: e.matmul(PS(bk_)[:, pr * 128:(pr + 1) * 128], lhsT=kr[:, pr * 128:(pr + 1) * 128], rhs=vdb[:, pr * 128:(pr + 1) * 128], start=True, stop=True),
                     R=['kr', 'vdb'], W=[PK(bk_)], inc=(pr == 3))
            S.op('pool', lambda e: e.tensor_tensor(out=retS, in0=retS, in1=c32rv('gam128f'), op=ALU.mult), R=['retS', 'c32r', 'retSb'], W=['retS'])
            for hp in range(2):
                S.op('dve', lambda e: e.tensor_tensor(out=retS[hp * 64:(hp + 1) * 64, :].rearrange("p (a e) -> p a e", a=4), in0=retS[hp * 64:(hp + 1) * 64, :].rearrange("p (a e) -> p a e", a=4),
                                                      in1=PS(bk_)[hp * 64:(hp + 1) * 64, :].rearrange("p (a f) -> p a f", a=4)[:, :, hp * 64:(hp + 1) * 64], op=ALU.add),
                     R=[PK(bk_), 'retS'], W=['retS'])
            S.op('act', lambda e: e.activation(out=retSb, in_=retS, func=AF.Copy), R=['retS'], W=['retSb'])

        def ret_tail(ti):
            if not full:
                return
            for f_ in ret_deferred:
                f_()
            del ret_deferred[:]
            group_norm_out(o_sb, 'o_sb', o_sq, 'o_sq', gst, 1e-5, ptile['ret_gn_g'], ptile['ret_gn_b'], 'pt_ret_gn_g', 'pt_ret_gn_b')
            sg_ = sggs[ti % 2]
            S.op('dve', lambda e: e.tensor_tensor(out=mixr, in0=o_sb, in1=sg_, op=ALU.mult), R=['o_sb', 'sgg%d' % (ti % 2)], W=['mixr'])
            ret_deferred.append(lambda: transpose_to(mixr, ['mixr'], 4, mixTv[:, 0:4, ti * 128:(ti + 1) * 128], ['mixT_r%d' % ti]))

        ret_head(0)
        for ti in range(NT_HALF):
            ret_mid(ti)
            if ti + 1 < NT_HALF:
                ret_head(ti + 1)
            ret_tail(ti)
        for f_ in ret_deferred:
            f_()
        A.release()
        S.barrier()
        if os.environ.get('MIX_STOP') != 'ret':
            rwkv(half, full)
        if full:
            for c_ in range(8):
                dump('mixT%d' % c_, mixTv[:, c_, :], [])
            dump('retS', retS, [])
            dump('rwA', rwA, [])
        A.release()

    def group_norm_out(o, okey, sq, sqk, gst, eps, gt, bt, gk, bk):
        o3 = o.rearrange("p (h d) -> p h d", h=8)
        S.op('act', lambda e: e.activation(out=sq, in_=o, func=AF.Square), R=[okey], W=[sqk])
        S.op('dve', lambda e: e.tensor_reduce(out=gst[:, 0:8], in_=o3, axis=AX.X, op=ALU.add), R=[okey], W=['gn0'])
        S.op('dve', lambda e: e.tensor_reduce(out=gst[:, 8:16], in_=sq.rearrange("p (h d) -> p h d", h=8), axis=AX.X, op=ALU.add), R=[sqk], W=['gn1'])
        S.op('dve', lambda e: e.tensor_scalar(out=gst[:, 16:24], in0=gst[:, 0:8], scalar1=1.0 / 64, scalar2=None, op0=ALU.mult), R=['gn0'], W=['gn2'])
        S.op('dve', lambda e: e.tensor_tensor(out=gst[:, 24:32], in0=gst[:, 16:24], in1=gst[:, 16:24], op=ALU.mult), R=['gn2'], W=['gn3'])
        S.op('dve', lambda e: e.scalar_tensor_tensor(out=gst[:, 32:40], in0=gst[:, 8:16], scalar=1.0 / 64, in1=gst[:, 24:32], op0=ALU.mult, op1=ALU.subtract), R=['gn1', 'gn3'], W=['gn4'])
        S.op('dve', lambda e: e.tensor_scalar(out=gst[:, 32:40], in0=gst[:, 32:40], scalar1=float(eps), scalar2=None, op0=ALU.add), R=['gn4'], W=['gn4'])
        S.op('act', lambda e: e.activation(out=gst[:, 40:48], in_=gst[:, 32:40], func=AF.Sqrt), R=['gn4'], W=['gn5'])
        S.op('dve', lambda e: e.reciprocal(out=gst[:, 48:56], in_=gst[:, 40:48]), R=['gn5'], W=['gn6'])
        S.op('dve', lambda e: e.tensor_tensor(out=o3, in0=o3, in1=gst[:, 16:24].unsqueeze(2).broadcast_to([128, 8, 64]), op=ALU.subtract), R=[okey, 'gn2'], W=[okey])
        S.op('dve', lambda e: e.tensor_tensor(out=o3, in0=o3, in1=gst[:, 48:56].unsqueeze(2).broadcast_to([128, 8, 64]), op=ALU.mult), R=[okey, 'gn6'], W=[okey])
        S.op('pool', lambda e: e.tensor_tensor(out=o, in0=o, in1=gt, op=ALU.mult), R=[okey, gk], W=[okey])
        S.op('pool', lambda e: e.tensor_tensor(out=o, in0=o, in1=bt, op=ALU.add), R=[okey, bk], W=[okey])

    def rwkv(half, full):
        A.mark()
        w_in = dr['w_in'].rearrange("(c p) f -> p c f", p=128)
        load_ptiles(['rw_k_k', 'rw_k_a', 'rw_r_k', 'rw_gn_g', 'rw_gn_b'])
        Wa = A.bf(8 * RWC)
        Wb = A.bf(8 * RWC)
        Wav = Wa.rearrange("p (c f) -> p c f", c=8)
        Wbv = Wb.rearrange("p (c f) -> p c f", c=8)
        A.mark()
        mu_t = A.f32(RWC)
        omu_t = A.f32(RWC)
        stg = [A.f32(RWC) for _ in range(2)]
        S.dma('sp', lambda e: e.dma_start(out=mu_t, in_=dr['rw_mu'].partition_broadcast(128)), R=[], W=['mu'])
        S.op('dve', lambda e: e.tensor_scalar(out=omu_t, in0=mu_t, scalar1=-1.0, scalar2=1.0, op0=ALU.mult, op1=ALU.add), R=['mu'], W=['omu'])
        for dc in range(8):
            sgb = stg[dc % 2]
            sk = 'stg%d' % (dc % 2)
            S.dma('sp', (lambda sgb, dc: lambda e: e.dma_start(out=sgb, in_=dr['w_in'][dc * 128:(dc + 1) * 128, RETC:INC]))(sgb, dc), R=[], W=[sk])
            S.op('dve', (lambda sgb, dc: lambda e: e.tensor_tensor(out=Wav[:, dc, :], in0=sgb, in1=omu_t, op=ALU.mult))(sgb, dc), R=[sk, 'omu'], W=['Wa'])
            S.op('pool', (lambda sgb, dc: lambda e: e.tensor_tensor(out=Wbv[:, dc, :], in0=sgb, in1=mu_t, op=ALU.mult))(sgb, dc), R=[sk, 'mu'], W=['Wb'])
        A.release()
        wup = A.f32(512)
        aup = A.f32(512)
        gup1 = A.bf(512)
        gup2 = A.bf(512)
        w0r = A.f32(512)
        a0r = A.f32(512)
        for t_, k_ in ((wup, 'wup'), (aup, 'aup'), (gup2, 'gup2'), (w0r, 'w0r'), (a0r, 'a0r')):
            S.op('pool', (lambda t_: lambda e: e.memset(t_, 0.0))(t_), R=[], W=[k_])
        S.dma('sp', lambda e: e.dma_start(out=wup[0:64, :], in_=dr['rw_w_up']), R=[], W=['wup'])
        S.dma('sp', lambda e: e.dma_start(out=aup[64:128, :], in_=dr['rw_a_up']), R=[], W=['aup'])
        S.dma('pool', lambda e: e.dma_start(out=gup1, in_=dr['rw_g_up'][0:128, :]), R=[], W=['gup1'])
        S.dma('pool', lambda e: e.dma_start(out=gup2[96:128, :], in_=dr['rw_g_up'][128:160, :]), R=[], W=['gup2'])
        S.dma('sp', lambda e: e.dma_start(out=w0r[0:1, :], in_=dr['rw_w0']), R=[], W=['w0r'])
        S.dma('sp', lambda e: e.dma_start(out=a0r[0:1, :], in_=dr['rw_a0']), R=[], W=['a0r'])
        ones_r = c32v('ones')
        twT = A.f32(128)
        sgd1 = A.bf(128)
        sgd2 = A.bf(128)
        sw = A.f32(512)
        a_t = A.f32(512)
        gi_t = A.f32(512)
        kkr = A.f32(512)
        sq = A.f32(512)
        kp = A.f32(512)
        u1 = sq
        g_t = sw
        v32 = A.f32(512)
        st = A.f32(64)
        gst2 = A.f32(64)
        r32 = A.f32(512)
        k32 = A.f32(512)
        gp_t = k32
        gC = A.f32(4)
        rh = A.bf(512)
        kt = A.bf(512)
        bt_ = A.bf(512)
        kh = A.bf(512)
        vb = A.bf(512)
        XT = A.bf(4 * 256)
        XTv = XT.rearrange("p (a b t) -> p a b t", a=4, b=2)
        btT = A.bf(512)
        khTm = A.bf(1024)
        rhTm = A.bf(1024)
        btTm = A.bf(1024)
        ktTm = A.bf(1024)
        for t_, k_ in ((khTm, 'khTm'), (rhTm, 'rhTm'), (btTm, 'btTm'), (ktTm, 'ktTm')):
            S.op('pool', (lambda t_: lambda e: e.memset(t_, 0.0))(t_), R=[], W=[k_])
        LkT = A.bf(1024)
        MbT = A.bf(1024)
        MkT = A.bf(1024)
        Abuf = [[A.bf(256) for _ in range(2)] for _ in range(4)]
        BQbuf = [[A.bf(512) for _ in range(2)] for _ in range(4)]
        TT = A.bf(1024)
        Xb = A.bf(512)
        Ub = A.bf(512)
        mixw = A.bf(512)
        bsum = st[:, 56:64]
        rw_deferred = []

        for ti in range(NT_HALF):
            c0 = XO + ti * 128
            xk = ['x1T_%d' % ti] + (['x1T_%d' % (ti - 1)] if ti > 0 else ['x1T_prev'])
            def proj(nm):
                qi = ['r', 'k', 'v'].index(nm)
                b = bank()
                n = 0
                for (Wv, wkey, sh) in ((Wav, 'Wa', 0), (Wbv, 'Wb', 1)):
                    for dc in range(8):
                        S.op('pe', (lambda Wv, sh, dc, b, qi, n: lambda e: e.matmul(PS(b), lhsT=x1Tv[:, dc, c0 - sh:c0 - sh + 128], rhs=Wv[:, dc, qi * 512:(qi + 1) * 512], start=(n == 0), stop=(n == 15)))(Wv, sh, dc, b, qi, n),
                             R=xk + [wkey], W=[PK(b)], inc=(n == 15))
                        n += 1
                return b
            bl = bank()
            for gi_, (cs, cn) in enumerate(((1536, 128), (1664, 128), (1696, 128))):
                n = 0
                for (Wv, wkey, sh) in ((Wav, 'Wa', 0), (Wbv, 'Wb', 1)):
                    for dc in range(8):
                        S.op('pe', (lambda Wv, sh, dc, cs, cn, gi_, n: lambda e: e.matmul(PS(bl)[0:cn, gi_ * 128:(gi_ + 1) * 128], lhsT=Wv[:, dc, cs:cs + cn], rhs=x1Tv[:, dc, c0 - sh:c0 - sh + 128], start=(n == 0), stop=(n == 15)))(Wv, sh, dc, cs, cn, gi_, n),
                             R=xk + [wkey], W=[PK(bl)], inc=(n == 15 and gi_ == 2))
                        n += 1
            plk = PK(bl)
            S.op('act', lambda e: e.activation(out=twT[0:64, :], in_=PS(bl)[0:64, 0:128], func=AF.Tanh), R=[plk], W=['twT'])
            S.op('act', lambda e: e.activation(out=twT[64:128, :], in_=PS(bl)[64:128, 0:128], func=AF.Copy), R=[plk], W=['twT'])
            S.op('act', lambda e: e.activation(out=sgd1, in_=PS(bl)[:, 128:256], func=AF.Sigmoid), R=[plk], W=['sgd1'])
            S.op('act', lambda e: e.activation(out=sgd2, in_=PS(bl)[:, 256:384], func=AF.Sigmoid), R=[plk], W=['sgd2'])
            bk2 = proj('k')
            kk_ = PK(bk2)
            S.op('act', lambda e: e.activation(out=k32, in_=PS(bk2), func=AF.Copy), R=[kk_], W=['k32'])
            bw = bank()
            S.op('pe', lambda e: e.matmul(PS(bw), lhsT=twT, rhs=wup, start=True, stop=False), R=['twT', 'wup'], W=[PK(bw)], inc=False)
            S.op('pe', lambda e: e.matmul(PS(bw), lhsT=ones_r, rhs=w0r, start=False, stop=True), R=['c32', 'w0r'], W=[PK(bw)])
            ba = bank()
            S.op('pe', lambda e: e.matmul(PS(ba), lhsT=twT, rhs=aup, start=True, stop=False), R=['twT', 'aup'], W=[PK(ba)], inc=False)
            S.op('pe', lambda e: e.matmul(PS(ba), lhsT=ones_r, rhs=a0r, start=False, stop=True), R=['c32', 'a0r'], W=[PK(ba)])
            S.op('act', lambda e: e.activation(out=sw, in_=PS(bw), func=AF.Sigmoid), R=[PK(bw)], W=['sw'])
            S.op('act', lambda e: e.activation(out=a_t, in_=PS(ba), func=AF.Sigmoid), R=[PK(ba)], W=['a_t'])
            bv = proj('v')
            vk_ = PK(bv)
            S.op('act', lambda e: e.activation(out=v32, in_=PS(bv), func=AF.Copy), R=[vk_], W=['v32'])
            S.op('dve', lambda e: e.tensor_copy(out=vb, in_=PS(bv)), R=[vk_], W=['vbw'])
            bc = bank()
            S.op('pe', lambda e: e.matmul(PS(bc), lhsT=c32v('tri_incl'), rhs=sw, start=True, stop=True), R=['c32', 'sw'], W=[PK(bc)])
            bx = bank()
            S.op('pe', lambda e: e.matmul(PS(bx), lhsT=c32v('tri_strict'), rhs=sw, start=True, stop=True), R=['c32', 'sw'], W=[PK(bx)])
            bgc = bank()
            for pr in range(4):
                S.op('pe', (lambda pr: lambda e: e.matmul(PS(bgc)[:, pr:pr + 1], lhsT=sw[:, pr * 128:(pr + 1) * 128], rhs=c32v('negcol'), start=True, stop=True))(pr), R=['sw', 'c32'], W=[PK(bgc)], inc=(pr == 3))
            if full:
                br = proj('r')
                rk_ = PK(br)
                S.op('act', lambda e: e.activation(out=r32, in_=PS(br), func=AF.Copy), R=[rk_], W=['r32'])
            for f_ in rw_deferred:
                f_()
            del rw_deferred[:]
            S.op('dve', lambda e: e.tensor_tensor(out=kkr, in0=k32, in1=ptile['rw_k_k'], op=ALU.mult), R=['k32', 'pt_rw_k_k'], W=['kkr'])
            S.op('act', lambda e: e.activation(out=sq, in_=kkr, func=AF.Square), R=['kkr'], W=['sq'])
            S.op('dve', lambda e: e.tensor_reduce(out=st[:, 0:8], in_=sq.rearrange("p (h d) -> p h d", h=8), axis=AX.X, op=ALU.add), R=['sq'], W=['st0'])
            S.op('act', lambda e: e.activation(out=st[:, 8:16], in_=st[:, 0:8], func=AF.Sqrt), R=['st0'], W=['st1'])
            S.op('dve', lambda e: e.tensor_scalar(out=st[:, 8:16], in0=st[:, 8:16], scalar1=1e-12, scalar2=None, op0=ALU.max), R=['st1'], W=['st1'])
            S.op('dve', lambda e: e.reciprocal(out=st[:, 16:24], in_=st[:, 8:16]), R=['st1'], W=['st2'])
            S.op('dve', lambda e: e.tensor_tensor(out=kkr.rearrange("p (h d) -> p h d", h=8), in0=kkr.rearrange("p (h d) -> p h d", h=8), in1=st[:, 16:24].unsqueeze(2).broadcast_to([128, 8, 64]), op=ALU.mult), R=['kkr', 'st2'], W=['kkr'])
            S.op('dve', lambda e: e.scalar_tensor_tensor(out=u1, in0=a_t, scalar=-1.0, in1=ptile['rw_k_a'], op0=ALU.add, op1=ALU.mult), R=['a_t', 'pt_rw_k_a'], W=['sq'])
            S.op('dve', lambda e: e.scalar_tensor_tensor(out=kp, in0=u1, scalar=1.0, in1=k32, op0=ALU.add, op1=ALU.mult), R=['sq', 'k32'], W=['kp'])
            if full:
                S.op('dve', lambda e: e.tensor_tensor(out=u1, in0=r32, in1=kp, op=ALU.mult), R=['r32', 'kp', 'sq'], W=['sq'])
                S.op('pool', lambda e: e.tensor_tensor(out=u1, in0=u1, in1=ptile['rw_r_k'], op=ALU.mult), R=['sq', 'pt_rw_r_k'], W=['sq'])
                S.op('dve', lambda e: e.tensor_reduce(out=bsum, in_=u1.rearrange("p (h d) -> p h d", h=8), axis=AX.X, op=ALU.add), R=['sq'], W=['bsum'])
            S.op('act', lambda e: e.activation(out=g_t, in_=PS(bc), func=AF.Exp), R=[PK(bc)], W=['sw'])
            S.op('act', lambda e: e.activation(out=gi_t, in_=PS(bc), func=AF.Exp, scale=-1.0), R=[PK(bc)], W=['gi_t'])
            S.op('act', lambda e: e.activation(out=gp_t, in_=PS(bx), func=AF.Exp), R=[PK(bx)], W=['k32'])
            S.op('act', lambda e: e.activation(out=gC, in_=PS(bgc)[:, 0:4], func=AF.Exp), R=[PK(bgc)], W=['gC'])
            if full:
                S.op('dve', lambda e: e.tensor_tensor(out=rh, in0=r32, in1=g_t, op=ALU.mult), R=['r32', 'sw'], W=['rh'])
            S.op('pool', lambda e: e.tensor_tensor(out=kt, in0=kp, in1=gi_t, op=ALU.mult), R=['kp', 'gi_t'], W=['kt'])
            S.op('pool', lambda e: e.tensor_tensor(out=sq, in0=kkr, in1=a_t, op=ALU.mult), R=['kkr', 'a_t', 'sq'], W=['sq'])
            S.op('pool', lambda e: e.tensor_tensor(out=bt_, in0=sq, in1=gi_t, op=ALU.mult), R=['sq', 'gi_t'], W=['bt'])
            S.op('pool', lambda e: e.tensor_tensor(out=kh, in0=kkr, in1=gp_t, op=ALU.mult), R=['kkr', 'k32'], W=['kh'])
            if full:
                S.op('dve', lambda e: e.tensor_tensor(out=gi_t.rearrange("p (h d) -> p h d", h=8), in0=v32.rearrange("p (h d) -> p h d", h=8), in1=bsum.unsqueeze(2).broadcast_to([128, 8, 64]), op=ALU.mult), R=['v32', 'bsum', 'gi_t'], W=['gi_t'])
            b1 = bank()
            for i in range(4):
                S.op('pe', (lambda i: lambda e: e.transpose(out=PSB(b1)[:, i * 256:i * 256 + 128], in_=kh[:, i * 128:(i + 1) * 128], identity=ident))(i), R=['kh', 'cb'], W=[PK(b1)], inc=(i == 3 and not full))
                if full:
                    S.op('pe', (lambda i: lambda e: e.transpose(out=PSB(b1)[:, i * 256 + 128:i * 256 + 256], in_=rh[:, i * 128:(i + 1) * 128], identity=ident))(i), R=['rh', 'cb'], W=[PK(b1)], inc=(i == 3))
            if full:
                S.op('act', lambda e: e.activation(out=XT, in_=PSB(b1), func=AF.Copy), R=[PK(b1)], W=['XT'])
            else:
                S.op('act', lambda e: e.activation(out=XT.rearrange("p (a c t) -> p a c t", a=4, c=2)[:, :, 0, :], in_=PSB(b1).rearrange("p (a c t) -> p a c t", a=4, c=2)[:, :, 0, :], func=AF.Copy), R=[PK(b1)], W=['XT'])
            for hp in range(2):
                rs_ = slice(hp * 64, (hp + 1) * 64)
                srcv = PSB(b1)[rs_, :].rearrange("p (a c t) -> p a c t", a=4, c=2)
                S.op('act', lambda e: e.activation(out=khTm[rs_, :].rearrange("p (a c t) -> p a c t", a=4, c=2)[:, :, hp, :], in_=srcv[:, :, 0, :], func=AF.Copy), R=[PK(b1)], W=['khTm'])
                if full:
                    S.op('dve', lambda e: e.tensor_copy(out=rhTm[rs_, :].rearrange("p (a c t) -> p a c t", a=4, c=2)[:, :, hp, :], in_=srcv[:, :, 1, :]), R=[PK(b1)], W=['rhTm'])
            b2 = bank()
            for i in range(4):
                S.op('pe', (lambda i: lambda e: e.transpose(out=PSB(b2)[:, i * 128:(i + 1) * 128], in_=bt_[:, i * 128:(i + 1) * 128], identity=ident))(i), R=['bt', 'cb'], W=[PK(b2)], inc=False)
            for i in range(4):
                S.op('pe', (lambda i: lambda e: e.transpose(out=PSB(b2)[:, 512 + i * 128:512 + (i + 1) * 128], in_=kt[:, i * 128:(i + 1) * 128], identity=ident))(i), R=['kt', 'cb'], W=[PK(b2)], inc=(i == 3))
            S.op('dve', lambda e: e.tensor_copy(out=btT, in_=PSB(b2)[:, 0:512]), R=[PK(b2)], W=['btT'])
            for hp in range(2):
                rs_ = slice(hp * 64, (hp + 1) * 64)
                S.op('act', lambda e: e.activation(out=btTm[rs_, :].rearrange("p (a c t) -> p a c t", a=4, c=2)[:, :, hp, :], in_=PSB(b2)[rs_, 0:512].rearrange("p (a t) -> p a t", a=4), func=AF.Copy), R=[PK(b2)], W=['btTm'])
                S.op('dve', lambda e: e.tensor_copy(out=ktTm[rs_, :].rearrange("p (a c t) -> p a c t", a=4, c=2)[:, :, hp, :], in_=PSB(b2)[rs_, 512:1024].rearrange("p (a t) -> p a t", a=4)), R=[PK(b2)], W=['ktTm'])
            for pr in range(4):
                bm1, bm2, bm3 = bank(), bank(), bank()
                for hp in range(2):
                    ps_ = slice(hp * 64, (hp + 1) * 64)
                    NX_ = 256 if full else 128
                    rhs_x = XT[:, pr * 256:pr * 256 + NX_]
                    hh_ = pr * 2 + hp
                    S.op('pe', (lambda hp, ps_, rhs_x, bm1: lambda e: e.matmul(PS(bm1)[:, hp * 256:hp * 256 + NX_], lhsT=btTm[:, hh_ * 128:(hh_ + 1) * 128], rhs=rhs_x, start=True, stop=True))(hp, ps_, rhs_x, bm1),
                         R=['btTm', 'XT'], W=[PK(bm1)], inc=(hp == 1))
                    S.op('pe', (lambda hp, ps_, rhs_x, bm2: lambda e: e.matmul(PS(bm2)[:, hp * 256:hp * 256 + NX_], lhsT=ktTm[:, hh_ * 128:(hh_ + 1) * 128], rhs=rhs_x, start=True, stop=True))(hp, ps_, rhs_x, bm2),
                         R=['ktTm', 'XT'], W=[PK(bm2)], inc=(hp == 1))
                    S.op('pe', (lambda hp, ps_, bm3: lambda e: e.matmul(PS(bm3)[:, hp * 128:(hp + 1) * 128], lhsT=khTm[:, hh_ * 128:(hh_ + 1) * 128], rhs=btT[:, pr * 128:(pr + 1) * 128], start=True, stop=True))(hp, ps_, bm3),
                         R=['btT', 'khTm'], W=[PK(bm3)], inc=(hp == 1))
                A0 = Abuf[pr][0]
                BQ0 = BQbuf[pr][0].rearrange("p (h x) -> p h x", h=2)
                m1v = cbv('m1').rearrange("p (h x) -> p h x", h=2)
                p1v = PS(bm1).rearrange("p (h x) -> p h x", h=2)
                S.op('dve', (lambda BQ0, p1v, m1v: lambda e: e.tensor_tensor(out=BQ0[:, :, 0:128], in0=p1v[:, :, 0:128], in1=m1v[:, :, 0:128], op=ALU.mult))(BQ0, p1v, m1v),
                     R=[PK(bm1), 'cb'], W=['BQ%d_0' % pr])
                S.op('pool', (lambda BQ0: lambda e: e.tensor_copy(out=BQ0[:, :, 128:256], in_=ident.unsqueeze(1).broadcast_to([128, 2, 128])))(BQ0), R=['cb'], W=['BQ%d_0' % pr])
                if full:
                    S.op('dve', (lambda p1v, m1v: lambda e: e.tensor_tensor(out=MbT[:, pr * 256:(pr + 1) * 256].rearrange("p (h x) -> p h x", h=2), in0=p1v[:, :, 128:256], in1=m1v[:, :, 128:256], op=ALU.mult))(p1v, m1v),
                         R=[PK(bm1), 'cb'], W=['MbT%d' % pr])
                m2v = cbv('m2').rearrange("p (h x) -> p h x", h=2)
                p2v = PS(bm2).rearrange("p (h x) -> p h x", h=2)
                S.op('dve', (lambda p2v, m2v: lambda e: e.tensor_tensor(out=LkT[:, pr * 256:(pr + 1) * 256].rearrange("p (h x) -> p h x", h=2), in0=p2v[:, :, 0:128], in1=m2v[:, :, 0:128], op=ALU.mult))(p2v, m2v),
                     R=[PK(bm2), 'cb'], W=['LkT%d' % pr])
                if full:
                    S.op('dve', (lambda p2v, m2v: lambda e: e.tensor_tensor(out=MkT[:, pr * 256:(pr + 1) * 256].rearrange("p (h x) -> p h x", h=2), in0=p2v[:, :, 128:256], in1=m2v[:, :, 128:256], op=ALU.mult))(p2v, m2v),
                         R=[PK(bm2), 'cb'], W=['MkT%d' % pr])
                S.op('dve', (lambda A0, bm3: lambda e: e.tensor_tensor(out=A0, in0=PS(bm3)[:, 0:256], in1=cbv('m3'), op=ALU.mult))(A0, bm3), R=[PK(bm3), 'cb'], W=['A%d_0' % pr])
            for lv in range(7):
                last = (lv == 6)
                for pr in range(4):
                    cur, nxt = lv % 2, (lv + 1) % 2
                    Ac = Abuf[pr][cur].rearrange("p (h x) -> p h x", h=2)
                    BQc = BQbuf[pr][cur].rearrange("p (h x) -> p h x", h=2)
                    ak, bqk = 'A%d_%d' % (pr, cur), 'BQ%d_%d' % (pr, cur)
                    if not last:
                        bA = bank()
                        for hp in range(2):
                            S.op('pe', (lambda hp, BQc, Ac, bA: lambda e: e.matmul(PS(bA)[:, hp * 128:(hp + 1) * 128], lhsT=BQc[:, hp, 0:128], rhs=Ac[:, hp, :], start=True, stop=True))(hp, BQc, Ac, bA),
                                 R=[ak, bqk], W=[PK(bA)], inc=(hp == 1))
                        bB = bank()
                        for hp in range(2):
                            S.op('pe', (lambda hp, BQc, Ac, bB: lambda e: e.matmul(PS(bB)[:, hp * 256:hp * 256 + 128], lhsT=Ac[:, hp, :], rhs=BQc[:, hp, 0:128], start=True, stop=True))(hp, BQc, Ac, bB),
                                 R=[ak, bqk], W=[PK(bB)], inc=False)
                            S.op('pe', (lambda hp, BQc, Ac, bB: lambda e: e.matmul(PS(bB)[:, hp * 256 + 128:hp * 256 + 256], lhsT=Ac[:, hp, :], rhs=BQc[:, hp, 128:256], start=True, stop=False))(hp, BQc, Ac, bB),
                                 R=[ak, bqk], W=[PK(bB)], inc=False)
                            S.op('pe', (lambda hp, BQc, bB: lambda e: e.matmul(PS(bB)[:, hp * 256 + 128:hp * 256 + 256], lhsT=ident, rhs=BQc[:, hp, 128:256], start=False, stop=True))(hp, BQc, bB),
                                 R=[bqk, 'cb'], W=[PK(bB)], inc=(hp == 1))
                        An = Abuf[pr][nxt]
                        BQn = BQbuf[pr][nxt]
                        S.op('dve' if pr % 2 == 0 else 'act', (lambda An, bA, pr: (lambda e: e.tensor_copy(out=An, in_=PS(bA)[:, 0:256])) if pr % 2 == 0 else (lambda e: e.activation(out=An, in_=PS(bA)[:, 0:256], func=AF.Copy)))(An, bA, pr),
                             R=[PK(bA)], W=['A%d_%d' % (pr, nxt)])
                        S.op('dve' if pr % 2 else 'act', (lambda BQn, bB, pr: (lambda e: e.tensor_copy(out=BQn, in_=PS(bB))) if pr % 2 else (lambda e: e.activation(out=BQn, in_=PS(bB), func=AF.Copy)))(BQn, bB, pr),
                             R=[PK(bB)], W=['BQ%d_%d' % (pr, nxt)])
                    else:
                        bB = bank()
                        for hp in range(2):
                            S.op('pe', (lambda hp, BQc, Ac, bB: lambda e: e.matmul(PS(bB)[:, hp * 128:(hp + 1) * 128], lhsT=Ac[:, hp, :], rhs=BQc[:, hp, 128:256], start=True, stop=False))(hp, BQc, Ac, bB),
                                 R=[ak, bqk], W=[PK(bB)], inc=False)
                            S.op('pe', (lambda hp, BQc, bB: lambda e: e.matmul(PS(bB)[:, hp * 128:(hp + 1) * 128], lhsT=ident, rhs=BQc[:, hp, 128:256], start=False, stop=True))(hp, BQc, bB),
                                 R=[bqk, 'cb'], W=[PK(bB)], inc=(hp == 1))
                        S.op('dve', (lambda bB, pr: lambda e: e.tensor_copy(out=TT[:, pr * 256:(pr + 1) * 256], in_=PS(bB)[:, 0:256]))(bB, pr), R=[PK(bB)], W=['TT%d' % pr])
            bX = bank()
            for h in range(8):
                pr, hp = h // 2, h % 2
                ps_ = slice(hp * 64, (hp + 1) * 64)
                S.op('pe', (lambda h, pr, ps_: lambda e: e.matmul(PS(bX)[:, h * 64:(h + 1) * 64], lhsT=khTm[:, h * 128:(h + 1) * 128], rhs=rwAb[:, pr * 64:(pr + 1) * 64], start=True, stop=False))(h, pr, ps_),
                     R=['khTm', 'rwAb'], W=[PK(bX)], inc=False)
                S.op('pe', (lambda h: lambda e: e.matmul(PS(bX)[:, h * 64:(h + 1) * 64], lhsT=LkT[:, h * 128:(h + 1) * 128], rhs=vb[:, h * 64:(h + 1) * 64], start=False, stop=True))(h),
                     R=['LkT%d' % pr, 'vbw'], W=[PK(bX)], inc=(h == 7))
            S.op('act', lambda e: e.activation(out=Xb, in_=PS(bX), func=AF.Copy, scale=-1.0), R=[PK(bX)], W=['Xb'])
            bU = bank()
            for h in range(8):
                S.op('pe', (lambda h: lambda e: e.matmul(PS(bU)[:, h * 64:(h + 1) * 64], lhsT=TT[:, h * 128:(h + 1) * 128], rhs=Xb[:, h * 64:(h + 1) * 64], start=True, stop=True))(h),
                     R=['TT%d' % (h // 2), 'Xb'], W=[PK(bU)], inc=(h == 7))
            S.op('act', lambda e: e.activation(out=Ub, in_=PS(bU), func=AF.Copy), R=[PK(bU)], W=['Ub'])
            if full:
                bY = bank()
                for h in range(8):
                    pr, hp = h // 2, h % 2
                    ps_ = slice(hp * 64, (hp + 1) * 64)
                    S.op('pe', (lambda h, pr, ps_: lambda e: e.matmul(PS(bY)[:, h * 64:(h + 1) * 64], lhsT=rhTm[:, h * 128:(h + 1) * 128], rhs=rwAb[:, pr * 64:(pr + 1) * 64], start=True, stop=False))(h, pr, ps_),
                         R=['rhTm', 'rwAb'], W=[PK(bY)], inc=False)
                    S.op('pe', (lambda h: lambda e: e.matmul(PS(bY)[:, h * 64:(h + 1) * 64], lhsT=MbT[:, h * 128:(h + 1) * 128], rhs=Ub[:, h * 64:(h + 1) * 64], start=False, stop=False))(h),
                         R=['MbT%d' % pr, 'Ub'], W=[PK(bY)], inc=False)
                    S.op('pe', (lambda h: lambda e: e.matmul(PS(bY)[:, h * 64:(h + 1) * 64], lhsT=MkT[:, h * 128:(h + 1) * 128], rhs=vb[:, h * 64:(h + 1) * 64], start=False, stop=True))(h),
                         R=['MkT%d' % pr, 'vbw'], W=[PK(bY)], inc=(h == 7))
                S.op('act', lambda e: e.activation(out=kp, in_=PS(bY), func=AF.Copy), R=[PK(bY)], W=['kp'])
            bS = bank()
            for pr in range(4):
                S.op('pe', (lambda pr: lambda e: e.matmul(PS(bS)[:, pr * 128:(pr + 1) * 128], lhsT=bt_[:, pr * 128:(pr + 1) * 128], rhs=Ub[:, pr * 128:(pr + 1) * 128], start=True, stop=False))(pr),
                     R=['bt', 'Ub'], W=[PK(bS)], inc=False)
                S.op('pe', (lambda pr: lambda e: e.matmul(PS(bS)[:, pr * 128:(pr + 1) * 128], lhsT=kt[:, pr * 128:(pr + 1) * 128], rhs=vb[:, pr * 128:(pr + 1) * 128], start=False, stop=True))(pr),
                     R=['kt', 'vbw'], W=[PK(bS)], inc=(pr == 3))
            for hp in range(2):
                S.op('dve', (lambda hp: lambda e: e.tensor_tensor(out=rwA[hp * 64:(hp + 1) * 64, :].rearrange("p (a e) -> p a e", a=4), in0=rwA[hp * 64:(hp + 1) * 64, :].rearrange("p (a e) -> p a e", a=4),
                                                                in1=PS(bS)[hp * 64:(hp + 1) * 64, :].rearrange("p (a f) -> p a f", a=4)[:, :, hp * 64:(hp + 1) * 64], op=ALU.add))(hp),
                     R=[PK(bS), 'rwA', 'rwAb'], W=['rwA'])
            S.op('dve', lambda e: e.tensor_tensor(out=rwA.rearrange("p (a e) -> p a e", a=4), in0=rwA.rearrange("p (a e) -> p a e", a=4), in1=gC.unsqueeze(2).broadcast_to([128, 4, 64]), op=ALU.mult), R=['rwA', 'gC'], W=['rwA'])
            S.op('act', lambda e: e.activation(out=rwAb, in_=rwA, func=AF.Copy), R=['rwA'], W=['rwAb'])
            if full:
                group_norm_out(kp, 'kp', sq, 'sq', gst2, 64e-5, ptile['rw_gn_g'], ptile['rw_gn_b'], 'pt_rw_gn_g', 'pt_rw_gn_b')
                S.op('pool', lambda e: e.tensor_tensor(out=kp, in0=kp, in1=gi_t, op=ALU.add), R=['kp', 'gi_t'], W=['kp'])
                bgt = bank()
                S.op('pe', lambda e: e.matmul(PS(bgt), lhsT=sgd1, rhs=gup1, start=True, stop=False), R=['sgd1', 'gup1'], W=[PK(bgt)], inc=False)
                S.op('pe', lambda e: e.matmul(PS(bgt), lhsT=sgd2, rhs=gup2, start=False, stop=True), R=['sgd2', 'gup2'], W=[PK(bgt)])
                S.op('dve', lambda e: e.tensor_tensor(out=mixw, in0=kp, in1=PS(bgt), op=ALU.mult), R=['kp', PK(bgt)], W=['mixw'])
                rw_deferred.append((lambda ti: lambda: transpose_to(mixw, ['mixw'], 4, mixTv[:, 4:8, ti * 128:(ti + 1) * 128], ['mixT_w%d' % ti]))(ti))
        for f_ in rw_deferred:
            f_()
        A.release()

    S.op('pool', lambda e: e.memset(retS, 0.0), R=[], W=['retS'])
    S.op('pool', lambda e: e.memset(retSb, 0.0), R=[], W=['retSb'])
    S.op('pool', lambda e: e.memset(rwA, 0.0), R=[], W=['rwA'])
    S.op('pool', lambda e: e.memset(rwAb, 0.0), R=[], W=['rwAb'])
    S.op('pool', lambda e: e.memset(x1T, 0.0), R=[], W=['x1T_prev'] + ['x1T_%d' % i for i in range(NT_HALF)])

    for half in range(2):
        full = (half == 1)
        A.mark()
        xTv = mixTv
        src = dr['xs'][half * S_HALF:(half + 1) * S_HALF, :]
        prep_xT(src, S_HALF, xTv, 0, lambda t0, n: ['xT_%d' % (t0 // 512)] if n == 128 else ['xT_%d' % (t0 // 512)])
        load_ln('ln1_g', 'ln1_b')
        x1b = [A.bf(1024) for _ in range(8)]
        if half == 1:
            S.op('pool', lambda e: e.tensor_copy(out=x1Tv[:, :, XO - 1:XO], in_=x1Tv[:, :, XO + S_HALF - 1:XO + S_HALF]), R=['x1T_%d' % (NT_HALF - 1)], W=['x1T_prev'])

        def cons1(ti, y, yk, half=half, full=full, x1b=x1b):
            if full:
                S.dma('sp', lambda e: e.dma_start(out=x1s[ti * 128:(ti + 1) * 128, :], in_=y), R=[yk], W=['x1s_%d' % ti], sk='x1s')
            b = x1b[ti % 8]
            bk = 'x1b%d' % (ti % 8)
            S.op('dve', lambda e: e.tensor_scalar(out=b, in0=y, scalar1=hmask[:, half:half + 1], scalar2=None, op0=ALU.mult), R=[yk, 'hmask'], W=[bk])
            return lambda: transpose_to(b, [bk], 8, x1Tv[:, :, XO + ti * 128:XO + (ti + 1) * 128], ['x1T_%d' % ti], evac_eng='dve')

        if 'ffn1_%d' % half in plan:
            ffn(xTv, 0, lambda t0, n: ['xT_%d' % (t0 // 512)], S_HALF, dr['ffn1_w_gu'], dr['ffn1_w_down'], src, LN_EPS / (ALPHA * ALPHA), cons1, 'f1')
        A.release()
        S.barrier()
        if 'mix_%d' % half in plan:
            mixer(half, full)
        S.barrier()

    A.mark()
    NTW = NT_HALF if 'wout' in plan else 0
    wo = A.bf(8 * 1024)
    wov = wo.rearrange("p (c f) -> p c f", c=8)
    w_out_v = dr['w_out'].rearrange("(c p) f -> p c f", p=128)
    for hh in range(2):
        S.dma('pool', (lambda hh: lambda e: e.dma_start(out=wov[:, :, hh * 512:(hh + 1) * 512], in_=w_out_v[:, :, hh * 512:(hh + 1) * 512]))(hh), R=[], W=['wo%d' % hh])
    load_ln('ln2_g', 'ln2_b')
    ybuf = [A.f32(1024) for _ in range(4)]
    x2b = [A.bf(1024) for _ in range(4)]
    tmp_sq = A.f32(1024)
    st = A.f32(16)
    wo_deferred = []
    for g in range(NTW // 2):
        tiles = []
        for ti in (2 * g, 2 * g + 1):
            y, yk = ybuf[ti % 4], 'y%d' % (ti % 4)
            S.dma('sp', (lambda y, ti: lambda e: e.dma_start(out=y, in_=x1s[ti * 128:(ti + 1) * 128, :]))(y, ti), R=['x1s_%d' % ti], W=[yk])
            for dh in range(2):
                b = bank()
                for c in range(8):
                    S.op('pe', (lambda c, b, dh, ti: lambda e: e.matmul(PS(b), lhsT=mixTv[:, c, ti * 128:(ti + 1) * 128], rhs=wov[:, c, dh * 512:(dh + 1) * 512], start=(c == 0), stop=(c == 7)))(c, b, dh, ti),
                         R=['mixT_r%d' % ti, 'mixT_w%d' % ti, 'wo%d' % dh], W=[PK(b)], inc=(c == 7))
                S.op('dve', (lambda y, b, dh: lambda e: e.scalar_tensor_tensor(out=y[:, dh * 512:(dh + 1) * 512], in0=PS(b), scalar=1.0 / ALPHA, in1=y[:, dh * 512:(dh + 1) * 512], op0=ALU.mult, op1=ALU.add))(y, b, dh),
                     R=[PK(b), yk], W=[yk])
            tiles.append((ti, y, yk))
        for f_ in wo_deferred:
            f_()
        del wo_deferred[:]
        layer_norm_group([(y, yk) for (ti, y, yk) in tiles], LN_EPS / (ALPHA * ALPHA), tmp_sq, st)
        for (ti, y, yk) in tiles:
            S.dma('sp', (lambda y, ti: lambda e: e.dma_start(out=x2s[ti * 128:(ti + 1) * 128, :], in_=y))(y, ti), R=[yk], W=['x2s_%d' % ti], sk='x2s')
            b2, b2k = x2b[ti % 4], 'x2b%d' % (ti % 4)
            S.op('dve', (lambda b2, y: lambda e: e.tensor_copy(out=b2, in_=y))(b2, y), R=[yk], W=[b2k])
            wo_deferred.append((lambda b2, b2k, ti: lambda: transpose_to(b2, [b2k], 8, x2Tv[:, :, XO + ti * 128:XO + (ti + 1) * 128], ['x2T_%d' % (ti // 4)], evac_eng='dve'))(b2, b2k, ti))
    for f_ in wo_deferred:
        f_()
    A.release()
    S.barrier()

    A.mark()
    load_ln('ln3_g', 'ln3_b')
    x3b = [A.bf(1024) for _ in range(8)]

    def cons3(ti, y, yk):
        S.dma('sp', lambda e: e.dma_start(out=x3s[ti * 128:(ti + 1) * 128, :], in_=y), R=[yk], W=['x3s_%d' % ti], sk='x3s')
        b = x3b[ti % 8]
        bk = 'x3b%d' % (ti % 8)
        S.op('dve', lambda e: e.tensor_copy(out=b, in_=y), R=[yk], W=[bk])
        return lambda: transpose_to(b, [bk], 8, mixTv[:, :, ti * 128:(ti + 1) * 128], ['x3T_%d' % ti], evac_eng='dve')

    if 'ffn2' in plan:
        ffn(x2Tv, XO, lambda t0, n: ['x2T_%d' % (t0 // 512)], S_HALF, dr['ffn2_w_gu'], dr['ffn2_w_down'], x2s, LN_EPS / (ALPHA * ALPHA), cons3, 'f2')
    A.release()
    S.barrier()

    A.mark()
    wgt = A.bf(8 * 1024)
    wgv = wgt.rearrange("p (c f) -> p c f", c=8)
    wpj = A.bf(2 * 1024)
    wpv = wpj.rearrange("p (c f) -> p c f", c=2)
    bgr = A.bf(1024)
    onesb = A.bf(128)
    S.op('pool', lambda e: e.memset(bgr, 0.0), R=[], W=['bgr'])
    ple_g = dr['ple_w_gate'].rearrange("(c p) f -> p c f", p=128)
    ple_p = dr['ple_w_proj'].rearrange("(c p) f -> p c f", p=128)
    for hh in range(2):
        S.dma('pool', (lambda hh: lambda e: e.dma_start(out=wgv[:, :, hh * 512:(hh + 1) * 512], in_=ple_g[:, :, hh * 512:(hh + 1) * 512]))(hh), R=[], W=['wgt%d' % hh])
    S.dma('pool', lambda e: e.dma_start(out=wpv, in_=ple_p), R=[], W=['wpj'])
    S.dma('pool', lambda e: e.dma_start(out=bgr[0:1, :], in_=dr['ple_b_gate']), R=[], W=['bgr'])
    S.op('dve', lambda e: e.tensor_copy(out=onesb, in_=c32v('ones')), R=['c32'], W=['onesb'])
    pb_ = [A.bf(256) for _ in range(2)]
    pT = [A.bf(256) for _ in range(2)]
    x3t = [A.f32(1024) for _ in range(2)]
    gsb = [A.f32(1024) for _ in range(2)]
    for ti in range(NT_HALF if 'ple' in plan else 0):
        pbt, pbk = pb_[ti % 2], 'pb%d' % (ti % 2)
        pTt, pTk = pT[ti % 2], 'pT%d' % (ti % 2)
        x3, x3k = x3t[ti % 2], 'x3t%d' % (ti % 2)
        gs, gsk = gsb[ti % 2], 'gs%d' % (ti % 2)
        S.dma('pool', (lambda pbt, ti: lambda e: e.dma_start(out=pbt, in_=dr['p'][ti * 128:(ti + 1) * 128, :]))(pbt, ti), R=[], W=[pbk])
        S.dma('sp', (lambda x3, ti: lambda e: e.dma_start(out=x3, in_=x3s[ti * 128:(ti + 1) * 128, :]))(x3, ti), R=['x3s_%d' % ti], W=[x3k])
        transpose_to(pbt, [pbk], 2, pTt.rearrange("p (c t) -> p c t", c=2), [pTk])
        for dh in range(2):
            bg_ = bank()
            for c in range(8):
                S.op('pe', (lambda c, bg_, dh, ti: lambda e: e.matmul(PS(bg_), lhsT=mixTv[:, c, ti * 128:(ti + 1) * 128], rhs=wgv[:, c, dh * 512:(dh + 1) * 512], start=(c == 0), stop=False))(c, bg_, dh, ti),
                     R=['x3T_%d' % ti, 'wgt%d' % dh], W=[PK(bg_)], inc=False)
            S.op('pe', (lambda bg_, dh: lambda e: e.matmul(PS(bg_), lhsT=onesb, rhs=bgr[:, dh * 512:(dh + 1) * 512], start=False, stop=True))(bg_, dh), R=['onesb', 'bgr'], W=[PK(bg_)])
            bp_ = bank()
            for c in range(2):
                S.op('pe', (lambda c, bp_, dh, pTt: lambda e: e.matmul(PS(bp_), lhsT=pTt[:, c * 128:(c + 1) * 128], rhs=wpv[:, c, dh * 512:(dh + 1) * 512], start=(c == 0), stop=(c == 1)))(c, bp_, dh, pTt),
                     R=[pTk, 'wpj'], W=[PK(bp_)], inc=(c == 1))
            S.op('act', (lambda gs, bg_, dh: lambda e: e.activation(out=gs[:, dh * 512:(dh + 1) * 512], in_=PS(bg_), func=AF.Sigmoid))(gs, bg_, dh), R=[PK(bg_)], W=[gsk])
            S.op('dve', (lambda gs, bp_, dh: lambda e: e.tensor_tensor(out=gs[:, dh * 512:(dh + 1) * 512], in0=gs[:, dh * 512:(dh + 1) * 512], in1=PS(bp_), op=ALU.mult))(gs, bp_, dh), R=[PK(bp_), gsk], W=[gsk])
        S.op('pool', (lambda gs, x3: lambda e: e.tensor_tensor(out=gs, in0=gs, in1=x3, op=ALU.add))(gs, x3), R=[gsk, x3k], W=[gsk])
        S.dma('sp', (lambda gs, ti: lambda e: e.dma_start(out=out[ti * 128:(ti + 1) * 128, :], in_=gs))(gs, ti), R=[gsk], W=['out_%d' % ti], sk='out')
    A.release()
    S.barrier()

    with nc.Block() as block:
        S.emit(nc, block)
    es.close()
    print("semaphores:", len(S.semkeys), "ops:", {e: len(l) for e, l in S.lists.items()}, "arena hi:", A.hi)
    return nc


_NC_CACHE = {}


def kernel(**inputs):
    x = np.asarray(inputs['x'], np.float32)
    p = np.asarray(inputs['p'], np.float32)[0]
    if 'nc' not in _NC_CACHE:
        _NC_CACHE['nc'] = build_program()
    nc = _NC_CACHE['nc']
    c32 = np.ascontiguousarray(np.concatenate([C32[k] for k in C32], axis=1).astype(np.float32))
    cb = np.ascontiguousarray(np.concatenate([CB[k] for k in CB], axis=1).astype(np.float32))
    c32r = np.ascontiguousarray(np.concatenate([C32R[k] for k in C32R], axis=1).astype(np.float32))
    wmap = {}
    for n in WEIGHT_NAMES:
        wmap[n] = np.ascontiguousarray(np.asarray(inputs[n], np.float32)[0].reshape(WSHAPES[n]))
    in_maps = []
    for c in range(8):
        b, half = c // 2, c % 2
        m = dict(wmap)
        if half == 1:
            xs = x[b]
            pos = np.arange(4096)
            hm = np.ones((128, 2), np.float32)
        else:
            xs = np.concatenate([np.zeros((S_HALF, D), np.float32), x[b, :S_HALF]], axis=0)
            pos = np.arange(4096) - S_HALF
            hm = np.ones((128, 2), np.float32)
            hm[:, 0] = 0.0
        cos, sin = rope_tables(pos)
        m['xs'] = np.ascontiguousarray(xs)
        m['p'] = np.ascontiguousarray(p[b, half * S_HALF:(half + 1) * S_HALF])
        m['hmask'] = hm
        m['cos'] = cos
        m['sin'] = sin
        m['c32'] = c32
        m['cb'] = cb
        m['c32r'] = c32r
        in_maps.append(m)
    res = run_bass_kernel_spmd(nc, in_maps, core_ids=list(range(8)))
    outp = np.zeros((4, 4096, D), np.float32)
    for c in range(8):
        b, half = c // 2, c % 2
        outp[b, half * S_HALF:(half + 1) * S_HALF] = res.results[c]['out']
    return outp
```

```python
import os
import numpy as np
import concourse.bass as bass
import concourse.mybir as mybir
from concourse.bass_utils import run_bass_kernel_spmd

F32 = mybir.dt.float32
BF16 = mybir.dt.bfloat16
AF = mybir.ActivationFunctionType
ALU = mybir.AluOpType
AX = mybir.AxisListType

D = 1024
DFF = 2816
NJ = DFF // 128
S_HALF = 2048
NT_HALF = S_HALF // 128
RETC = 2048
RWC = 1824
INC = RETC + RWC
ALPHA = 2.0 ** 0.25
LN_EPS = 1e-5
EDEC = float(np.exp(-0.5))

STRICT = True


class _Rec:
    def __init__(self):
        self.calls = []

    def __getattr__(self, name):
        def f(*a, **k):
            self.calls.append((name, a, k))
            return self
        return f


def _capture(fn):
    r = _Rec()
    fn(r)
    assert len(r.calls) == 1, r.calls
    name, a, k = r.calls[0]
    return lambda e: getattr(e, name)(*a, **k)


class Sched:
    def __init__(self):
        self.engs = ['pe', 'act', 'dve', 'pool', 'sp']
        self.lists = {e: [] for e in self.engs}
        self.cnt = {e: 0 for e in self.engs}
        self.lastw = {}
        self.readers = {}
        self.waited = {e: {} for e in self.engs}
        self.dmacnt = {}
        self.semkeys = set(self.engs)
        self.alltok = {}

    def _deps(self, eng, R, W, is_dma):
        deps = []
        raw = set()
        for k in R:
            t = self.lastw.get(k)
            if t:
                deps.append(t)
                raw.add(t)
            if k.startswith('ps'):
                deps.extend(tk for tk in self.readers.get(k, ()) if tk[0] != eng)
        for k in W:
            t = self.lastw.get(k)
            if t:
                deps.append(t)
            deps.extend(self.readers.get(k, ()))
        waits = {}
        for (sk, v) in deps:
            if sk == eng and not is_dma and (eng == 'pe' or not STRICT or (sk, v) not in raw):
                continue
            if self.waited[eng].get(sk, 0) >= v:
                continue
            waits[sk] = max(waits.get(sk, 0), v)
        for sk, v in waits.items():
            self.waited[eng][sk] = v
        return list(waits.items())

    def _commit(self, tok, R, W):
        for k in W:
            self.lastw[k] = tok
            self.readers[k] = []
        for k in R:
            if k not in W:
                self.readers.setdefault(k, []).append(tok)
        self.alltok[tok[0]] = max(self.alltok.get(tok[0], 0), tok[1])

    def op(self, eng, fn, R=(), W=(), inc=True):
        self._clean = False
        waits = self._deps(eng, R, W, False)
        if inc:
            self.cnt[eng] += 1
            tok = (eng, self.cnt[eng])
        else:
            tok = (eng, self.cnt[eng] + 1)
        self.lists[eng].append((waits, _capture(fn), (eng, 1) if inc else None))
        self._commit(tok, R, W)

    def dma(self, eng, fn, R, W, sk=None):
        self._clean = False
        waits = self._deps(eng, R, W, True)
        sk = 'd:' + (sk or W[0])
        self.semkeys.add(sk)
        self.dmacnt[sk] = self.dmacnt.get(sk, 0) + 16
        tok = (sk, self.dmacnt[sk])
        self.lists[eng].append((waits, _capture(fn), (sk, 16)))
        self._commit(tok, R, W)

    def barrier(self):
        if getattr(self, '_clean', False):
            return
        self._clean = True
        for e in ['pe', 'act', 'dve', 'pool']:
            if self.lists[e] and self.lists[e][-1][2] is not None and self.lists[e][-1][2][0] == e:
                continue
            self.cnt[e] += 1
            self.alltok[e] = self.cnt[e]
            self.lists[e].append(([], 'nop', (e, 1)))
        for e in self.engs:
            waits = []
            for sk, v in self.alltok.items():
                if self.waited[e].get(sk, 0) >= v:
                    continue
                if sk == e:
                    continue
                waits.append((sk, v))
                self.waited[e][sk] = v
            self.lists[e].append((waits, None, None))
        self.lastw = {}
        self.readers = {}

    def emit(self, nc, block):
        sems = {sk: nc.alloc_semaphore(name=("s_" + sk.replace(':', '_').replace('.', '_'))[:40]) for sk in sorted(self.semkeys)}
        engobj = {'pe': 'tensor', 'act': 'scalar', 'dve': 'vector', 'pool': 'gpsimd', 'sp': 'sync'}

        def make(ename):
            lst = self.lists[ename]

            def body(e):
                for (waits, fn, inc) in lst:
                    for (sk, v) in waits:
                        e.wait_ge(sems[sk], v)
                    if fn is None:
                        continue
                    if fn == 'nop':
                        ins = e.nop()
                    else:
                        ins = fn(e)
                    if inc is not None:
                        ins.then_inc(sems[inc[0]], inc[1])
            return body

        for ename in self.engs:
            getattr(block, engobj[ename])(make(ename))


def gammas():
    return 1.0 - 2.0 ** (-5.0 - np.arange(8, dtype=np.float64))


def host_consts():
    g = gammas()
    i = np.arange(128)
    c = {}
    s_le_t = (i[:, None] <= i[None, :]).astype(np.float32)
    s_lt_t = (i[:, None] < i[None, :]).astype(np.float32)
    c['tri_incl'] = -EDEC * s_le_t
    c['tri_strict'] = -EDEC * s_lt_t
    c['negcol'] = np.full((128, 1), -EDEC, np.float32)
    c['ones'] = np.ones((128, 128), np.float32)
    rel = (i[None, :] - i[:, None]).astype(np.float64)
    dm = np.zeros((128, 8, 128), np.float64)
    for h in range(8):
        dm[:, h, :] = np.where(rel >= 0, 0.125 * np.exp(np.where(rel >= 0, rel, 0) * np.log(g[h])), 0.0)
    cr = {}
    cr['dmask'] = dm.reshape(128, 1024).astype(np.float32)
    kd = np.zeros((128, 8), np.float64)
    for h in range(8):
        kd[:, h] = 0.125 * g[h] ** (127.0 - i)
    c['kdec'] = kd.astype(np.float32)
    qd = np.zeros((128, 4, 128), np.float64)
    gm = np.zeros((128, 4), np.float64)
    for pr in range(4):
        for hp in range(2):
            h = 2 * pr + hp
            qd[hp * 64:(hp + 1) * 64, pr, :] = (g[h] ** (i + 1.0))[None, :]
            gm[hp * 64:(hp + 1) * 64, pr] = g[h] ** 128.0
    cr['qdec'] = qd.reshape(128, 512).astype(np.float32)
    cr['kdecf'] = np.repeat(kd, 64, axis=1).astype(np.float32)
    cr['gam128f'] = np.repeat(gm, 64, axis=1).astype(np.float32)
    c['gam128'] = gm.astype(np.float32)
    b = {}
    b['ident'] = np.eye(128, dtype=np.float32)
    m1 = np.concatenate([-s_lt_t, s_le_t], axis=1)
    m2 = np.concatenate([s_lt_t, s_le_t], axis=1)
    b['m1'] = np.concatenate([m1, m1], axis=1)
    b['m2'] = np.concatenate([m2, m2], axis=1)
    m3 = -(i[:, None] > i[None, :]).astype(np.float32)
    b['m3'] = np.concatenate([m3, m3], axis=1)
    return c, cr, b


def rope_tables(pos):
    inv = 10000.0 ** (-np.arange(0, 64, 2, dtype=np.float32) / 64.0)
    ang = pos.astype(np.float32)[:, None] * inv[None, :]
    cos = np.cos(ang).astype(np.float32)
    sin = np.sin(ang).astype(np.float32)
    n = pos.shape[0] // 128
    cos = cos.reshape(n, 128, 32).transpose(1, 0, 2).reshape(128, n * 32)
    sin = sin.reshape(n, 128, 32).transpose(1, 0, 2).reshape(128, n * 32)
    return np.ascontiguousarray(cos), np.ascontiguousarray(sin)


C32, C32R, CB = host_consts()
C32_OFF = {}
_o = 0
for _k, _v in C32.items():
    C32_OFF[_k] = (_o, _v.shape[1])
    _o += _v.shape[1]
C32_N = _o
C32R_OFF = {}
_o = 0
for _k, _v in C32R.items():
    C32R_OFF[_k] = (_o, _v.shape[1])
    _o += _v.shape[1]
C32R_N = _o
CB_OFF = {}
_o = 0
for _k, _v in CB.items():
    CB_OFF[_k] = (_o, _v.shape[1])
    _o += _v.shape[1]
CB_N = _o

WEIGHT_NAMES = ['ffn1_w_gu', 'ffn1_w_down', 'ln1_g', 'ln1_b', 'w_in', 'ret_gn_g', 'ret_gn_b', 'rw_mu',
                'rw_w0', 'rw_w_up', 'rw_a0', 'rw_a_up', 'rw_g_up', 'rw_k_k', 'rw_k_a', 'rw_r_k', 'rw_gn_g',
                'rw_gn_b', 'w_out', 'ln2_g', 'ln2_b', 'ffn2_w_gu', 'ffn2_w_down', 'ln3_g', 'ln3_b',
                'ple_w_proj', 'ple_w_gate', 'ple_b_gate']
WSHAPES = {'ffn1_w_gu': [D, 2 * DFF], 'ffn1_w_down': [DFF, D], 'ln1_g': [1, D], 'ln1_b': [1, D], 'w_in': [D, INC],
           'ret_gn_g': [1, 512], 'ret_gn_b': [1, 512], 'rw_mu': [1, RWC], 'rw_w0': [1, 512], 'rw_w_up': [64, 512],
           'rw_a0': [1, 512], 'rw_a_up': [64, 512], 'rw_g_up': [160, 512], 'rw_k_k': [1, 512], 'rw_k_a': [1, 512],
           'rw_r_k': [1, 512], 'rw_gn_g': [1, 512], 'rw_gn_b': [1, 512], 'w_out': [D, D], 'ln2_g': [1, D],
           'ln2_b': [1, D], 'ffn2_w_gu': [D, 2 * DFF], 'ffn2_w_down': [DFF, D], 'ln3_g': [1, D], 'ln3_b': [1, D],
           'ple_w_proj': [256, D], 'ple_w_gate': [D, D], 'ple_b_gate': [1, D]}


def build_program(plan=None, dbg=False):
    nc = bass.Bass("TRN2", target_bir_lowering=False)
    dr = {}
    dr['xs'] = nc.dram_tensor("xs", [2 * S_HALF, D], F32, kind="ExternalInput").ap()
    dr['p'] = nc.dram_tensor("p", [S_HALF, 256], F32, kind="ExternalInput").ap()
    dr['hmask'] = nc.dram_tensor("hmask", [128, 2], F32, kind="ExternalInput").ap()
    dr['cos'] = nc.dram_tensor("cos", [128, 1024], F32, kind="ExternalInput").ap()
    dr['sin'] = nc.dram_tensor("sin", [128, 1024], F32, kind="ExternalInput").ap()
    dr['c32'] = nc.dram_tensor("c32", [128, C32_N], F32, kind="ExternalInput").ap()
    dr['cb'] = nc.dram_tensor("cb", [128, CB_N], F32, kind="ExternalInput").ap()
    dr['c32r'] = nc.dram_tensor("c32r", [128, C32R_N], F32, kind="ExternalInput").ap()
    for n in WEIGHT_NAMES:
        dr[n] = nc.dram_tensor(n, WSHAPES[n], F32, kind="ExternalInput").ap()
    out = nc.dram_tensor("out", [S_HALF, D], F32, kind="ExternalOutput").ap()
    skind = "ExternalOutput" if dbg else "Internal"
    x1s = nc.dram_tensor("x1s", [S_HALF, D], F32, kind=skind).ap()
    x2s = nc.dram_tensor("x2s", [S_HALF, D], F32, kind=skind).ap()
    x3s = nc.dram_tensor("x3s", [S_HALF, D], F32, kind=skind).ap()
    wab_s = nc.dram_tensor("wab_s", [128, 2 * 8 * RWC], BF16, kind="Internal").ap()
    if plan is None:
        plan = ['ffn1_0', 'mix_0', 'ffn1_1', 'mix_1', 'wout', 'ffn2', 'ple']

    S = Sched()
    dumped = {}

    def dump(name, ap, keys):
        if not dbg or name in dumped:
            return
        shp = list(ap.shape)
        d_ = nc.dram_tensor("dbg_" + name, shp, F32, kind="ExternalOutput").ap()
        dumped[name] = d_
        S.dma('pool', lambda e: e.dma_start(out=d_, in_=ap), R=list(keys), W=['dbg_' + name])
    ARENA_W = int(os.environ.get('ARENA_W', '53200'))
    from contextlib import ExitStack
    es = ExitStack()
    arena = es.enter_context(nc.sbuf_tensor("arena", [128, ARENA_W], F32))
    psf = [es.enter_context(nc.psum_tensor("ps%d" % i, [128, 512], F32)) for i in range(8)]

    class Alloc:
        def __init__(self):
            self.p = 0
            self.marks = []

        def f32(self, n, parts=(0, 128)):
            if os.environ.get('DRY'):
                self.p += n
                self.hi = max(getattr(self, 'hi', 0), self.p)
                return arena[parts[0]:parts[1], 0:n]
            a = arena[parts[0]:parts[1], self.p:self.p + n]
            self.p += n
            self.hi = max(getattr(self, 'hi', 0), self.p)
            assert self.p <= ARENA_W, ("arena overflow", self.p)
            return a

        def bf(self, n, parts=(0, 128)):
            w = (n + 1) // 2
            if os.environ.get('DRY'):
                self.p += w
                self.hi = max(getattr(self, 'hi', 0), self.p)
                return arena[parts[0]:parts[1], 0:w].bitcast(BF16)
            a = arena[parts[0]:parts[1], self.p:self.p + w].bitcast(BF16)
            self.p += w
            self.hi = max(getattr(self, 'hi', 0), self.p)
            assert self.p <= ARENA_W, ("arena overflow", self.p)
            return a

        def mark(self):
            self.marks.append(self.p)

        def release(self):
            self.p = self.marks.pop()
            S.barrier()

    A = Alloc()
    bankctr = [0]
    bankgen = [0] * 8

    class Bk(int):
        pass

    def bank():
        b = Bk(bankctr[0] % 8)
        bankctr[0] += 1
        bankgen[int(b)] = bankctr[0]
        b.gen = bankctr[0]
        return b

    def PS(b):
        return psf[int(b)][:, :]

    def PSB(b):
        return psf[int(b)][:, :].bitcast(BF16)

    def PK(b):
        assert bankgen[int(b)] == b.gen, "stale PSUM bank use"
        return 'ps%d' % int(b)

    c32 = A.f32(C32_N)
    cbt = A.bf(CB_N)
    S.dma('sp', lambda e: e.dma_start(out=c32, in_=dr['c32']), R=[], W=['c32'])
    S.dma('pool', lambda e: e.dma_start(out=cbt, in_=dr['cb']), R=[], W=['cb'])

    def c32v(name, parts=(0, 128)):
        o, n = C32_OFF[name]
        return c32[parts[0]:parts[1], o:o + n]

    def cbv(name):
        o, n = CB_OFF[name]
        return cbt[:, o:o + n]

    ident = cbv('ident')
    hmask = A.f32(2)
    S.dma('sp', lambda e: e.dma_start(out=hmask, in_=dr['hmask']), R=[], W=['hmask'])
    ptile = {}
    LN = {}
    retS = A.f32(256)
    retSb = A.bf(256)
    rwA = A.f32(256)
    rwAb = A.bf(256)
    x1T = A.bf(8 * (S_HALF + 8))
    x1Tv = x1T.rearrange("p (c t) -> p c t", c=8)
    XO = 8
    mixT = A.bf(8 * S_HALF)
    mixTv = mixT.rearrange("p (c t) -> p c t", c=8)
    x2Tv = x1Tv

    def load_ln(gn, bn):
        lng = A.f32(1024)
        lnb = A.f32(1024)
        LN['g'] = lng
        LN['b'] = lnb
        S.dma('sp', lambda e: e.dma_start(out=lng, in_=dr[gn].partition_broadcast(128)), R=[], W=['lng'])
        S.dma('sp', lambda e: e.dma_start(out=lnb, in_=dr[bn].partition_broadcast(128)), R=[], W=['lnb'])

    def load_ptiles(names):
        for n in names:
            t = A.f32(512)
            ptile[n] = t
            S.dma('sp', (lambda t, n: lambda e: e.dma_start(out=t, in_=dr[n].partition_broadcast(128)))(t, n), R=[], W=['pt_' + n])

    def transpose_to(src_bf, src_keys, n_blocks, dst_view, dst_keys, evac_eng='act'):
        b = bank()
        pk = PK(b)
        for i in range(n_blocks):
            S.op('pe', (lambda i, b: lambda e: e.transpose(out=PSB(b)[:, i * 128:(i + 1) * 128],
                                                         in_=src_bf[:, i * 128:(i + 1) * 128], identity=ident))(i, b),
                 R=list(src_keys) + ['cb'], W=[pk], inc=(i == n_blocks - 1))
        src = PSB(b)[:, 0:n_blocks * 128].rearrange("p (c t) -> p c t", c=n_blocks)
        if evac_eng == 'act':
            S.op('act', lambda e: e.activation(out=dst_view, in_=src, func=AF.Copy), R=[pk], W=list(dst_keys))
        else:
            S.op(evac_eng, lambda e: e.tensor_copy(out=dst_view, in_=src), R=[pk], W=list(dst_keys))

    def layer_norm_group(tiles, eps, junk, st):
        n_ = len(tiles)
        sl = lambda i, c: st[:, i * 8 + c:i * 8 + c + 1]
        k = lambda i, c: 'lnst%d_%d' % (i, c)
        for i, (y, yk) in enumerate(tiles):
            S.op('act', (lambda i, y: lambda e: e.activation(out=junk, in_=y, func=AF.Square, accum_out=sl(i, 0)))(i, y), R=[yk], W=['lnjunk', k(i, 0)])
            S.op('act', (lambda i, y: lambda e: e.activation(out=junk, in_=y, func=AF.Identity, accum_out=sl(i, 1)))(i, y), R=[yk], W=['lnjunk', k(i, 1)])
        for i in range(n_):
            S.op('dve', (lambda i: lambda e: e.tensor_scalar(out=sl(i, 2), in0=sl(i, 1), scalar1=1.0 / 1024, scalar2=None, op0=ALU.mult))(i), R=[k(i, 1)], W=[k(i, 2)])
        for i in range(n_):
            S.op('dve', (lambda i: lambda e: e.tensor_tensor(out=sl(i, 3), in0=sl(i, 2), in1=sl(i, 2), op=ALU.mult))(i), R=[k(i, 2)], W=[k(i, 3)])
        for i in range(n_):
            S.op('dve', (lambda i: lambda e: e.scalar_tensor_tensor(out=sl(i, 4), in0=sl(i, 0), scalar=1.0 / 1024, in1=sl(i, 3), op0=ALU.mult, op1=ALU.subtract))(i), R=[k(i, 0), k(i, 3)], W=[k(i, 4)])
        for i in range(n_):
            S.op('dve', (lambda i: lambda e: e.tensor_scalar(out=sl(i, 4), in0=sl(i, 4), scalar1=float(eps), scalar2=None, op0=ALU.add))(i), R=[k(i, 4)], W=[k(i, 4)])
        for i in range(n_):
            S.op('act', (lambda i: lambda e: e.activation(out=sl(i, 5), in_=sl(i, 4), func=AF.Sqrt))(i), R=[k(i, 4)], W=[k(i, 5)])
        for i in range(n_):
            S.op('dve', (lambda i: lambda e: e.reciprocal(out=sl(i, 6), in_=sl(i, 5)))(i), R=[k(i, 5)], W=[k(i, 6)])
        for i in range(n_):
            S.op('dve', (lambda i: lambda e: e.scalar_tensor_tensor(out=sl(i, 7), in0=sl(i, 2), scalar=-1.0, in1=sl(i, 6), op0=ALU.mult, op1=ALU.mult))(i), R=[k(i, 2), k(i, 6)], W=[k(i, 7)])
        for i, (y, yk) in enumerate(tiles):
            S.op('act', (lambda i, y: lambda e: e.activation(out=y, in_=y, func=AF.Identity, scale=sl(i, 6), bias=sl(i, 7)))(i, y), R=[yk, k(i, 6), k(i, 7)], W=[yk])
        for i, (y, yk) in enumerate(tiles):
            S.op('dve', (lambda y: lambda e: e.tensor_tensor(out=y, in0=y, in1=LN['g'], op=ALU.mult))(y), R=[yk, 'lng'], W=[yk])
            S.op('dve', (lambda y: lambda e: e.tensor_tensor(out=y, in0=y, in1=LN['b'], op=ALU.add))(y), R=[yk, 'lnb'], W=[yk])

    def layer_norm_tile(y, ykey, eps, outs, tmp_sq, st):
        S.op('act', lambda e: e.activation(out=tmp_sq, in_=y, func=AF.Square), R=[ykey], W=['lnsq'])
        S.op('dve', lambda e: e.reduce_sum(out=st[:, 0:1], in_=tmp_sq, axis=AX.X), R=['lnsq'], W=['lnst0'])
        S.op('dve', lambda e: e.reduce_sum(out=st[:, 1:2], in_=y, axis=AX.X), R=[ykey], W=['lnst1'])
        S.op('dve', lambda e: e.tensor_scalar(out=st[:, 2:3], in0=st[:, 1:2], scalar1=1.0 / 1024, scalar2=None, op0=ALU.mult), R=['lnst1'], W=['lnst2'])
        S.op('dve', lambda e: e.tensor_tensor(out=st[:, 3:4], in0=st[:, 2:3], in1=st[:, 2:3], op=ALU.mult), R=['lnst2'], W=['lnst3'])
        S.op('dve', lambda e: e.scalar_tensor_tensor(out=st[:, 4:5], in0=st[:, 0:1], scalar=1.0 / 1024, in1=st[:, 3:4], op0=ALU.mult, op1=ALU.subtract), R=['lnst0', 'lnst3'], W=['lnst4'])
        S.op('dve', lambda e: e.tensor_scalar(out=st[:, 4:5], in0=st[:, 4:5], scalar1=float(eps), scalar2=None, op0=ALU.add), R=['lnst4'], W=['lnst4'])
        S.op('act', lambda e: e.activation(out=st[:, 5:6], in_=st[:, 4:5], func=AF.Sqrt), R=['lnst4'], W=['lnst5'])
        S.op('dve', lambda e: e.reciprocal(out=st[:, 6:7], in_=st[:, 5:6]), R=['lnst5'], W=['lnst6'])
        S.op('dve', lambda e: e.scalar_tensor_tensor(out=st[:, 7:8], in0=st[:, 2:3], scalar=-1.0, in1=st[:, 6:7], op0=ALU.mult, op1=ALU.mult), R=['lnst2', 'lnst6'], W=['lnst7'])
        S.op('act', lambda e: e.activation(out=y, in_=y, func=AF.Identity, scale=st[:, 6:7], bias=st[:, 7:8]), R=[ykey, 'lnst6', 'lnst7'], W=[ykey])
        S.op('dve', lambda e: e.tensor_tensor(out=y, in0=y, in1=LN['g'], op=ALU.mult), R=[ykey, 'lng'], W=[ykey])
        S.op('dve', lambda e: e.tensor_tensor(out=y, in0=y, in1=LN['b'], op=ALU.add), R=[ykey, 'lnb'], W=[ykey])

    def ffn(xTv_src, xoff, xkey_fn, ntok, wgu, wdown, res_dram, eps, consumer, tagp):
        A.mark()
        NB = 1024
        hhT = A.bf(NJ * NB)
        hhv = hhT.rearrange("p (j t) -> p j t", j=NJ)
        wg = [A.bf(8 * 512) for _ in range(2)]
        wd = [A.bf(1024) for _ in range(3)]
        sg = [A.f32(512) for _ in range(2)]
        ybuf = [A.f32(1024) for _ in range(8)]
        tmp_sq = A.f32(1024)
        st = A.f32(32)
        wgu_v = wgu.rearrange("(c p) f -> p c f", p=128)
        deferred = []

        def run_deferred():
            for f_ in deferred:
                f_()
            del deferred[:]
        wi = 0
        di = 0
        for blk in range(ntok // NB):
            t0 = blk * NB
            for jg in range(NJ // 2):
                w = wg[wi % 2]
                wk = 'wg%d' % (wi % 2)
                wv = w.rearrange("p (c f) -> p c f", c=8)
                wi += 1
                S.dma('pool', (lambda wv, jg: lambda e: e.dma_start(out=wv[:, :, 0:256], in_=wgu_v[:, :, jg * 256:(jg + 1) * 256]))(wv, jg), R=[], W=[wk + 'g'])
                S.dma('pool', (lambda wv, jg: lambda e: e.dma_start(out=wv[:, :, 256:512], in_=wgu_v[:, :, DFF + jg * 256:DFF + (jg + 1) * 256]))(wv, jg), R=[], W=[wk + 'u'])
                for jj in range(2):
                    j = jg * 2 + jj
                    for sb in range(NB // 512):
                        bg = bank()
                        bu = bank()
                        c0 = xoff + t0 + sb * 512
                        xk = xkey_fn(t0 + sb * 512, 512)
                        for dc in range(8):
                            S.op('pe', (lambda wv, dc, jj, bg, c0: lambda e: e.matmul(PS(bg), lhsT=wv[:, dc, jj * 128:(jj + 1) * 128], rhs=xTv_src[:, dc, c0:c0 + 512], start=(dc == 0), stop=(dc == 7)))(wv, dc, jj, bg, c0),
                                 R=[wk + 'g'] + xk, W=[PK(bg)], inc=(dc == 7))
                        for dc in range(8):
                            S.op('pe', (lambda wv, dc, jj, bu, c0: lambda e: e.matmul(PS(bu), lhsT=wv[:, dc, 256 + jj * 128:256 + (jj + 1) * 128], rhs=xTv_src[:, dc, c0:c0 + 512], start=(dc == 0), stop=(dc == 7)))(wv, dc, jj, bu, c0),
                                 R=[wk + 'u'] + xk, W=[PK(bu)], inc=(dc == 7))
                        sgt = sg[(j * 2 + sb) % 2]
                        sgk = 'sg%d' % ((j * 2 + sb) % 2)
                        S.op('act', (lambda sgt, bg: lambda e: e.activation(out=sgt, in_=PS(bg), func=AF.Silu))(sgt, bg), R=[PK(bg)], W=[sgk])
                        S.op('dve', (lambda sgt, bu, j, sb: lambda e: e.tensor_tensor(out=hhv[:, j, sb * 512:(sb + 1) * 512], in0=sgt, in1=PS(bu), op=ALU.mult))(sgt, bu, j, sb),
                             R=[sgk, PK(bu)], W=['hh%d_%d' % (j, sb)])
            if blk == 0:
                dump(tagp + '_hh0', hhv[:, 0, :], ['hh0_0', 'hh0_1'])
                dump(tagp + '_hh21', hhv[:, 21, :], ['hh21_0', 'hh21_1'])
                dump(tagp + '_xT0', xTv_src[:, 0, xoff:xoff + 512], xkey_fn(0, 512))
            if os.environ.get('FFN_STOP') == 'up':
                continue
            for rnd in range(NB // 512):
                banks = [[bank(), bank()] for _ in range(4)]
                for j in range(NJ):
                    w = wd[di % 3]
                    wk = 'wd%d' % (di % 3)
                    di += 1
                    S.dma('pool', (lambda w, j: lambda e: e.dma_start(out=w, in_=wdown[j * 128:(j + 1) * 128, :]))(w, j), R=[], W=[wk])
                    for tt in range(4):
                        for dh in range(2):
                            b = banks[tt][dh]
                            S.op('pe', (lambda w, j, tt, dh, b, rnd: lambda e: e.matmul(PS(b), lhsT=hhv[:, j, rnd * 512 + tt * 128:rnd * 512 + (tt + 1) * 128], rhs=w[:, dh * 512:(dh + 1) * 512], start=(j == 0), stop=(j == NJ - 1)))(w, j, tt, dh, b, rnd),
                                 R=[wk, 'hh%d_%d' % (j, rnd)], W=[PK(b)], inc=(j == NJ - 1 or (tt == 3 and dh == 1)))
                cur = []
                for tt in range(4):
                    ti = (t0 + rnd * 512) // 128 + tt
                    y = ybuf[ti % 8]
                    yk = 'y%d' % (ti % 8)
                    S.dma('sp', (lambda y, ti: lambda e: e.dma_start(out=y, in_=res_dram[ti * 128:(ti + 1) * 128, :]))(y, ti), R=[], W=[yk])
                    for dh in range(2):
                        b = banks[tt][dh]
                        S.op('dve', (lambda y, b, dh: lambda e: e.scalar_tensor_tensor(out=y[:, dh * 512:(dh + 1) * 512], in0=PS(b), scalar=0.5 / ALPHA, in1=y[:, dh * 512:(dh + 1) * 512], op0=ALU.mult, op1=ALU.add))(y, b, dh),
                             R=[PK(b), yk], W=[yk])
                    cur.append((ti, y, yk))
                run_deferred()
                layer_norm_group([(y, yk) for (ti, y, yk) in cur], eps, tmp_sq, st)
                for (ti, y, yk) in cur:
                    later = consumer(ti, y, yk)
                    if later is not None:
                        deferred.append(later)
        run_deferred()
        A.release()

    def prep_xT(src_dram, ntok, dstv, doff, keyfn):
        A.mark()
        xb = [A.bf(1024) for _ in range(2)]
        for ti in range(ntok // 128):
            b = xb[ti % 2]
            bk = 'xb%d' % (ti % 2)
            S.dma('pool', (lambda b, ti: lambda e: e.dma_start(out=b, in_=src_dram[ti * 128:(ti + 1) * 128, :]))(b, ti), R=[], W=[bk])
            transpose_to(b, [bk], 8, dstv[:, :, doff + ti * 128:doff + (ti + 1) * 128], keyfn(ti * 128, 128))
        A.release()

    def mixer(half, full):
        A.mark()
        w_in = dr['w_in'].rearrange("(c p) f -> p c f", p=128)
        A.mark()
        wret = A.bf(8 * RETC)
        wretv = wret.rearrange("p (c f) -> p c f", c=8)
        c32r = A.f32(C32R_N)
        S.dma('sp', lambda e: e.dma_start(out=c32r, in_=dr['c32r']), R=[], W=['c32r'])

        def c32rv(name):
            o, n = C32R_OFF[name]
            return c32r[:, o:o + n]
        cos_t = A.f32(512)
        sin_t = A.f32(512)
        S.dma('sp', lambda e: e.dma_start(out=cos_t, in_=dr['cos'][:, half * 512:(half + 1) * 512]), R=[], W=['cos'])
        S.dma('sp', lambda e: e.dma_start(out=sin_t, in_=dr['sin'][:, half * 512:(half + 1) * 512]), R=[], W=['sin'])
        load_ptiles(['ret_gn_g', 'ret_gn_b'])
        for q4 in range(4):
            S.dma('pool', (lambda q4: lambda e: e.dma_start(out=wretv[:, :, q4 * 512:(q4 + 1) * 512], in_=w_in[:, :, q4 * 512:(q4 + 1) * 512]))(q4), R=[], W=['wret%d' % q4])
        qr = A.bf(512)
        kr = A.bf(512)
        vb = A.bf(512)
        vdb = A.bf(512)
        sgg = A.f32(512)
        t1 = A.f32(256)
        t2 = A.f32(256)
        t3 = A.f32(256)
        t4 = A.f32(256)
        qT = A.bf(512)
        qsTm = A.bf(1024)
        kTm = A.bf(1024)
        S.op('pool', lambda e: e.memset(qsTm, 0.0), R=[], W=['qsTm'])
        S.op('pool', lambda e: e.memset(kTm, 0.0), R=[], W=['kTm'])
        scm = A.bf(1024)
        o_sb = A.f32(512)
        o_sq = A.f32(512)
        gst = A.f32(64)
        mixr = A.bf(512)
        ret_deferred = []
        kdec = c32v('kdec')
        for ti in range(NT_HALF):
            c0 = XO + ti * 128
            gt = ti
            xk = ['x1T_%d' % ti]
            cosv = cos_t[:, gt * 32:(gt + 1) * 32].unsqueeze(1).broadcast_to([128, 8, 32])
            sinv = sin_t[:, gt * 32:(gt + 1) * 32].unsqueeze(1).broadcast_to([128, 8, 32])
            pb = {}
            which = ['q', 'k', 'v', 'g'] if full else ['k', 'v']
            RSUB = os.environ.get('RET_SUB', '')
            if RSUB == 'none':
                continue
            for nm in which:
                q4 = ['q', 'k', 'v', 'g'].index(nm)
                b = bank()
                pb[nm] = b
                for dc in range(8):
                    S.op('pe', (lambda dc, b, q4, c0: lambda e: e.matmul(PS(b), lhsT=x1Tv[:, dc, c0:c0 + 128], rhs=wretv[:, dc, q4 * 512:(q4 + 1) * 512], start=(dc == 0), stop=(dc == 7)))(dc, b, q4, c0),
                         R=xk + ['wret%d' % q4], W=[PK(b)], inc=(dc == 7))

            def rotary(b, dst, dkey):
                src = PS(b).rearrange("p (h d) -> p h d", h=8)
                dv = dst.rearrange("p (h d) -> p h d", h=8)
                a1 = t1.rearrange("p (h d) -> p h d", h=8)
                a2 = t2.rearrange("p (h d) -> p h d", h=8)
                a3 = t3.rearrange("p (h d) -> p h d", h=8)
                a4 = t4.rearrange("p (h d) -> p h d", h=8)
                pk = PK(b)
                S.op('dve', lambda e: e.tensor_tensor(out=a1, in0=src[:, :, 0:32], in1=cosv, op=ALU.mult), R=[pk, 'cos'], W=['rt1'])
                S.op('dve', lambda e: e.tensor_tensor(out=a2, in0=src[:, :, 32:64], in1=sinv, op=ALU.mult), R=[pk, 'sin'], W=['rt2'])
                S.op('dve', lambda e: e.tensor_tensor(out=a3, in0=src[:, :, 0:32], in1=sinv, op=ALU.mult), R=[pk, 'sin'], W=['rt3'])
                S.op('dve', lambda e: e.tensor_tensor(out=a4, in0=src[:, :, 32:64], in1=cosv, op=ALU.mult), R=[pk, 'cos'], W=['rt4'])
                S.op('pool', lambda e: e.tensor_tensor(out=dv[:, :, 0:32], in0=a1, in1=a2, op=ALU.subtract), R=['rt1', 'rt2'], W=[dkey])
                S.op('pool', lambda e: e.tensor_tensor(out=dv[:, :, 32:64], in0=a3, in1=a4, op=ALU.add), R=['rt3', 'rt4'], W=[dkey])

            for f_ in ret_deferred:
                f_()
            del ret_deferred[:]
            if RSUB == 'proj':
                continue
            rotary(pb['k'], kr, 'kr')
            if full:
                rotary(pb['q'], qr, 'qr')
            if RSUB == 'rot':
                continue
            bv = pb['v']
            S.op('act', (lambda bv: lambda e: e.activation(out=vb, in_=PS(bv), func=AF.Copy))(bv), R=[PK(bv)], W=['vb'])
            if RSUB == 'vb':
                continue
            S.op('dve', (lambda bv: lambda e: e.tensor_tensor(out=vdb, in0=PS(bv), in1=c32rv('kdecf'), op=ALU.mult))(bv), R=[PK(bv), 'c32r', 'vb'], W=['vdb'])
            RS = int(os.environ.get('RET_STOP', '99'))
            if RS <= 1:
                continue
            if full:
                bgg = pb['g']
                S.op('act', (lambda bgg: lambda e: e.activation(out=sgg, in_=PS(bgg), func=AF.Silu))(bgg), R=[PK(bgg)], W=['sgg'])
                b = bank()
                pk = PK(b)
                for i in range(4):
                    S.op('pe', (lambda i, b: lambda e: e.transpose(out=PSB(b)[:, i * 128:(i + 1) * 128], in_=qr[:, i * 128:(i + 1) * 128], identity=ident))(i, b), R=['qr', 'cb'], W=[pk], inc=False)
                for i in range(4):
                    S.op('pe', (lambda i, b: lambda e: e.transpose(out=PSB(b)[:, 512 + i * 128:512 + (i + 1) * 128], in_=kr[:, i * 128:(i + 1) * 128], identity=ident))(i, b), R=['kr', 'cb'], W=[pk], inc=(i == 3))
                S.op('act', (lambda b: lambda e: e.activation(out=qT, in_=PSB(b)[:, 0:512], func=AF.Copy))(b), R=[pk], W=['qT'])
                for hp in range(2):
                    rs_ = slice(hp * 64, (hp + 1) * 64)
                    S.op('dve', (lambda b, hp, rs_: lambda e: e.tensor_tensor(out=qsTm[rs_, :].rearrange("p (a c t) -> p a c t", a=4, c=2)[:, :, hp, :],
                                                                             in0=PSB(b)[rs_, 0:512].rearrange("p (a t) -> p a t", a=4),
                                                                             in1=c32rv('qdec')[rs_, :].rearrange("p (a t) -> p a t", a=4), op=ALU.mult))(b, hp, rs_), R=[pk, 'c32r'], W=['qsTm'])
                    S.op('act', (lambda b, hp, rs_: lambda e: e.activation(out=kTm[rs_, :].rearrange("p (a c t) -> p a c t", a=4, c=2)[:, :, hp, :],
                                                                          in_=PSB(b)[rs_, 512:1024].rearrange("p (a t) -> p a t", a=4), func=AF.Copy))(b, hp, rs_), R=[pk], W=['kTm'])
                if RS <= 2:
                    continue
                sb_ = [bank(), bank()]
                for h in range(8):
                    pr, hp = h // 2, h % 2
                    b = sb_[h // 4]
                    S.op('pe', (lambda h, pr, hp, b: lambda e: e.matmul(PS(b)[:, (h % 4) * 128:(h % 4 + 1) * 128], lhsT=kTm[:, h * 128:(h + 1) * 128],
                                                                       rhs=qT[:, pr * 128:(pr + 1) * 128], start=True, stop=True))(h, pr, hp, b),
                         R=['kTm', 'qT'], W=[PK(b)], inc=(h % 4 == 3))
                if RS == 24:
                    continue
                for hb in range(2):
                    b = sb_[hb]
                    S.op('dve', (lambda hb, b: lambda e: e.tensor_tensor(out=scm[:, hb * 512:(hb + 1) * 512], in0=PS(b), in1=c32rv('dmask')[:, hb * 512:(hb + 1) * 512], op=ALU.mult))(hb, b),
                         R=[PK(b), 'c32r'], W=['scm%d' % hb])
                if RS == 25:
                    continue
                bo = bank()
                for h in range(8):
                    pr, hp = h // 2, h % 2
                    S.op('pe', (lambda h, bo: lambda e: e.matmul(PS(bo)[:, h * 64:(h + 1) * 64], lhsT=scm[:, h * 128:(h + 1) * 128], rhs=vb[:, h * 64:(h + 1) * 64], start=True, stop=(RS == 26)))(h, bo),
                         R=['scm%d' % (h // 4), 'vb'], W=[PK(bo)], inc=(RS == 26 and h == 7))
                    if RS == 26:
                        continue
                    S.op('pe', (lambda h, pr, hp, bo: lambda e: e.matmul(PS(bo)[:, h * 64:(h + 1) * 64], lhsT=qsTm[:, h * 128:(h + 1) * 128],
                                                                        rhs=retSb[:, pr * 64:(pr + 1) * 64], start=False, stop=True))(h, pr, hp, bo),
                         R=['qsTm', 'retSb'], W=[PK(bo)], inc=(h == 7))
                S.op('act', (lambda bo: lambda e: e.activation(out=o_sb, in_=PS(bo), func=AF.Copy))(bo), R=[PK(bo)], W=['o_sb'])
            if RS <= 3:
                continue
            bk_ = bank()
            for pr in range(4):
                S.op('pe', (lambda pr, bk_: lambda e: e.matmul(PS(bk_)[:, pr * 128:(pr + 1) * 128], lhsT=kr[:, pr * 128:(pr + 1) * 128], rhs=vdb[:, pr * 128:(pr + 1) * 128], start=True, stop=True))(pr, bk_),
                     R=['kr', 'vdb'], W=[PK(bk_)], inc=(pr == 3))
            rs3 = retS.rearrange("p (a e) -> p a e", a=4)
            S.op('pool', lambda e: e.tensor_tensor(out=retS, in0=retS, in1=c32rv('gam128f'), op=ALU.mult), R=['retS', 'c32r', 'retSb'], W=['retS'])
            for hp in range(2):
                S.op('dve', (lambda hp, bk_: lambda e: e.tensor_tensor(out=retS[hp * 64:(hp + 1) * 64, :].rearrange("p (a e) -> p a e", a=4), in0=retS[hp * 64:(hp + 1) * 64, :].rearrange("p (a e) -> p a e", a=4),
                                                                    in1=PS(bk_)[hp * 64:(hp + 1) * 64, :].rearrange("p (a f) -> p a f", a=4)[:, :, hp * 64:(hp + 1) * 64], op=ALU.add))(hp, bk_),
                     R=[PK(bk_), 'retS'], W=['retS'])
            S.op('act', lambda e: e.activation(out=retSb, in_=retS, func=AF.Copy), R=['retS'], W=['retSb'])
            if full and ti == 0:
                dump('retS1', retS, ['retS'])
            if full and ti == 1:
                dump('retS2', retS, ['retS'])
                dump('kr1', kr, ['kr'])
            if RS <= 4:
                continue
            if full:
                group_norm_out(o_sb, 'o_sb', o_sq, 'o_sq', gst, 1e-5, ptile['ret_gn_g'], ptile['ret_gn_b'], 'pt_ret_gn_g', 'pt_ret_gn_b')
                S.op('dve', lambda e: e.tensor_tensor(out=mixr, in0=o_sb, in1=sgg, op=ALU.mult), R=['o_sb', 'sgg'], W=['mixr'])
                ret_deferred.append((lambda ti: lambda: transpose_to(mixr, ['mixr'], 4, mixTv[:, 0:4, ti * 128:(ti + 1) * 128], ['mixT_r%d' % ti]))(ti))
        for f_ in ret_deferred:
            f_()
        A.release()
        S.barrier()
        if os.environ.get('MIX_STOP') != 'ret':
            rwkv(half, full)
        if full:
            for c_ in range(8):
                dump('mixT%d' % c_, mixTv[:, c_, :], [])
            dump('retS', retS, [])
            dump('rwA', rwA, [])
        A.release()

    def group_norm_out(o, okey, sq, sqk, gst, eps, gt, bt, gk, bk):
        o3 = o.rearrange("p (h d) -> p h d", h=8)
        S.op('act', lambda e: e.activation(out=sq, in_=o, func=AF.Square), R=[okey], W=[sqk])
        S.op('dve', lambda e: e.tensor_reduce(out=gst[:, 0:8], in_=o3, axis=AX.X, op=ALU.add), R=[okey], W=['gn0'])
        S.op('dve', lambda e: e.tensor_reduce(out=gst[:, 8:16], in_=sq.rearrange("p (h d) -> p h d", h=8), axis=AX.X, op=ALU.add), R=[sqk], W=['gn1'])
        S.op('dve', lambda e: e.tensor_scalar(out=gst[:, 16:24], in0=gst[:, 0:8], scalar1=1.0 / 64, scalar2=None, op0=ALU.mult), R=['gn0'], W=['gn2'])
        S.op('dve', lambda e: e.tensor_tensor(out=gst[:, 24:32], in0=gst[:, 16:24], in1=gst[:, 16:24], op=ALU.mult), R=['gn2'], W=['gn3'])
        S.op('dve', lambda e: e.scalar_tensor_tensor(out=gst[:, 32:40], in0=gst[:, 8:16], scalar=1.0 / 64, in1=gst[:, 24:32], op0=ALU.mult, op1=ALU.subtract), R=['gn1', 'gn3'], W=['gn4'])
        S.op('dve', lambda e: e.tensor_scalar(out=gst[:, 32:40], in0=gst[:, 32:40], scalar1=float(eps), scalar2=None, op0=ALU.add), R=['gn4'], W=['gn4'])
        S.op('act', lambda e: e.activation(out=gst[:, 40:48], in_=gst[:, 32:40], func=AF.Sqrt), R=['gn4'], W=['gn5'])
        S.op('dve', lambda e: e.reciprocal(out=gst[:, 48:56], in_=gst[:, 40:48]), R=['gn5'], W=['gn6'])
        S.op('dve', lambda e: e.tensor_tensor(out=o3, in0=o3, in1=gst[:, 16:24].unsqueeze(2).broadcast_to([128, 8, 64]), op=ALU.subtract), R=[okey, 'gn2'], W=[okey])
        S.op('dve', lambda e: e.tensor_tensor(out=o3, in0=o3, in1=gst[:, 48:56].unsqueeze(2).broadcast_to([128, 8, 64]), op=ALU.mult), R=[okey, 'gn6'], W=[okey])
        S.op('pool', lambda e: e.tensor_tensor(out=o, in0=o, in1=gt, op=ALU.mult), R=[okey, gk], W=[okey])
        S.op('pool', lambda e: e.tensor_tensor(out=o, in0=o, in1=bt, op=ALU.add), R=[okey, bk], W=[okey])

    def rwkv(half, full):
        A.mark()
        w_in = dr['w_in'].rearrange("(c p) f -> p c f", p=128)
        load_ptiles(['rw_k_k', 'rw_k_a', 'rw_r_k', 'rw_gn_g', 'rw_gn_b'])
        Wa = A.bf(8 * RWC)
        Wb = A.bf(8 * RWC)
        Wav = Wa.rearrange("p (c f) -> p c f", c=8)
        Wbv = Wb.rearrange("p (c f) -> p c f", c=8)
        NW_ = 8 * RWC
        if half == 0 or 'mix_0' not in plan:
            A.mark()
            mu_t = A.f32(RWC)
            omu_t = A.f32(RWC)
            stg = [A.f32(RWC) for _ in range(2)]
            S.dma('sp', lambda e: e.dma_start(out=mu_t, in_=dr['rw_mu'].partition_broadcast(128)), R=[], W=['mu'])
            S.op('dve', lambda e: e.tensor_scalar(out=omu_t, in0=mu_t, scalar1=-1.0, scalar2=1.0, op0=ALU.mult, op1=ALU.add), R=['mu'], W=['omu'])
            for dc in range(8):
                sgb = stg[dc % 2]
                sk = 'stg%d' % (dc % 2)
                S.dma('sp', lambda e: e.dma_start(out=sgb, in_=dr['w_in'][dc * 128:(dc + 1) * 128, RETC:INC]), R=[], W=[sk])
                S.op('dve', lambda e: e.tensor_tensor(out=Wav[:, dc, :], in0=sgb, in1=omu_t, op=ALU.mult), R=[sk, 'omu'], W=['Wa'])
                S.op('pool', lambda e: e.tensor_tensor(out=Wbv[:, dc, :], in0=sgb, in1=mu_t, op=ALU.mult), R=[sk, 'mu'], W=['Wb'])
            for q_ in range(4):
                S.dma('sp', lambda e: e.dma_start(out=wab_s[:, q_ * (NW_ // 4):(q_ + 1) * (NW_ // 4)], in_=Wa[:, q_ * (NW_ // 4):(q_ + 1) * (NW_ // 4)]), R=['Wa'], W=['wabs_a%d' % q_], sk='wabs')
                S.dma('sp', lambda e: e.dma_start(out=wab_s[:, NW_ + q_ * (NW_ // 4):NW_ + (q_ + 1) * (NW_ // 4)], in_=Wb[:, q_ * (NW_ // 4):(q_ + 1) * (NW_ // 4)]), R=['Wb'], W=['wabs_b%d' % q_], sk='wabs')
            A.release()
        else:
            for q_ in range(4):
                S.dma('sp', lambda e: e.dma_start(out=Wa[:, q_ * (NW_ // 4):(q_ + 1) * (NW_ // 4)], in_=wab_s[:, q_ * (NW_ // 4):(q_ + 1) * (NW_ // 4)]), R=[], W=['Wa'], sk='wa_ld%d' % q_)
                S.dma('sp', lambda e: e.dma_start(out=Wb[:, q_ * (NW_ // 4):(q_ + 1) * (NW_ // 4)], in_=wab_s[:, NW_ + q_ * (NW_ // 4):NW_ + (q_ + 1) * (NW_ // 4)]), R=[], W=['Wb'], sk='wb_ld%d' % q_)
        wup = A.f32(512)
        aup = A.f32(512)
        gup1 = A.bf(512)
        gup2 = A.bf(512)
        w0r = A.f32(512)
        a0r = A.f32(512)
        for t_, k_ in ((wup, 'wup'), (aup, 'aup'), (gup2, 'gup2'), (w0r, 'w0r'), (a0r, 'a0r')):
            S.op('pool', (lambda t_: lambda e: e.memset(t_, 0.0))(t_), R=[], W=[k_])
        S.dma('sp', lambda e: e.dma_start(out=wup[0:64, :], in_=dr['rw_w_up']), R=[], W=['wup'])
        S.dma('sp', lambda e: e.dma_start(out=aup[64:128, :], in_=dr['rw_a_up']), R=[], W=['aup'])
        S.dma('pool', lambda e: e.dma_start(out=gup1, in_=dr['rw_g_up'][0:128, :]), R=[], W=['gup1'])
        S.dma('pool', lambda e: e.dma_start(out=gup2[96:128, :], in_=dr['rw_g_up'][128:160, :]), R=[], W=['gup2'])
        S.dma('sp', lambda e: e.dma_start(out=w0r[0:1, :], in_=dr['rw_w0']), R=[], W=['w0r'])
        S.dma('sp', lambda e: e.dma_start(out=a0r[0:1, :], in_=dr['rw_a0']), R=[], W=['a0r'])
        ones_r = c32v('ones')
        twT = A.f32(128)
        sgd1 = A.bf(128)
        sgd2 = A.bf(128)
        sw = A.f32(512)
        a_t = A.f32(512)
        gi_t = A.f32(512)
        kkr = A.f32(512)
        sq = A.f32(512)
        kp = A.f32(512)
        u1 = sq
        g_t = sw
        v32 = A.f32(512)
        st = A.f32(64)
        gst2 = A.f32(64)
        r32 = A.f32(512)
        k32 = A.f32(512)
        gp_t = k32
        gC = A.f32(4)
        rh = A.bf(512)
        kt = A.bf(512)
        bt_ = A.bf(512)
        kh = A.bf(512)
        vb = A.bf(512)
        XT = A.bf(4 * 256)
        XTv = XT.rearrange("p (a b t) -> p a b t", a=4, b=2)
        btT = A.bf(512)
        khTm = A.bf(1024)
        rhTm = A.bf(1024)
        btTm = A.bf(1024)
        ktTm = A.bf(1024)
        for t_, k_ in ((khTm, 'khTm'), (rhTm, 'rhTm'), (btTm, 'btTm'), (ktTm, 'ktTm')):
            S.op('pool', (lambda t_: lambda e: e.memset(t_, 0.0))(t_), R=[], W=[k_])
        LkT = A.bf(1024)
        MbT = A.bf(1024)
        MkT = A.bf(1024)
        Abuf = [[A.bf(256) for _ in range(2)] for _ in range(4)]
        BQbuf = [[A.bf(512) for _ in range(2)] for _ in range(4)]
        TT = A.bf(1024)
        Xb = A.bf(512)
        Ub = A.bf(512)
        mixw = A.bf(512)
        bsum = st[:, 56:64]
        rw_deferred = []

        for ti in range(NT_HALF):
            c0 = XO + ti * 128
            xk = ['x1T_%d' % ti] + (['x1T_%d' % (ti - 1)] if ti > 0 else ['x1T_prev'])
            def proj(nm):
                qi = ['r', 'k', 'v'].index(nm)
                b = bank()
                n = 0
                for (Wv, wkey, sh) in ((Wav, 'Wa', 0), (Wbv, 'Wb', 1)):
                    for dc in range(8):
                        S.op('pe', (lambda Wv, sh, dc, b, qi, n: lambda e: e.matmul(PS(b), lhsT=x1Tv[:, dc, c0 - sh:c0 - sh + 128], rhs=Wv[:, dc, qi * 512:(qi + 1) * 512], start=(n == 0), stop=(n == 15)))(Wv, sh, dc, b, qi, n),
                             R=xk + [wkey], W=[PK(b)], inc=(n == 15))
                        n += 1
                return b
            bl = bank()
            for gi_, (cs, cn) in enumerate(((1536, 128), (1664, 128), (1696, 128))):
                n = 0
                for (Wv, wkey, sh) in ((Wav, 'Wa', 0), (Wbv, 'Wb', 1)):
                    for dc in range(8):
                        S.op('pe', (lambda Wv, sh, dc, cs, cn, gi_, n: lambda e: e.matmul(PS(bl)[0:cn, gi_ * 128:(gi_ + 1) * 128], lhsT=Wv[:, dc, cs:cs + cn], rhs=x1Tv[:, dc, c0 - sh:c0 - sh + 128], start=(n == 0), stop=(n == 15)))(Wv, sh, dc, cs, cn, gi_, n),
                             R=xk + [wkey], W=[PK(bl)], inc=(n == 15 and gi_ == 2))
                        n += 1
            plk = PK(bl)
            S.op('act', lambda e: e.activation(out=twT[0:64, :], in_=PS(bl)[0:64, 0:128], func=AF.Tanh), R=[plk], W=['twT'])
            S.op('act', lambda e: e.activation(out=twT[64:128, :], in_=PS(bl)[64:128, 0:128], func=AF.Copy), R=[plk], W=['twT'])
            S.op('act', lambda e: e.activation(out=sgd1, in_=PS(bl)[:, 128:256], func=AF.Sigmoid), R=[plk], W=['sgd1'])
            S.op('act', lambda e: e.activation(out=sgd2, in_=PS(bl)[:, 256:384], func=AF.Sigmoid), R=[plk], W=['sgd2'])
            bk2 = proj('k')
            kk_ = PK(bk2)
            S.op('act', lambda e: e.activation(out=k32, in_=PS(bk2), func=AF.Copy), R=[kk_], W=['k32'])
            bw = bank()
            S.op('pe', lambda e: e.matmul(PS(bw), lhsT=twT, rhs=wup, start=True, stop=False), R=['twT', 'wup'], W=[PK(bw)], inc=False)
            S.op('pe', lambda e: e.matmul(PS(bw), lhsT=ones_r, rhs=w0r, start=False, stop=True), R=['c32', 'w0r'], W=[PK(bw)])
            ba = bank()
            S.op('pe', lambda e: e.matmul(PS(ba), lhsT=twT, rhs=aup, start=True, stop=False), R=['twT', 'aup'], W=[PK(ba)], inc=False)
            S.op('pe', lambda e: e.matmul(PS(ba), lhsT=ones_r, rhs=a0r, start=False, stop=True), R=['c32', 'a0r'], W=[PK(ba)])
            S.op('act', lambda e: e.activation(out=sw, in_=PS(bw), func=AF.Sigmoid), R=[PK(bw)], W=['sw'])
            S.op('act', lambda e: e.activation(out=a_t, in_=PS(ba), func=AF.Sigmoid), R=[PK(ba)], W=['a_t'])
            bv = proj('v')
            vk_ = PK(bv)
            S.op('act', lambda e: e.activation(out=v32, in_=PS(bv), func=AF.Copy), R=[vk_], W=['v32'])
            S.op('dve', lambda e: e.tensor_copy(out=vb, in_=PS(bv)), R=[vk_], W=['vbw'])
            bc = bank()
            S.op('pe', lambda e: e.matmul(PS(bc), lhsT=c32v('tri_incl'), rhs=sw, start=True, stop=True), R=['c32', 'sw'], W=[PK(bc)])
            bx = bank()
            S.op('pe', lambda e: e.matmul(PS(bx), lhsT=c32v('tri_strict'), rhs=sw, start=True, stop=True), R=['c32', 'sw'], W=[PK(bx)])
            bgc = bank()
            for pr in range(4):
                S.op('pe', (lambda pr: lambda e: e.matmul(PS(bgc)[:, pr:pr + 1], lhsT=sw[:, pr * 128:(pr + 1) * 128], rhs=c32v('negcol'), start=True, stop=True))(pr), R=['sw', 'c32'], W=[PK(bgc)], inc=(pr == 3))
            if full:
                br = proj('r')
                rk_ = PK(br)
                S.op('act', lambda e: e.activation(out=r32, in_=PS(br), func=AF.Copy), R=[rk_], W=['r32'])
            for f_ in rw_deferred:
                f_()
            del rw_deferred[:]
            S.op('dve', lambda e: e.tensor_tensor(out=kkr, in0=k32, in1=ptile['rw_k_k'], op=ALU.mult), R=['k32', 'pt_rw_k_k'], W=['kkr'])
            S.op('act', lambda e: e.activation(out=sq, in_=kkr, func=AF.Square), R=['kkr'], W=['sq'])
            S.op('dve', lambda e: e.tensor_reduce(out=st[:, 0:8], in_=sq.rearrange("p (h d) -> p h d", h=8), axis=AX.X, op=ALU.add), R=['sq'], W=['st0'])
            S.op('act', lambda e: e.activation(out=st[:, 8:16], in_=st[:, 0:8], func=AF.Sqrt), R=['st0'], W=['st1'])
            S.op('dve', lambda e: e.tensor_scalar(out=st[:, 8:16], in0=st[:, 8:16], scalar1=1e-12, scalar2=None, op0=ALU.max), R=['st1'], W=['st1'])
            S.op('dve', lambda e: e.reciprocal(out=st[:, 16:24], in_=st[:, 8:16]), R=['st1'], W=['st2'])
            S.op('dve', lambda e: e.tensor_tensor(out=kkr.rearrange("p (h d) -> p h d", h=8), in0=kkr.rearrange("p (h d) -> p h d", h=8), in1=st[:, 16:24].unsqueeze(2).broadcast_to([128, 8, 64]), op=ALU.mult), R=['kkr', 'st2'], W=['kkr'])
            S.op('dve', lambda e: e.scalar_tensor_tensor(out=u1, in0=a_t, scalar=-1.0, in1=ptile['rw_k_a'], op0=ALU.add, op1=ALU.mult), R=['a_t', 'pt_rw_k_a'], W=['sq'])
            S.op('dve', lambda e: e.scalar_tensor_tensor(out=kp, in0=u1, scalar=1.0, in1=k32, op0=ALU.add, op1=ALU.mult), R=['sq', 'k32'], W=['kp'])
            if full:
                S.op('dve', lambda e: e.tensor_tensor(out=u1, in0=r32, in1=kp, op=ALU.mult), R=['r32', 'kp', 'sq'], W=['sq'])
                S.op('pool', lambda e: e.tensor_tensor(out=u1, in0=u1, in1=ptile['rw_r_k'], op=ALU.mult), R=['sq', 'pt_rw_r_k'], W=['sq'])
                S.op('dve', lambda e: e.tensor_reduce(out=bsum, in_=u1.rearrange("p (h d) -> p h d", h=8), axis=AX.X, op=ALU.add), R=['sq'], W=['bsum'])
            S.op('act', lambda e: e.activation(out=g_t, in_=PS(bc), func=AF.Exp), R=[PK(bc)], W=['sw'])
            S.op('act', lambda e: e.activation(out=gi_t, in_=PS(bc), func=AF.Exp, scale=-1.0), R=[PK(bc)], W=['gi_t'])
            S.op('act', lambda e: e.activation(out=gp_t, in_=PS(bx), func=AF.Exp), R=[PK(bx)], W=['k32'])
            S.op('act', lambda e: e.activation(out=gC, in_=PS(bgc)[:, 0:4], func=AF.Exp), R=[PK(bgc)], W=['gC'])
            if full:
                S.op('dve', lambda e: e.tensor_tensor(out=rh, in0=r32, in1=g_t, op=ALU.mult), R=['r32', 'sw'], W=['rh'])
            S.op('pool', lambda e: e.tensor_tensor(out=kt, in0=kp, in1=gi_t, op=ALU.mult), R=['kp', 'gi_t'], W=['kt'])
            S.op('pool', lambda e: e.tensor_tensor(out=sq, in0=kkr, in1=a_t, op=ALU.mult), R=['kkr', 'a_t', 'sq'], W=['sq'])
            S.op('pool', lambda e: e.tensor_tensor(out=bt_, in0=sq, in1=gi_t, op=ALU.mult), R=['sq', 'gi_t'], W=['bt'])
            S.op('pool', lambda e: e.tensor_tensor(out=kh, in0=kkr, in1=gp_t, op=ALU.mult), R=['kkr', 'k32'], W=['kh'])
            if full:
                S.op('dve', lambda e: e.tensor_tensor(out=gi_t.rearrange("p (h d) -> p h d", h=8), in0=v32.rearrange("p (h d) -> p h d", h=8), in1=bsum.unsqueeze(2).broadcast_to([128, 8, 64]), op=ALU.mult), R=['v32', 'bsum', 'gi_t'], W=['gi_t'])
            b1 = bank()
            for i in range(4):
                S.op('pe', (lambda i: lambda e: e.transpose(out=PSB(b1)[:, i * 256:i * 256 + 128], in_=kh[:, i * 128:(i + 1) * 128], identity=ident))(i), R=['kh', 'cb'], W=[PK(b1)], inc=(i == 3 and not full))
                if full:
                    S.op('pe', (lambda i: lambda e: e.transpose(out=PSB(b1)[:, i * 256 + 128:i * 256 + 256], in_=rh[:, i * 128:(i + 1) * 128], identity=ident))(i), R=['rh', 'cb'], W=[PK(b1)], inc=(i == 3))
            if full:
                S.op('act', lambda e: e.activation(out=XT, in_=PSB(b1), func=AF.Copy), R=[PK(b1)], W=['XT'])
            else:
                S.op('act', lambda e: e.activation(out=XT.rearrange("p (a c t) -> p a c t", a=4, c=2)[:, :, 0, :], in_=PSB(b1).rearrange("p (a c t) -> p a c t", a=4, c=2)[:, :, 0, :], func=AF.Copy), R=[PK(b1)], W=['XT'])
            for hp in range(2):
                rs_ = slice(hp * 64, (hp + 1) * 64)
                srcv = PSB(b1)[rs_, :].rearrange("p (a c t) -> p a c t", a=4, c=2)
                S.op('act', lambda e: e.activation(out=khTm[rs_, :].rearrange("p (a c t) -> p a c t", a=4, c=2)[:, :, hp, :], in_=srcv[:, :, 0, :], func=AF.Copy), R=[PK(b1)], W=['khTm'])
                if full:
                    S.op('dve', lambda e: e.tensor_copy(out=rhTm[rs_, :].rearrange("p (a c t) -> p a c t", a=4, c=2)[:, :, hp, :], in_=srcv[:, :, 1, :]), R=[PK(b1)], W=['rhTm'])
            b2 = bank()
            for i in range(4):
                S.op('pe', (lambda i: lambda e: e.transpose(out=PSB(b2)[:, i * 128:(i + 1) * 128], in_=bt_[:, i * 128:(i + 1) * 128], identity=ident))(i), R=['bt', 'cb'], W=[PK(b2)], inc=False)
            for i in range(4):
                S.op('pe', (lambda i: lambda e: e.transpose(out=PSB(b2)[:, 512 + i * 128:512 + (i + 1) * 128], in_=kt[:, i * 128:(i + 1) * 128], identity=ident))(i), R=['kt', 'cb'], W=[PK(b2)], inc=(i == 3))
            S.op('dve', lambda e: e.tensor_copy(out=btT, in_=PSB(b2)[:, 0:512]), R=[PK(b2)], W=['btT'])
            for hp in range(2):
                rs_ = slice(hp * 64, (hp + 1) * 64)
                S.op('act', lambda e: e.activation(out=btTm[rs_, :].rearrange("p (a c t) -> p a c t", a=4, c=2)[:, :, hp, :], in_=PSB(b2)[rs_, 0:512].rearrange("p (a t) -> p a t", a=4), func=AF.Copy), R=[PK(b2)], W=['btTm'])
                S.op('dve', lambda e: e.tensor_copy(out=ktTm[rs_, :].rearrange("p (a c t) -> p a c t", a=4, c=2)[:, :, hp, :], in_=PSB(b2)[rs_, 512:1024].rearrange("p (a t) -> p a t", a=4)), R=[PK(b2)], W=['ktTm'])
            for pr in range(4):
                bm1, bm2, bm3 = bank(), bank(), bank()
                for hp in range(2):
                    ps_ = slice(hp * 64, (hp + 1) * 64)
                    NX_ = 256 if full else 128
                    rhs_x = XT[:, pr * 256:pr * 256 + NX_]
                    hh_ = pr * 2 + hp
                    S.op('pe', (lambda hp, ps_, rhs_x, bm1: lambda e: e.matmul(PS(bm1)[:, hp * 256:hp * 256 + NX_], lhsT=btTm[:, hh_ * 128:(hh_ + 1) * 128], rhs=rhs_x, start=True, stop=True))(hp, ps_, rhs_x, bm1),
                         R=['btTm', 'XT'], W=[PK(bm1)], inc=(hp == 1))
                    S.op('pe', (lambda hp, ps_, rhs_x, bm2: lambda e: e.matmul(PS(bm2)[:, hp * 256:hp * 256 + NX_], lhsT=ktTm[:, hh_ * 128:(hh_ + 1) * 128], rhs=rhs_x, start=True, stop=True))(hp, ps_, rhs_x, bm2),
                         R=['ktTm', 'XT'], W=[PK(bm2)], inc=(hp == 1))
                    S.op('pe', (lambda hp, ps_, bm3: lambda e: e.matmul(PS(bm3)[:, hp * 128:(hp + 1) * 128], lhsT=khTm[:, hh_ * 128:(hh_ + 1) * 128], rhs=btT[:, pr * 128:(pr + 1) * 128], start=True, stop=True))(hp, ps_, bm3),
                         R=['btT', 'khTm'], W=[PK(bm3)], inc=(hp == 1))
                A0 = Abuf[pr][0]
                BQ0 = BQbuf[pr][0].rearrange("p (h x) -> p h x", h=2)
                m1v = cbv('m1').rearrange("p (h x) -> p h x", h=2)
                p1v = PS(bm1).rearrange("p (h x) -> p h x", h=2)
                S.op('dve', (lambda BQ0, p1v, m1v: lambda e: e.tensor_tensor(out=BQ0[:, :, 0:128], in0=p1v[:, :, 0:128], in1=m1v[:, :, 0:128], op=ALU.mult))(BQ0, p1v, m1v),
                     R=[PK(bm1), 'cb'], W=['BQ%d_0' % pr])
                S.op('pool', (lambda BQ0: lambda e: e.tensor_copy(out=BQ0[:, :, 128:256], in_=ident.unsqueeze(1).broadcast_to([128, 2, 128])))(BQ0), R=['cb'], W=['BQ%d_0' % pr])
                if full:
                    S.op('dve', (lambda p1v, m1v: lambda e: e.tensor_tensor(out=MbT[:, pr * 256:(pr + 1) * 256].rearrange("p (h x) -> p h x", h=2), in0=p1v[:, :, 128:256], in1=m1v[:, :, 128:256], op=ALU.mult))(p1v, m1v),
                         R=[PK(bm1), 'cb'], W=['MbT%d' % pr])
                m2v = cbv('m2').rearrange("p (h x) -> p h x", h=2)
                p2v = PS(bm2).rearrange("p (h x) -> p h x", h=2)
                S.op('dve', (lambda p2v, m2v: lambda e: e.tensor_tensor(out=LkT[:, pr * 256:(pr + 1) * 256].rearrange("p (h x) -> p h x", h=2), in0=p2v[:, :, 0:128], in1=m2v[:, :, 0:128], op=ALU.mult))(p2v, m2v),
                     R=[PK(bm2), 'cb'], W=['LkT%d' % pr])
                if full:
                    S.op('dve', (lambda p2v, m2v: lambda e: e.tensor_tensor(out=MkT[:, pr * 256:(pr + 1) * 256].rearrange("p (h x) -> p h x", h=2), in0=p2v[:, :, 128:256], in1=m2v[:, :, 128:256], op=ALU.mult))(p2v, m2v),
                         R=[PK(bm2), 'cb'], W=['MkT%d' % pr])
                S.op('dve', (lambda A0, bm3: lambda e: e.tensor_tensor(out=A0, in0=PS(bm3)[:, 0:256], in1=cbv('m3'), op=ALU.mult))(A0, bm3), R=[PK(bm3), 'cb'], W=['A%d_0' % pr])
            for lv in range(7):
                last = (lv == 6)
                for pr in range(4):
                    cur, nxt = lv % 2, (lv + 1) % 2
                    Ac = Abuf[pr][cur].rearrange("p (h x) -> p h x", h=2)
                    BQc = BQbuf[pr][cur].rearrange("p (h x) -> p h x", h=2)
                    ak, bqk = 'A%d_%d' % (pr, cur), 'BQ%d_%d' % (pr, cur)
                    if not last:
                        bA = bank()
                        for hp in range(2):
                            S.op('pe', (lambda hp, BQc, Ac, bA: lambda e: e.matmul(PS(bA)[:, hp * 128:(hp + 1) * 128], lhsT=BQc[:, hp, 0:128], rhs=Ac[:, hp, :], start=True, stop=True))(hp, BQc, Ac, bA),
                                 R=[ak, bqk], W=[PK(bA)], inc=(hp == 1))
                        bB = bank()
                        for hp in range(2):
                            S.op('pe', (lambda hp, BQc, Ac, bB: lambda e: e.matmul(PS(bB)[:, hp * 256:hp * 256 + 128], lhsT=Ac[:, hp, :], rhs=BQc[:, hp, 0:128], start=True, stop=True))(hp, BQc, Ac, bB),
                                 R=[ak, bqk], W=[PK(bB)], inc=False)
                            S.op('pe', (lambda hp, BQc, Ac, bB: lambda e: e.matmul(PS(bB)[:, hp * 256 + 128:hp * 256 + 256], lhsT=Ac[:, hp, :], rhs=BQc[:, hp, 128:256], start=True, stop=False))(hp, BQc, Ac, bB),
                                 R=[ak, bqk], W=[PK(bB)], inc=False)
                            S.op('pe', (lambda hp, BQc, bB: lambda e: e.matmul(PS(bB)[:, hp * 256 + 128:hp * 256 + 256], lhsT=ident, rhs=BQc[:, hp, 128:256], start=False, stop=True))(hp, BQc, bB),
                                 R=[bqk, 'cb'], W=[PK(bB)], inc=(hp == 1))
                        An = Abuf[pr][nxt]
                        BQn = BQbuf[pr][nxt]
                        S.op('dve' if pr % 2 == 0 else 'act', (lambda An, bA, pr: (lambda e: e.tensor_copy(out=An, in_=PS(bA)[:, 0:256])) if pr % 2 == 0 else (lambda e: e.activation(out=An, in_=PS(bA)[:, 0:256], func=AF.Copy)))(An, bA, pr),
                             R=[PK(bA)], W=['A%d_%d' % (pr, nxt)])
                        S.op('dve' if pr % 2 else 'act', (lambda BQn, bB, pr: (lambda e: e.tensor_copy(out=BQn, in_=PS(bB))) if pr % 2 else (lambda e: e.activation(out=BQn, in_=PS(bB), func=AF.Copy)))(BQn, bB, pr),
                             R=[PK(bB)], W=['BQ%d_%d' % (pr, nxt)])
                    else:
                        bB = bank()
                        for hp in range(2):
                            S.op('pe', (lambda hp, BQc, Ac, bB: lambda e: e.matmul(PS(bB)[:, hp * 128:(hp + 1) * 128], lhsT=Ac[:, hp, :], rhs=BQc[:, hp, 128:256], start=True, stop=False))(hp, BQc, Ac, bB),
                                 R=[ak, bqk], W=[PK(bB)], inc=False)
                            S.op('pe', (lambda hp, BQc, bB: lambda e: e.matmul(PS(bB)[:, hp * 128:(hp + 1) * 128], lhsT=ident, rhs=BQc[:, hp, 128:256], start=False, stop=True))(hp, BQc, bB),
                                 R=[bqk, 'cb'], W=[PK(bB)], inc=(hp == 1))
                        S.op('dve', (lambda bB, pr: lambda e: e.tensor_copy(out=TT[:, pr * 256:(pr + 1) * 256], in_=PS(bB)[:, 0:256]))(bB, pr), R=[PK(bB)], W=['TT%d' % pr])
            bX = bank()
            for h in range(8):
                pr, hp = h // 2, h % 2
                ps_ = slice(hp * 64, (hp + 1) * 64)
                S.op('pe', (lambda h, pr, ps_: lambda e: e.matmul(PS(bX)[:, h * 64:(h + 1) * 64], lhsT=khTm[:, h * 128:(h + 1) * 128], rhs=rwAb[:, pr * 64:(pr + 1) * 64], start=True, stop=False))(h, pr, ps_),
                     R=['khTm', 'rwAb'], W=[PK(bX)], inc=False)
                S.op('pe', (lambda h: lambda e: e.matmul(PS(bX)[:, h * 64:(h + 1) * 64], lhsT=LkT[:, h * 128:(h + 1) * 128], rhs=vb[:, h * 64:(h + 1) * 64], start=False, stop=True))(h),
                     R=['LkT%d' % pr, 'vbw'], W=[PK(bX)], inc=(h == 7))
            S.op('act', lambda e: e.activation(out=Xb, in_=PS(bX), func=AF.Copy, scale=-1.0), R=[PK(bX)], W=['Xb'])
            bU = bank()
            for h in range(8):
                S.op('pe', (lambda h: lambda e: e.matmul(PS(bU)[:, h * 64:(h + 1) * 64], lhsT=TT[:, h * 128:(h + 1) * 128], rhs=Xb[:, h * 64:(h + 1) * 64], start=True, stop=True))(h),
                     R=['TT%d' % (h // 2), 'Xb'], W=[PK(bU)], inc=(h == 7))
            S.op('act', lambda e: e.activation(out=Ub, in_=PS(bU), func=AF.Copy), R=[PK(bU)], W=['Ub'])
            if full:
                bY = bank()
                for h in range(8):
                    pr, hp = h // 2, h % 2
                    ps_ = slice(hp * 64, (hp + 1) * 64)
                    S.op('pe', (lambda h, pr, ps_: lambda e: e.matmul(PS(bY)[:, h * 64:(h + 1) * 64], lhsT=rhTm[:, h * 128:(h + 1) * 128], rhs=rwAb[:, pr * 64:(pr + 1) * 64], start=True, stop=False))(h, pr, ps_),
                         R=['rhTm', 'rwAb'], W=[PK(bY)], inc=False)
                    S.op('pe', (lambda h: lambda e: e.matmul(PS(bY)[:, h * 64:(h + 1) * 64], lhsT=MbT[:, h * 128:(h + 1) * 128], rhs=Ub[:, h * 64:(h + 1) * 64], start=False, stop=False))(h),
                         R=['MbT%d' % pr, 'Ub'], W=[PK(bY)], inc=False)
                    S.op('pe', (lambda h: lambda e: e.matmul(PS(bY)[:, h * 64:(h + 1) * 64], lhsT=MkT[:, h * 128:(h + 1) * 128], rhs=vb[:, h * 64:(h + 1) * 64], start=False, stop=True))(h),
                         R=['MkT%d' % pr, 'vbw'], W=[PK(bY)], inc=(h == 7))
                S.op('act', lambda e: e.activation(out=kp, in_=PS(bY), func=AF.Copy), R=[PK(bY)], W=['kp'])
            bS = bank()
            for pr in range(4):
                S.op('pe', (lambda pr: lambda e: e.matmul(PS(bS)[:, pr * 128:(pr + 1) * 128], lhsT=bt_[:, pr * 128:(pr + 1) * 128], rhs=Ub[:, pr * 128:(pr + 1) * 128], start=True, stop=False))(pr),
                     R=['bt', 'Ub'], W=[PK(bS)], inc=False)
                S.op('pe', (lambda pr: lambda e: e.matmul(PS(bS)[:, pr * 128:(pr + 1) * 128], lhsT=kt[:, pr * 128:(pr + 1) * 128], rhs=vb[:, pr * 128:(pr + 1) * 128], start=False, stop=True))(pr),
                     R=['kt', 'vbw'], W=[PK(bS)], inc=(pr == 3))
            for hp in range(2):
                S.op('dve', (lambda hp: lambda e: e.tensor_tensor(out=rwA[hp * 64:(hp + 1) * 64, :].rearrange("p (a e) -> p a e", a=4), in0=rwA[hp * 64:(hp + 1) * 64, :].rearrange("p (a e) -> p a e", a=4),
                                                                in1=PS(bS)[hp * 64:(hp + 1) * 64, :].rearrange("p (a f) -> p a f", a=4)[:, :, hp * 64:(hp + 1) * 64], op=ALU.add))(hp),
                     R=[PK(bS), 'rwA', 'rwAb'], W=['rwA'])
            S.op('dve', lambda e: e.tensor_tensor(out=rwA.rearrange("p (a e) -> p a e", a=4), in0=rwA.rearrange("p (a e) -> p a e", a=4), in1=gC.unsqueeze(2).broadcast_to([128, 4, 64]), op=ALU.mult), R=['rwA', 'gC'], W=['rwA'])
            S.op('act', lambda e: e.activation(out=rwAb, in_=rwA, func=AF.Copy), R=['rwA'], W=['rwAb'])
            if full:
                group_norm_out(kp, 'kp', sq, 'sq', gst2, 64e-5, ptile['rw_gn_g'], ptile['rw_gn_b'], 'pt_rw_gn_g', 'pt_rw_gn_b')
                S.op('pool', lambda e: e.tensor_tensor(out=kp, in0=kp, in1=gi_t, op=ALU.add), R=['kp', 'gi_t'], W=['kp'])
                bgt = bank()
                S.op('pe', lambda e: e.matmul(PS(bgt), lhsT=sgd1, rhs=gup1, start=True, stop=False), R=['sgd1', 'gup1'], W=[PK(bgt)], inc=False)
                S.op('pe', lambda e: e.matmul(PS(bgt), lhsT=sgd2, rhs=gup2, start=False, stop=True), R=['sgd2', 'gup2'], W=[PK(bgt)])
                S.op('dve', lambda e: e.tensor_tensor(out=mixw, in0=kp, in1=PS(bgt), op=ALU.mult), R=['kp', PK(bgt)], W=['mixw'])
                rw_deferred.append((lambda ti: lambda: transpose_to(mixw, ['mixw'], 4, mixTv[:, 4:8, ti * 128:(ti + 1) * 128], ['mixT_w%d' % ti]))(ti))
        for f_ in rw_deferred:
            f_()
        A.release()

    S.op('pool', lambda e: e.memset(retS, 0.0), R=[], W=['retS'])
    S.op('pool', lambda e: e.memset(retSb, 0.0), R=[], W=['retSb'])
    S.op('pool', lambda e: e.memset(rwA, 0.0), R=[], W=['rwA'])
    S.op('pool', lambda e: e.memset(rwAb, 0.0), R=[], W=['rwAb'])
    S.op('pool', lambda e: e.memset(x1T, 0.0), R=[], W=['x1T_prev'] + ['x1T_%d' % i for i in range(NT_HALF)])

    for half in range(2):
        full = (half == 1)
        A.mark()
        xTv = mixTv
        src = dr['xs'][half * S_HALF:(half + 1) * S_HALF, :]
        prep_xT(src, S_HALF, xTv, 0, lambda t0, n: ['xT_%d' % (t0 // 512)] if n == 128 else ['xT_%d' % (t0 // 512)])
        load_ln('ln1_g', 'ln1_b')
        x1b = [A.bf(1024) for _ in range(8)]
        if half == 1:
            S.op('pool', lambda e: e.tensor_copy(out=x1Tv[:, :, XO - 1:XO], in_=x1Tv[:, :, XO + S_HALF - 1:XO + S_HALF]), R=['x1T_%d' % (NT_HALF - 1)], W=['x1T_prev'])

        def cons1(ti, y, yk, half=half, full=full, x1b=x1b):
            if full:
                S.dma('sp', lambda e: e.dma_start(out=x1s[ti * 128:(ti + 1) * 128, :], in_=y), R=[yk], W=['x1s_%d' % ti], sk='x1s')
            b = x1b[ti % 8]
            bk = 'x1b%d' % (ti % 8)
            S.op('dve', lambda e: e.tensor_scalar(out=b, in0=y, scalar1=hmask[:, half:half + 1], scalar2=None, op0=ALU.mult), R=[yk, 'hmask'], W=[bk])
            return lambda: transpose_to(b, [bk], 8, x1Tv[:, :, XO + ti * 128:XO + (ti + 1) * 128], ['x1T_%d' % ti], evac_eng='dve')

        if 'ffn1_%d' % half in plan:
            ffn(xTv, 0, lambda t0, n: ['xT_%d' % (t0 // 512)], S_HALF, dr['ffn1_w_gu'], dr['ffn1_w_down'], src, LN_EPS / (ALPHA * ALPHA), cons1, 'f1')
        A.release()
        S.barrier()
        if 'mix_%d' % half in plan:
            mixer(half, full)
        S.barrier()

    A.mark()
    NTW = NT_HALF if 'wout' in plan else 0
    wo = A.bf(8 * 1024)
    wov = wo.rearrange("p (c f) -> p c f", c=8)
    w_out_v = dr['w_out'].rearrange("(c p) f -> p c f", p=128)
    for hh in range(2):
        S.dma('pool', (lambda hh: lambda e: e.dma_start(out=wov[:, :, hh * 512:(hh + 1) * 512], in_=w_out_v[:, :, hh * 512:(hh + 1) * 512]))(hh), R=[], W=['wo%d' % hh])
    load_ln('ln2_g', 'ln2_b')
    ybuf = [A.f32(1024) for _ in range(4)]
    x2b = [A.bf(1024) for _ in range(4)]
    tmp_sq = A.f32(1024)
    st = A.f32(16)
    wo_deferred = []
    for g in range(NTW // 2):
        tiles = []
        for ti in (2 * g, 2 * g + 1):
            y, yk = ybuf[ti % 4], 'y%d' % (ti % 4)
            S.dma('sp', (lambda y, ti: lambda e: e.dma_start(out=y, in_=x1s[ti * 128:(ti + 1) * 128, :]))(y, ti), R=['x1s_%d' % ti], W=[yk])
            for dh in range(2):
                b = bank()
                for c in range(8):
                    S.op('pe', (lambda c, b, dh, ti: lambda e: e.matmul(PS(b), lhsT=mixTv[:, c, ti * 128:(ti + 1) * 128], rhs=wov[:, c, dh * 512:(dh + 1) * 512], start=(c == 0), stop=(c == 7)))(c, b, dh, ti),
                         R=['mixT_r%d' % ti, 'mixT_w%d' % ti, 'wo%d' % dh], W=[PK(b)], inc=(c == 7))
                S.op('dve', (lambda y, b, dh: lambda e: e.scalar_tensor_tensor(out=y[:, dh * 512:(dh + 1) * 512], in0=PS(b), scalar=1.0 / ALPHA, in1=y[:, dh * 512:(dh + 1) * 512], op0=ALU.mult, op1=ALU.add))(y, b, dh),
                     R=[PK(b), yk], W=[yk])
            tiles.append((ti, y, yk))
        for f_ in wo_deferred:
            f_()
        del wo_deferred[:]
        layer_norm_group([(y, yk) for (ti, y, yk) in tiles], LN_EPS / (ALPHA * ALPHA), tmp_sq, st)
        for (ti, y, yk) in tiles:
            S.dma('sp', (lambda y, ti: lambda e: e.dma_start(out=x2s[ti * 128:(ti + 1) * 128, :], in_=y))(y, ti), R=[yk], W=['x2s_%d' % ti], sk='x2s')
            b2, b2k = x2b[ti % 4], 'x2b%d' % (ti % 4)
            S.op('dve', (lambda b2, y: lambda e: e.tensor_copy(out=b2, in_=y))(b2, y), R=[yk], W=[b2k])
            wo_deferred.append((lambda b2, b2k, ti: lambda: transpose_to(b2, [b2k], 8, x2Tv[:, :, XO + ti * 128:XO + (ti + 1) * 128], ['x2T_%d' % (ti // 4)], evac_eng='dve'))(b2, b2k, ti))
    for f_ in wo_deferred:
        f_()
    A.release()
    S.barrier()

    A.mark()
    load_ln('ln3_g', 'ln3_b')
    x3b = [A.bf(1024) for _ in range(8)]

    def cons3(ti, y, yk):
        S.dma('sp', lambda e: e.dma_start(out=x3s[ti * 128:(ti + 1) * 128, :], in_=y), R=[yk], W=['x3s_%d' % ti], sk='x3s')
        b = x3b[ti % 8]
        bk = 'x3b%d' % (ti % 8)
        S.op('dve', lambda e: e.tensor_copy(out=b, in_=y), R=[yk], W=[bk])
        return lambda: transpose_to(b, [bk], 8, mixTv[:, :, ti * 128:(ti + 1) * 128], ['x3T_%d' % ti], evac_eng='dve')

    if 'ffn2' in plan:
        ffn(x2Tv, XO, lambda t0, n: ['x2T_%d' % (t0 // 512)], S_HALF, dr['ffn2_w_gu'], dr['ffn2_w_down'], x2s, LN_EPS / (ALPHA * ALPHA), cons3, 'f2')
    A.release()
    S.barrier()

    A.mark()
    wgt = A.bf(8 * 1024)
    wgv = wgt.rearrange("p (c f) -> p c f", c=8)
    wpj = A.bf(2 * 1024)
    wpv = wpj.rearrange("p (c f) -> p c f", c=2)
    bgr = A.bf(1024)
    onesb = A.bf(128)
    S.op('pool', lambda e: e.memset(bgr, 0.0), R=[], W=['bgr'])
    ple_g = dr['ple_w_gate'].rearrange("(c p) f -> p c f", p=128)
    ple_p = dr['ple_w_proj'].rearrange("(c p) f -> p c f", p=128)
    for hh in range(2):
        S.dma('pool', (lambda hh: lambda e: e.dma_start(out=wgv[:, :, hh * 512:(hh + 1) * 512], in_=ple_g[:, :, hh * 512:(hh + 1) * 512]))(hh), R=[], W=['wgt%d' % hh])
    S.dma('pool', lambda e: e.dma_start(out=wpv, in_=ple_p), R=[], W=['wpj'])
    S.dma('pool', lambda e: e.dma_start(out=bgr[0:1, :], in_=dr['ple_b_gate']), R=[], W=['bgr'])
    S.op('dve', lambda e: e.tensor_copy(out=onesb, in_=c32v('ones')), R=['c32'], W=['onesb'])
    pb_ = [A.bf(256) for _ in range(2)]
    pT = [A.bf(256) for _ in range(2)]
    x3t = [A.f32(1024) for _ in range(2)]
    gsb = [A.f32(1024) for _ in range(2)]
    for ti in range(NT_HALF if 'ple' in plan else 0):
        pbt, pbk = pb_[ti % 2], 'pb%d' % (ti % 2)
        pTt, pTk = pT[ti % 2], 'pT%d' % (ti % 2)
        x3, x3k = x3t[ti % 2], 'x3t%d' % (ti % 2)
        gs, gsk = gsb[ti % 2], 'gs%d' % (ti % 2)
        S.dma('pool', (lambda pbt, ti: lambda e: e.dma_start(out=pbt, in_=dr['p'][ti * 128:(ti + 1) * 128, :]))(pbt, ti), R=[], W=[pbk])
        S.dma('sp', (lambda x3, ti: lambda e: e.dma_start(out=x3, in_=x3s[ti * 128:(ti + 1) * 128, :]))(x3, ti), R=['x3s_%d' % ti], W=[x3k])
        transpose_to(pbt, [pbk], 2, pTt.rearrange("p (c t) -> p c t", c=2), [pTk])
        for dh in range(2):
            bg_ = bank()
            for c in range(8):
                S.op('pe', (lambda c, bg_, dh, ti: lambda e: e.matmul(PS(bg_), lhsT=mixTv[:, c, ti * 128:(ti + 1) * 128], rhs=wgv[:, c, dh * 512:(dh + 1) * 512], start=(c == 0), stop=False))(c, bg_, dh, ti),
                     R=['x3T_%d' % ti, 'wgt%d' % dh], W=[PK(bg_)], inc=False)
            S.op('pe', (lambda bg_, dh: lambda e: e.matmul(PS(bg_), lhsT=onesb, rhs=bgr[:, dh * 512:(dh + 1) * 512], start=False, stop=True))(bg_, dh), R=['onesb', 'bgr'], W=[PK(bg_)])
            bp_ = bank()
            for c in range(2):
                S.op('pe', (lambda c, bp_, dh, pTt: lambda e: e.matmul(PS(bp_), lhsT=pTt[:, c * 128:(c + 1) * 128], rhs=wpv[:, c, dh * 512:(dh + 1) * 512], start=(c == 0), stop=(c == 1)))(c, bp_, dh, pTt),
                     R=[pTk, 'wpj'], W=[PK(bp_)], inc=(c == 1))
            S.op('act', (lambda gs, bg_, dh: lambda e: e.activation(out=gs[:, dh * 512:(dh + 1) * 512], in_=PS(bg_), func=AF.Sigmoid))(gs, bg_, dh), R=[PK(bg_)], W=[gsk])
            S.op('dve', (lambda gs, bp_, dh: lambda e: e.tensor_tensor(out=gs[:, dh * 512:(dh + 1) * 512], in0=gs[:, dh * 512:(dh + 1) * 512], in1=PS(bp_), op=ALU.mult))(gs, bp_, dh), R=[PK(bp_), gsk], W=[gsk])
        S.op('dve', (lambda gs, x3: lambda e: e.tensor_tensor(out=gs, in0=gs, in1=x3, op=ALU.add))(gs, x3), R=[gsk, x3k], W=[gsk])
        S.dma('sp', (lambda gs, ti: lambda e: e.dma_start(out=out[ti * 128:(ti + 1) * 128, :], in_=gs))(gs, ti), R=[gsk], W=['out_%d' % ti], sk='out')
    A.release()
    S.barrier()

    with nc.Block() as block:
        S.emit(nc, block)
    es.close()
    print("semaphores:", len(S.semkeys), "ops:", {e: len(l) for e, l in S.lists.items()}, "arena hi:", A.hi)
    return nc


_NC_CACHE = {}


def kernel(**inputs):
    x = np.asarray(inputs['x'], np.float32)
    p = np.asarray(inputs['p'], np.float32)[0]
    if 'nc' not in _NC_CACHE:
        _NC_CACHE['nc'] = build_program()
    nc = _NC_CACHE['nc']
    c32 = np.ascontiguousarray(np.concatenate([C32[k] for k in C32], axis=1).astype(np.float32))
    cb = np.ascontiguousarray(np.concatenate([CB[k] for k in CB], axis=1).astype(np.float32))
    c32r = np.ascontiguousarray(np.concatenate([C32R[k] for k in C32R], axis=1).astype(np.float32))
    wmap = {}
    for n in WEIGHT_NAMES:
        wmap[n] = np.ascontiguousarray(np.asarray(inputs[n], np.float32)[0].reshape(WSHAPES[n]))
    in_maps = []
    for c in range(8):
        b, half = c // 2, c % 2
        m = dict(wmap)
        if half == 1:
            xs = x[b]
            pos = np.arange(4096)
            hm = np.ones((128, 2), np.float32)
        else:
            xs = np.concatenate([np.zeros((S_HALF, D), np.float32), x[b, :S_HALF]], axis=0)
            pos = np.arange(4096) - S_HALF
            hm = np.ones((128, 2), np.float32)
            hm[:, 0] = 0.0
        cos, sin = rope_tables(pos)
        m['xs'] = np.ascontiguousarray(xs)
        m['p'] = np.ascontiguousarray(p[b, half * S_HALF:(half + 1) * S_HALF])
        m['hmask'] = hm
        m['cos'] = cos
        m['sin'] = sin
        m['c32'] = c32
        m['cb'] = cb
        m['c32r'] = c32r
        in_maps.append(m)
    res = run_bass_kernel_spmd(nc, in_maps, core_ids=list(range(8)))
    outp = np.zeros((4, 4096, D), np.float32)
    for c in range(8):
        b, half = c // 2, c % 2
        outp[b, half * S_HALF:(half + 1) * S_HALF] = res.results[c]['out']
    return outp
```

```python
import os
import numpy as np
import concourse.bass as bass
import concourse.mybir as mybir
from concourse.bass_utils import run_bass_kernel_spmd

F32 = mybir.dt.float32
BF16 = mybir.dt.bfloat16
AF = mybir.ActivationFunctionType
ALU = mybir.AluOpType
AX = mybir.AxisListType

D = 1024
DFF = 2816
NJ = DFF // 128
S_HALF = 2048
NT_HALF = S_HALF // 128
RETC = 2048
RWC = 1824
INC = RETC + RWC
ALPHA = 2.0 ** 0.25
LN_EPS = 1e-5
EDEC = float(np.exp(-0.5))

STRICT = True


class _Rec:
    def __init__(self):
        self.calls = []

    def __getattr__(self, name):
        def f(*a, **k):
            self.calls.append((name, a, k))
            return self
        return f


def _capture(fn):
    r = _Rec()
    fn(r)
    assert len(r.calls) == 1, r.calls
    name, a, k = r.calls[0]
    return lambda e: getattr(e, name)(*a, **k)


class Sched:
    def __init__(self):
        self.engs = ['pe', 'act', 'dve', 'pool', 'sp']
        self.lists = {e: [] for e in self.engs}
        self.cnt = {e: 0 for e in self.engs}
        self.lastw = {}
        self.readers = {}
        self.waited = {e: {} for e in self.engs}
        self.dmacnt = {}
        self.semkeys = set(self.engs)
        self.alltok = {}

    def _deps(self, eng, R, W, is_dma):
        deps = []
        raw = set()
        for k in R:
            t = self.lastw.get(k)
            if t:
                deps.append(t)
                raw.add(t)
            if k.startswith('ps'):
                deps.extend(tk for tk in self.readers.get(k, ()) if tk[0] != eng)
        for k in W:
            t = self.lastw.get(k)
            if t:
                deps.append(t)
            deps.extend(self.readers.get(k, ()))
        waits = {}
        for (sk, v) in deps:
            if sk == eng and not is_dma and (eng == 'pe' or not STRICT or (sk, v) not in raw):
                continue
            if self.waited[eng].get(sk, 0) >= v:
                continue
            waits[sk] = max(waits.get(sk, 0), v)
        for sk, v in waits.items():
            self.waited[eng][sk] = v
        return list(waits.items())

    def _commit(self, tok, R, W):
        for k in W:
            self.lastw[k] = tok
            self.readers[k] = []
        for k in R:
            if k not in W:
                self.readers.setdefault(k, []).append(tok)
        self.alltok[tok[0]] = max(self.alltok.get(tok[0], 0), tok[1])

    def op(self, eng, fn, R=(), W=(), inc=True):
        self._clean = False
        waits = self._deps(eng, R, W, False)
        if inc:
            self.cnt[eng] += 1
            tok = (eng, self.cnt[eng])
        else:
            tok = (eng, self.cnt[eng] + 1)
        self.lists[eng].append((waits, _capture(fn), (eng, 1) if inc else None))
        self._commit(tok, R, W)

    def dma(self, eng, fn, R, W, sk=None):
        self._clean = False
        waits = self._deps(eng, R, W, True)
        sk = 'd:' + (sk or W[0])
        self.semkeys.add(sk)
        self.dmacnt[sk] = self.dmacnt.get(sk, 0) + 16
        tok = (sk, self.dmacnt[sk])
        self.lists[eng].append((waits, _capture(fn), (sk, 16)))
        self._commit(tok, R, W)

    def barrier(self):
        if getattr(self, '_clean', False):
            return
        self._clean = True
        for e in ['pe', 'act', 'dve', 'pool']:
            if self.lists[e] and self.lists[e][-1][2] is not None and self.lists[e][-1][2][0] == e:
                continue
            self.cnt[e] += 1
            self.alltok[e] = self.cnt[e]
            self.lists[e].append(([], 'nop', (e, 1)))
        for e in self.engs:
            waits = []
            for sk, v in self.alltok.items():
                if self.waited[e].get(sk, 0) >= v:
                    continue
                if sk == e:
                    continue
                waits.append((sk, v))
                self.waited[e][sk] = v
            self.lists[e].append((waits, None, None))
        self.lastw = {}
        self.readers = {}

    def emit(self, nc, block):
        sems = {sk: nc.alloc_semaphore(name=("s_" + sk.replace(':', '_').replace('.', '_'))[:40]) for sk in sorted(self.semkeys)}
        engobj = {'pe': 'tensor', 'act': 'scalar', 'dve': 'vector', 'pool': 'gpsimd', 'sp': 'sync'}

        def make(ename):
            lst = self.lists[ename]

            def body(e):
                for (waits, fn, inc) in lst:
                    for (sk, v) in waits:
                        e.wait_ge(sems[sk], v)
                    if fn is None:
                        continue
                    if fn == 'nop':
                        ins = e.nop()
                    else:
                        ins = fn(e)
                    if inc is not None:
                        ins.then_inc(sems[inc[0]], inc[1])
            return body

        for ename in self.engs:
            getattr(block, engobj[ename])(make(ename))


def gammas():
    return 1.0 - 2.0 ** (-5.0 - np.arange(8, dtype=np.float64))


def host_consts():
    g = gammas()
    i = np.arange(128)
    c = {}
    s_le_t = (i[:, None] <= i[None, :]).astype(np.float32)
    s_lt_t = (i[:, None] < i[None, :]).astype(np.float32)
    c['tri_incl'] = -EDEC * s_le_t
    c['tri_strict'] = -EDEC * s_lt_t
    c['negcol'] = np.full((128, 1), -EDEC, np.float32)
    c['ones'] = np.ones((128, 128), np.float32)
    rel = (i[None, :] - i[:, None]).astype(np.float64)
    dm = np.zeros((128, 8, 128), np.float64)
    for h in range(8):
        dm[:, h, :] = np.where(rel >= 0, 0.125 * np.exp(np.where(rel >= 0, rel, 0) * np.log(g[h])), 0.0)
    cr = {}
    cr['dmask'] = dm.reshape(128, 1024).astype(np.float32)
    kd = np.zeros((128, 8), np.float64)
    for h in range(8):
        kd[:, h] = 0.125 * g[h] ** (127.0 - i)
    c['kdec'] = kd.astype(np.float32)
    qd = np.zeros((128, 4, 128), np.float64)
    gm = np.zeros((128, 4), np.float64)
    for pr in range(4):
        for hp in range(2):
            h = 2 * pr + hp
            qd[hp * 64:(hp + 1) * 64, pr, :] = (g[h] ** (i + 1.0))[None, :]
            gm[hp * 64:(hp + 1) * 64, pr] = g[h] ** 128.0
    cr['qdec'] = qd.reshape(128, 512).astype(np.float32)
    cr['kdecf'] = np.repeat(kd, 64, axis=1).astype(np.float32)
    cr['gam128f'] = np.repeat(gm, 64, axis=1).astype(np.float32)
    c['gam128'] = gm.astype(np.float32)
    b = {}
    b['ident'] = np.eye(128, dtype=np.float32)
    m1 = np.concatenate([-s_lt_t, s_le_t], axis=1)
    m2 = np.concatenate([s_lt_t, s_le_t], axis=1)
    b['m1'] = np.concatenate([m1, m1], axis=1)
    b['m2'] = np.concatenate([m2, m2], axis=1)
    m3 = -(i[:, None] > i[None, :]).astype(np.float32)
    b['m3'] = np.concatenate([m3, m3], axis=1)
    return c, cr, b


def rope_tables(pos):
    inv = 10000.0 ** (-np.arange(0, 64, 2, dtype=np.float32) / 64.0)
    ang = pos.astype(np.float32)[:, None] * inv[None, :]
    cos = np.cos(ang).astype(np.float32)
    sin = np.sin(ang).astype(np.float32)
    n = pos.shape[0] // 128
    cos = cos.reshape(n, 128, 32).transpose(1, 0, 2).reshape(128, n * 32)
    sin = sin.reshape(n, 128, 32).transpose(1, 0, 2).reshape(128, n * 32)
    return np.ascontiguousarray(cos), np.ascontiguousarray(sin)


C32, C32R, CB = host_consts()
C32_OFF = {}
_o = 0
for _k, _v in C32.items():
    C32_OFF[_k] = (_o, _v.shape[1])
    _o += _v.shape[1]
C32_N = _o
C32R_OFF = {}
_o = 0
for _k, _v in C32R.items():
    C32R_OFF[_k] = (_o, _v.shape[1])
    _o += _v.shape[1]
C32R_N = _o
CB_OFF = {}
_o = 0
for _k, _v in CB.items():
    CB_OFF[_k] = (_o, _v.shape[1])
    _o += _v.shape[1]
CB_N = _o

WEIGHT_NAMES = ['ffn1_w_gu', 'ffn1_w_down', 'ln1_g', 'ln1_b', 'w_in', 'ret_gn_g', 'ret_gn_b', 'rw_mu',
                'rw_w0', 'rw_w_up', 'rw_a0', 'rw_a_up', 'rw_g_up', 'rw_k_k', 'rw_k_a', 'rw_r_k', 'rw_gn_g',
                'rw_gn_b', 'w_out', 'ln2_g', 'ln2_b', 'ffn2_w_gu', 'ffn2_w_down', 'ln3_g', 'ln3_b',
                'ple_w_proj', 'ple_w_gate', 'ple_b_gate']
WSHAPES = {'ffn1_w_gu': [D, 2 * DFF], 'ffn1_w_down': [DFF, D], 'ln1_g': [1, D], 'ln1_b': [1, D], 'w_in': [D, INC],
           'ret_gn_g': [1, 512], 'ret_gn_b': [1, 512], 'rw_mu': [1, RWC], 'rw_w0': [1, 512], 'rw_w_up': [64, 512],
           'rw_a0': [1, 512], 'rw_a_up': [64, 512], 'rw_g_up': [160, 512], 'rw_k_k': [1, 512], 'rw_k_a': [1, 512],
           'rw_r_k': [1, 512], 'rw_gn_g': [1, 512], 'rw_gn_b': [1, 512], 'w_out': [D, D], 'ln2_g': [1, D],
           'ln2_b': [1, D], 'ffn2_w_gu': [D, 2 * DFF], 'ffn2_w_down': [DFF, D], 'ln3_g': [1, D], 'ln3_b': [1, D],
           'ple_w_proj': [256, D], 'ple_w_gate': [D, D], 'ple_b_gate': [1, D]}


def build_program(plan=None, dbg=False):
    nc = bass.Bass("TRN2", target_bir_lowering=False)
    dr = {}
    dr['xs'] = nc.dram_tensor("xs", [2 * S_HALF, D], F32, kind="ExternalInput").ap()
    dr['p'] = nc.dram_tensor("p", [S_HALF, 256], F32, kind="ExternalInput").ap()
    dr['hmask'] = nc.dram_tensor("hmask", [128, 2], F32, kind="ExternalInput").ap()
    dr['cos'] = nc.dram_tensor("cos", [128, 1024], F32, kind="ExternalInput").ap()
    dr['sin'] = nc.dram_tensor("sin", [128, 1024], F32, kind="ExternalInput").ap()
    dr['c32'] = nc.dram_tensor("c32", [128, C32_N], F32, kind="ExternalInput").ap()
    dr['cb'] = nc.dram_tensor("cb", [128, CB_N], F32, kind="ExternalInput").ap()
    dr['c32r'] = nc.dram_tensor("c32r", [128, C32R_N], F32, kind="ExternalInput").ap()
    for n in WEIGHT_NAMES:
        dr[n] = nc.dram_tensor(n, WSHAPES[n], F32, kind="ExternalInput").ap()
    out = nc.dram_tensor("out", [S_HALF, D], F32, kind="ExternalOutput").ap()
    skind = "ExternalOutput" if dbg else "Internal"
    x1s = nc.dram_tensor("x1s", [S_HALF, D], F32, kind=skind).ap()
    x2s = nc.dram_tensor("x2s", [S_HALF, D], F32, kind=skind).ap()
    x3s = nc.dram_tensor("x3s", [S_HALF, D], F32, kind=skind).ap()
    wab_s = nc.dram_tensor("wab_s", [128, 2 * 8 * RWC], BF16, kind="Internal").ap()
    if plan is None:
        plan = ['ffn1_0', 'mix_0', 'ffn1_1', 'mix_1', 'wout', 'ffn2', 'ple']

    S = Sched()
    dumped = {}

    def dump(name, ap, keys):
        if not dbg or name in dumped:
            return
        shp = list(ap.shape)
        d_ = nc.dram_tensor("dbg_" + name, shp, F32, kind="ExternalOutput").ap()
        dumped[name] = d_
        S.dma('pool', lambda e: e.dma_start(out=d_, in_=ap), R=list(keys), W=['dbg_' + name])
    ARENA_W = int(os.environ.get('ARENA_W', '53200'))
    from contextlib import ExitStack
    es = ExitStack()
    arena = es.enter_context(nc.sbuf_tensor("arena", [128, ARENA_W], F32))
    psf = [es.enter_context(nc.psum_tensor("ps%d" % i, [128, 512], F32)) for i in range(8)]

    class Alloc:
        def __init__(self):
            self.p = 0
            self.marks = []

        def f32(self, n, parts=(0, 128)):
            if os.environ.get('DRY'):
                self.p += n
                self.hi = max(getattr(self, 'hi', 0), self.p)
                return arena[parts[0]:parts[1], 0:n]
            a = arena[parts[0]:parts[1], self.p:self.p + n]
            self.p += n
            self.hi = max(getattr(self, 'hi', 0), self.p)
            assert self.p <= ARENA_W, ("arena overflow", self.p)
            return a

        def bf(self, n, parts=(0, 128)):
            w = (n + 1) // 2
            if os.environ.get('DRY'):
                self.p += w
                self.hi = max(getattr(self, 'hi', 0), self.p)
                return arena[parts[0]:parts[1], 0:w].bitcast(BF16)
            a = arena[parts[0]:parts[1], self.p:self.p + w].bitcast(BF16)
            self.p += w
            self.hi = max(getattr(self, 'hi', 0), self.p)
            assert self.p <= ARENA_W, ("arena overflow", self.p)
            return a

        def mark(self):
            self.marks.append(self.p)

        def release(self):
            self.p = self.marks.pop()
            S.barrier()

    A = Alloc()
    bankctr = [0]
    bankgen = [0] * 8

    class Bk(int):
        pass

    def bank():
        b = Bk(bankctr[0] % 8)
        bankctr[0] += 1
        bankgen[int(b)] = bankctr[0]
        b.gen = bankctr[0]
        return b

    def PS(b):
        return psf[int(b)][:, :]

    def PSB(b):
        return psf[int(b)][:, :].bitcast(BF16)

    def PK(b):
        assert bankgen[int(b)] == b.gen, "stale PSUM bank use"
        return 'ps%d' % int(b)

    c32 = A.f32(C32_N)
    cbt = A.bf(CB_N)
    S.dma('sp', lambda e: e.dma_start(out=c32, in_=dr['c32']), R=[], W=['c32'])
    S.dma('pool', lambda e: e.dma_start(out=cbt, in_=dr['cb']), R=[], W=['cb'])

    def c32v(name, parts=(0, 128)):
        o, n = C32_OFF[name]
        return c32[parts[0]:parts[1], o:o + n]

    def cbv(name):
        o, n = CB_OFF[name]
        return cbt[:, o:o + n]

    ident = cbv('ident')
    hmask = A.f32(2)
    S.dma('sp', lambda e: e.dma_start(out=hmask, in_=dr['hmask']), R=[], W=['hmask'])
    ptile = {}
    LN = {}
    retS = A.f32(256)
    retSb = A.bf(256)
    rwA = A.f32(256)
    rwAb = A.bf(256)
    x1T = A.bf(8 * (S_HALF + 8))
    x1Tv = x1T.rearrange("p (c t) -> p c t", c=8)
    XO = 8
    mixT = A.bf(8 * S_HALF)
    mixTv = mixT.rearrange("p (c t) -> p c t", c=8)
    x2Tv = x1Tv

    def load_ln(gn, bn):
        lng = A.f32(1024)
        lnb = A.f32(1024)
        LN['g'] = lng
        LN['b'] = lnb
        S.dma('sp', lambda e: e.dma_start(out=lng, in_=dr[gn].partition_broadcast(128)), R=[], W=['lng'])
        S.dma('sp', lambda e: e.dma_start(out=lnb, in_=dr[bn].partition_broadcast(128)), R=[], W=['lnb'])

    def load_ptiles(names):
        for n in names:
            t = A.f32(512)
            ptile[n] = t
            S.dma('sp', (lambda t, n: lambda e: e.dma_start(out=t, in_=dr[n].partition_broadcast(128)))(t, n), R=[], W=['pt_' + n])

    def transpose_to(src_bf, src_keys, n_blocks, dst_view, dst_keys, evac_eng='act'):
        b = bank()
        pk = PK(b)
        for i in range(n_blocks):
            S.op('pe', (lambda i, b: lambda e: e.transpose(out=PSB(b)[:, i * 128:(i + 1) * 128],
                                                         in_=src_bf[:, i * 128:(i + 1) * 128], identity=ident))(i, b),
                 R=list(src_keys) + ['cb'], W=[pk], inc=(i == n_blocks - 1))
        src = PSB(b)[:, 0:n_blocks * 128].rearrange("p (c t) -> p c t", c=n_blocks)
        if evac_eng == 'act':
            S.op('act', lambda e: e.activation(out=dst_view, in_=src, func=AF.Copy), R=[pk], W=list(dst_keys))
        else:
            S.op(evac_eng, lambda e: e.tensor_copy(out=dst_view, in_=src), R=[pk], W=list(dst_keys))

    def layer_norm_group(tiles, eps, junk, st):
        n_ = len(tiles)
        sl = lambda i, c: st[:, i * 8 + c:i * 8 + c + 1]
        k = lambda i, c: 'lnst%d_%d' % (i, c)
        for i, (y, yk) in enumerate(tiles):
            S.op('act', (lambda i, y: lambda e: e.activation(out=junk, in_=y, func=AF.Square, accum_out=sl(i, 0)))(i, y), R=[yk], W=['lnjunk', k(i, 0)])
            S.op('act', (lambda i, y: lambda e: e.activation(out=junk, in_=y, func=AF.Identity, accum_out=sl(i, 1)))(i, y), R=[yk], W=['lnjunk', k(i, 1)])
        for i in range(n_):
            S.op('dve', (lambda i: lambda e: e.tensor_scalar(out=sl(i, 2), in0=sl(i, 1), scalar1=1.0 / 1024, scalar2=None, op0=ALU.mult))(i), R=[k(i, 1)], W=[k(i, 2)])
        for i in range(n_):
            S.op('dve', (lambda i: lambda e: e.tensor_tensor(out=sl(i, 3), in0=sl(i, 2), in1=sl(i, 2), op=ALU.mult))(i), R=[k(i, 2)], W=[k(i, 3)])
        for i in range(n_):
            S.op('dve', (lambda i: lambda e: e.scalar_tensor_tensor(out=sl(i, 4), in0=sl(i, 0), scalar=1.0 / 1024, in1=sl(i, 3), op0=ALU.mult, op1=ALU.subtract))(i), R=[k(i, 0), k(i, 3)], W=[k(i, 4)])
        for i in range(n_):
            S.op('dve', (lambda i: lambda e: e.tensor_scalar(out=sl(i, 4), in0=sl(i, 4), scalar1=float(eps), scalar2=None, op0=ALU.add))(i), R=[k(i, 4)], W=[k(i, 4)])
        for i in range(n_):
            S.op('act', (lambda i: lambda e: e.activation(out=sl(i, 5), in_=sl(i, 4), func=AF.Sqrt))(i), R=[k(i, 4)], W=[k(i, 5)])
        for i in range(n_):
            S.op('dve', (lambda i: lambda e: e.reciprocal(out=sl(i, 6), in_=sl(i, 5)))(i), R=[k(i, 5)], W=[k(i, 6)])
        for i in range(n_):
            S.op('dve', (lambda i: lambda e: e.scalar_tensor_tensor(out=sl(i, 7), in0=sl(i, 2), scalar=-1.0, in1=sl(i, 6), op0=ALU.mult, op1=ALU.mult))(i), R=[k(i, 2), k(i, 6)], W=[k(i, 7)])
        for i, (y, yk) in enumerate(tiles):
            S.op('act', (lambda i, y: lambda e: e.activation(out=y, in_=y, func=AF.Identity, scale=sl(i, 6), bias=sl(i, 7)))(i, y), R=[yk, k(i, 6), k(i, 7)], W=[yk])
        for i, (y, yk) in enumerate(tiles):
            S.op('dve', (lambda y: lambda e: e.tensor_tensor(out=y, in0=y, in1=LN['g'], op=ALU.mult))(y), R=[yk, 'lng'], W=[yk])
            S.op('dve', (lambda y: lambda e: e.tensor_tensor(out=y, in0=y, in1=LN['b'], op=ALU.add))(y), R=[yk, 'lnb'], W=[yk])

    def layer_norm_tile(y, ykey, eps, outs, tmp_sq, st):
        S.op('act', lambda e: e.activation(out=tmp_sq, in_=y, func=AF.Square), R=[ykey], W=['lnsq'])
        S.op('dve', lambda e: e.reduce_sum(out=st[:, 0:1], in_=tmp_sq, axis=AX.X), R=['lnsq'], W=['lnst0'])
        S.op('dve', lambda e: e.reduce_sum(out=st[:, 1:2], in_=y, axis=AX.X), R=[ykey], W=['lnst1'])
        S.op('dve', lambda e: e.tensor_scalar(out=st[:, 2:3], in0=st[:, 1:2], scalar1=1.0 / 1024, scalar2=None, op0=ALU.mult), R=['lnst1'], W=['lnst2'])
        S.op('dve', lambda e: e.tensor_tensor(out=st[:, 3:4], in0=st[:, 2:3], in1=st[:, 2:3], op=ALU.mult), R=['lnst2'], W=['lnst3'])
        S.op('dve', lambda e: e.scalar_tensor_tensor(out=st[:, 4:5], in0=st[:, 0:1], scalar=1.0 / 1024, in1=st[:, 3:4], op0=ALU.mult, op1=ALU.subtract), R=['lnst0', 'lnst3'], W=['lnst4'])
        S.op('dve', lambda e: e.tensor_scalar(out=st[:, 4:5], in0=st[:, 4:5], scalar1=float(eps), scalar2=None, op0=ALU.add), R=['lnst4'], W=['lnst4'])
        S.op('act', lambda e: e.activation(out=st[:, 5:6], in_=st[:, 4:5], func=AF.Sqrt), R=['lnst4'], W=['lnst5'])
        S.op('dve', lambda e: e.reciprocal(out=st[:, 6:7], in_=st[:, 5:6]), R=['lnst5'], W=['lnst6'])
        S.op('dve', lambda e: e.scalar_tensor_tensor(out=st[:, 7:8], in0=st[:, 2:3], scalar=-1.0, in1=st[:, 6:7], op0=ALU.mult, op1=ALU.mult), R=['lnst2', 'lnst6'], W=['lnst7'])
        S.op('act', lambda e: e.activation(out=y, in_=y, func=AF.Identity, scale=st[:, 6:7], bias=st[:, 7:8]), R=[ykey, 'lnst6', 'lnst7'], W=[ykey])
        S.op('dve', lambda e: e.tensor_tensor(out=y, in0=y, in1=LN['g'], op=ALU.mult), R=[ykey, 'lng'], W=[ykey])
        S.op('dve', lambda e: e.tensor_tensor(out=y, in0=y, in1=LN['b'], op=ALU.add), R=[ykey, 'lnb'], W=[ykey])

    def ffn(xTv_src, xoff, xkey_fn, ntok, wgu, wdown, res_dram, eps, consumer, tagp):
        A.mark()
        NB = 1024
        hhT = A.bf(NJ * NB)
        hhv = hhT.rearrange("p (j t) -> p j t", j=NJ)
        wg = [A.bf(8 * 512) for _ in range(2)]
        wd = [A.bf(1024) for _ in range(6)]
        sg = [A.f32(512) for _ in range(2)]
        ybuf = [A.f32(1024) for _ in range(8)]
        tmp_sq = A.f32(1024)
        st = A.f32(32)
        wgu_v = wgu.rearrange("(c p) f -> p c f", p=128)
        deferred = []

        def run_deferred():
            for f_ in deferred:
                f_()
            del deferred[:]
        wi = 0
        di = 0
        for blk in range(ntok // NB):
            t0 = blk * NB
            for jg in range(NJ // 2):
                w = wg[wi % 2]
                wk = 'wg%d' % (wi % 2)
                wv = w.rearrange("p (c f) -> p c f", c=8)
                wi += 1
                S.dma('pool', (lambda wv, jg: lambda e: e.dma_start(out=wv[:, :, 0:256], in_=wgu_v[:, :, jg * 256:(jg + 1) * 256]))(wv, jg), R=[], W=[wk + 'g'])
                S.dma('pool', (lambda wv, jg: lambda e: e.dma_start(out=wv[:, :, 256:512], in_=wgu_v[:, :, DFF + jg * 256:DFF + (jg + 1) * 256]))(wv, jg), R=[], W=[wk + 'u'])
                for jj in range(2):
                    j = jg * 2 + jj
                    for sb in range(NB // 512):
                        bg = bank()
                        bu = bank()
                        c0 = xoff + t0 + sb * 512
                        xk = xkey_fn(t0 + sb * 512, 512)
                        for dc in range(8):
                            S.op('pe', (lambda wv, dc, jj, bg, c0: lambda e: e.matmul(PS(bg), lhsT=wv[:, dc, jj * 128:(jj + 1) * 128], rhs=xTv_src[:, dc, c0:c0 + 512], start=(dc == 0), stop=(dc == 7)))(wv, dc, jj, bg, c0),
                                 R=[wk + 'g'] + xk, W=[PK(bg)], inc=(dc == 7))
                        for dc in range(8):
                            S.op('pe', (lambda wv, dc, jj, bu, c0: lambda e: e.matmul(PS(bu), lhsT=wv[:, dc, 256 + jj * 128:256 + (jj + 1) * 128], rhs=xTv_src[:, dc, c0:c0 + 512], start=(dc == 0), stop=(dc == 7)))(wv, dc, jj, bu, c0),
                                 R=[wk + 'u'] + xk, W=[PK(bu)], inc=(dc == 7))
                        sgt = sg[(j * 2 + sb) % 2]
                        sgk = 'sg%d' % ((j * 2 + sb) % 2)
                        S.op('act', (lambda sgt, bg: lambda e: e.activation(out=sgt, in_=PS(bg), func=AF.Silu))(sgt, bg), R=[PK(bg)], W=[sgk])
                        S.op('dve', (lambda sgt, bu, j, sb: lambda e: e.tensor_tensor(out=hhv[:, j, sb * 512:(sb + 1) * 512], in0=sgt, in1=PS(bu), op=ALU.mult))(sgt, bu, j, sb),
                             R=[sgk, PK(bu)], W=['hh%d_%d' % (j, sb)])
            if blk == 0:
                dump(tagp + '_hh0', hhv[:, 0, :], ['hh0_0', 'hh0_1'])
                dump(tagp + '_hh21', hhv[:, 21, :], ['hh21_0', 'hh21_1'])
                dump(tagp + '_xT0', xTv_src[:, 0, xoff:xoff + 512], xkey_fn(0, 512))
            if os.environ.get('FFN_STOP') == 'up':
                continue
            for rnd in range(NB // 512):
                banks = [[bank(), bank()] for _ in range(4)]
                for j in range(NJ):
                    w = wd[di % 6]
                    wk = 'wd%d' % (di % 6)
                    di += 1
                    S.dma('pool', (lambda w, j: lambda e: e.dma_start(out=w, in_=wdown[j * 128:(j + 1) * 128, :]))(w, j), R=[], W=[wk])
                    for tt in range(4):
                        for dh in range(2):
                            b = banks[tt][dh]
                            S.op('pe', (lambda w, j, tt, dh, b, rnd: lambda e: e.matmul(PS(b), lhsT=hhv[:, j, rnd * 512 + tt * 128:rnd * 512 + (tt + 1) * 128], rhs=w[:, dh * 512:(dh + 1) * 512], start=(j == 0), stop=(j == NJ - 1)))(w, j, tt, dh, b, rnd),
                                 R=[wk, 'hh%d_%d' % (j, rnd)], W=[PK(b)], inc=(j == NJ - 1 or (tt == 3 and dh == 1)))
                cur = []
                for tt in range(4):
                    ti = (t0 + rnd * 512) // 128 + tt
                    y = ybuf[ti % 8]
                    yk = 'y%d' % (ti % 8)
                    S.dma('sp', (lambda y, ti: lambda e: e.dma_start(out=y, in_=res_dram[ti * 128:(ti + 1) * 128, :]))(y, ti), R=[], W=[yk])
                    for dh in range(2):
                        b = banks[tt][dh]
                        S.op('dve', (lambda y, b, dh: lambda e: e.scalar_tensor_tensor(out=y[:, dh * 512:(dh + 1) * 512], in0=PS(b), scalar=0.5 / ALPHA, in1=y[:, dh * 512:(dh + 1) * 512], op0=ALU.mult, op1=ALU.add))(y, b, dh),
                             R=[PK(b), yk], W=[yk])
                    cur.append((ti, y, yk))
                run_deferred()
                layer_norm_group([(y, yk) for (ti, y, yk) in cur], eps, tmp_sq, st)
                for (ti, y, yk) in cur:
                    later = consumer(ti, y, yk)
                    if later is not None:
                        deferred.append(later)
        run_deferred()
        A.release()

    def prep_xT(src_dram, ntok, dstv, doff, keyfn):
        A.mark()
        xb = [A.bf(1024) for _ in range(2)]
        for ti in range(ntok // 128):
            b = xb[ti % 2]
            bk = 'xb%d' % (ti % 2)
            S.dma('pool', (lambda b, ti: lambda e: e.dma_start(out=b, in_=src_dram[ti * 128:(ti + 1) * 128, :]))(b, ti), R=[], W=[bk])
            transpose_to(b, [bk], 8, dstv[:, :, doff + ti * 128:doff + (ti + 1) * 128], keyfn(ti * 128, 128))
        A.release()

    def mixer(half, full):
        A.mark()
        w_in = dr['w_in'].rearrange("(c p) f -> p c f", p=128)
        A.mark()
        wret = A.bf(8 * RETC)
        wretv = wret.rearrange("p (c f) -> p c f", c=8)
        c32r = A.f32(C32R_N)
        S.dma('sp', lambda e: e.dma_start(out=c32r, in_=dr['c32r']), R=[], W=['c32r'])

        def c32rv(name):
            o, n = C32R_OFF[name]
            return c32r[:, o:o + n]
        cos_t = A.f32(512)
        sin_t = A.f32(512)
        S.dma('sp', lambda e: e.dma_start(out=cos_t, in_=dr['cos'][:, half * 512:(half + 1) * 512]), R=[], W=['cos'])
        S.dma('sp', lambda e: e.dma_start(out=sin_t, in_=dr['sin'][:, half * 512:(half + 1) * 512]), R=[], W=['sin'])
        load_ptiles(['ret_gn_g', 'ret_gn_b'])
        for q4 in range(4):
            S.dma('pool', (lambda q4: lambda e: e.dma_start(out=wretv[:, :, q4 * 512:(q4 + 1) * 512], in_=w_in[:, :, q4 * 512:(q4 + 1) * 512]))(q4), R=[], W=['wret%d' % q4])
        qr = A.bf(512)
        kr = A.bf(512)
        vb = A.bf(512)
        vdb = A.bf(512)
        sgg = A.f32(512)
        t1 = A.f32(256)
        t2 = A.f32(256)
        t3 = A.f32(256)
        t4 = A.f32(256)
        qT = A.bf(512)
        qsTm = A.bf(1024)
        kTm = A.bf(1024)
        S.op('pool', lambda e: e.memset(qsTm, 0.0), R=[], W=['qsTm'])
        S.op('pool', lambda e: e.memset(kTm, 0.0), R=[], W=['kTm'])
        scm = A.bf(1024)
        o_sb = A.f32(512)
        o_sq = A.f32(512)
        gst = A.f32(64)
        mixr = A.bf(512)
        ret_deferred = []
        kdec = c32v('kdec')
        for ti in range(NT_HALF):
            c0 = XO + ti * 128
            gt = ti
            xk = ['x1T_%d' % ti]
            cosv = cos_t[:, gt * 32:(gt + 1) * 32].unsqueeze(1).broadcast_to([128, 8, 32])
            sinv = sin_t[:, gt * 32:(gt + 1) * 32].unsqueeze(1).broadcast_to([128, 8, 32])
            pb = {}
            which = ['q', 'k', 'v', 'g'] if full else ['k', 'v']
            RSUB = os.environ.get('RET_SUB', '')
            if RSUB == 'none':
                continue
            for nm in which:
                q4 = ['q', 'k', 'v', 'g'].index(nm)
                b = bank()
                pb[nm] = b
                for dc in range(8):
                    S.op('pe', (lambda dc, b, q4, c0: lambda e: e.matmul(PS(b), lhsT=x1Tv[:, dc, c0:c0 + 128], rhs=wretv[:, dc, q4 * 512:(q4 + 1) * 512], start=(dc == 0), stop=(dc == 7)))(dc, b, q4, c0),
                         R=xk + ['wret%d' % q4], W=[PK(b)], inc=(dc == 7))

            def rotary(b, dst, dkey):
                src = PS(b).rearrange("p (h d) -> p h d", h=8)
                dv = dst.rearrange("p (h d) -> p h d", h=8)
                a1 = t1.rearrange("p (h d) -> p h d", h=8)
                a2 = t2.rearrange("p (h d) -> p h d", h=8)
                a3 = t3.rearrange("p (h d) -> p h d", h=8)
                a4 = t4.rearrange("p (h d) -> p h d", h=8)
                pk = PK(b)
                S.op('dve', lambda e: e.tensor_tensor(out=a1, in0=src[:, :, 0:32], in1=cosv, op=ALU.mult), R=[pk, 'cos'], W=['rt1'])
                S.op('dve', lambda e: e.tensor_tensor(out=a2, in0=src[:, :, 32:64], in1=sinv, op=ALU.mult), R=[pk, 'sin'], W=['rt2'])
                S.op('dve', lambda e: e.tensor_tensor(out=a3, in0=src[:, :, 0:32], in1=sinv, op=ALU.mult), R=[pk, 'sin'], W=['rt3'])
                S.op('dve', lambda e: e.tensor_tensor(out=a4, in0=src[:, :, 32:64], in1=cosv, op=ALU.mult), R=[pk, 'cos'], W=['rt4'])
                S.op('pool', lambda e: e.tensor_tensor(out=dv[:, :, 0:32], in0=a1, in1=a2, op=ALU.subtract), R=['rt1', 'rt2'], W=[dkey])
                S.op('pool', lambda e: e.tensor_tensor(out=dv[:, :, 32:64], in0=a3, in1=a4, op=ALU.add), R=['rt3', 'rt4'], W=[dkey])

            for f_ in ret_deferred:
                f_()
            del ret_deferred[:]
            if RSUB == 'proj':
                continue
            rotary(pb['k'], kr, 'kr')
            if full:
                rotary(pb['q'], qr, 'qr')
            if RSUB == 'rot':
                continue
            bv = pb['v']
            S.op('act', (lambda bv: lambda e: e.activation(out=vb, in_=PS(bv), func=AF.Copy))(bv), R=[PK(bv)], W=['vb'])
            if RSUB == 'vb':
                continue
            S.op('dve', (lambda bv: lambda e: e.tensor_tensor(out=vdb, in0=PS(bv), in1=c32rv('kdecf'), op=ALU.mult))(bv), R=[PK(bv), 'c32r', 'vb'], W=['vdb'])
            RS = int(os.environ.get('RET_STOP', '99'))
            if RS <= 1:
                continue
            if full:
                bgg = pb['g']
                S.op('act', (lambda bgg: lambda e: e.activation(out=sgg, in_=PS(bgg), func=AF.Silu))(bgg), R=[PK(bgg)], W=['sgg'])
                b = bank()
                pk = PK(b)
                for i in range(4):
                    S.op('pe', (lambda i, b: lambda e: e.transpose(out=PSB(b)[:, i * 128:(i + 1) * 128], in_=qr[:, i * 128:(i + 1) * 128], identity=ident))(i, b), R=['qr', 'cb'], W=[pk], inc=False)
                for i in range(4):
                    S.op('pe', (lambda i, b: lambda e: e.transpose(out=PSB(b)[:, 512 + i * 128:512 + (i + 1) * 128], in_=kr[:, i * 128:(i + 1) * 128], identity=ident))(i, b), R=['kr', 'cb'], W=[pk], inc=(i == 3))
                S.op('act', (lambda b: lambda e: e.activation(out=qT, in_=PSB(b)[:, 0:512], func=AF.Copy))(b), R=[pk], W=['qT'])
                for hp in range(2):
                    rs_ = slice(hp * 64, (hp + 1) * 64)
                    S.op('dve', (lambda b, hp, rs_: lambda e: e.tensor_tensor(out=qsTm[rs_, :].rearrange("p (a c t) -> p a c t", a=4, c=2)[:, :, hp, :],
                                                                             in0=PSB(b)[rs_, 0:512].rearrange("p (a t) -> p a t", a=4),
                                                                             in1=c32rv('qdec')[rs_, :].rearrange("p (a t) -> p a t", a=4), op=ALU.mult))(b, hp, rs_), R=[pk, 'c32r'], W=['qsTm'])
                    S.op('act', (lambda b, hp, rs_: lambda e: e.activation(out=kTm[rs_, :].rearrange("p (a c t) -> p a c t", a=4, c=2)[:, :, hp, :],
                                                                          in_=PSB(b)[rs_, 512:1024].rearrange("p (a t) -> p a t", a=4), func=AF.Copy))(b, hp, rs_), R=[pk], W=['kTm'])
                if RS <= 2:
                    continue
                sb_ = [bank(), bank()]
                for h in range(8):
                    pr, hp = h // 2, h % 2
                    b = sb_[h // 4]
                    S.op('pe', (lambda h, pr, hp, b: lambda e: e.matmul(PS(b)[:, (h % 4) * 128:(h % 4 + 1) * 128], lhsT=kTm[:, h * 128:(h + 1) * 128],
                                                                       rhs=qT[:, pr * 128:(pr + 1) * 128], start=True, stop=True))(h, pr, hp, b),
                         R=['kTm', 'qT'], W=[PK(b)], inc=(h % 4 == 3))
                if RS == 24:
                    continue
                for hb in range(2):
                    b = sb_[hb]
                    S.op('dve', (lambda hb, b: lambda e: e.tensor_tensor(out=scm[:, hb * 512:(hb + 1) * 512], in0=PS(b), in1=c32rv('dmask')[:, hb * 512:(hb + 1) * 512], op=ALU.mult))(hb, b),
                         R=[PK(b), 'c32r'], W=['scm%d' % hb])
                if RS == 25:
                    continue
                bo = bank()
                for h in range(8):
                    pr, hp = h // 2, h % 2
                    S.op('pe', (lambda h, bo: lambda e: e.matmul(PS(bo)[:, h * 64:(h + 1) * 64], lhsT=scm[:, h * 128:(h + 1) * 128], rhs=vb[:, h * 64:(h + 1) * 64], start=True, stop=(RS == 26)))(h, bo),
                         R=['scm%d' % (h // 4), 'vb'], W=[PK(bo)], inc=(RS == 26 and h == 7))
                    if RS == 26:
                        continue
                    S.op('pe', (lambda h, pr, hp, bo: lambda e: e.matmul(PS(bo)[:, h * 64:(h + 1) * 64], lhsT=qsTm[:, h * 128:(h + 1) * 128],
                                                                        rhs=retSb[:, pr * 64:(pr + 1) * 64], start=False, stop=True))(h, pr, hp, bo),
                         R=['qsTm', 'retSb'], W=[PK(bo)], inc=(h == 7))
                S.op('act', (lambda bo: lambda e: e.activation(out=o_sb, in_=PS(bo), func=AF.Copy))(bo), R=[PK(bo)], W=['o_sb'])
            if RS <= 3:
                continue
            bk_ = bank()
            for pr in range(4):
                S.op('pe', (lambda pr, bk_: lambda e: e.matmul(PS(bk_)[:, pr * 128:(pr + 1) * 128], lhsT=kr[:, pr * 128:(pr + 1) * 128], rhs=vdb[:, pr * 128:(pr + 1) * 128], start=True, stop=True))(pr, bk_),
                     R=['kr', 'vdb'], W=[PK(bk_)], inc=(pr == 3))
            rs3 = retS.rearrange("p (a e) -> p a e", a=4)
            S.op('pool', lambda e: e.tensor_tensor(out=retS, in0=retS, in1=c32rv('gam128f'), op=ALU.mult), R=['retS', 'c32r', 'retSb'], W=['retS'])
            for hp in range(2):
                S.op('dve', (lambda hp, bk_: lambda e: e.tensor_tensor(out=retS[hp * 64:(hp + 1) * 64, :].rearrange("p (a e) -> p a e", a=4), in0=retS[hp * 64:(hp + 1) * 64, :].rearrange("p (a e) -> p a e", a=4),
                                                                    in1=PS(bk_)[hp * 64:(hp + 1) * 64, :].rearrange("p (a f) -> p a f", a=4)[:, :, hp * 64:(hp + 1) * 64], op=ALU.add))(hp, bk_),
                     R=[PK(bk_), 'retS'], W=['retS'])
            S.op('act', lambda e: e.activation(out=retSb, in_=retS, func=AF.Copy), R=['retS'], W=['retSb'])
            if full and ti == 0:
                dump('retS1', retS, ['retS'])
            if full and ti == 1:
                dump('retS2', retS, ['retS'])
                dump('kr1', kr, ['kr'])
            if RS <= 4:
                continue
            if full:
                group_norm_out(o_sb, 'o_sb', o_sq, 'o_sq', gst, 1e-5, ptile['ret_gn_g'], ptile['ret_gn_b'], 'pt_ret_gn_g', 'pt_ret_gn_b')
                S.op('dve', lambda e: e.tensor_tensor(out=mixr, in0=o_sb, in1=sgg, op=ALU.mult), R=['o_sb', 'sgg'], W=['mixr'])
                ret_deferred.append((lambda ti: lambda: transpose_to(mixr, ['mixr'], 4, mixTv[:, 0:4, ti * 128:(ti + 1) * 128], ['mixT_r%d' % ti]))(ti))
        for f_ in ret_deferred:
            f_()
        A.release()
        S.barrier()
        if os.environ.get('MIX_STOP') != 'ret':
            rwkv(half, full)
        if full:
            for c_ in range(8):
                dump('mixT%d' % c_, mixTv[:, c_, :], [])
            dump('retS', retS, [])
            dump('rwA', rwA, [])
        A.release()

    def group_norm_out(o, okey, sq, sqk, gst, eps, gt, bt, gk, bk):
        o3 = o.rearrange("p (h d) -> p h d", h=8)
        S.op('act', lambda e: e.activation(out=sq, in_=o, func=AF.Square), R=[okey], W=[sqk])
        S.op('dve', lambda e: e.tensor_reduce(out=gst[:, 0:8], in_=o3, axis=AX.X, op=ALU.add), R=[okey], W=['gn0'])
        S.op('dve', lambda e: e.tensor_reduce(out=gst[:, 8:16], in_=sq.rearrange("p (h d) -> p h d", h=8), axis=AX.X, op=ALU.add), R=[sqk], W=['gn1'])
        S.op('dve', lambda e: e.tensor_scalar(out=gst[:, 16:24], in0=gst[:, 0:8], scalar1=1.0 / 64, scalar2=None, op0=ALU.mult), R=['gn0'], W=['gn2'])
        S.op('dve', lambda e: e.tensor_tensor(out=gst[:, 24:32], in0=gst[:, 16:24], in1=gst[:, 16:24], op=ALU.mult), R=['gn2'], W=['gn3'])
        S.op('dve', lambda e: e.scalar_tensor_tensor(out=gst[:, 32:40], in0=gst[:, 8:16], scalar=1.0 / 64, in1=gst[:, 24:32], op0=ALU.mult, op1=ALU.subtract), R=['gn1', 'gn3'], W=['gn4'])
        S.op('dve', lambda e: e.tensor_scalar(out=gst[:, 32:40], in0=gst[:, 32:40], scalar1=float(eps), scalar2=None, op0=ALU.add), R=['gn4'], W=['gn4'])
        S.op('act', lambda e: e.activation(out=gst[:, 40:48], in_=gst[:, 32:40], func=AF.Sqrt), R=['gn4'], W=['gn5'])
        S.op('dve', lambda e: e.reciprocal(out=gst[:, 48:56], in_=gst[:, 40:48]), R=['gn5'], W=['gn6'])
        S.op('dve', lambda e: e.tensor_tensor(out=o3, in0=o3, in1=gst[:, 16:24].unsqueeze(2).broadcast_to([128, 8, 64]), op=ALU.subtract), R=[okey, 'gn2'], W=[okey])
        S.op('dve', lambda e: e.tensor_tensor(out=o3, in0=o3, in1=gst[:, 48:56].unsqueeze(2).broadcast_to([128, 8, 64]), op=ALU.mult), R=[okey, 'gn6'], W=[okey])
        S.op('pool', lambda e: e.tensor_tensor(out=o, in0=o, in1=gt, op=ALU.mult), R=[okey, gk], W=[okey])
        S.op('pool', lambda e: e.tensor_tensor(out=o, in0=o, in1=bt, op=ALU.add), R=[okey, bk], W=[okey])

    def rwkv(half, full):
        A.mark()
        w_in = dr['w_in'].rearrange("(c p) f -> p c f", p=128)
        load_ptiles(['rw_k_k', 'rw_k_a', 'rw_r_k', 'rw_gn_g', 'rw_gn_b'])
        Wa = A.bf(8 * RWC)
        Wb = A.bf(8 * RWC)
        Wav = Wa.rearrange("p (c f) -> p c f", c=8)
        Wbv = Wb.rearrange("p (c f) -> p c f", c=8)
        NW_ = 8 * RWC
        if half == 0 or 'mix_0' not in plan:
            A.mark()
            mu_t = A.f32(RWC)
            omu_t = A.f32(RWC)
            stg = [A.f32(RWC) for _ in range(2)]
            S.dma('sp', lambda e: e.dma_start(out=mu_t, in_=dr['rw_mu'].partition_broadcast(128)), R=[], W=['mu'])
            S.op('dve', lambda e: e.tensor_scalar(out=omu_t, in0=mu_t, scalar1=-1.0, scalar2=1.0, op0=ALU.mult, op1=ALU.add), R=['mu'], W=['omu'])
            for dc in range(8):
                sgb = stg[dc % 2]
                sk = 'stg%d' % (dc % 2)
                S.dma('sp', lambda e: e.dma_start(out=sgb, in_=dr['w_in'][dc * 128:(dc + 1) * 128, RETC:INC]), R=[], W=[sk])
                S.op('dve', lambda e: e.tensor_tensor(out=Wav[:, dc, :], in0=sgb, in1=omu_t, op=ALU.mult), R=[sk, 'omu'], W=['Wa'])
                S.op('pool', lambda e: e.tensor_tensor(out=Wbv[:, dc, :], in0=sgb, in1=mu_t, op=ALU.mult), R=[sk, 'mu'], W=['Wb'])
            for q_ in range(4):
                S.dma('sp', lambda e: e.dma_start(out=wab_s[:, q_ * (NW_ // 4):(q_ + 1) * (NW_ // 4)], in_=Wa[:, q_ * (NW_ // 4):(q_ + 1) * (NW_ // 4)]), R=['Wa'], W=['wabs_a%d' % q_], sk='wabs')
                S.dma('sp', lambda e: e.dma_start(out=wab_s[:, NW_ + q_ * (NW_ // 4):NW_ + (q_ + 1) * (NW_ // 4)], in_=Wb[:, q_ * (NW_ // 4):(q_ + 1) * (NW_ // 4)]), R=['Wb'], W=['wabs_b%d' % q_], sk='wabs')
            A.release()
        else:
            for q_ in range(4):
                S.dma('sp', lambda e: e.dma_start(out=Wa[:, q_ * (NW_ // 4):(q_ + 1) * (NW_ // 4)], in_=wab_s[:, q_ * (NW_ // 4):(q_ + 1) * (NW_ // 4)]), R=[], W=['Wa'], sk='wa_ld%d' % q_)
                S.dma('sp', lambda e: e.dma_start(out=Wb[:, q_ * (NW_ // 4):(q_ + 1) * (NW_ // 4)], in_=wab_s[:, NW_ + q_ * (NW_ // 4):NW_ + (q_ + 1) * (NW_ // 4)]), R=[], W=['Wb'], sk='wb_ld%d' % q_)
        wup = A.f32(512)
        aup = A.f32(512)
        gup1 = A.bf(512)
        gup2 = A.bf(512)
        w0r = A.f32(512)
        a0r = A.f32(512)
        for t_, k_ in ((wup, 'wup'), (aup, 'aup'), (gup2, 'gup2'), (w0r, 'w0r'), (a0r, 'a0r')):
            S.op('pool', (lambda t_: lambda e: e.memset(t_, 0.0))(t_), R=[], W=[k_])
        S.dma('sp', lambda e: e.dma_start(out=wup[0:64, :], in_=dr['rw_w_up']), R=[], W=['wup'])
        S.dma('sp', lambda e: e.dma_start(out=aup[64:128, :], in_=dr['rw_a_up']), R=[], W=['aup'])
        S.dma('pool', lambda e: e.dma_start(out=gup1, in_=dr['rw_g_up'][0:128, :]), R=[], W=['gup1'])
        S.dma('pool', lambda e: e.dma_start(out=gup2[96:128, :], in_=dr['rw_g_up'][128:160, :]), R=[], W=['gup2'])
        S.dma('sp', lambda e: e.dma_start(out=w0r[0:1, :], in_=dr['rw_w0']), R=[], W=['w0r'])
        S.dma('sp', lambda e: e.dma_start(out=a0r[0:1, :], in_=dr['rw_a0']), R=[], W=['a0r'])
        ones_r = c32v('ones')
        twT = A.f32(128)
        sgd1 = A.bf(128)
        sgd2 = A.bf(128)
        sw = A.f32(512)
        a_t = A.f32(512)
        gi_t = A.f32(512)
        kkr = A.f32(512)
        sq = A.f32(512)
        kp = A.f32(512)
        u1 = sq
        g_t = sw
        v32 = A.f32(512)
        st = A.f32(64)
        gst2 = A.f32(64)
        r32 = A.f32(512)
        k32 = A.f32(512)
        gp_t = k32
        gC = A.f32(4)
        rh = A.bf(512)
        kt = A.bf(512)
        bt_ = A.bf(512)
        kh = A.bf(512)
        vb = A.bf(512)
        XT = A.bf(4 * 256)
        XTv = XT.rearrange("p (a b t) -> p a b t", a=4, b=2)
        btT = A.bf(512)
        khTm = A.bf(1024)
        rhTm = A.bf(1024)
        btTm = A.bf(1024)
        ktTm = A.bf(1024)
        for t_, k_ in ((khTm, 'khTm'), (rhTm, 'rhTm'), (btTm, 'btTm'), (ktTm, 'ktTm')):
            S.op('pool', (lambda t_: lambda e: e.memset(t_, 0.0))(t_), R=[], W=[k_])
        LkT = A.bf(1024)
        MbT = A.bf(1024)
        MkT = A.bf(1024)
        Abuf = [[A.bf(256) for _ in range(2)] for _ in range(4)]
        BQbuf = [[A.bf(512) for _ in range(2)] for _ in range(4)]
        TT = A.bf(1024)
        Xb = A.bf(512)
        Ub = A.bf(512)
        mixw = A.bf(512)
        bsum = st[:, 56:64]
        rw_deferred = []

        for ti in range(NT_HALF):
            c0 = XO + ti * 128
            xk = ['x1T_%d' % ti] + (['x1T_%d' % (ti - 1)] if ti > 0 else ['x1T_prev'])
            def proj(nm):
                qi = ['r', 'k', 'v'].index(nm)
                b = bank()
                n = 0
                for (Wv, wkey, sh) in ((Wav, 'Wa', 0), (Wbv, 'Wb', 1)):
                    for dc in range(8):
                        S.op('pe', (lambda Wv, sh, dc, b, qi, n: lambda e: e.matmul(PS(b), lhsT=x1Tv[:, dc, c0 - sh:c0 - sh + 128], rhs=Wv[:, dc, qi * 512:(qi + 1) * 512], start=(n == 0), stop=(n == 15)))(Wv, sh, dc, b, qi, n),
                             R=xk + [wkey], W=[PK(b)], inc=(n == 15))
                        n += 1
                return b
            bl = bank()
            for gi_, (cs, cn) in enumerate(((1536, 128), (1664, 128), (1696, 128))):
                n = 0
                for (Wv, wkey, sh) in ((Wav, 'Wa', 0), (Wbv, 'Wb', 1)):
                    for dc in range(8):
                        S.op('pe', (lambda Wv, sh, dc, cs, cn, gi_, n: lambda e: e.matmul(PS(bl)[0:cn, gi_ * 128:(gi_ + 1) * 128], lhsT=Wv[:, dc, cs:cs + cn], rhs=x1Tv[:, dc, c0 - sh:c0 - sh + 128], start=(n == 0), stop=(n == 15)))(Wv, sh, dc, cs, cn, gi_, n),
                             R=xk + [wkey], W=[PK(bl)], inc=(n == 15 and gi_ == 2))
                        n += 1
            plk = PK(bl)
            S.op('act', lambda e: e.activation(out=twT[0:64, :], in_=PS(bl)[0:64, 0:128], func=AF.Tanh), R=[plk], W=['twT'])
            S.op('act', lambda e: e.activation(out=twT[64:128, :], in_=PS(bl)[64:128, 0:128], func=AF.Copy), R=[plk], W=['twT'])
            S.op('act', lambda e: e.activation(out=sgd1, in_=PS(bl)[:, 128:256], func=AF.Sigmoid), R=[plk], W=['sgd1'])
            S.op('act', lambda e: e.activation(out=sgd2, in_=PS(bl)[:, 256:384], func=AF.Sigmoid), R=[plk], W=['sgd2'])
            bk2 = proj('k')
            kk_ = PK(bk2)
            S.op('act', lambda e: e.activation(out=k32, in_=PS(bk2), func=AF.Copy), R=[kk_], W=['k32'])
            bw = bank()
            S.op('pe', lambda e: e.matmul(PS(bw), lhsT=twT, rhs=wup, start=True, stop=False), R=['twT', 'wup'], W=[PK(bw)], inc=False)
            S.op('pe', lambda e: e.matmul(PS(bw), lhsT=ones_r, rhs=w0r, start=False, stop=True), R=['c32', 'w0r'], W=[PK(bw)])
            ba = bank()
            S.op('pe', lambda e: e.matmul(PS(ba), lhsT=twT, rhs=aup, start=True, stop=False), R=['twT', 'aup'], W=[PK(ba)], inc=False)
            S.op('pe', lambda e: e.matmul(PS(ba), lhsT=ones_r, rhs=a0r, start=False, stop=True), R=['c32', 'a0r'], W=[PK(ba)])
            S.op('act', lambda e: e.activation(out=sw, in_=PS(bw), func=AF.Sigmoid), R=[PK(bw)], W=['sw'])
            S.op('act', lambda e: e.activation(out=a_t, in_=PS(ba), func=AF.Sigmoid), R=[PK(ba)], W=['a_t'])
            bv = proj('v')
            vk_ = PK(bv)
            S.op('act', lambda e: e.activation(out=v32, in_=PS(bv), func=AF.Copy), R=[vk_], W=['v32'])
            S.op('dve', lambda e: e.tensor_copy(out=vb, in_=PS(bv)), R=[vk_], W=['vbw'])
            bc = bank()
            S.op('pe', lambda e: e.matmul(PS(bc), lhsT=c32v('tri_incl'), rhs=sw, start=True, stop=True), R=['c32', 'sw'], W=[PK(bc)])
            bx = bank()
            S.op('pe', lambda e: e.matmul(PS(bx), lhsT=c32v('tri_strict'), rhs=sw, start=True, stop=True), R=['c32', 'sw'], W=[PK(bx)])
            bgc = bank()
            for pr in range(4):
                S.op('pe', (lambda pr: lambda e: e.matmul(PS(bgc)[:, pr:pr + 1], lhsT=sw[:, pr * 128:(pr + 1) * 128], rhs=c32v('negcol'), start=True, stop=True))(pr), R=['sw', 'c32'], W=[PK(bgc)], inc=(pr == 3))
            if full:
                br = proj('r')
                rk_ = PK(br)
                S.op('act', lambda e: e.activation(out=r32, in_=PS(br), func=AF.Copy), R=[rk_], W=['r32'])
            for f_ in rw_deferred:
                f_()
            del rw_deferred[:]
            S.op('dve', lambda e: e.tensor_tensor(out=kkr, in0=k32, in1=ptile['rw_k_k'], op=ALU.mult), R=['k32', 'pt_rw_k_k'], W=['kkr'])
            S.op('act', lambda e: e.activation(out=sq, in_=kkr, func=AF.Square), R=['kkr'], W=['sq'])
            S.op('dve', lambda e: e.tensor_reduce(out=st[:, 0:8], in_=sq.rearrange("p (h d) -> p h d", h=8), axis=AX.X, op=ALU.add), R=['sq'], W=['st0'])
            S.op('act', lambda e: e.activation(out=st[:, 8:16], in_=st[:, 0:8], func=AF.Sqrt), R=['st0'], W=['st1'])
            S.op('dve', lambda e: e.tensor_scalar(out=st[:, 8:16], in0=st[:, 8:16], scalar1=1e-12, scalar2=None, op0=ALU.max), R=['st1'], W=['st1'])
            S.op('dve', lambda e: e.reciprocal(out=st[:, 16:24], in_=st[:, 8:16]), R=['st1'], W=['st2'])
            S.op('dve', lambda e: e.tensor_tensor(out=kkr.rearrange("p (h d) -> p h d", h=8), in0=kkr.rearrange("p (h d) -> p h d", h=8), in1=st[:, 16:24].unsqueeze(2).broadcast_to([128, 8, 64]), op=ALU.mult), R=['kkr', 'st2'], W=['kkr'])
            S.op('dve', lambda e: e.scalar_tensor_tensor(out=u1, in0=a_t, scalar=-1.0, in1=ptile['rw_k_a'], op0=ALU.add, op1=ALU.mult), R=['a_t', 'pt_rw_k_a'], W=['sq'])
            S.op('dve', lambda e: e.scalar_tensor_tensor(out=kp, in0=u1, scalar=1.0, in1=k32, op0=ALU.add, op1=ALU.mult), R=['sq', 'k32'], W=['kp'])
            if full:
                S.op('dve', lambda e: e.tensor_tensor(out=u1, in0=r32, in1=kp, op=ALU.mult), R=['r32', 'kp', 'sq'], W=['sq'])
                S.op('pool', lambda e: e.tensor_tensor(out=u1, in0=u1, in1=ptile['rw_r_k'], op=ALU.mult), R=['sq', 'pt_rw_r_k'], W=['sq'])
                S.op('dve', lambda e: e.tensor_reduce(out=bsum, in_=u1.rearrange("p (h d) -> p h d", h=8), axis=AX.X, op=ALU.add), R=['sq'], W=['bsum'])
            S.op('act', lambda e: e.activation(out=g_t, in_=PS(bc), func=AF.Exp), R=[PK(bc)], W=['sw'])
            S.op('act', lambda e: e.activation(out=gi_t, in_=PS(bc), func=AF.Exp, scale=-1.0), R=[PK(bc)], W=['gi_t'])
            S.op('act', lambda e: e.activation(out=gp_t, in_=PS(bx), func=AF.Exp), R=[PK(bx)], W=['k32'])
            S.op('act', lambda e: e.activation(out=gC, in_=PS(bgc)[:, 0:4], func=AF.Exp), R=[PK(bgc)], W=['gC'])
            if full:
                S.op('dve', lambda e: e.tensor_tensor(out=rh, in0=r32, in1=g_t, op=ALU.mult), R=['r32', 'sw'], W=['rh'])
            S.op('pool', lambda e: e.tensor_tensor(out=kt, in0=kp, in1=gi_t, op=ALU.mult), R=['kp', 'gi_t'], W=['kt'])
            S.op('pool', lambda e: e.tensor_tensor(out=sq, in0=kkr, in1=a_t, op=ALU.mult), R=['kkr', 'a_t', 'sq'], W=['sq'])
            S.op('pool', lambda e: e.tensor_tensor(out=bt_, in0=sq, in1=gi_t, op=ALU.mult), R=['sq', 'gi_t'], W=['bt'])
            S.op('pool', lambda e: e.tensor_tensor(out=kh, in0=kkr, in1=gp_t, op=ALU.mult), R=['kkr', 'k32'], W=['kh'])
            if full:
                S.op('dve', lambda e: e.tensor_tensor(out=gi_t.rearrange("p (h d) -> p h d", h=8), in0=v32.rearrange("p (h d) -> p h d", h=8), in1=bsum.unsqueeze(2).broadcast_to([128, 8, 64]), op=ALU.mult), R=['v32', 'bsum', 'gi_t'], W=['gi_t'])
            b1 = bank()
            for i in range(4):
                S.op('pe', (lambda i: lambda e: e.transpose(out=PSB(b1)[:, i * 256:i * 256 + 128], in_=kh[:, i * 128:(i + 1) * 128], identity=ident))(i), R=['kh', 'cb'], W=[PK(b1)], inc=(i == 3 and not full))
                if full:
                    S.op('pe', (lambda i: lambda e: e.transpose(out=PSB(b1)[:, i * 256 + 128:i * 256 + 256], in_=rh[:, i * 128:(i + 1) * 128], identity=ident))(i), R=['rh', 'cb'], W=[PK(b1)], inc=(i == 3))
            if full:
                S.op('act', lambda e: e.activation(out=XT, in_=PSB(b1), func=AF.Copy), R=[PK(b1)], W=['XT'])
            else:
                S.op('act', lambda e: e.activation(out=XT.rearrange("p (a c t) -> p a c t", a=4, c=2)[:, :, 0, :], in_=PSB(b1).rearrange("p (a c t) -> p a c t", a=4, c=2)[:, :, 0, :], func=AF.Copy), R=[PK(b1)], W=['XT'])
            for hp in range(2):
                rs_ = slice(hp * 64, (hp + 1) * 64)
                srcv = PSB(b1)[rs_, :].rearrange("p (a c t) -> p a c t", a=4, c=2)
                S.op('act', lambda e: e.activation(out=khTm[rs_, :].rearrange("p (a c t) -> p a c t", a=4, c=2)[:, :, hp, :], in_=srcv[:, :, 0, :], func=AF.Copy), R=[PK(b1)], W=['khTm'])
                if full:
                    S.op('dve', lambda e: e.tensor_copy(out=rhTm[rs_, :].rearrange("p (a c t) -> p a c t", a=4, c=2)[:, :, hp, :], in_=srcv[:, :, 1, :]), R=[PK(b1)], W=['rhTm'])
            b2 = bank()
            for i in range(4):
                S.op('pe', (lambda i: lambda e: e.transpose(out=PSB(b2)[:, i * 128:(i + 1) * 128], in_=bt_[:, i * 128:(i + 1) * 128], identity=ident))(i), R=['bt', 'cb'], W=[PK(b2)], inc=False)
            for i in range(4):
                S.op('pe', (lambda i: lambda e: e.transpose(out=PSB(b2)[:, 512 + i * 128:512 + (i + 1) * 128], in_=kt[:, i * 128:(i + 1) * 128], identity=ident))(i), R=['kt', 'cb'], W=[PK(b2)], inc=(i == 3))
            S.op('dve', lambda e: e.tensor_copy(out=btT, in_=PSB(b2)[:, 0:512]), R=[PK(b2)], W=['btT'])
            for hp in range(2):
                rs_ = slice(hp * 64, (hp + 1) * 64)
                S.op('act', lambda e: e.activation(out=btTm[rs_, :].rearrange("p (a c t) -> p a c t", a=4, c=2)[:, :, hp, :], in_=PSB(b2)[rs_, 0:512].rearrange("p (a t) -> p a t", a=4), func=AF.Copy), R=[PK(b2)], W=['btTm'])
                S.op('dve', lambda e: e.tensor_copy(out=ktTm[rs_, :].rearrange("p (a c t) -> p a c t", a=4, c=2)[:, :, hp, :], in_=PSB(b2)[rs_, 512:1024].rearrange("p (a t) -> p a t", a=4)), R=[PK(b2)], W=['ktTm'])
            for pr in range(4):
                bm1, bm2, bm3 = bank(), bank(), bank()
                for hp in range(2):
                    ps_ = slice(hp * 64, (hp + 1) * 64)
                    NX_ = 256 if full else 128
                    rhs_x = XT[:, pr * 256:pr * 256 + NX_]
                    hh_ = pr * 2 + hp
                    S.op('pe', (lambda hp, ps_, rhs_x, bm1: lambda e: e.matmul(PS(bm1)[:, hp * 256:hp * 256 + NX_], lhsT=btTm[:, hh_ * 128:(hh_ + 1) * 128], rhs=rhs_x, start=True, stop=True))(hp, ps_, rhs_x, bm1),
                         R=['btTm', 'XT'], W=[PK(bm1)], inc=(hp == 1))
                    S.op('pe', (lambda hp, ps_, rhs_x, bm2: lambda e: e.matmul(PS(bm2)[:, hp * 256:hp * 256 + NX_], lhsT=ktTm[:, hh_ * 128:(hh_ + 1) * 128], rhs=rhs_x, start=True, stop=True))(hp, ps_, rhs_x, bm2),
                         R=['ktTm', 'XT'], W=[PK(bm2)], inc=(hp == 1))
                    S.op('pe', (lambda hp, ps_, bm3: lambda e: e.matmul(PS(bm3)[:, hp * 128:(hp + 1) * 128], lhsT=khTm[:, hh_ * 128:(hh_ + 1) * 128], rhs=btT[:, pr * 128:(pr + 1) * 128], start=True, stop=True))(hp, ps_, bm3),
                         R=['btT', 'khTm'], W=[PK(bm3)], inc=(hp == 1))
                A0 = Abuf[pr][0]
                BQ0 = BQbuf[pr][0].rearrange("p (h x) -> p h x", h=2)
                m1v = cbv('m1').rearrange("p (h x) -> p h x", h=2)
                p1v = PS(bm1).rearrange("p (h x) -> p h x", h=2)
                S.op('dve', (lambda BQ0, p1v, m1v: lambda e: e.tensor_tensor(out=BQ0[:, :, 0:128], in0=p1v[:, :, 0:128], in1=m1v[:, :, 0:128], op=ALU.mult))(BQ0, p1v, m1v),
                     R=[PK(bm1), 'cb'], W=['BQ%d_0' % pr])
                S.op('pool', (lambda BQ0: lambda e: e.tensor_copy(out=BQ0[:, :, 128:256], in_=ident.unsqueeze(1).broadcast_to([128, 2, 128])))(BQ0), R=['cb'], W=['BQ%d_0' % pr])
                if full:
                    S.op('dve', (lambda p1v, m1v: lambda e: e.tensor_tensor(out=MbT[:, pr * 256:(pr + 1) * 256].rearrange("p (h x) -> p h x", h=2), in0=p1v[:, :, 128:256], in1=m1v[:, :, 128:256], op=ALU.mult))(p1v, m1v),
                         R=[PK(bm1), 'cb'], W=['MbT%d' % pr])
                m2v = cbv('m2').rearrange("p (h x) -> p h x", h=2)
                p2v = PS(bm2).rearrange("p (h x) -> p h x", h=2)
                S.op('dve', (lambda p2v, m2v: lambda e: e.tensor_tensor(out=LkT[:, pr * 256:(pr + 1) * 256].rearrange("p (h x) -> p h x", h=2), in0=p2v[:, :, 0:128], in1=m2v[:, :, 0:128], op=ALU.mult))(p2v, m2v),
                     R=[PK(bm2), 'cb'], W=['LkT%d' % pr])
                if full:
                    S.op('dve', (lambda p2v, m2v: lambda e: e.tensor_tensor(out=MkT[:, pr * 256:(pr + 1) * 256].rearrange("p (h x) -> p h x", h=2), in0=p2v[:, :, 128:256], in1=m2v[:, :, 128:256], op=ALU.mult))(p2v, m2v),
                         R=[PK(bm2), 'cb'], W=['MkT%d' % pr])
                S.op('dve', (lambda A0, bm3: lambda e: e.tensor_tensor(out=A0, in0=PS(bm3)[:, 0:256], in1=cbv('m3'), op=ALU.mult))(A0, bm3), R=[PK(bm3), 'cb'], W=['A%d_0' % pr])
            for lv in range(7):
                last = (lv == 6)
                for pr in range(4):
                    cur, nxt = lv % 2, (lv + 1) % 2
                    Ac = Abuf[pr][cur].rearrange("p (h x) -> p h x", h=2)
                    BQc = BQbuf[pr][cur].rearrange("p (h x) -> p h x", h=2)
                    ak, bqk = 'A%d_%d' % (pr, cur), 'BQ%d_%d' % (pr, cur)
                    if not last:
                        bA = bank()
                        for hp in range(2):
                            S.op('pe', (lambda hp, BQc, Ac, bA: lambda e: e.matmul(PS(bA)[:, hp * 128:(hp + 1) * 128], lhsT=BQc[:, hp, 0:128], rhs=Ac[:, hp, :], start=True, stop=True))(hp, BQc, Ac, bA),
                                 R=[ak, bqk], W=[PK(bA)], inc=(hp == 1))
                        bB = bank()
                        for hp in range(2):
                            S.op('pe', (lambda hp, BQc, Ac, bB: lambda e: e.matmul(PS(bB)[:, hp * 256:hp * 256 + 256], lhsT=Ac[:, hp, :], rhs=BQc[:, hp, :], start=True, stop=False, skip_group_check=True))(hp, BQc, Ac, bB),
                                 R=[ak, bqk], W=[PK(bB)], inc=False)
                            S.op('pe', (lambda hp, BQc, bB: lambda e: e.matmul(PS(bB)[:, hp * 256 + 128:hp * 256 + 256], lhsT=ident, rhs=BQc[:, hp, 128:256], start=False, stop=True, skip_group_check=True))(hp, BQc, bB),
                                 R=[bqk, 'cb'], W=[PK(bB)], inc=(hp == 1))
                        An = Abuf[pr][nxt]
                        BQn = BQbuf[pr][nxt]
                        S.op('dve' if pr % 2 == 0 else 'act', (lambda An, bA, pr: (lambda e: e.tensor_copy(out=An, in_=PS(bA)[:, 0:256])) if pr % 2 == 0 else (lambda e: e.activation(out=An, in_=PS(bA)[:, 0:256], func=AF.Copy)))(An, bA, pr),
                             R=[PK(bA)], W=['A%d_%d' % (pr, nxt)])
                        S.op('dve' if pr % 2 else 'act', (lambda BQn, bB, pr: (lambda e: e.tensor_copy(out=BQn, in_=PS(bB))) if pr % 2 else (lambda e: e.activation(out=BQn, in_=PS(bB), func=AF.Copy)))(BQn, bB, pr),
                             R=[PK(bB)], W=['BQ%d_%d' % (pr, nxt)])
                    else:
                        bB = bank()
                        for hp in range(2):
                            S.op('pe', (lambda hp, BQc, Ac, bB: lambda e: e.matmul(PS(bB)[:, hp * 128:(hp + 1) * 128], lhsT=Ac[:, hp, :], rhs=BQc[:, hp, 128:256], start=True, stop=False))(hp, BQc, Ac, bB),
                                 R=[ak, bqk], W=[PK(bB)], inc=False)
                            S.op('pe', (lambda hp, BQc, bB: lambda e: e.matmul(PS(bB)[:, hp * 128:(hp + 1) * 128], lhsT=ident, rhs=BQc[:, hp, 128:256], start=False, stop=True))(hp, BQc, bB),
                                 R=[bqk, 'cb'], W=[PK(bB)], inc=(hp == 1))
                        S.op('dve', (lambda bB, pr: lambda e: e.tensor_copy(out=TT[:, pr * 256:(pr + 1) * 256], in_=PS(bB)[:, 0:256]))(bB, pr), R=[PK(bB)], W=['TT%d' % pr])
            bX = bank()
            for h in range(8):
                pr, hp = h // 2, h % 2
                ps_ = slice(hp * 64, (hp + 1) * 64)
                S.op('pe', (lambda h, pr, ps_: lambda e: e.matmul(PS(bX)[:, h * 64:(h + 1) * 64], lhsT=khTm[:, h * 128:(h + 1) * 128], rhs=rwAb[:, pr * 64:(pr + 1) * 64], start=True, stop=False))(h, pr, ps_),
                     R=['khTm', 'rwAb'], W=[PK(bX)], inc=False)
                S.op('pe', (lambda h: lambda e: e.matmul(PS(bX)[:, h * 64:(h + 1) * 64], lhsT=LkT[:, h * 128:(h + 1) * 128], rhs=vb[:, h * 64:(h + 1) * 64], start=False, stop=True))(h),
                     R=['LkT%d' % pr, 'vbw'], W=[PK(bX)], inc=(h == 7))
            S.op('act', lambda e: e.activation(out=Xb, in_=PS(bX), func=AF.Copy, scale=-1.0), R=[PK(bX)], W=['Xb'])
            bU = bank()
            for h in range(8):
                S.op('pe', (lambda h: lambda e: e.matmul(PS(bU)[:, h * 64:(h + 1) * 64], lhsT=TT[:, h * 128:(h + 1) * 128], rhs=Xb[:, h * 64:(h + 1) * 64], start=True, stop=True))(h),
                     R=['TT%d' % (h // 2), 'Xb'], W=[PK(bU)], inc=(h == 7))
            S.op('act', lambda e: e.activation(out=Ub, in_=PS(bU), func=AF.Copy), R=[PK(bU)], W=['Ub'])
            if full:
                bY = bank()
                for h in range(8):
                    pr, hp = h // 2, h % 2
                    ps_ = slice(hp * 64, (hp + 1) * 64)
                    S.op('pe', (lambda h, pr, ps_: lambda e: e.matmul(PS(bY)[:, h * 64:(h + 1) * 64], lhsT=rhTm[:, h * 128:(h + 1) * 128], rhs=rwAb[:, pr * 64:(pr + 1) * 64], start=True, stop=False))(h, pr, ps_),
                         R=['rhTm', 'rwAb'], W=[PK(bY)], inc=False)
                    S.op('pe', (lambda h: lambda e: e.matmul(PS(bY)[:, h * 64:(h + 1) * 64], lhsT=MbT[:, h * 128:(h + 1) * 128], rhs=Ub[:, h * 64:(h + 1) * 64], start=False, stop=False))(h),
                         R=['MbT%d' % pr, 'Ub'], W=[PK(bY)], inc=False)
                    S.op('pe', (lambda h: lambda e: e.matmul(PS(bY)[:, h * 64:(h + 1) * 64], lhsT=MkT[:, h * 128:(h + 1) * 128], rhs=vb[:, h * 64:(h + 1) * 64], start=False, stop=True))(h),
                         R=['MkT%d' % pr, 'vbw'], W=[PK(bY)], inc=(h == 7))
                S.op('act', lambda e: e.activation(out=kp, in_=PS(bY), func=AF.Copy), R=[PK(bY)], W=['kp'])
            bS = bank()
            for pr in range(4):
                S.op('pe', (lambda pr: lambda e: e.matmul(PS(bS)[:, pr * 128:(pr + 1) * 128], lhsT=bt_[:, pr * 128:(pr + 1) * 128], rhs=Ub[:, pr * 128:(pr + 1) * 128], start=True, stop=False))(pr),
                     R=['bt', 'Ub'], W=[PK(bS)], inc=False)
                S.op('pe', (lambda pr: lambda e: e.matmul(PS(bS)[:, pr * 128:(pr + 1) * 128], lhsT=kt[:, pr * 128:(pr + 1) * 128], rhs=vb[:, pr * 128:(pr + 1) * 128], start=False, stop=True))(pr),
                     R=['kt', 'vbw'], W=[PK(bS)], inc=(pr == 3))
            for hp in range(2):
                S.op('dve', (lambda hp: lambda e: e.tensor_tensor(out=rwA[hp * 64:(hp + 1) * 64, :].rearrange("p (a e) -> p a e", a=4), in0=rwA[hp * 64:(hp + 1) * 64, :].rearrange("p (a e) -> p a e", a=4),
                                                                in1=PS(bS)[hp * 64:(hp + 1) * 64, :].rearrange("p (a f) -> p a f", a=4)[:, :, hp * 64:(hp + 1) * 64], op=ALU.add))(hp),
                     R=[PK(bS), 'rwA', 'rwAb'], W=['rwA'])
            S.op('dve', lambda e: e.tensor_tensor(out=rwA.rearrange("p (a e) -> p a e", a=4), in0=rwA.rearrange("p (a e) -> p a e", a=4), in1=gC.unsqueeze(2).broadcast_to([128, 4, 64]), op=ALU.mult), R=['rwA', 'gC'], W=['rwA'])
            S.op('act', lambda e: e.activation(out=rwAb, in_=rwA, func=AF.Copy), R=['rwA'], W=['rwAb'])
            if full:
                group_norm_out(kp, 'kp', sq, 'sq', gst2, 64e-5, ptile['rw_gn_g'], ptile['rw_gn_b'], 'pt_rw_gn_g', 'pt_rw_gn_b')
                S.op('pool', lambda e: e.tensor_tensor(out=kp, in0=kp, in1=gi_t, op=ALU.add), R=['kp', 'gi_t'], W=['kp'])
                bgt = bank()
                S.op('pe', lambda e: e.matmul(PS(bgt), lhsT=sgd1, rhs=gup1, start=True, stop=False), R=['sgd1', 'gup1'], W=[PK(bgt)], inc=False)
                S.op('pe', lambda e: e.matmul(PS(bgt), lhsT=sgd2, rhs=gup2, start=False, stop=True), R=['sgd2', 'gup2'], W=[PK(bgt)])
                S.op('dve', lambda e: e.tensor_tensor(out=mixw, in0=kp, in1=PS(bgt), op=ALU.mult), R=['kp', PK(bgt)], W=['mixw'])
                rw_deferred.append((lambda ti: lambda: transpose_to(mixw, ['mixw'], 4, mixTv[:, 4:8, ti * 128:(ti + 1) * 128], ['mixT_w%d' % ti]))(ti))
        for f_ in rw_deferred:
            f_()
        A.release()

    S.op('pool', lambda e: e.memset(retS, 0.0), R=[], W=['retS'])
    S.op('pool', lambda e: e.memset(retSb, 0.0), R=[], W=['retSb'])
    S.op('pool', lambda e: e.memset(rwA, 0.0), R=[], W=['rwA'])
    S.op('pool', lambda e: e.memset(rwAb, 0.0), R=[], W=['rwAb'])
    S.op('pool', lambda e: e.memset(x1T, 0.0), R=[], W=['x1T_prev'] + ['x1T_%d' % i for i in range(NT_HALF)])

    for half in range(2):
        full = (half == 1)
        A.mark()
        xTv = mixTv
        src = dr['xs'][half * S_HALF:(half + 1) * S_HALF, :]
        prep_xT(src, S_HALF, xTv, 0, lambda t0, n: ['xT_%d' % (t0 // 512)] if n == 128 else ['xT_%d' % (t0 // 512)])
        load_ln('ln1_g', 'ln1_b')
        x1b = [A.bf(1024) for _ in range(8)]
        if half == 1:
            S.op('pool', lambda e: e.tensor_copy(out=x1Tv[:, :, XO - 1:XO], in_=x1Tv[:, :, XO + S_HALF - 1:XO + S_HALF]), R=['x1T_%d' % (NT_HALF - 1)], W=['x1T_prev'])

        def cons1(ti, y, yk, half=half, full=full, x1b=x1b):
            if full:
                S.dma('sp', lambda e: e.dma_start(out=x1s[ti * 128:(ti + 1) * 128, :], in_=y), R=[yk], W=['x1s_%d' % ti], sk='x1s')
            b = x1b[ti % 8]
            bk = 'x1b%d' % (ti % 8)
            S.op('dve', lambda e: e.tensor_scalar(out=b, in0=y, scalar1=hmask[:, half:half + 1], scalar2=None, op0=ALU.mult), R=[yk, 'hmask'], W=[bk])
            return lambda: transpose_to(b, [bk], 8, x1Tv[:, :, XO + ti * 128:XO + (ti + 1) * 128], ['x1T_%d' % ti], evac_eng='dve')

        if 'ffn1_%d' % half in plan:
            ffn(xTv, 0, lambda t0, n: ['xT_%d' % (t0 // 512)], S_HALF, dr['ffn1_w_gu'], dr['ffn1_w_down'], src, LN_EPS / (ALPHA * ALPHA), cons1, 'f1')
        A.release()
        S.barrier()
        if 'mix_%d' % half in plan:
            mixer(half, full)
        S.barrier()

    A.mark()
    NTW = NT_HALF if 'wout' in plan else 0
    wo = A.bf(8 * 1024)
    wov = wo.rearrange("p (c f) -> p c f", c=8)
    w_out_v = dr['w_out'].rearrange("(c p) f -> p c f", p=128)
    for hh in range(2):
        S.dma('pool', (lambda hh: lambda e: e.dma_start(out=wov[:, :, hh * 512:(hh + 1) * 512], in_=w_out_v[:, :, hh * 512:(hh + 1) * 512]))(hh), R=[], W=['wo%d' % hh])
    load_ln('ln2_g', 'ln2_b')
    ybuf = [A.f32(1024) for _ in range(4)]
    x2b = [A.bf(1024) for _ in range(4)]
    tmp_sq = A.f32(1024)
    st = A.f32(16)
    wo_deferred = []
    for g in range(NTW // 2):
        tiles = []
        for ti in (2 * g, 2 * g + 1):
            y, yk = ybuf[ti % 4], 'y%d' % (ti % 4)
            S.dma('sp', (lambda y, ti: lambda e: e.dma_start(out=y, in_=x1s[ti * 128:(ti + 1) * 128, :]))(y, ti), R=['x1s_%d' % ti], W=[yk])
            for dh in range(2):
                b = bank()
                for c in range(8):
                    S.op('pe', (lambda c, b, dh, ti: lambda e: e.matmul(PS(b), lhsT=mixTv[:, c, ti * 128:(ti + 1) * 128], rhs=wov[:, c, dh * 512:(dh + 1) * 512], start=(c == 0), stop=(c == 7)))(c, b, dh, ti),
                         R=['mixT_r%d' % ti, 'mixT_w%d' % ti, 'wo%d' % dh], W=[PK(b)], inc=(c == 7))
                S.op('dve', (lambda y, b, dh: lambda e: e.scalar_tensor_tensor(out=y[:, dh * 512:(dh + 1) * 512], in0=PS(b), scalar=1.0 / ALPHA, in1=y[:, dh * 512:(dh + 1) * 512], op0=ALU.mult, op1=ALU.add))(y, b, dh),
                     R=[PK(b), yk], W=[yk])
            tiles.append((ti, y, yk))
        for f_ in wo_deferred:
            f_()
        del wo_deferred[:]
        layer_norm_group([(y, yk) for (ti, y, yk) in tiles], LN_EPS / (ALPHA * ALPHA), tmp_sq, st)
        for (ti, y, yk) in tiles:
            S.dma('sp', (lambda y, ti: lambda e: e.dma_start(out=x2s[ti * 128:(ti + 1) * 128, :], in_=y))(y, ti), R=[yk], W=['x2s_%d' % ti], sk='x2s')
            b2, b2k = x2b[ti % 4], 'x2b%d' % (ti % 4)
            S.op('dve', (lambda b2, y: lambda e: e.tensor_copy(out=b2, in_=y))(b2, y), R=[yk], W=[b2k])
            wo_deferred.append((lambda b2, b2k, ti: lambda: transpose_to(b2, [b2k], 8, x2Tv[:, :, XO + ti * 128:XO + (ti + 1) * 128], ['x2T_%d' % (ti // 4)], evac_eng='dve'))(b2, b2k, ti))
    for f_ in wo_deferred:
        f_()
    A.release()
    S.barrier()

    A.mark()
    load_ln('ln3_g', 'ln3_b')
    x3b = [A.bf(1024) for _ in range(8)]

    def cons3(ti, y, yk):
        S.dma('sp', lambda e: e.dma_start(out=x3s[ti * 128:(ti + 1) * 128, :], in_=y), R=[yk], W=['x3s_%d' % ti], sk='x3s')
        b = x3b[ti % 8]
        bk = 'x3b%d' % (ti % 8)
        S.op('dve', lambda e: e.tensor_copy(out=b, in_=y), R=[yk], W=[bk])
        return lambda: transpose_to(b, [bk], 8, mixTv[:, :, ti * 128:(ti + 1) * 128], ['x3T_%d' % ti], evac_eng='dve')

    if 'ffn2' in plan:
        ffn(x2Tv, XO, lambda t0, n: ['x2T_%d' % (t0 // 512)], S_HALF, dr['ffn2_w_gu'], dr['ffn2_w_down'], x2s, LN_EPS / (ALPHA * ALPHA), cons3, 'f2')
    A.release()
    S.barrier()

    A.mark()
    wgt = A.bf(8 * 1024)
    wgv = wgt.rearrange("p (c f) -> p c f", c=8)
    wpj = A.bf(2 * 1024)
    wpv = wpj.rearrange("p (c f) -> p c f", c=2)
    bgr = A.bf(1024)
    onesb = A.bf(128)
    S.op('pool', lambda e: e.memset(bgr, 0.0), R=[], W=['bgr'])
    ple_g = dr['ple_w_gate'].rearrange("(c p) f -> p c f", p=128)
    ple_p = dr['ple_w_proj'].rearrange("(c p) f -> p c f", p=128)
    for hh in range(2):
        S.dma('pool', (lambda hh: lambda e: e.dma_start(out=wgv[:, :, hh * 512:(hh + 1) * 512], in_=ple_g[:, :, hh * 512:(hh + 1) * 512]))(hh), R=[], W=['wgt%d' % hh])
    S.dma('pool', lambda e: e.dma_start(out=wpv, in_=ple_p), R=[], W=['wpj'])
    S.dma('pool', lambda e: e.dma_start(out=bgr[0:1, :], in_=dr['ple_b_gate']), R=[], W=['bgr'])
    S.op('dve', lambda e: e.tensor_copy(out=onesb, in_=c32v('ones')), R=['c32'], W=['onesb'])
    pb_ = [A.bf(256) for _ in range(2)]
    pT = [A.bf(256) for _ in range(2)]
    x3t = [A.f32(1024) for _ in range(2)]
    gsb = [A.f32(1024) for _ in range(2)]
    for ti in range(NT_HALF if 'ple' in plan else 0):
        pbt, pbk = pb_[ti % 2], 'pb%d' % (ti % 2)
        pTt, pTk = pT[ti % 2], 'pT%d' % (ti % 2)
        x3, x3k = x3t[ti % 2], 'x3t%d' % (ti % 2)
        gs, gsk = gsb[ti % 2], 'gs%d' % (ti % 2)
        S.dma('pool', (lambda pbt, ti: lambda e: e.dma_start(out=pbt, in_=dr['p'][ti * 128:(ti + 1) * 128, :]))(pbt, ti), R=[], W=[pbk])
        S.dma('sp', (lambda x3, ti: lambda e: e.dma_start(out=x3, in_=x3s[ti * 128:(ti + 1) * 128, :]))(x3, ti), R=['x3s_%d' % ti], W=[x3k])
        transpose_to(pbt, [pbk], 2, pTt.rearrange("p (c t) -> p c t", c=2), [pTk])
        for dh in range(2):
            bg_ = bank()
            for c in range(8):
                S.op('pe', (lambda c, bg_, dh, ti: lambda e: e.matmul(PS(bg_), lhsT=mixTv[:, c, ti * 128:(ti + 1) * 128], rhs=wgv[:, c, dh * 512:(dh + 1) * 512], start=(c == 0), stop=False))(c, bg_, dh, ti),
                     R=['x3T_%d' % ti, 'wgt%d' % dh], W=[PK(bg_)], inc=False)
            S.op('pe', (lambda bg_, dh: lambda e: e.matmul(PS(bg_), lhsT=onesb, rhs=bgr[:, dh * 512:(dh + 1) * 512], start=False, stop=True))(bg_, dh), R=['onesb', 'bgr'], W=[PK(bg_)])
            bp_ = bank()
            for c in range(2):
                S.op('pe', (lambda c, bp_, dh, pTt: lambda e: e.matmul(PS(bp_), lhsT=pTt[:, c * 128:(c + 1) * 128], rhs=wpv[:, c, dh * 512:(dh + 1) * 512], start=(c == 0), stop=(c == 1)))(c, bp_, dh, pTt),
                     R=[pTk, 'wpj'], W=[PK(bp_)], inc=(c == 1))
            S.op('act', (lambda gs, bg_, dh: lambda e: e.activation(out=gs[:, dh * 512:(dh + 1) * 512], in_=PS(bg_), func=AF.Sigmoid))(gs, bg_, dh), R=[PK(bg_)], W=[gsk])
            S.op('dve', (lambda gs, bp_, dh: lambda e: e.tensor_tensor(out=gs[:, dh * 512:(dh + 1) * 512], in0=gs[:, dh * 512:(dh + 1) * 512], in1=PS(bp_), op=ALU.mult))(gs, bp_, dh), R=[PK(bp_), gsk], W=[gsk])
        S.op('dve', (lambda gs, x3: lambda e: e.tensor_tensor(out=gs, in0=gs, in1=x3, op=ALU.add))(gs, x3), R=[gsk, x3k], W=[gsk])
        S.dma('sp', (lambda gs, ti: lambda e: e.dma_start(out=out[ti * 128:(ti + 1) * 128, :], in_=gs))(gs, ti), R=[gsk], W=['out_%d' % ti], sk='out')
    A.release()
    S.barrier()

    with nc.Block() as block:
        S.emit(nc, block)
    es.close()
    print("semaphores:", len(S.semkeys), "ops:", {e: len(l) for e, l in S.lists.items()}, "arena hi:", A.hi)
    return nc


_NC_CACHE = {}


def kernel(**inputs):
    x = np.asarray(inputs['x'], np.float32)
    p = np.asarray(inputs['p'], np.float32)[0]
    if 'nc' not in _NC_CACHE:
        _NC_CACHE['nc'] = build_program()
    nc = _NC_CACHE['nc']
    c32 = np.ascontiguousarray(np.concatenate([C32[k] for k in C32], axis=1).astype(np.float32))
    cb = np.ascontiguousarray(np.concatenate([CB[k] for k in CB], axis=1).astype(np.float32))
    c32r = np.ascontiguousarray(np.concatenate([C32R[k] for k in C32R], axis=1).astype(np.float32))
    wmap = {}
    for n in WEIGHT_NAMES:
        wmap[n] = np.ascontiguousarray(np.asarray(inputs[n], np.float32)[0].reshape(WSHAPES[n]))
    in_maps = []
    for c in range(8):
        b, half = c // 2, c % 2
        m = dict(wmap)
        if half == 1:
            xs = x[b]
            pos = np.arange(4096)
            hm = np.ones((128, 2), np.float32)
        else:
            xs = np.concatenate([np.zeros((S_HALF, D), np.float32), x[b, :S_HALF]], axis=0)
            pos = np.arange(4096) - S_HALF
            hm = np.ones((128, 2), np.float32)
            hm[:, 0] = 0.0
        cos, sin = rope_tables(pos)
        m['xs'] = np.ascontiguousarray(xs)
        m['p'] = np.ascontiguousarray(p[b, half * S_HALF:(half + 1) * S_HALF])
        m['hmask'] = hm
        m['cos'] = cos
        m['sin'] = sin
        m['c32'] = c32
        m['cb'] = cb
        m['c32r'] = c32r
        in_maps.append(m)
    res = run_bass_kernel_spmd(nc, in_maps, core_ids=list(range(8)))
    outp = np.zeros((4, 4096, D), np.float32)
    for c in range(8):
        b, half = c // 2, c % 2
        outp[b, half * S_HALF:(half + 1) * S_HALF] = res.results[c]['out']
    return outp
```

```python
import os
import numpy as np
import concourse.bass as bass
import concourse.mybir as mybir
from concourse.bass_utils import run_bass_kernel_spmd

F32 = mybir.dt.float32
BF16 = mybir.dt.bfloat16
AF = mybir.ActivationFunctionType
ALU = mybir.AluOpType
AX = mybir.AxisListType

D = 1024
DFF = 2816
NJ = DFF // 128
S_HALF = 2048
NT_HALF = S_HALF // 128
RETC = 2048
RWC = 1824
INC = RETC + RWC
ALPHA = 2.0 ** 0.25
LN_EPS = 1e-5
EDEC = float(np.exp(-0.5))

STRICT = True


class _Rec:
    def __init__(self):
        self.calls = []

    def __getattr__(self, name):
        def f(*a, **k):
            self.calls.append((name, a, k))
            return self
        return f


def _capture(fn):
    r = _Rec()
    fn(r)
    assert len(r.calls) == 1, r.calls
    name, a, k = r.calls[0]
    return lambda e: getattr(e, name)(*a, **k)


class Sched:
    def __init__(self):
        self.engs = ['pe', 'act', 'dve', 'pool', 'sp']
        self.lists = {e: [] for e in self.engs}
        self.cnt = {e: 0 for e in self.engs}
        self.lastw = {}
        self.readers = {}
        self.waited = {e: {} for e in self.engs}
        self.dmacnt = {}
        self.semkeys = set(self.engs)
        self.alltok = {}

    def _deps(self, eng, R, W, is_dma):
        deps = []
        raw = set()
        for k in R:
            t = self.lastw.get(k)
            if t:
                deps.append(t)
                raw.add(t)
            if k.startswith('ps'):
                deps.extend(tk for tk in self.readers.get(k, ()) if tk[0] != eng)
        for k in W:
            t = self.lastw.get(k)
            if t:
                deps.append(t)
            deps.extend(self.readers.get(k, ()))
        waits = {}
        for (sk, v) in deps:
            if sk == eng and not is_dma and (eng == 'pe' or not STRICT or (sk, v) not in raw):
                continue
            if self.waited[eng].get(sk, 0) >= v:
                continue
            waits[sk] = max(waits.get(sk, 0), v)
        for sk, v in waits.items():
            self.waited[eng][sk] = v
        return list(waits.items())

    def _commit(self, tok, R, W):
        for k in W:
            self.lastw[k] = tok
            self.readers[k] = []
        for k in R:
            if k not in W:
                self.readers.setdefault(k, []).append(tok)
        self.alltok[tok[0]] = max(self.alltok.get(tok[0], 0), tok[1])

    def op(self, eng, fn, R=(), W=(), inc=True):
        self._clean = False
        waits = self._deps(eng, R, W, False)
        if inc:
            self.cnt[eng] += 1
            tok = (eng, self.cnt[eng])
        else:
            tok = (eng, self.cnt[eng] + 1)
        self.lists[eng].append((waits, _capture(fn), (eng, 1) if inc else None))
        self._commit(tok, R, W)

    def dma(self, eng, fn, R, W, sk=None):
        self._clean = False
        waits = self._deps(eng, R, W, True)
        sk = 'd:' + (sk or W[0])
        self.semkeys.add(sk)
        self.dmacnt[sk] = self.dmacnt.get(sk, 0) + 16
        tok = (sk, self.dmacnt[sk])
        self.lists[eng].append((waits, _capture(fn), (sk, 16)))
        self._commit(tok, R, W)

    def barrier(self):
        if getattr(self, '_clean', False):
            return
        self._clean = True
        for e in ['pe', 'act', 'dve', 'pool']:
            if self.lists[e] and self.lists[e][-1][2] is not None and self.lists[e][-1][2][0] == e:
                continue
            self.cnt[e] += 1
            self.alltok[e] = self.cnt[e]
            self.lists[e].append(([], 'nop', (e, 1)))
        for e in self.engs:
            waits = []
            for sk, v in self.alltok.items():
                if self.waited[e].get(sk, 0) >= v:
                    continue
                if sk == e:
                    continue
                waits.append((sk, v))
                self.waited[e][sk] = v
            self.lists[e].append((waits, None, None))
        self.lastw = {}
        self.readers = {}

    def emit(self, nc, block):
        sems = {sk: nc.alloc_semaphore(name=("s_" + sk.replace(':', '_').replace('.', '_'))[:40]) for sk in sorted(self.semkeys)}
        engobj = {'pe': 'tensor', 'act': 'scalar', 'dve': 'vector', 'pool': 'gpsimd', 'sp': 'sync'}

        def make(ename):
            lst = self.lists[ename]

            def body(e):
                for (waits, fn, inc) in lst:
                    for (sk, v) in waits:
                        e.wait_ge(sems[sk], v)
                    if fn is None:
                        continue
                    if fn == 'nop':
                        ins = e.nop()
                    else:
                        ins = fn(e)
                    if inc is not None:
                        ins.then_inc(sems[inc[0]], inc[1])
            return body

        for ename in self.engs:
            getattr(block, engobj[ename])(make(ename))


def gammas():
    return 1.0 - 2.0 ** (-5.0 - np.arange(8, dtype=np.float64))


def host_consts():
    g = gammas()
    i = np.arange(128)
    c = {}
    s_le_t = (i[:, None] <= i[None, :]).astype(np.float32)
    s_lt_t = (i[:, None] < i[None, :]).astype(np.float32)
    c['tri_incl'] = -EDEC * s_le_t
    c['tri_strict'] = -EDEC * s_lt_t
    c['negcol'] = np.full((128, 1), -EDEC, np.float32)
    c['ones'] = np.ones((128, 128), np.float32)
    rel = (i[None, :] - i[:, None]).astype(np.float64)
    dm = np.zeros((128, 8, 128), np.float64)
    for h in range(8):
        dm[:, h, :] = np.where(rel >= 0, 0.125 * np.exp(np.where(rel >= 0, rel, 0) * np.log(g[h])), 0.0)
    cr = {}
    cr['dmask'] = dm.reshape(128, 1024).astype(np.float32)
    kd = np.zeros((128, 8), np.float64)
    for h in range(8):
        kd[:, h] = 0.125 * g[h] ** (127.0 - i)
    c['kdec'] = kd.astype(np.float32)
    qd = np.zeros((128, 4, 128), np.float64)
    gm = np.zeros((128, 4), np.float64)
    for pr in range(4):
        for hp in range(2):
            h = 2 * pr + hp
            qd[hp * 64:(hp + 1) * 64, pr, :] = (g[h] ** (i + 1.0))[None, :]
            gm[hp * 64:(hp + 1) * 64, pr] = g[h] ** 128.0
    cr['qdec'] = qd.reshape(128, 512).astype(np.float32)
    cr['kdecf'] = np.repeat(kd, 64, axis=1).astype(np.float32)
    cr['gam128f'] = np.repeat(gm, 64, axis=1).astype(np.float32)
    c['gam128'] = gm.astype(np.float32)
    b = {}
    b['ident'] = np.eye(128, dtype=np.float32)
    m1 = np.concatenate([-s_lt_t, s_le_t], axis=1)
    m2 = np.concatenate([s_lt_t, s_le_t], axis=1)
    b['m1'] = np.concatenate([m1, m1], axis=1)
    b['m2'] = np.concatenate([m2, m2], axis=1)
    m3 = -(i[:, None] > i[None, :]).astype(np.float32)
    b['m3'] = np.concatenate([m3, m3], axis=1)
    return c, cr, b


def rope_tables(pos):
    inv = 10000.0 ** (-np.arange(0, 64, 2, dtype=np.float32) / 64.0)
    ang = pos.astype(np.float32)[:, None] * inv[None, :]
    cos = np.cos(ang).astype(np.float32)
    sin = np.sin(ang).astype(np.float32)
    n = pos.shape[0] // 128
    cos = cos.reshape(n, 128, 32).transpose(1, 0, 2).reshape(128, n * 32)
    sin = sin.reshape(n, 128, 32).transpose(1, 0, 2).reshape(128, n * 32)
    return np.ascontiguousarray(cos), np.ascontiguousarray(sin)


C32, C32R, CB = host_consts()
C32_OFF = {}
_o = 0
for _k, _v in C32.items():
    C32_OFF[_k] = (_o, _v.shape[1])
    _o += _v.shape[1]
C32_N = _o
C32R_OFF = {}
_o = 0
for _k, _v in C32R.items():
    C32R_OFF[_k] = (_o, _v.shape[1])
    _o += _v.shape[1]
C32R_N = _o
CB_OFF = {}
_o = 0
for _k, _v in CB.items():
    CB_OFF[_k] = (_o, _v.shape[1])
    _o += _v.shape[1]
CB_N = _o

WEIGHT_NAMES = ['ffn1_w_gu', 'ffn1_w_down', 'ln1_g', 'ln1_b', 'w_in', 'ret_gn_g', 'ret_gn_b', 'rw_mu',
                'rw_w0', 'rw_w_up', 'rw_a0', 'rw_a_up', 'rw_g_up', 'rw_k_k', 'rw_k_a', 'rw_r_k', 'rw_gn_g',
                'rw_gn_b', 'w_out', 'ln2_g', 'ln2_b', 'ffn2_w_gu', 'ffn2_w_down', 'ln3_g', 'ln3_b',
                'ple_w_proj', 'ple_w_gate', 'ple_b_gate']
WSHAPES = {'ffn1_w_gu': [D, 2 * DFF], 'ffn1_w_down': [DFF, D], 'ln1_g': [1, D], 'ln1_b': [1, D], 'w_in': [D, INC],
           'ret_gn_g': [1, 512], 'ret_gn_b': [1, 512], 'rw_mu': [1, RWC], 'rw_w0': [1, 512], 'rw_w_up': [64, 512],
           'rw_a0': [1, 512], 'rw_a_up': [64, 512], 'rw_g_up': [160, 512], 'rw_k_k': [1, 512], 'rw_k_a': [1, 512],
           'rw_r_k': [1, 512], 'rw_gn_g': [1, 512], 'rw_gn_b': [1, 512], 'w_out': [D, D], 'ln2_g': [1, D],
           'ln2_b': [1, D], 'ffn2_w_gu': [D, 2 * DFF], 'ffn2_w_down': [DFF, D], 'ln3_g': [1, D], 'ln3_b': [1, D],
           'ple_w_proj': [256, D], 'ple_w_gate': [D, D], 'ple_b_gate': [1, D]}


def build_program(plan=None, dbg=False):
    nc = bass.Bass("TRN2", target_bir_lowering=False)
    dr = {}
    dr['xs'] = nc.dram_tensor("xs", [2 * S_HALF, D], F32, kind="ExternalInput").ap()
    dr['p'] = nc.dram_tensor("p", [S_HALF, 256], F32, kind="ExternalInput").ap()
    dr['hmask'] = nc.dram_tensor("hmask", [128, 2], F32, kind="ExternalInput").ap()
    dr['cos'] = nc.dram_tensor("cos", [128, 1024], F32, kind="ExternalInput").ap()
    dr['sin'] = nc.dram_tensor("sin", [128, 1024], F32, kind="ExternalInput").ap()
    dr['c32'] = nc.dram_tensor("c32", [128, C32_N], F32, kind="ExternalInput").ap()
    dr['cb'] = nc.dram_tensor("cb", [128, CB_N], F32, kind="ExternalInput").ap()
    dr['c32r'] = nc.dram_tensor("c32r", [128, C32R_N], F32, kind="ExternalInput").ap()
    for n in WEIGHT_NAMES:
        dr[n] = nc.dram_tensor(n, WSHAPES[n], F32, kind="ExternalInput").ap()
    out = nc.dram_tensor("out", [S_HALF, D], F32, kind="ExternalOutput").ap()
    skind = "ExternalOutput" if dbg else "Internal"
    x1s = nc.dram_tensor("x1s", [S_HALF, D], F32, kind=skind).ap()
    x2s = nc.dram_tensor("x2s", [S_HALF, D], F32, kind=skind).ap()
    x3s = nc.dram_tensor("x3s", [S_HALF, D], F32, kind=skind).ap()
    wab_s = nc.dram_tensor("wab_s", [128, 2 * 8 * RWC], BF16, kind="Internal").ap()
    if plan is None:
        plan = ['ffn1_0', 'mix_0', 'ffn1_1', 'mix_1', 'wout', 'ffn2', 'ple']

    S = Sched()
    dumped = {}

    def dump(name, ap, keys):
        if not dbg or name in dumped:
            return
        shp = list(ap.shape)
        d_ = nc.dram_tensor("dbg_" + name, shp, F32, kind="ExternalOutput").ap()
        dumped[name] = d_
        S.dma('pool', lambda e: e.dma_start(out=d_, in_=ap), R=list(keys), W=['dbg_' + name])
    ARENA_W = int(os.environ.get('ARENA_W', '53200'))
    from contextlib import ExitStack
    es = ExitStack()
    arena = es.enter_context(nc.sbuf_tensor("arena", [128, ARENA_W], F32))
    psf = [es.enter_context(nc.psum_tensor("ps%d" % i, [128, 512], F32)) for i in range(8)]

    class Alloc:
        def __init__(self):
            self.p = 0
            self.marks = []

        def f32(self, n, parts=(0, 128)):
            if os.environ.get('DRY'):
                self.p += n
                self.hi = max(getattr(self, 'hi', 0), self.p)
                return arena[parts[0]:parts[1], 0:n]
            a = arena[parts[0]:parts[1], self.p:self.p + n]
            self.p += n
            self.hi = max(getattr(self, 'hi', 0), self.p)
            assert self.p <= ARENA_W, ("arena overflow", self.p)
            return a

        def bf(self, n, parts=(0, 128)):
            w = (n + 1) // 2
            if os.environ.get('DRY'):
                self.p += w
                self.hi = max(getattr(self, 'hi', 0), self.p)
                return arena[parts[0]:parts[1], 0:w].bitcast(BF16)
            a = arena[parts[0]:parts[1], self.p:self.p + w].bitcast(BF16)
            self.p += w
            self.hi = max(getattr(self, 'hi', 0), self.p)
            assert self.p <= ARENA_W, ("arena overflow", self.p)
            return a

        def mark(self):
            self.marks.append(self.p)

        def release(self):
            self.p = self.marks.pop()
            S.barrier()

    A = Alloc()
    bankctr = [0]
    bankgen = [0] * 8

    class Bk(int):
        pass

    def bank():
        b = Bk(bankctr[0] % 8)
        bankctr[0] += 1
        bankgen[int(b)] = bankctr[0]
        b.gen = bankctr[0]
        return b

    def PS(b):
        return psf[int(b)][:, :]

    def PSB(b):
        return psf[int(b)][:, :].bitcast(BF16)

    def PK(b):
        assert bankgen[int(b)] == b.gen, "stale PSUM bank use"
        return 'ps%d' % int(b)

    c32 = A.f32(C32_N)
    cbt = A.bf(CB_N)
    S.dma('sp', lambda e: e.dma_start(out=c32, in_=dr['c32']), R=[], W=['c32'])
    S.dma('pool', lambda e: e.dma_start(out=cbt, in_=dr['cb']), R=[], W=['cb'])

    def c32v(name, parts=(0, 128)):
        o, n = C32_OFF[name]
        return c32[parts[0]:parts[1], o:o + n]

    def cbv(name):
        o, n = CB_OFF[name]
        return cbt[:, o:o + n]

    ident = cbv('ident')
    hmask = A.f32(2)
    S.dma('sp', lambda e: e.dma_start(out=hmask, in_=dr['hmask']), R=[], W=['hmask'])
    ptile = {}
    LN = {}
    retS = A.f32(256)
    retSb = A.bf(256)
    rwA = A.f32(256)
    rwAb = A.bf(256)
    x1T = A.bf(8 * (S_HALF + 8))
    x1Tv = x1T.rearrange("p (c t) -> p c t", c=8)
    XO = 8
    mixT = A.bf(8 * S_HALF)
    mixTv = mixT.rearrange("p (c t) -> p c t", c=8)
    x2Tv = x1Tv

    def load_ln(gn, bn):
        lng = A.f32(1024)
        lnb = A.f32(1024)
        LN['g'] = lng
        LN['b'] = lnb
        S.dma('sp', lambda e: e.dma_start(out=lng, in_=dr[gn].partition_broadcast(128)), R=[], W=['lng'])
        S.dma('sp', lambda e: e.dma_start(out=lnb, in_=dr[bn].partition_broadcast(128)), R=[], W=['lnb'])

    def load_ptiles(names):
        for n in names:
            t = A.f32(512)
            ptile[n] = t
            S.dma('sp', (lambda t, n: lambda e: e.dma_start(out=t, in_=dr[n].partition_broadcast(128)))(t, n), R=[], W=['pt_' + n])

    def transpose_to(src_bf, src_keys, n_blocks, dst_view, dst_keys, evac_eng='act'):
        b = bank()
        pk = PK(b)
        for i in range(n_blocks):
            S.op('pe', (lambda i, b: lambda e: e.transpose(out=PSB(b)[:, i * 128:(i + 1) * 128],
                                                         in_=src_bf[:, i * 128:(i + 1) * 128], identity=ident))(i, b),
                 R=list(src_keys) + ['cb'], W=[pk], inc=(i == n_blocks - 1))
        src = PSB(b)[:, 0:n_blocks * 128].rearrange("p (c t) -> p c t", c=n_blocks)
        if evac_eng == 'act':
            S.op('act', lambda e: e.activation(out=dst_view, in_=src, func=AF.Copy), R=[pk], W=list(dst_keys))
        else:
            S.op(evac_eng, lambda e: e.tensor_copy(out=dst_view, in_=src), R=[pk], W=list(dst_keys))

    def layer_norm_group(tiles, eps, junk, st):
        n_ = len(tiles)
        sl = lambda i, c: st[:, i * 8 + c:i * 8 + c + 1]
        k = lambda i, c: 'lnst%d_%d' % (i, c)
        for i, (y, yk) in enumerate(tiles):
            S.op('act', (lambda i, y: lambda e: e.activation(out=junk, in_=y, func=AF.Square, accum_out=sl(i, 0)))(i, y), R=[yk], W=['lnjunk', k(i, 0)])
            S.op('act', (lambda i, y: lambda e: e.activation(out=junk, in_=y, func=AF.Identity, accum_out=sl(i, 1)))(i, y), R=[yk], W=['lnjunk', k(i, 1)])
        for i in range(n_):
            S.op('dve', (lambda i: lambda e: e.tensor_scalar(out=sl(i, 2), in0=sl(i, 1), scalar1=1.0 / 1024, scalar2=None, op0=ALU.mult))(i), R=[k(i, 1)], W=[k(i, 2)])
        for i in range(n_):
            S.op('dve', (lambda i: lambda e: e.tensor_tensor(out=sl(i, 3), in0=sl(i, 2), in1=sl(i, 2), op=ALU.mult))(i), R=[k(i, 2)], W=[k(i, 3)])
        for i in range(n_):
            S.op('dve', (lambda i: lambda e: e.scalar_tensor_tensor(out=sl(i, 4), in0=sl(i, 0), scalar=1.0 / 1024, in1=sl(i, 3), op0=ALU.mult, op1=ALU.subtract))(i), R=[k(i, 0), k(i, 3)], W=[k(i, 4)])
        for i in range(n_):
            S.op('dve', (lambda i: lambda e: e.tensor_scalar(out=sl(i, 4), in0=sl(i, 4), scalar1=float(eps), scalar2=None, op0=ALU.add))(i), R=[k(i, 4)], W=[k(i, 4)])
        for i in range(n_):
            S.op('act', (lambda i: lambda e: e.activation(out=sl(i, 5), in_=sl(i, 4), func=AF.Sqrt))(i), R=[k(i, 4)], W=[k(i, 5)])
        for i in range(n_):
            S.op('dve', (lambda i: lambda e: e.reciprocal(out=sl(i, 6), in_=sl(i, 5)))(i), R=[k(i, 5)], W=[k(i, 6)])
        for i in range(n_):
            S.op('dve', (lambda i: lambda e: e.scalar_tensor_tensor(out=sl(i, 7), in0=sl(i, 2), scalar=-1.0, in1=sl(i, 6), op0=ALU.mult, op1=ALU.mult))(i), R=[k(i, 2), k(i, 6)], W=[k(i, 7)])
        for i, (y, yk) in enumerate(tiles):
            S.op('act', (lambda i, y: lambda e: e.activation(out=y, in_=y, func=AF.Identity, scale=sl(i, 6), bias=sl(i, 7)))(i, y), R=[yk, k(i, 6), k(i, 7)], W=[yk])
        for i, (y, yk) in enumerate(tiles):
            S.op('dve', (lambda y: lambda e: e.tensor_tensor(out=y, in0=y, in1=LN['g'], op=ALU.mult))(y), R=[yk, 'lng'], W=[yk])
            S.op('dve', (lambda y: lambda e: e.tensor_tensor(out=y, in0=y, in1=LN['b'], op=ALU.add))(y), R=[yk, 'lnb'], W=[yk])

    def layer_norm_tile(y, ykey, eps, outs, tmp_sq, st):
        S.op('act', lambda e: e.activation(out=tmp_sq, in_=y, func=AF.Square), R=[ykey], W=['lnsq'])
        S.op('dve', lambda e: e.reduce_sum(out=st[:, 0:1], in_=tmp_sq, axis=AX.X), R=['lnsq'], W=['lnst0'])
        S.op('dve', lambda e: e.reduce_sum(out=st[:, 1:2], in_=y, axis=AX.X), R=[ykey], W=['lnst1'])
        S.op('dve', lambda e: e.tensor_scalar(out=st[:, 2:3], in0=st[:, 1:2], scalar1=1.0 / 1024, scalar2=None, op0=ALU.mult), R=['lnst1'], W=['lnst2'])
        S.op('dve', lambda e: e.tensor_tensor(out=st[:, 3:4], in0=st[:, 2:3], in1=st[:, 2:3], op=ALU.mult), R=['lnst2'], W=['lnst3'])
        S.op('dve', lambda e: e.scalar_tensor_tensor(out=st[:, 4:5], in0=st[:, 0:1], scalar=1.0 / 1024, in1=st[:, 3:4], op0=ALU.mult, op1=ALU.subtract), R=['lnst0', 'lnst3'], W=['lnst4'])
        S.op('dve', lambda e: e.tensor_scalar(out=st[:, 4:5], in0=st[:, 4:5], scalar1=float(eps), scalar2=None, op0=ALU.add), R=['lnst4'], W=['lnst4'])
        S.op('act', lambda e: e.activation(out=st[:, 5:6], in_=st[:, 4:5], func=AF.Sqrt), R=['lnst4'], W=['lnst5'])
        S.op('dve', lambda e: e.reciprocal(out=st[:, 6:7], in_=st[:, 5:6]), R=['lnst5'], W=['lnst6'])
        S.op('dve', lambda e: e.scalar_tensor_tensor(out=st[:, 7:8], in0=st[:, 2:3], scalar=-1.0, in1=st[:, 6:7], op0=ALU.mult, op1=ALU.mult), R=['lnst2', 'lnst6'], W=['lnst7'])
        S.op('act', lambda e: e.activation(out=y, in_=y, func=AF.Identity, scale=st[:, 6:7], bias=st[:, 7:8]), R=[ykey, 'lnst6', 'lnst7'], W=[ykey])
        S.op('dve', lambda e: e.tensor_tensor(out=y, in0=y, in1=LN['g'], op=ALU.mult), R=[ykey, 'lng'], W=[ykey])
        S.op('dve', lambda e: e.tensor_tensor(out=y, in0=y, in1=LN['b'], op=ALU.add), R=[ykey, 'lnb'], W=[ykey])

    def ffn(xTv_src, xoff, xkey_fn, ntok, wgu, wdown, res_dram, eps, consumer, tagp):
        A.mark()
        NB = 1024
        hhT = A.bf(NJ * NB)
        hhv = hhT.rearrange("p (j t) -> p j t", j=NJ)
        wg = [A.bf(8 * 512) for _ in range(2)]
        wd = [A.bf(1024) for _ in range(6)]
        sg = [A.f32(512) for _ in range(2)]
        ybuf = [A.f32(1024) for _ in range(8)]
        tmp_sq = A.f32(1024)
        st = A.f32(32)
        wgu_v = wgu.rearrange("(c p) f -> p c f", p=128)
        deferred = []

        def run_deferred():
            for f_ in deferred:
                f_()
            del deferred[:]
        wi = 0
        di = 0
        for blk in range(ntok // NB):
            t0 = blk * NB
            for jg in range(NJ // 2):
                w = wg[wi % 2]
                wk = 'wg%d' % (wi % 2)
                wv = w.rearrange("p (c f) -> p c f", c=8)
                wi += 1
                S.dma('pool', (lambda wv, jg: lambda e: e.dma_start(out=wv[:, :, 0:256], in_=wgu_v[:, :, jg * 256:(jg + 1) * 256]))(wv, jg), R=[], W=[wk + 'g'])
                S.dma('pool', (lambda wv, jg: lambda e: e.dma_start(out=wv[:, :, 256:512], in_=wgu_v[:, :, DFF + jg * 256:DFF + (jg + 1) * 256]))(wv, jg), R=[], W=[wk + 'u'])
                for jj in range(2):
                    j = jg * 2 + jj
                    for sb in range(NB // 512):
                        bg = bank()
                        bu = bank()
                        c0 = xoff + t0 + sb * 512
                        xk = xkey_fn(t0 + sb * 512, 512)
                        for dc in range(8):
                            S.op('pe', (lambda wv, dc, jj, bg, c0: lambda e: e.matmul(PS(bg), lhsT=wv[:, dc, jj * 128:(jj + 1) * 128], rhs=xTv_src[:, dc, c0:c0 + 512], start=(dc == 0), stop=(dc == 7)))(wv, dc, jj, bg, c0),
                                 R=[wk + 'g'] + xk, W=[PK(bg)], inc=(dc == 7))
                        for dc in range(8):
                            S.op('pe', (lambda wv, dc, jj, bu, c0: lambda e: e.matmul(PS(bu), lhsT=wv[:, dc, 256 + jj * 128:256 + (jj + 1) * 128], rhs=xTv_src[:, dc, c0:c0 + 512], start=(dc == 0), stop=(dc == 7)))(wv, dc, jj, bu, c0),
                                 R=[wk + 'u'] + xk, W=[PK(bu)], inc=(dc == 7))
                        sgt = sg[(j * 2 + sb) % 2]
                        sgk = 'sg%d' % ((j * 2 + sb) % 2)
                        S.op('act', (lambda sgt, bg: lambda e: e.activation(out=sgt, in_=PS(bg), func=AF.Silu))(sgt, bg), R=[PK(bg)], W=[sgk])
                        S.op('dve', (lambda sgt, bu, j, sb: lambda e: e.tensor_tensor(out=hhv[:, j, sb * 512:(sb + 1) * 512], in0=sgt, in1=PS(bu), op=ALU.mult))(sgt, bu, j, sb),
                             R=[sgk, PK(bu)], W=['hh%d_%d' % (j, sb)])
            if blk == 0:
                dump(tagp + '_hh0', hhv[:, 0, :], ['hh0_0', 'hh0_1'])
                dump(tagp + '_hh21', hhv[:, 21, :], ['hh21_0', 'hh21_1'])
                dump(tagp + '_xT0', xTv_src[:, 0, xoff:xoff + 512], xkey_fn(0, 512))
            if os.environ.get('FFN_STOP') == 'up':
                continue
            for rnd in range(NB // 512):
                banks = [[bank(), bank()] for _ in range(4)]
                for j in range(NJ):
                    w = wd[di % 6]
                    wk = 'wd%d' % (di % 6)
                    di += 1
                    S.dma('pool', (lambda w, j: lambda e: e.dma_start(out=w, in_=wdown[j * 128:(j + 1) * 128, :]))(w, j), R=[], W=[wk])
                    for tt in range(4):
                        for dh in range(2):
                            b = banks[tt][dh]
                            S.op('pe', (lambda w, j, tt, dh, b, rnd: lambda e: e.matmul(PS(b), lhsT=hhv[:, j, rnd * 512 + tt * 128:rnd * 512 + (tt + 1) * 128], rhs=w[:, dh * 512:(dh + 1) * 512], start=(j == 0), stop=(j == NJ - 1)))(w, j, tt, dh, b, rnd),
                                 R=[wk, 'hh%d_%d' % (j, rnd)], W=[PK(b)], inc=(j == NJ - 1 or (tt == 3 and dh == 1)))
                cur = []
                for tt in range(4):
                    ti = (t0 + rnd * 512) // 128 + tt
                    y = ybuf[ti % 8]
                    yk = 'y%d' % (ti % 8)
                    S.dma('sp', (lambda y, ti: lambda e: e.dma_start(out=y, in_=res_dram[ti * 128:(ti + 1) * 128, :]))(y, ti), R=[], W=[yk])
                    for dh in range(2):
                        b = banks[tt][dh]
                        S.op('dve', (lambda y, b, dh: lambda e: e.scalar_tensor_tensor(out=y[:, dh * 512:(dh + 1) * 512], in0=PS(b), scalar=0.5 / ALPHA, in1=y[:, dh * 512:(dh + 1) * 512], op0=ALU.mult, op1=ALU.add))(y, b, dh),
                             R=[PK(b), yk], W=[yk])
                    cur.append((ti, y, yk))
                run_deferred()
                layer_norm_group([(y, yk) for (ti, y, yk) in cur], eps, tmp_sq, st)
                for (ti, y, yk) in cur:
                    later = consumer(ti, y, yk)
                    if later is not None:
                        deferred.append(later)
        run_deferred()
        A.release()

    def prep_xT(src_dram, ntok, dstv, doff, keyfn):
        A.mark()
        xb = [A.bf(1024) for _ in range(2)]
        for ti in range(ntok // 128):
            b = xb[ti % 2]
            bk = 'xb%d' % (ti % 2)
            S.dma('pool', (lambda b, ti: lambda e: e.dma_start(out=b, in_=src_dram[ti * 128:(ti + 1) * 128, :]))(b, ti), R=[], W=[bk])
            transpose_to(b, [bk], 8, dstv[:, :, doff + ti * 128:doff + (ti + 1) * 128], keyfn(ti * 128, 128))
        A.release()

    def mixer(half, full):
        A.mark()
        w_in = dr['w_in'].rearrange("(c p) f -> p c f", p=128)
        A.mark()
        wret = A.bf(8 * RETC)
        wretv = wret.rearrange("p (c f) -> p c f", c=8)
        c32r = A.f32(C32R_N)
        S.dma('sp', lambda e: e.dma_start(out=c32r, in_=dr['c32r']), R=[], W=['c32r'])

        def c32rv(name):
            o, n = C32R_OFF[name]
            return c32r[:, o:o + n]
        cos_t = A.f32(512)
        sin_t = A.f32(512)
        S.dma('sp', lambda e: e.dma_start(out=cos_t, in_=dr['cos'][:, half * 512:(half + 1) * 512]), R=[], W=['cos'])
        S.dma('sp', lambda e: e.dma_start(out=sin_t, in_=dr['sin'][:, half * 512:(half + 1) * 512]), R=[], W=['sin'])
        load_ptiles(['ret_gn_g', 'ret_gn_b'])
        for q4 in range(4):
            S.dma('pool', (lambda q4: lambda e: e.dma_start(out=wretv[:, :, q4 * 512:(q4 + 1) * 512], in_=w_in[:, :, q4 * 512:(q4 + 1) * 512]))(q4), R=[], W=['wret%d' % q4])
        qr = A.bf(512)
        kr = A.bf(512)
        vb = A.bf(512)
        vdb = A.bf(512)
        sgg = A.f32(512)
        t1 = A.f32(256)
        t2 = A.f32(256)
        t3 = A.f32(256)
        t4 = A.f32(256)
        qT = A.bf(512)
        qsTm = A.bf(1024)
        kTm = A.bf(1024)
        S.op('pool', lambda e: e.memset(qsTm, 0.0), R=[], W=['qsTm'])
        S.op('pool', lambda e: e.memset(kTm, 0.0), R=[], W=['kTm'])
        scm = A.bf(1024)
        o_sb = A.f32(512)
        o_sq = A.f32(512)
        gst = A.f32(64)
        mixr = A.bf(512)
        ret_deferred = []
        kdec = c32v('kdec')
        for ti in range(NT_HALF):
            c0 = XO + ti * 128
            gt = ti
            xk = ['x1T_%d' % ti]
            cosv = cos_t[:, gt * 32:(gt + 1) * 32].unsqueeze(1).broadcast_to([128, 8, 32])
            sinv = sin_t[:, gt * 32:(gt + 1) * 32].unsqueeze(1).broadcast_to([128, 8, 32])
            pb = {}
            which = ['q', 'k', 'v', 'g'] if full else ['k', 'v']
            RSUB = os.environ.get('RET_SUB', '')
            if RSUB == 'none':
                continue
            for nm in which:
                q4 = ['q', 'k', 'v', 'g'].index(nm)
                b = bank()
                pb[nm] = b
                for dc in range(8):
                    S.op('pe', (lambda dc, b, q4, c0: lambda e: e.matmul(PS(b), lhsT=x1Tv[:, dc, c0:c0 + 128], rhs=wretv[:, dc, q4 * 512:(q4 + 1) * 512], start=(dc == 0), stop=(dc == 7)))(dc, b, q4, c0),
                         R=xk + ['wret%d' % q4], W=[PK(b)], inc=(dc == 7))

            def rotary(b, dst, dkey):
                src = PS(b).rearrange("p (h d) -> p h d", h=8)
                dv = dst.rearrange("p (h d) -> p h d", h=8)
                a1 = t1.rearrange("p (h d) -> p h d", h=8)
                a2 = t2.rearrange("p (h d) -> p h d", h=8)
                a3 = t3.rearrange("p (h d) -> p h d", h=8)
                a4 = t4.rearrange("p (h d) -> p h d", h=8)
                pk = PK(b)
                S.op('dve', lambda e: e.tensor_tensor(out=a1, in0=src[:, :, 0:32], in1=cosv, op=ALU.mult), R=[pk, 'cos'], W=['rt1'])
                S.op('dve', lambda e: e.tensor_tensor(out=a2, in0=src[:, :, 32:64], in1=sinv, op=ALU.mult), R=[pk, 'sin'], W=['rt2'])
                S.op('dve', lambda e: e.tensor_tensor(out=a3, in0=src[:, :, 0:32], in1=sinv, op=ALU.mult), R=[pk, 'sin'], W=['rt3'])
                S.op('dve', lambda e: e.tensor_tensor(out=a4, in0=src[:, :, 32:64], in1=cosv, op=ALU.mult), R=[pk, 'cos'], W=['rt4'])
                S.op('pool', lambda e: e.tensor_tensor(out=dv[:, :, 0:32], in0=a1, in1=a2, op=ALU.subtract), R=['rt1', 'rt2'], W=[dkey])
                S.op('pool', lambda e: e.tensor_tensor(out=dv[:, :, 32:64], in0=a3, in1=a4, op=ALU.add), R=['rt3', 'rt4'], W=[dkey])

            for f_ in ret_deferred:
                f_()
            del ret_deferred[:]
            if RSUB == 'proj':
                continue
            rotary(pb['k'], kr, 'kr')
            if full:
                rotary(pb['q'], qr, 'qr')
            if RSUB == 'rot':
                continue
            bv = pb['v']
            S.op('act', (lambda bv: lambda e: e.activation(out=vb, in_=PS(bv), func=AF.Copy))(bv), R=[PK(bv)], W=['vb'])
            if RSUB == 'vb':
                continue
            S.op('dve', (lambda bv: lambda e: e.tensor_tensor(out=vdb, in0=PS(bv), in1=c32rv('kdecf'), op=ALU.mult))(bv), R=[PK(bv), 'c32r', 'vb'], W=['vdb'])
            RS = int(os.environ.get('RET_STOP', '99'))
            if RS <= 1:
                continue
            if full:
                bgg = pb['g']
                S.op('act', (lambda bgg: lambda e: e.activation(out=sgg, in_=PS(bgg), func=AF.Silu))(bgg), R=[PK(bgg)], W=['sgg'])
                b = bank()
                pk = PK(b)
                for i in range(4):
                    S.op('pe', (lambda i, b: lambda e: e.transpose(out=PSB(b)[:, i * 128:(i + 1) * 128], in_=qr[:, i * 128:(i + 1) * 128], identity=ident))(i, b), R=['qr', 'cb'], W=[pk], inc=False)
                for i in range(4):
                    S.op('pe', (lambda i, b: lambda e: e.transpose(out=PSB(b)[:, 512 + i * 128:512 + (i + 1) * 128], in_=kr[:, i * 128:(i + 1) * 128], identity=ident))(i, b), R=['kr', 'cb'], W=[pk], inc=(i == 3))
                S.op('act', (lambda b: lambda e: e.activation(out=qT, in_=PSB(b)[:, 0:512], func=AF.Copy))(b), R=[pk], W=['qT'])
                for hp in range(2):
                    rs_ = slice(hp * 64, (hp + 1) * 64)
                    S.op('dve', (lambda b, hp, rs_: lambda e: e.tensor_tensor(out=qsTm[rs_, :].rearrange("p (a c t) -> p a c t", a=4, c=2)[:, :, hp, :],
                                                                             in0=PSB(b)[rs_, 0:512].rearrange("p (a t) -> p a t", a=4),
                                                                             in1=c32rv('qdec')[rs_, :].rearrange("p (a t) -> p a t", a=4), op=ALU.mult))(b, hp, rs_), R=[pk, 'c32r'], W=['qsTm'])
                    S.op('act', (lambda b, hp, rs_: lambda e: e.activation(out=kTm[rs_, :].rearrange("p (a c t) -> p a c t", a=4, c=2)[:, :, hp, :],
                                                                          in_=PSB(b)[rs_, 512:1024].rearrange("p (a t) -> p a t", a=4), func=AF.Copy))(b, hp, rs_), R=[pk], W=['kTm'])
                if RS <= 2:
                    continue
                sb_ = [bank(), bank()]
                for h in range(8):
                    pr, hp = h // 2, h % 2
                    b = sb_[h // 4]
                    S.op('pe', (lambda h, pr, hp, b: lambda e: e.matmul(PS(b)[:, (h % 4) * 128:(h % 4 + 1) * 128], lhsT=kTm[:, h * 128:(h + 1) * 128],
                                                                       rhs=qT[:, pr * 128:(pr + 1) * 128], start=True, stop=True))(h, pr, hp, b),
                         R=['kTm', 'qT'], W=[PK(b)], inc=(h % 4 == 3))
                if RS == 24:
                    continue
                for hb in range(2):
                    b = sb_[hb]
                    S.op('dve', (lambda hb, b: lambda e: e.tensor_tensor(out=scm[:, hb * 512:(hb + 1) * 512], in0=PS(b), in1=c32rv('dmask')[:, hb * 512:(hb + 1) * 512], op=ALU.mult))(hb, b),
                         R=[PK(b), 'c32r'], W=['scm%d' % hb])
                if RS == 25:
                    continue
                bo = bank()
                for h in range(8):
                    pr, hp = h // 2, h % 2
                    S.op('pe', (lambda h, bo: lambda e: e.matmul(PS(bo)[:, h * 64:(h + 1) * 64], lhsT=scm[:, h * 128:(h + 1) * 128], rhs=vb[:, h * 64:(h + 1) * 64], start=True, stop=(RS == 26)))(h, bo),
                         R=['scm%d' % (h // 4), 'vb'], W=[PK(bo)], inc=(RS == 26 and h == 7))
                    if RS == 26:
                        continue
                    S.op('pe', (lambda h, pr, hp, bo: lambda e: e.matmul(PS(bo)[:, h * 64:(h + 1) * 64], lhsT=qsTm[:, h * 128:(h + 1) * 128],
                                                                        rhs=retSb[:, pr * 64:(pr + 1) * 64], start=False, stop=True))(h, pr, hp, bo),
                         R=['qsTm', 'retSb'], W=[PK(bo)], inc=(h == 7))
                S.op('act', (lambda bo: lambda e: e.activation(out=o_sb, in_=PS(bo), func=AF.Copy))(bo), R=[PK(bo)], W=['o_sb'])
            if RS <= 3:
                continue
            bk_ = bank()
            for pr in range(4):
                S.op('pe', (lambda pr, bk_: lambda e: e.matmul(PS(bk_)[:, pr * 128:(pr + 1) * 128], lhsT=kr[:, pr * 128:(pr + 1) * 128], rhs=vdb[:, pr * 128:(pr + 1) * 128], start=True, stop=True))(pr, bk_),
                     R=['kr', 'vdb'], W=[PK(bk_)], inc=(pr == 3))
            rs3 = retS.rearrange("p (a e) -> p a e", a=4)
            S.op('pool', lambda e: e.tensor_tensor(out=retS, in0=retS, in1=c32rv('gam128f'), op=ALU.mult), R=['retS', 'c32r', 'retSb'], W=['retS'])
            for hp in range(2):
                S.op('dve', (lambda hp, bk_: lambda e: e.tensor_tensor(out=retS[hp * 64:(hp + 1) * 64, :].rearrange("p (a e) -> p a e", a=4), in0=retS[hp * 64:(hp + 1) * 64, :].rearrange("p (a e) -> p a e", a=4),
                                                                    in1=PS(bk_)[hp * 64:(hp + 1) * 64, :].rearrange("p (a f) -> p a f", a=4)[:, :, hp * 64:(hp + 1) * 64], op=ALU.add))(hp, bk_),
                     R=[PK(bk_), 'retS'], W=['retS'])
            S.op('act', lambda e: e.activation(out=retSb, in_=retS, func=AF.Copy), R=['retS'], W=['retSb'])
            if full and ti == 0:
                dump('retS1', retS, ['retS'])
            if full and ti == 1:
                dump('retS2', retS, ['retS'])
                dump('kr1', kr, ['kr'])
            if RS <= 4:
                continue
            if full:
                group_norm_out(o_sb, 'o_sb', o_sq, 'o_sq', gst, 1e-5, ptile['ret_gn_g'], ptile['ret_gn_b'], 'pt_ret_gn_g', 'pt_ret_gn_b')
                S.op('dve', lambda e: e.tensor_tensor(out=mixr, in0=o_sb, in1=sgg, op=ALU.mult), R=['o_sb', 'sgg'], W=['mixr'])
                ret_deferred.append((lambda ti: lambda: transpose_to(mixr, ['mixr'], 4, mixTv[:, 0:4, ti * 128:(ti + 1) * 128], ['mixT_r%d' % ti]))(ti))
        for f_ in ret_deferred:
            f_()
        A.release()
        S.barrier()
        if os.environ.get('MIX_STOP') != 'ret':
            rwkv(half, full)
        if full:
            for c_ in range(8):
                dump('mixT%d' % c_, mixTv[:, c_, :], [])
            dump('retS', retS, [])
            dump('rwA', rwA, [])
        A.release()

    def group_norm_out(o, okey, sq, sqk, gst, eps, gt, bt, gk, bk):
        o3 = o.rearrange("p (h d) -> p h d", h=8)
        S.op('act', lambda e: e.activation(out=sq, in_=o, func=AF.Square), R=[okey], W=[sqk])
        S.op('dve', lambda e: e.tensor_reduce(out=gst[:, 0:8], in_=o3, axis=AX.X, op=ALU.add), R=[okey], W=['gn0'])
        S.op('dve', lambda e: e.tensor_reduce(out=gst[:, 8:16], in_=sq.rearrange("p (h d) -> p h d", h=8), axis=AX.X, op=ALU.add), R=[sqk], W=['gn1'])
        S.op('dve', lambda e: e.tensor_scalar(out=gst[:, 16:24], in0=gst[:, 0:8], scalar1=1.0 / 64, scalar2=None, op0=ALU.mult), R=['gn0'], W=['gn2'])
        S.op('dve', lambda e: e.tensor_tensor(out=gst[:, 24:32], in0=gst[:, 16:24], in1=gst[:, 16:24], op=ALU.mult), R=['gn2'], W=['gn3'])
        S.op('dve', lambda e: e.scalar_tensor_tensor(out=gst[:, 32:40], in0=gst[:, 8:16], scalar=1.0 / 64, in1=gst[:, 24:32], op0=ALU.mult, op1=ALU.subtract), R=['gn1', 'gn3'], W=['gn4'])
        S.op('dve', lambda e: e.tensor_scalar(out=gst[:, 32:40], in0=gst[:, 32:40], scalar1=float(eps), scalar2=None, op0=ALU.add), R=['gn4'], W=['gn4'])
        S.op('act', lambda e: e.activation(out=gst[:, 40:48], in_=gst[:, 32:40], func=AF.Sqrt), R=['gn4'], W=['gn5'])
        S.op('dve', lambda e: e.reciprocal(out=gst[:, 48:56], in_=gst[:, 40:48]), R=['gn5'], W=['gn6'])
        S.op('dve', lambda e: e.tensor_tensor(out=o3, in0=o3, in1=gst[:, 16:24].unsqueeze(2).broadcast_to([128, 8, 64]), op=ALU.subtract), R=[okey, 'gn2'], W=[okey])
        S.op('dve', lambda e: e.tensor_tensor(out=o3, in0=o3, in1=gst[:, 48:56].unsqueeze(2).broadcast_to([128, 8, 64]), op=ALU.mult), R=[okey, 'gn6'], W=[okey])
        S.op('pool', lambda e: e.tensor_tensor(out=o, in0=o, in1=gt, op=ALU.mult), R=[okey, gk], W=[okey])
        S.op('pool', lambda e: e.tensor_tensor(out=o, in0=o, in1=bt, op=ALU.add), R=[okey, bk], W=[okey])

    def rwkv(half, full):
        A.mark()
        w_in = dr['w_in'].rearrange("(c p) f -> p c f", p=128)
        load_ptiles(['rw_k_k', 'rw_k_a', 'rw_r_k', 'rw_gn_g', 'rw_gn_b'])
        Wa = A.bf(8 * RWC)
        Wb = A.bf(8 * RWC)
        Wav = Wa.rearrange("p (c f) -> p c f", c=8)
        Wbv = Wb.rearrange("p (c f) -> p c f", c=8)
        NW_ = 8 * RWC
        if half == 0 or 'mix_0' not in plan:
            A.mark()
            mu_t = A.f32(RWC)
            omu_t = A.f32(RWC)
            stg = [A.f32(RWC) for _ in range(2)]
            S.dma('sp', lambda e: e.dma_start(out=mu_t, in_=dr['rw_mu'].partition_broadcast(128)), R=[], W=['mu'])
            S.op('dve', lambda e: e.tensor_scalar(out=omu_t, in0=mu_t, scalar1=-1.0, scalar2=1.0, op0=ALU.mult, op1=ALU.add), R=['mu'], W=['omu'])
            for dc in range(8):
                sgb = stg[dc % 2]
                sk = 'stg%d' % (dc % 2)
                S.dma('sp', lambda e: e.dma_start(out=sgb, in_=dr['w_in'][dc * 128:(dc + 1) * 128, RETC:INC]), R=[], W=[sk])
                S.op('dve', lambda e: e.tensor_tensor(out=Wav[:, dc, :], in0=sgb, in1=omu_t, op=ALU.mult), R=[sk, 'omu'], W=['Wa'])
                S.op('pool', lambda e: e.tensor_tensor(out=Wbv[:, dc, :], in0=sgb, in1=mu_t, op=ALU.mult), R=[sk, 'mu'], W=['Wb'])
            for q_ in range(4):
                S.dma('sp', lambda e: e.dma_start(out=wab_s[:, q_ * (NW_ // 4):(q_ + 1) * (NW_ // 4)], in_=Wa[:, q_ * (NW_ // 4):(q_ + 1) * (NW_ // 4)]), R=['Wa'], W=['wabs_a%d' % q_], sk='wabs')
                S.dma('sp', lambda e: e.dma_start(out=wab_s[:, NW_ + q_ * (NW_ // 4):NW_ + (q_ + 1) * (NW_ // 4)], in_=Wb[:, q_ * (NW_ // 4):(q_ + 1) * (NW_ // 4)]), R=['Wb'], W=['wabs_b%d' % q_], sk='wabs')
            A.release()
        else:
            for q_ in range(4):
                S.dma('sp', lambda e: e.dma_start(out=Wa[:, q_ * (NW_ // 4):(q_ + 1) * (NW_ // 4)], in_=wab_s[:, q_ * (NW_ // 4):(q_ + 1) * (NW_ // 4)]), R=[], W=['Wa'], sk='wa_ld%d' % q_)
                S.dma('sp', lambda e: e.dma_start(out=Wb[:, q_ * (NW_ // 4):(q_ + 1) * (NW_ // 4)], in_=wab_s[:, NW_ + q_ * (NW_ // 4):NW_ + (q_ + 1) * (NW_ // 4)]), R=[], W=['Wb'], sk='wb_ld%d' % q_)
        wup = A.f32(512)
        aup = A.f32(512)
        gup1 = A.bf(512)
        gup2 = A.bf(512)
        w0r = A.f32(512)
        a0r = A.f32(512)
        for t_, k_ in ((wup, 'wup'), (aup, 'aup'), (gup2, 'gup2'), (w0r, 'w0r'), (a0r, 'a0r')):
            S.op('pool', (lambda t_: lambda e: e.memset(t_, 0.0))(t_), R=[], W=[k_])
        S.dma('sp', lambda e: e.dma_start(out=wup[0:64, :], in_=dr['rw_w_up']), R=[], W=['wup'])
        S.dma('sp', lambda e: e.dma_start(out=aup[64:128, :], in_=dr['rw_a_up']), R=[], W=['aup'])
        S.dma('pool', lambda e: e.dma_start(out=gup1, in_=dr['rw_g_up'][0:128, :]), R=[], W=['gup1'])
        S.dma('pool', lambda e: e.dma_start(out=gup2[96:128, :], in_=dr['rw_g_up'][128:160, :]), R=[], W=['gup2'])
        S.dma('sp', lambda e: e.dma_start(out=w0r[0:1, :], in_=dr['rw_w0']), R=[], W=['w0r'])
        S.dma('sp', lambda e: e.dma_start(out=a0r[0:1, :], in_=dr['rw_a0']), R=[], W=['a0r'])
        ones_r = c32v('ones')
        twT = A.f32(128)
        sgd1 = A.bf(128)
        sgd2 = A.bf(128)
        sw = A.f32(512)
        a_t = A.f32(512)
        gi_t = A.f32(512)
        kkr = A.f32(512)
        sq = A.f32(512)
        kp = A.f32(512)
        u1 = sq
        g_t = sw
        v32 = A.f32(512)
        st = A.f32(64)
        gst2 = A.f32(64)
        r32 = A.f32(512)
        k32 = A.f32(512)
        gp_t = k32
        gC = A.f32(4)
        rh = A.bf(512)
        kt = A.bf(512)
        bt_ = A.bf(512)
        kh = A.bf(512)
        vb = A.bf(512)
        XT = A.bf(4 * 256)
        XTv = XT.rearrange("p (a b t) -> p a b t", a=4, b=2)
        btT = A.bf(512)
        khTm = A.bf(1024)
        rhTm = A.bf(1024)
        btTm = A.bf(1024)
        ktTm = A.bf(1024)
        for t_, k_ in ((khTm, 'khTm'), (rhTm, 'rhTm'), (btTm, 'btTm'), (ktTm, 'ktTm')):
            S.op('pool', (lambda t_: lambda e: e.memset(t_, 0.0))(t_), R=[], W=[k_])
        LkT = A.bf(1024)
        MbT = A.bf(1024)
        MkT = A.bf(1024)
        Abuf = [[A.bf(256) for _ in range(2)] for _ in range(4)]
        BQbuf = [[A.bf(512) for _ in range(2)] for _ in range(4)]
        TT = A.bf(1024)
        Xb = A.bf(512)
        Ub = A.bf(512)
        mixw = A.bf(512)
        bsum = st[:, 56:64]
        rw_deferred = []

        for ti in range(NT_HALF):
            c0 = XO + ti * 128
            xk = ['x1T_%d' % ti] + (['x1T_%d' % (ti - 1)] if ti > 0 else ['x1T_prev'])
            def proj(nm):
                qi = ['r', 'k', 'v'].index(nm)
                b = bank()
                n = 0
                for (Wv, wkey, sh) in ((Wav, 'Wa', 0), (Wbv, 'Wb', 1)):
                    for dc in range(8):
                        S.op('pe', (lambda Wv, sh, dc, b, qi, n: lambda e: e.matmul(PS(b), lhsT=x1Tv[:, dc, c0 - sh:c0 - sh + 128], rhs=Wv[:, dc, qi * 512:(qi + 1) * 512], start=(n == 0), stop=(n == 15)))(Wv, sh, dc, b, qi, n),
                             R=xk + [wkey], W=[PK(b)], inc=(n == 15))
                        n += 1
                return b
            bl = bank()
            lora_groups = ((1536, 128), (1664, 128), (1696, 128)) if full else ((1536, 128),)
            for gi_, (cs, cn) in enumerate(lora_groups):
                n = 0
                for (Wv, wkey, sh) in ((Wav, 'Wa', 0), (Wbv, 'Wb', 1)):
                    for dc in range(8):
                        S.op('pe', (lambda Wv, sh, dc, cs, cn, gi_, n: lambda e: e.matmul(PS(bl)[0:cn, gi_ * 128:(gi_ + 1) * 128], lhsT=Wv[:, dc, cs:cs + cn], rhs=x1Tv[:, dc, c0 - sh:c0 - sh + 128], start=(n == 0), stop=(n == 15)))(Wv, sh, dc, cs, cn, gi_, n),
                             R=xk + [wkey], W=[PK(bl)], inc=(n == 15 and gi_ == len(lora_groups) - 1))
                        n += 1
            plk = PK(bl)
            S.op('act', lambda e: e.activation(out=twT[0:64, :], in_=PS(bl)[0:64, 0:128], func=AF.Tanh), R=[plk], W=['twT'])
            S.op('act', lambda e: e.activation(out=twT[64:128, :], in_=PS(bl)[64:128, 0:128], func=AF.Copy), R=[plk], W=['twT'])
            if full:
                S.op('act', lambda e: e.activation(out=sgd1, in_=PS(bl)[:, 128:256], func=AF.Sigmoid), R=[plk], W=['sgd1'])
                S.op('act', lambda e: e.activation(out=sgd2, in_=PS(bl)[:, 256:384], func=AF.Sigmoid), R=[plk], W=['sgd2'])
            bk2 = proj('k')
            kk_ = PK(bk2)
            S.op('act', lambda e: e.activation(out=k32, in_=PS(bk2), func=AF.Copy), R=[kk_], W=['k32'])
            bw = bank()
            S.op('pe', lambda e: e.matmul(PS(bw), lhsT=twT, rhs=wup, start=True, stop=False), R=['twT', 'wup'], W=[PK(bw)], inc=False)
            S.op('pe', lambda e: e.matmul(PS(bw), lhsT=ones_r, rhs=w0r, start=False, stop=True), R=['c32', 'w0r'], W=[PK(bw)])
            ba = bank()
            S.op('pe', lambda e: e.matmul(PS(ba), lhsT=twT, rhs=aup, start=True, stop=False), R=['twT', 'aup'], W=[PK(ba)], inc=False)
            S.op('pe', lambda e: e.matmul(PS(ba), lhsT=ones_r, rhs=a0r, start=False, stop=True), R=['c32', 'a0r'], W=[PK(ba)])
            S.op('act', lambda e: e.activation(out=sw, in_=PS(bw), func=AF.Sigmoid), R=[PK(bw)], W=['sw'])
            S.op('act', lambda e: e.activation(out=a_t, in_=PS(ba), func=AF.Sigmoid), R=[PK(ba)], W=['a_t'])
            bv = proj('v')
            vk_ = PK(bv)
            S.op('act', lambda e: e.activation(out=v32, in_=PS(bv), func=AF.Copy), R=[vk_], W=['v32'])
            S.op('dve', lambda e: e.tensor_copy(out=vb, in_=PS(bv)), R=[vk_], W=['vbw'])
            bc = bank()
            S.op('pe', lambda e: e.matmul(PS(bc), lhsT=c32v('tri_incl'), rhs=sw, start=True, stop=True), R=['c32', 'sw'], W=[PK(bc)])
            bx = bank()
            S.op('pe', lambda e: e.matmul(PS(bx), lhsT=c32v('tri_strict'), rhs=sw, start=True, stop=True), R=['c32', 'sw'], W=[PK(bx)])
            bgc = bank()
            for pr in range(4):
                S.op('pe', (lambda pr: lambda e: e.matmul(PS(bgc)[:, pr:pr + 1], lhsT=sw[:, pr * 128:(pr + 1) * 128], rhs=c32v('negcol'), start=True, stop=True))(pr), R=['sw', 'c32'], W=[PK(bgc)], inc=(pr == 3))
            if full:
                br = proj('r')
                rk_ = PK(br)
                S.op('act', lambda e: e.activation(out=r32, in_=PS(br), func=AF.Copy), R=[rk_], W=['r32'])
            for f_ in rw_deferred:
                f_()
            del rw_deferred[:]
            S.op('dve', lambda e: e.tensor_tensor(out=kkr, in0=k32, in1=ptile['rw_k_k'], op=ALU.mult), R=['k32', 'pt_rw_k_k'], W=['kkr'])
            S.op('act', lambda e: e.activation(out=sq, in_=kkr, func=AF.Square), R=['kkr'], W=['sq'])
            S.op('dve', lambda e: e.tensor_reduce(out=st[:, 0:8], in_=sq.rearrange("p (h d) -> p h d", h=8), axis=AX.X, op=ALU.add), R=['sq'], W=['st0'])
            S.op('act', lambda e: e.activation(out=st[:, 8:16], in_=st[:, 0:8], func=AF.Sqrt), R=['st0'], W=['st1'])
            S.op('dve', lambda e: e.tensor_scalar(out=st[:, 8:16], in0=st[:, 8:16], scalar1=1e-12, scalar2=None, op0=ALU.max), R=['st1'], W=['st1'])
            S.op('dve', lambda e: e.reciprocal(out=st[:, 16:24], in_=st[:, 8:16]), R=['st1'], W=['st2'])
            S.op('dve', lambda e: e.tensor_tensor(out=kkr.rearrange("p (h d) -> p h d", h=8), in0=kkr.rearrange("p (h d) -> p h d", h=8), in1=st[:, 16:24].unsqueeze(2).broadcast_to([128, 8, 64]), op=ALU.mult), R=['kkr', 'st2'], W=['kkr'])
            S.op('dve', lambda e: e.scalar_tensor_tensor(out=u1, in0=a_t, scalar=-1.0, in1=ptile['rw_k_a'], op0=ALU.add, op1=ALU.mult), R=['a_t', 'pt_rw_k_a'], W=['sq'])
            S.op('dve', lambda e: e.scalar_tensor_tensor(out=kp, in0=u1, scalar=1.0, in1=k32, op0=ALU.add, op1=ALU.mult), R=['sq', 'k32'], W=['kp'])
            if full:
                S.op('dve', lambda e: e.tensor_tensor(out=u1, in0=r32, in1=kp, op=ALU.mult), R=['r32', 'kp', 'sq'], W=['sq'])
                S.op('pool', lambda e: e.tensor_tensor(out=u1, in0=u1, in1=ptile['rw_r_k'], op=ALU.mult), R=['sq', 'pt_rw_r_k'], W=['sq'])
                S.op('dve', lambda e: e.tensor_reduce(out=bsum, in_=u1.rearrange("p (h d) -> p h d", h=8), axis=AX.X, op=ALU.add), R=['sq'], W=['bsum'])
            S.op('act', lambda e: e.activation(out=g_t, in_=PS(bc), func=AF.Exp), R=[PK(bc)], W=['sw'])
            S.op('act', lambda e: e.activation(out=gi_t, in_=PS(bc), func=AF.Exp, scale=-1.0), R=[PK(bc)], W=['gi_t'])
            S.op('act', lambda e: e.activation(out=gp_t, in_=PS(bx), func=AF.Exp), R=[PK(bx)], W=['k32'])
            S.op('act', lambda e: e.activation(out=gC, in_=PS(bgc)[:, 0:4], func=AF.Exp), R=[PK(bgc)], W=['gC'])
            if full:
                S.op('dve', lambda e: e.tensor_tensor(out=rh, in0=r32, in1=g_t, op=ALU.mult), R=['r32', 'sw'], W=['rh'])
            S.op('pool', lambda e: e.tensor_tensor(out=kt, in0=kp, in1=gi_t, op=ALU.mult), R=['kp', 'gi_t'], W=['kt'])
            S.op('pool', lambda e: e.tensor_tensor(out=sq, in0=kkr, in1=a_t, op=ALU.mult), R=['kkr', 'a_t', 'sq'], W=['sq'])
            S.op('pool', lambda e: e.tensor_tensor(out=bt_, in0=sq, in1=gi_t, op=ALU.mult), R=['sq', 'gi_t'], W=['bt'])
            S.op('pool', lambda e: e.tensor_tensor(out=kh, in0=kkr, in1=gp_t, op=ALU.mult), R=['kkr', 'k32'], W=['kh'])
            if full:
                S.op('dve', lambda e: e.tensor_tensor(out=gi_t.rearrange("p (h d) -> p h d", h=8), in0=v32.rearrange("p (h d) -> p h d", h=8), in1=bsum.unsqueeze(2).broadcast_to([128, 8, 64]), op=ALU.mult), R=['v32', 'bsum', 'gi_t'], W=['gi_t'])
            b1 = bank()
            for i in range(4):
                S.op('pe', (lambda i: lambda e: e.transpose(out=PSB(b1)[:, i * 256:i * 256 + 128], in_=kh[:, i * 128:(i + 1) * 128], identity=ident))(i), R=['kh', 'cb'], W=[PK(b1)], inc=(i == 3 and not full))
                if full:
                    S.op('pe', (lambda i: lambda e: e.transpose(out=PSB(b1)[:, i * 256 + 128:i * 256 + 256], in_=rh[:, i * 128:(i + 1) * 128], identity=ident))(i), R=['rh', 'cb'], W=[PK(b1)], inc=(i == 3))
            if full:
                S.op('act', lambda e: e.activation(out=XT, in_=PSB(b1), func=AF.Copy), R=[PK(b1)], W=['XT'])
            else:
                S.op('act', lambda e: e.activation(out=XT.rearrange("p (a c t) -> p a c t", a=4, c=2)[:, :, 0, :], in_=PSB(b1).rearrange("p (a c t) -> p a c t", a=4, c=2)[:, :, 0, :], func=AF.Copy), R=[PK(b1)], W=['XT'])
            for hp in range(2):
                rs_ = slice(hp * 64, (hp + 1) * 64)
                srcv = PSB(b1)[rs_, :].rearrange("p (a c t) -> p a c t", a=4, c=2)
                S.op('act', lambda e: e.activation(out=khTm[rs_, :].rearrange("p (a c t) -> p a c t", a=4, c=2)[:, :, hp, :], in_=srcv[:, :, 0, :], func=AF.Copy), R=[PK(b1)], W=['khTm'])
                if full:
                    S.op('dve', lambda e: e.tensor_copy(out=rhTm[rs_, :].rearrange("p (a c t) -> p a c t", a=4, c=2)[:, :, hp, :], in_=srcv[:, :, 1, :]), R=[PK(b1)], W=['rhTm'])
            b2 = bank()
            for i in range(4):
                S.op('pe', (lambda i: lambda e: e.transpose(out=PSB(b2)[:, i * 128:(i + 1) * 128], in_=bt_[:, i * 128:(i + 1) * 128], identity=ident))(i), R=['bt', 'cb'], W=[PK(b2)], inc=False)
            for i in range(4):
                S.op('pe', (lambda i: lambda e: e.transpose(out=PSB(b2)[:, 512 + i * 128:512 + (i + 1) * 128], in_=kt[:, i * 128:(i + 1) * 128], identity=ident))(i), R=['kt', 'cb'], W=[PK(b2)], inc=(i == 3))
            S.op('dve', lambda e: e.tensor_copy(out=btT, in_=PSB(b2)[:, 0:512]), R=[PK(b2)], W=['btT'])
            for hp in range(2):
                rs_ = slice(hp * 64, (hp + 1) * 64)
                S.op('act', lambda e: e.activation(out=btTm[rs_, :].rearrange("p (a c t) -> p a c t", a=4, c=2)[:, :, hp, :], in_=PSB(b2)[rs_, 0:512].rearrange("p (a t) -> p a t", a=4), func=AF.Copy), R=[PK(b2)], W=['btTm'])
                S.op('dve', lambda e: e.tensor_copy(out=ktTm[rs_, :].rearrange("p (a c t) -> p a c t", a=4, c=2)[:, :, hp, :], in_=PSB(b2)[rs_, 512:1024].rearrange("p (a t) -> p a t", a=4)), R=[PK(b2)], W=['ktTm'])
            for pr in range(4):
                bm1, bm2, bm3 = bank(), bank(), bank()
                for hp in range(2):
                    ps_ = slice(hp * 64, (hp + 1) * 64)
                    NX_ = 256 if full else 128
                    rhs_x = XT[:, pr * 256:pr * 256 + NX_]
                    hh_ = pr * 2 + hp
                    S.op('pe', (lambda hp, ps_, rhs_x, bm1: lambda e: e.matmul(PS(bm1)[:, hp * 256:hp * 256 + NX_], lhsT=btTm[:, hh_ * 128:(hh_ + 1) * 128], rhs=rhs_x, start=True, stop=True))(hp, ps_, rhs_x, bm1),
                         R=['btTm', 'XT'], W=[PK(bm1)], inc=(hp == 1))
                    S.op('pe', (lambda hp, ps_, rhs_x, bm2: lambda e: e.matmul(PS(bm2)[:, hp * 256:hp * 256 + NX_], lhsT=ktTm[:, hh_ * 128:(hh_ + 1) * 128], rhs=rhs_x, start=True, stop=True))(hp, ps_, rhs_x, bm2),
                         R=['ktTm', 'XT'], W=[PK(bm2)], inc=(hp == 1))
                    S.op('pe', (lambda hp, ps_, bm3: lambda e: e.matmul(PS(bm3)[:, hp * 128:(hp + 1) * 128], lhsT=khTm[:, hh_ * 128:(hh_ + 1) * 128], rhs=btT[:, pr * 128:(pr + 1) * 128], start=True, stop=True))(hp, ps_, bm3),
                         R=['btT', 'khTm'], W=[PK(bm3)], inc=(hp == 1))
                A0 = Abuf[pr][0]
                BQ0 = BQbuf[pr][0].rearrange("p (h x) -> p h x", h=2)
                m1v = cbv('m1').rearrange("p (h x) -> p h x", h=2)
                p1v = PS(bm1).rearrange("p (h x) -> p h x", h=2)
                S.op('dve', (lambda BQ0, p1v, m1v: lambda e: e.tensor_tensor(out=BQ0[:, :, 0:128], in0=p1v[:, :, 0:128], in1=m1v[:, :, 0:128], op=ALU.mult))(BQ0, p1v, m1v),
                     R=[PK(bm1), 'cb'], W=['BQ%d_0' % pr])
                S.op('pool', (lambda BQ0: lambda e: e.tensor_copy(out=BQ0[:, :, 128:256], in_=ident.unsqueeze(1).broadcast_to([128, 2, 128])))(BQ0), R=['cb'], W=['BQ%d_0' % pr])
                if full:
                    S.op('dve', (lambda p1v, m1v: lambda e: e.tensor_tensor(out=MbT[:, pr * 256:(pr + 1) * 256].rearrange("p (h x) -> p h x", h=2), in0=p1v[:, :, 128:256], in1=m1v[:, :, 128:256], op=ALU.mult))(p1v, m1v),
                         R=[PK(bm1), 'cb'], W=['MbT%d' % pr])
                m2v = cbv('m2').rearrange("p (h x) -> p h x", h=2)
                p2v = PS(bm2).rearrange("p (h x) -> p h x", h=2)
                S.op('dve', (lambda p2v, m2v: lambda e: e.tensor_tensor(out=LkT[:, pr * 256:(pr + 1) * 256].rearrange("p (h x) -> p h x", h=2), in0=p2v[:, :, 0:128], in1=m2v[:, :, 0:128], op=ALU.mult))(p2v, m2v),
                     R=[PK(bm2), 'cb'], W=['LkT%d' % pr])
                if full:
                    S.op('dve', (lambda p2v, m2v: lambda e: e.tensor_tensor(out=MkT[:, pr * 256:(pr + 1) * 256].rearrange("p (h x) -> p h x", h=2), in0=p2v[:, :, 128:256], in1=m2v[:, :, 128:256], op=ALU.mult))(p2v, m2v),
                         R=[PK(bm2), 'cb'], W=['MkT%d' % pr])
                S.op('dve', (lambda A0, bm3: lambda e: e.tensor_tensor(out=A0, in0=PS(bm3)[:, 0:256], in1=cbv('m3'), op=ALU.mult))(A0, bm3), R=[PK(bm3), 'cb'], W=['A%d_0' % pr])
            for lv in range(7):
                last = (lv == 6)
                for pr in range(4):
                    cur, nxt = lv % 2, (lv + 1) % 2
                    Ac = Abuf[pr][cur].rearrange("p (h x) -> p h x", h=2)
                    BQc = BQbuf[pr][cur].rearrange("p (h x) -> p h x", h=2)
                    ak, bqk = 'A%d_%d' % (pr, cur), 'BQ%d_%d' % (pr, cur)
                    if not last:
                        bA = bank()
                        for hp in range(2):
                            S.op('pe', (lambda hp, BQc, Ac, bA: lambda e: e.matmul(PS(bA)[:, hp * 128:(hp + 1) * 128], lhsT=BQc[:, hp, 0:128], rhs=Ac[:, hp, :], start=True, stop=True))(hp, BQc, Ac, bA),
                                 R=[ak, bqk], W=[PK(bA)], inc=(hp == 1))
                        bB = bank()
                        for hp in range(2):
                            S.op('pe', (lambda hp, BQc, Ac, bB: lambda e: e.matmul(PS(bB)[:, hp * 256:hp * 256 + 256], lhsT=Ac[:, hp, :], rhs=BQc[:, hp, :], start=True, stop=False, skip_group_check=True))(hp, BQc, Ac, bB),
                                 R=[ak, bqk], W=[PK(bB)], inc=False)
                            S.op('pe', (lambda hp, BQc, bB: lambda e: e.matmul(PS(bB)[:, hp * 256 + 128:hp * 256 + 256], lhsT=ident, rhs=BQc[:, hp, 128:256], start=False, stop=True, skip_group_check=True))(hp, BQc, bB),
                                 R=[bqk, 'cb'], W=[PK(bB)], inc=(hp == 1))
                        An = Abuf[pr][nxt]
                        BQn = BQbuf[pr][nxt]
                        S.op('dve' if pr % 2 == 0 else 'act', (lambda An, bA, pr: (lambda e: e.tensor_copy(out=An, in_=PS(bA)[:, 0:256])) if pr % 2 == 0 else (lambda e: e.activation(out=An, in_=PS(bA)[:, 0:256], func=AF.Copy)))(An, bA, pr),
                             R=[PK(bA)], W=['A%d_%d' % (pr, nxt)])
                        S.op('dve' if pr % 2 else 'act', (lambda BQn, bB, pr: (lambda e: e.tensor_copy(out=BQn, in_=PS(bB))) if pr % 2 else (lambda e: e.activation(out=BQn, in_=PS(bB), func=AF.Copy)))(BQn, bB, pr),
                             R=[PK(bB)], W=['BQ%d_%d' % (pr, nxt)])
                    else:
                        bB = bank()
                        for hp in range(2):
                            S.op('pe', (lambda hp, BQc, Ac, bB: lambda e: e.matmul(PS(bB)[:, hp * 128:(hp + 1) * 128], lhsT=Ac[:, hp, :], rhs=BQc[:, hp, 128:256], start=True, stop=False))(hp, BQc, Ac, bB),
                                 R=[ak, bqk], W=[PK(bB)], inc=False)
                            S.op('pe', (lambda hp, BQc, bB: lambda e: e.matmul(PS(bB)[:, hp * 128:(hp + 1) * 128], lhsT=ident, rhs=BQc[:, hp, 128:256], start=False, stop=True))(hp, BQc, bB),
                                 R=[bqk, 'cb'], W=[PK(bB)], inc=(hp == 1))
                        S.op('dve', (lambda bB, pr: lambda e: e.tensor_copy(out=TT[:, pr * 256:(pr + 1) * 256], in_=PS(bB)[:, 0:256]))(bB, pr), R=[PK(bB)], W=['TT%d' % pr])
            bX = bank()
            for h in range(8):
                pr, hp = h // 2, h % 2
                ps_ = slice(hp * 64, (hp + 1) * 64)
                S.op('pe', (lambda h, pr, ps_: lambda e: e.matmul(PS(bX)[:, h * 64:(h + 1) * 64], lhsT=khTm[:, h * 128:(h + 1) * 128], rhs=rwAb[:, pr * 64:(pr + 1) * 64], start=True, stop=False))(h, pr, ps_),
                     R=['khTm', 'rwAb'], W=[PK(bX)], inc=False)
                S.op('pe', (lambda h: lambda e: e.matmul(PS(bX)[:, h * 64:(h + 1) * 64], lhsT=LkT[:, h * 128:(h + 1) * 128], rhs=vb[:, h * 64:(h + 1) * 64], start=False, stop=True))(h),
                     R=['LkT%d' % pr, 'vbw'], W=[PK(bX)], inc=(h == 7))
            S.op('act', lambda e: e.activation(out=Xb, in_=PS(bX), func=AF.Copy, scale=-1.0), R=[PK(bX)], W=['Xb'])
            bU = bank()
            for h in range(8):
                S.op('pe', (lambda h: lambda e: e.matmul(PS(bU)[:, h * 64:(h + 1) * 64], lhsT=TT[:, h * 128:(h + 1) * 128], rhs=Xb[:, h * 64:(h + 1) * 64], start=True, stop=True))(h),
                     R=['TT%d' % (h // 2), 'Xb'], W=[PK(bU)], inc=(h == 7))
            S.op('act', lambda e: e.activation(out=Ub, in_=PS(bU), func=AF.Copy), R=[PK(bU)], W=['Ub'])
            if full:
                bY = bank()
                for h in range(8):
                    pr, hp = h // 2, h % 2
                    ps_ = slice(hp * 64, (hp + 1) * 64)
                    S.op('pe', (lambda h, pr, ps_: lambda e: e.matmul(PS(bY)[:, h * 64:(h + 1) * 64], lhsT=rhTm[:, h * 128:(h + 1) * 128], rhs=rwAb[:, pr * 64:(pr + 1) * 64], start=True, stop=False))(h, pr, ps_),
                         R=['rhTm', 'rwAb'], W=[PK(bY)], inc=False)
                    S.op('pe', (lambda h: lambda e: e.matmul(PS(bY)[:, h * 64:(h + 1) * 64], lhsT=MbT[:, h * 128:(h + 1) * 128], rhs=Ub[:, h * 64:(h + 1) * 64], start=False, stop=False))(h),
                         R=['MbT%d' % pr, 'Ub'], W=[PK(bY)], inc=False)
                    S.op('pe', (lambda h: lambda e: e.matmul(PS(bY)[:, h * 64:(h + 1) * 64], lhsT=MkT[:, h * 128:(h + 1) * 128], rhs=vb[:, h * 64:(h + 1) * 64], start=False, stop=True))(h),
                         R=['MkT%d' % pr, 'vbw'], W=[PK(bY)], inc=(h == 7))
                S.op('act', lambda e: e.activation(out=kp, in_=PS(bY), func=AF.Copy), R=[PK(bY)], W=['kp'])
            bS = bank()
            for pr in range(4):
                S.op('pe', (lambda pr: lambda e: e.matmul(PS(bS)[:, pr * 128:(pr + 1) * 128], lhsT=bt_[:, pr * 128:(pr + 1) * 128], rhs=Ub[:, pr * 128:(pr + 1) * 128], start=True, stop=False))(pr),
                     R=['bt', 'Ub'], W=[PK(bS)], inc=False)
                S.op('pe', (lambda pr: lambda e: e.matmul(PS(bS)[:, pr * 128:(pr + 1) * 128], lhsT=kt[:, pr * 128:(pr + 1) * 128], rhs=vb[:, pr * 128:(pr + 1) * 128], start=False, stop=True))(pr),
                     R=['kt', 'vbw'], W=[PK(bS)], inc=(pr == 3))
            for hp in range(2):
                S.op('dve', (lambda hp: lambda e: e.tensor_tensor(out=rwA[hp * 64:(hp + 1) * 64, :].rearrange("p (a e) -> p a e", a=4), in0=rwA[hp * 64:(hp + 1) * 64, :].rearrange("p (a e) -> p a e", a=4),
                                                                in1=PS(bS)[hp * 64:(hp + 1) * 64, :].rearrange("p (a f) -> p a f", a=4)[:, :, hp * 64:(hp + 1) * 64], op=ALU.add))(hp),
                     R=[PK(bS), 'rwA', 'rwAb'], W=['rwA'])
            S.op('dve', lambda e: e.tensor_tensor(out=rwA.rearrange("p (a e) -> p a e", a=4), in0=rwA.rearrange("p (a e) -> p a e", a=4), in1=gC.unsqueeze(2).broadcast_to([128, 4, 64]), op=ALU.mult), R=['rwA', 'gC'], W=['rwA'])
            S.op('act', lambda e: e.activation(out=rwAb, in_=rwA, func=AF.Copy), R=['rwA'], W=['rwAb'])
            if full:
                group_norm_out(kp, 'kp', sq, 'sq', gst2, 64e-5, ptile['rw_gn_g'], ptile['rw_gn_b'], 'pt_rw_gn_g', 'pt_rw_gn_b')
                S.op('pool', lambda e: e.tensor_tensor(out=kp, in0=kp, in1=gi_t, op=ALU.add), R=['kp', 'gi_t'], W=['kp'])
                bgt = bank()
                S.op('pe', lambda e: e.matmul(PS(bgt), lhsT=sgd1, rhs=gup1, start=True, stop=False), R=['sgd1', 'gup1'], W=[PK(bgt)], inc=False)
                S.op('pe', lambda e: e.matmul(PS(bgt), lhsT=sgd2, rhs=gup2, start=False, stop=True), R=['sgd2', 'gup2'], W=[PK(bgt)])
                S.op('dve', lambda e: e.tensor_tensor(out=mixw, in0=kp, in1=PS(bgt), op=ALU.mult), R=['kp', PK(bgt)], W=['mixw'])
                rw_deferred.append((lambda ti: lambda: transpose_to(mixw, ['mixw'], 4, mixTv[:, 4:8, ti * 128:(ti + 1) * 128], ['mixT_w%d' % ti]))(ti))
        for f_ in rw_deferred:
            f_()
        A.release()

    S.op('pool', lambda e: e.memset(retS, 0.0), R=[], W=['retS'])
    S.op('pool', lambda e: e.memset(retSb, 0.0), R=[], W=['retSb'])
    S.op('pool', lambda e: e.memset(rwA, 0.0), R=[], W=['rwA'])
    S.op('pool', lambda e: e.memset(rwAb, 0.0), R=[], W=['rwAb'])
    S.op('pool', lambda e: e.memset(x1T, 0.0), R=[], W=['x1T_prev'] + ['x1T_%d' % i for i in range(NT_HALF)])

    for half in range(2):
        full = (half == 1)
        A.mark()
        xTv = mixTv
        src = dr['xs'][half * S_HALF:(half + 1) * S_HALF, :]
        prep_xT(src, S_HALF, xTv, 0, lambda t0, n: ['xT_%d' % (t0 // 512)] if n == 128 else ['xT_%d' % (t0 // 512)])
        load_ln('ln1_g', 'ln1_b')
        x1b = [A.bf(1024) for _ in range(8)]
        if half == 1:
            S.op('pool', lambda e: e.tensor_copy(out=x1Tv[:, :, XO - 1:XO], in_=x1Tv[:, :, XO + S_HALF - 1:XO + S_HALF]), R=['x1T_%d' % (NT_HALF - 1)], W=['x1T_prev'])

        def cons1(ti, y, yk, half=half, full=full, x1b=x1b):
            if full:
                S.dma('sp', lambda e: e.dma_start(out=x1s[ti * 128:(ti + 1) * 128, :], in_=y), R=[yk], W=['x1s_%d' % ti], sk='x1s')
            b = x1b[ti % 8]
            bk = 'x1b%d' % (ti % 8)
            S.op('dve', lambda e: e.tensor_scalar(out=b, in0=y, scalar1=hmask[:, half:half + 1], scalar2=None, op0=ALU.mult), R=[yk, 'hmask'], W=[bk])
            return lambda: transpose_to(b, [bk], 8, x1Tv[:, :, XO + ti * 128:XO + (ti + 1) * 128], ['x1T_%d' % ti], evac_eng='dve')

        if 'ffn1_%d' % half in plan:
            ffn(xTv, 0, lambda t0, n: ['xT_%d' % (t0 // 512)], S_HALF, dr['ffn1_w_gu'], dr['ffn1_w_down'], src, LN_EPS / (ALPHA * ALPHA), cons1, 'f1')
        A.release()
        S.barrier()
        if 'mix_%d' % half in plan:
            mixer(half, full)
        S.barrier()

    A.mark()
    NTW = NT_HALF if 'wout' in plan else 0
    wo = A.bf(8 * 1024)
    wov = wo.rearrange("p (c f) -> p c f", c=8)
    w_out_v = dr['w_out'].rearrange("(c p) f -> p c f", p=128)
    for hh in range(2):
        S.dma('pool', (lambda hh: lambda e: e.dma_start(out=wov[:, :, hh * 512:(hh + 1) * 512], in_=w_out_v[:, :, hh * 512:(hh + 1) * 512]))(hh), R=[], W=['wo%d' % hh])
    load_ln('ln2_g', 'ln2_b')
    ybuf = [A.f32(1024) for _ in range(4)]
    x2b = [A.bf(1024) for _ in range(4)]
    tmp_sq = A.f32(1024)
    st = A.f32(16)
    wo_deferred = []
    for g in range(NTW // 2):
        tiles = []
        for ti in (2 * g, 2 * g + 1):
            y, yk = ybuf[ti % 4], 'y%d' % (ti % 4)
            S.dma('sp', (lambda y, ti: lambda e: e.dma_start(out=y, in_=x1s[ti * 128:(ti + 1) * 128, :]))(y, ti), R=['x1s_%d' % ti], W=[yk])
            for dh in range(2):
                b = bank()
                for c in range(8):
                    S.op('pe', (lambda c, b, dh, ti: lambda e: e.matmul(PS(b), lhsT=mixTv[:, c, ti * 128:(ti + 1) * 128], rhs=wov[:, c, dh * 512:(dh + 1) * 512], start=(c == 0), stop=(c == 7)))(c, b, dh, ti),
                         R=['mixT_r%d' % ti, 'mixT_w%d' % ti, 'wo%d' % dh], W=[PK(b)], inc=(c == 7))
                S.op('dve', (lambda y, b, dh: lambda e: e.scalar_tensor_tensor(out=y[:, dh * 512:(dh + 1) * 512], in0=PS(b), scalar=1.0 / ALPHA, in1=y[:, dh * 512:(dh + 1) * 512], op0=ALU.mult, op1=ALU.add))(y, b, dh),
                     R=[PK(b), yk], W=[yk])
            tiles.append((ti, y, yk))
        for f_ in wo_deferred:
            f_()
        del wo_deferred[:]
        layer_norm_group([(y, yk) for (ti, y, yk) in tiles], LN_EPS / (ALPHA * ALPHA), tmp_sq, st)
        for (ti, y, yk) in tiles:
            S.dma('sp', (lambda y, ti: lambda e: e.dma_start(out=x2s[ti * 128:(ti + 1) * 128, :], in_=y))(y, ti), R=[yk], W=['x2s_%d' % ti], sk='x2s')
            b2, b2k = x2b[ti % 4], 'x2b%d' % (ti % 4)
            S.op('dve', (lambda b2, y: lambda e: e.tensor_copy(out=b2, in_=y))(b2, y), R=[yk], W=[b2k])
            wo_deferred.append((lambda b2, b2k, ti: lambda: transpose_to(b2, [b2k], 8, x2Tv[:, :, XO + ti * 128:XO + (ti + 1) * 128], ['x2T_%d' % (ti // 4)], evac_eng='dve'))(b2, b2k, ti))
    for f_ in wo_deferred:
        f_()
    A.release()
    S.barrier()

    A.mark()
    load_ln('ln3_g', 'ln3_b')
    x3b = [A.bf(1024) for _ in range(8)]

    def cons3(ti, y, yk):
        S.dma('sp', lambda e: e.dma_start(out=x3s[ti * 128:(ti + 1) * 128, :], in_=y), R=[yk], W=['x3s_%d' % ti], sk='x3s')
        b = x3b[ti % 8]
        bk = 'x3b%d' % (ti % 8)
        S.op('dve', lambda e: e.tensor_copy(out=b, in_=y), R=[yk], W=[bk])
        return lambda: transpose_to(b, [bk], 8, mixTv[:, :, ti * 128:(ti + 1) * 128], ['x3T_%d' % ti], evac_eng='dve')

    if 'ffn2' in plan:
        ffn(x2Tv, XO, lambda t0, n: ['x2T_%d' % (t0 // 512)], S_HALF, dr['ffn2_w_gu'], dr['ffn2_w_down'], x2s, LN_EPS / (ALPHA * ALPHA), cons3, 'f2')
    A.release()
    S.barrier()

    A.mark()
    wgt = A.bf(8 * 1024)
    wgv = wgt.rearrange("p (c f) -> p c f", c=8)
    wpj = A.bf(2 * 1024)
    wpv = wpj.rearrange("p (c f) -> p c f", c=2)
    bgr = A.bf(1024)
    onesb = A.bf(128)
    S.op('pool', lambda e: e.memset(bgr, 0.0), R=[], W=['bgr'])
    ple_g = dr['ple_w_gate'].rearrange("(c p) f -> p c f", p=128)
    ple_p = dr['ple_w_proj'].rearrange("(c p) f -> p c f", p=128)
    for hh in range(2):
        S.dma('pool', (lambda hh: lambda e: e.dma_start(out=wgv[:, :, hh * 512:(hh + 1) * 512], in_=ple_g[:, :, hh * 512:(hh + 1) * 512]))(hh), R=[], W=['wgt%d' % hh])
    S.dma('pool', lambda e: e.dma_start(out=wpv, in_=ple_p), R=[], W=['wpj'])
    S.dma('pool', lambda e: e.dma_start(out=bgr[0:1, :], in_=dr['ple_b_gate']), R=[], W=['bgr'])
    S.op('dve', lambda e: e.tensor_copy(out=onesb, in_=c32v('ones')), R=['c32'], W=['onesb'])
    pb_ = [A.bf(256) for _ in range(2)]
    pT = [A.bf(256) for _ in range(2)]
    x3t = [A.f32(1024) for _ in range(2)]
    gsb = [A.f32(1024) for _ in range(2)]
    for ti in range(NT_HALF if 'ple' in plan else 0):
        pbt, pbk = pb_[ti % 2], 'pb%d' % (ti % 2)
        pTt, pTk = pT[ti % 2], 'pT%d' % (ti % 2)
        x3, x3k = x3t[ti % 2], 'x3t%d' % (ti % 2)
        gs, gsk = gsb[ti % 2], 'gs%d' % (ti % 2)
        S.dma('pool', (lambda pbt, ti: lambda e: e.dma_start(out=pbt, in_=dr['p'][ti * 128:(ti + 1) * 128, :]))(pbt, ti), R=[], W=[pbk])
        S.dma('sp', (lambda x3, ti: lambda e: e.dma_start(out=x3, in_=x3s[ti * 128:(ti + 1) * 128, :]))(x3, ti), R=['x3s_%d' % ti], W=[x3k])
        transpose_to(pbt, [pbk], 2, pTt.rearrange("p (c t) -> p c t", c=2), [pTk])
        for dh in range(2):
            bg_ = bank()
            for c in range(8):
                S.op('pe', (lambda c, bg_, dh, ti: lambda e: e.matmul(PS(bg_), lhsT=mixTv[:, c, ti * 128:(ti + 1) * 128], rhs=wgv[:, c, dh * 512:(dh + 1) * 512], start=(c == 0), stop=False))(c, bg_, dh, ti),
                     R=['x3T_%d' % ti, 'wgt%d' % dh], W=[PK(bg_)], inc=False)
            S.op('pe', (lambda bg_, dh: lambda e: e.matmul(PS(bg_), lhsT=onesb, rhs=bgr[:, dh * 512:(dh + 1) * 512], start=False, stop=True))(bg_, dh), R=['onesb', 'bgr'], W=[PK(bg_)])
            bp_ = bank()
            for c in range(2):
                S.op('pe', (lambda c, bp_, dh, pTt: lambda e: e.matmul(PS(bp_), lhsT=pTt[:, c * 128:(c + 1) * 128], rhs=wpv[:, c, dh * 512:(dh + 1) * 512], start=(c == 0), stop=(c == 1)))(c, bp_, dh, pTt),
                     R=[pTk, 'wpj'], W=[PK(bp_)], inc=(c == 1))
            S.op('act', (lambda gs, bg_, dh: lambda e: e.activation(out=gs[:, dh * 512:(dh + 1) * 512], in_=PS(bg_), func=AF.Sigmoid))(gs, bg_, dh), R=[PK(bg_)], W=[gsk])
            S.op('dve', (lambda gs, bp_, dh: lambda e: e.tensor_tensor(out=gs[:, dh * 512:(dh + 1) * 512], in0=gs[:, dh * 512:(dh + 1) * 512], in1=PS(bp_), op=ALU.mult))(gs, bp_, dh), R=[PK(bp_), gsk], W=[gsk])
        S.op('dve', (lambda gs, x3: lambda e: e.tensor_tensor(out=gs, in0=gs, in1=x3, op=ALU.add))(gs, x3), R=[gsk, x3k], W=[gsk])
        S.dma('sp', (lambda gs, ti: lambda e: e.dma_start(out=out[ti * 128:(ti + 1) * 128, :], in_=gs))(gs, ti), R=[gsk], W=['out_%d' % ti], sk='out')
    A.release()
    S.barrier()

    with nc.Block() as block:
        S.emit(nc, block)
    es.close()
    print("semaphores:", len(S.semkeys), "ops:", {e: len(l) for e, l in S.lists.items()}, "arena hi:", A.hi)
    return nc


_NC_CACHE = {}


def kernel(**inputs):
    x = np.asarray(inputs['x'], np.float32)
    p = np.asarray(inputs['p'], np.float32)[0]
    if 'nc' not in _NC_CACHE:
        _NC_CACHE['nc'] = build_program()
    nc = _NC_CACHE['nc']
    c32 = np.ascontiguousarray(np.concatenate([C32[k] for k in C32], axis=1).astype(np.float32))
    cb = np.ascontiguousarray(np.concatenate([CB[k] for k in CB], axis=1).astype(np.float32))
    c32r = np.ascontiguousarray(np.concatenate([C32R[k] for k in C32R], axis=1).astype(np.float32))
    wmap = {}
    for n in WEIGHT_NAMES:
        wmap[n] = np.ascontiguousarray(np.asarray(inputs[n], np.float32)[0].reshape(WSHAPES[n]))
    in_maps = []
    for c in range(8):
        b, half = c // 2, c % 2
        m = dict(wmap)
        if half == 1:
            xs = x[b]
            pos = np.arange(4096)
            hm = np.ones((128, 2), np.float32)
        else:
            xs = np.concatenate([np.zeros((S_HALF, D), np.float32), x[b, :S_HALF]], axis=0)
            pos = np.arange(4096) - S_HALF
            hm = np.ones((128, 2), np.float32)
            hm[:, 0] = 0.0
        cos, sin = rope_tables(pos)
        m['xs'] = np.ascontiguousarray(xs)
        m['p'] = np.ascontiguousarray(p[b, half * S_HALF:(half + 1) * S_HALF])
        m['hmask'] = hm
        m['cos'] = cos
        m['sin'] = sin
        m['c32'] = c32
        m['cb'] = cb
        m['c32r'] = c32r
        in_maps.append(m)
    res = run_bass_kernel_spmd(nc, in_maps, core_ids=list(range(8)))
    outp = np.zeros((4, 4096, D), np.float32)
    for c in range(8):
        b, half = c // 2, c % 2
        outp[b, half * S_HALF:(half + 1) * S_HALF] = res.results[c]['out']
    return outp
```

```python
import os
import numpy as np
import concourse.bass as bass
import concourse.mybir as mybir
from concourse.bass_utils import run_bass_kernel_spmd

F32 = mybir.dt.float32
BF16 = mybir.dt.bfloat16
AF = mybir.ActivationFunctionType
ALU = mybir.AluOpType
AX = mybir.AxisListType

D = 1024
DFF = 2816
NJ = DFF // 128
S_HALF = 2048
NT_HALF = S_HALF // 128
RETC = 2048
RWC = 1824
INC = RETC + RWC
ALPHA = 2.0 ** 0.25
LN_EPS = 1e-5
EDEC = float(np.exp(-0.5))

STRICT = True


class _Rec:
    def __init__(self):
        self.calls = []

    def __getattr__(self, name):
        def f(*a, **k):
            self.calls.append((name, a, k))
            return self
        return f


def _capture(fn):
    r = _Rec()
    fn(r)
    assert len(r.calls) == 1, r.calls
    name, a, k = r.calls[0]
    return lambda e: getattr(e, name)(*a, **k)


class Sched:
    def __init__(self):
        self.engs = ['pe', 'act', 'dve', 'pool', 'sp']
        self.lists = {e: [] for e in self.engs}
        self.cnt = {e: 0 for e in self.engs}
        self.lastw = {}
        self.readers = {}
        self.waited = {e: {} for e in self.engs}
        self.dmacnt = {}
        self.semkeys = set(self.engs)
        self.alltok = {}

    def _deps(self, eng, R, W, is_dma):
        deps = []
        raw = set()
        for k in R:
            t = self.lastw.get(k)
            if t:
                deps.append(t)
                raw.add(t)
            if k.startswith('ps'):
                deps.extend(tk for tk in self.readers.get(k, ()) if tk[0] != eng)
        for k in W:
            t = self.lastw.get(k)
            if t:
                deps.append(t)
            deps.extend(self.readers.get(k, ()))
        waits = {}
        for (sk, v) in deps:
            if sk == eng and not is_dma and (eng == 'pe' or not STRICT or (sk, v) not in raw):
                continue
            if self.waited[eng].get(sk, 0) >= v:
                continue
            waits[sk] = max(waits.get(sk, 0), v)
        for sk, v in waits.items():
            self.waited[eng][sk] = v
        return list(waits.items())

    def _commit(self, tok, R, W):
        for k in W:
            self.lastw[k] = tok
            self.readers[k] = []
        for k in R:
            if k not in W:
                self.readers.setdefault(k, []).append(tok)
        self.alltok[tok[0]] = max(self.alltok.get(tok[0], 0), tok[1])

    def op(self, eng, fn, R=(), W=(), inc=True):
        self._clean = False
        waits = self._deps(eng, R, W, False)
        if inc:
            self.cnt[eng] += 1
            tok = (eng, self.cnt[eng])
        else:
            tok = (eng, self.cnt[eng] + 1)
        self.lists[eng].append((waits, _capture(fn), (eng, 1) if inc else None))
        self._commit(tok, R, W)

    def dma(self, eng, fn, R, W, sk=None):
        self._clean = False
        waits = self._deps(eng, R, W, True)
        sk = 'd:' + (sk or W[0])
        self.semkeys.add(sk)
        self.dmacnt[sk] = self.dmacnt.get(sk, 0) + 16
        tok = (sk, self.dmacnt[sk])
        self.lists[eng].append((waits, _capture(fn), (sk, 16)))
        self._commit(tok, R, W)

    def barrier(self):
        if getattr(self, '_clean', False):
            return
        self._clean = True
        for e in ['pe', 'act', 'dve', 'pool']:
            if self.lists[e] and self.lists[e][-1][2] is not None and self.lists[e][-1][2][0] == e:
                continue
            self.cnt[e] += 1
            self.alltok[e] = self.cnt[e]
            self.lists[e].append(([], 'nop', (e, 1)))
        for e in self.engs:
            waits = []
            for sk, v in self.alltok.items():
                if self.waited[e].get(sk, 0) >= v:
                    continue
                if sk == e:
                    continue
                waits.append((sk, v))
                self.waited[e][sk] = v
            self.lists[e].append((waits, None, None))
        self.lastw = {}
        self.readers = {}

    def emit(self, nc, block):
        sems = {sk: nc.alloc_semaphore(name=("s_" + sk.replace(':', '_').replace('.', '_'))[:40]) for sk in sorted(self.semkeys)}
        engobj = {'pe': 'tensor', 'act': 'scalar', 'dve': 'vector', 'pool': 'gpsimd', 'sp': 'sync'}

        def make(ename):
            lst = self.lists[ename]

            def body(e):
                for (waits, fn, inc) in lst:
                    for (sk, v) in waits:
                        e.wait_ge(sems[sk], v)
                    if fn is None:
                        continue
                    if fn == 'nop':
                        ins = e.nop()
                    else:
                        ins = fn(e)
                    if inc is not None:
                        ins.then_inc(sems[inc[0]], inc[1])
            return body

        for ename in self.engs:
            getattr(block, engobj[ename])(make(ename))


def gammas():
    return 1.0 - 2.0 ** (-5.0 - np.arange(8, dtype=np.float64))


def host_consts():
    g = gammas()
    i = np.arange(128)
    c = {}
    s_le_t = (i[:, None] <= i[None, :]).astype(np.float32)
    s_lt_t = (i[:, None] < i[None, :]).astype(np.float32)
    c['tri_incl'] = -EDEC * s_le_t
    c['tri_strict'] = -EDEC * s_lt_t
    c['negcol'] = np.full((128, 1), -EDEC, np.float32)
    c['ones'] = np.ones((128, 128), np.float32)
    rel = (i[None, :] - i[:, None]).astype(np.float64)
    dm = np.zeros((128, 8, 128), np.float64)
    for h in range(8):
        dm[:, h, :] = np.where(rel >= 0, 0.125 * np.exp(np.where(rel >= 0, rel, 0) * np.log(g[h])), 0.0)
    cr = {}
    cr['dmask'] = dm.reshape(128, 1024).astype(np.float32)
    kd = np.zeros((128, 8), np.float64)
    for h in range(8):
        kd[:, h] = 0.125 * g[h] ** (127.0 - i)
    c['kdec'] = kd.astype(np.float32)
    qd = np.zeros((128, 4, 128), np.float64)
    gm = np.zeros((128, 4), np.float64)
    for pr in range(4):
        for hp in range(2):
            h = 2 * pr + hp
            qd[hp * 64:(hp + 1) * 64, pr, :] = (g[h] ** (i + 1.0))[None, :]
            gm[hp * 64:(hp + 1) * 64, pr] = g[h] ** 128.0
    cr['qdec'] = qd.reshape(128, 512).astype(np.float32)
    cr['kdecf'] = np.repeat(kd, 64, axis=1).astype(np.float32)
    cr['gam128f'] = np.repeat(gm, 64, axis=1).astype(np.float32)
    c['gam128'] = gm.astype(np.float32)
    b = {}
    b['ident'] = np.eye(128, dtype=np.float32)
    m1 = np.concatenate([-s_lt_t, s_le_t], axis=1)
    m2 = np.concatenate([s_lt_t, s_le_t], axis=1)
    b['m1'] = np.concatenate([m1, m1], axis=1)
    b['m2'] = np.concatenate([m2, m2], axis=1)
    m3 = -(i[:, None] > i[None, :]).astype(np.float32)
    b['m3'] = np.concatenate([m3, m3], axis=1)
    return c, cr, b


def rope_tables(pos):
    inv = 10000.0 ** (-np.arange(0, 64, 2, dtype=np.float32) / 64.0)
    ang = pos.astype(np.float32)[:, None] * inv[None, :]
    cos = np.cos(ang).astype(np.float32)
    sin = np.sin(ang).astype(np.float32)
    n = pos.shape[0] // 128
    cos = cos.reshape(n, 128, 32).transpose(1, 0, 2).reshape(128, n * 32)
    sin = sin.reshape(n, 128, 32).transpose(1, 0, 2).reshape(128, n * 32)
    return np.ascontiguousarray(cos), np.ascontiguousarray(sin)


C32, C32R, CB = host_consts()
C32_OFF = {}
_o = 0
for _k, _v in C32.items():
    C32_OFF[_k] = (_o, _v.shape[1])
    _o += _v.shape[1]
C32_N = _o
C32R_OFF = {}
_o = 0
for _k, _v in C32R.items():
    C32R_OFF[_k] = (_o, _v.shape[1])
    _o += _v.shape[1]
C32R_N = _o
CB_OFF = {}
_o = 0
for _k, _v in CB.items():
    CB_OFF[_k] = (_o, _v.shape[1])
    _o += _v.shape[1]
CB_N = _o

WEIGHT_NAMES = ['ffn1_w_gu', 'ffn1_w_down', 'ln1_g', 'ln1_b', 'w_in', 'ret_gn_g', 'ret_gn_b', 'rw_mu',
                'rw_w0', 'rw_w_up', 'rw_a0', 'rw_a_up', 'rw_g_up', 'rw_k_k', 'rw_k_a', 'rw_r_k', 'rw_gn_g',
                'rw_gn_b', 'w_out', 'ln2_g', 'ln2_b', 'ffn2_w_gu', 'ffn2_w_down', 'ln3_g', 'ln3_b',
                'ple_w_proj', 'ple_w_gate', 'ple_b_gate']
WSHAPES = {'ffn1_w_gu': [D, 2 * DFF], 'ffn1_w_down': [DFF, D], 'ln1_g': [1, D], 'ln1_b': [1, D], 'w_in': [D, INC],
           'ret_gn_g': [1, 512], 'ret_gn_b': [1, 512], 'rw_mu': [1, RWC], 'rw_w0': [1, 512], 'rw_w_up': [64, 512],
           'rw_a0': [1, 512], 'rw_a_up': [64, 512], 'rw_g_up': [160, 512], 'rw_k_k': [1, 512], 'rw_k_a': [1, 512],
           'rw_r_k': [1, 512], 'rw_gn_g': [1, 512], 'rw_gn_b': [1, 512], 'w_out': [D, D], 'ln2_g': [1, D],
           'ln2_b': [1, D], 'ffn2_w_gu': [D, 2 * DFF], 'ffn2_w_down': [DFF, D], 'ln3_g': [1, D], 'ln3_b': [1, D],
           'ple_w_proj': [256, D], 'ple_w_gate': [D, D], 'ple_b_gate': [1, D]}


def build_program(plan=None, dbg=False):
    nc = bass.Bass("TRN2", target_bir_lowering=False)
    dr = {}
    dr['xs'] = nc.dram_tensor("xs", [2 * S_HALF, D], F32, kind="ExternalInput").ap()
    dr['p'] = nc.dram_tensor("p", [S_HALF, 256], F32, kind="ExternalInput").ap()
    dr['hmask'] = nc.dram_tensor("hmask", [128, 2], F32, kind="ExternalInput").ap()
    dr['cos'] = nc.dram_tensor("cos", [128, 1024], F32, kind="ExternalInput").ap()
    dr['sin'] = nc.dram_tensor("sin", [128, 1024], F32, kind="ExternalInput").ap()
    dr['c32'] = nc.dram_tensor("c32", [128, C32_N], F32, kind="ExternalInput").ap()
    dr['cb'] = nc.dram_tensor("cb", [128, CB_N], F32, kind="ExternalInput").ap()
    dr['c32r'] = nc.dram_tensor("c32r", [128, C32R_N], F32, kind="ExternalInput").ap()
    for n in WEIGHT_NAMES:
        dr[n] = nc.dram_tensor(n, WSHAPES[n], F32, kind="ExternalInput").ap()
    out = nc.dram_tensor("out", [S_HALF, D], F32, kind="ExternalOutput").ap()
    skind = "ExternalOutput" if dbg else "Internal"
    x1s = nc.dram_tensor("x1s", [S_HALF, D], F32, kind=skind).ap()
    x2s = nc.dram_tensor("x2s", [S_HALF, D], F32, kind=skind).ap()
    x3s = nc.dram_tensor("x3s", [S_HALF, D], F32, kind=skind).ap()
    wab_s = nc.dram_tensor("wab_s", [128, 2 * 8 * RWC], BF16, kind="Internal").ap()
    if plan is None:
        plan = ['ffn1_0', 'mix_0', 'ffn1_1', 'mix_1', 'wout', 'ffn2', 'ple']

    S = Sched()
    dumped = {}

    def dump(name, ap, keys):
        if not dbg or name in dumped:
            return
        shp = list(ap.shape)
        d_ = nc.dram_tensor("dbg_" + name, shp, F32, kind="ExternalOutput").ap()
        dumped[name] = d_
        S.dma('pool', lambda e: e.dma_start(out=d_, in_=ap), R=list(keys), W=['dbg_' + name])
    ARENA_W = int(os.environ.get('ARENA_W', '53200'))
    from contextlib import ExitStack
    es = ExitStack()
    arena = es.enter_context(nc.sbuf_tensor("arena", [128, ARENA_W], F32))
    psf = [es.enter_context(nc.psum_tensor("ps%d" % i, [128, 512], F32)) for i in range(8)]

    class Alloc:
        def __init__(self):
            self.p = 0
            self.marks = []

        def f32(self, n, parts=(0, 128)):
            if os.environ.get('DRY'):
                self.p += n
                self.hi = max(getattr(self, 'hi', 0), self.p)
                return arena[parts[0]:parts[1], 0:n]
            a = arena[parts[0]:parts[1], self.p:self.p + n]
            self.p += n
            self.hi = max(getattr(self, 'hi', 0), self.p)
            assert self.p <= ARENA_W, ("arena overflow", self.p)
            return a

        def bf(self, n, parts=(0, 128)):
            w = (n + 1) // 2
            if os.environ.get('DRY'):
                self.p += w
                self.hi = max(getattr(self, 'hi', 0), self.p)
                return arena[parts[0]:parts[1], 0:w].bitcast(BF16)
            a = arena[parts[0]:parts[1], self.p:self.p + w].bitcast(BF16)
            self.p += w
            self.hi = max(getattr(self, 'hi', 0), self.p)
            assert self.p <= ARENA_W, ("arena overflow", self.p)
            return a

        def mark(self):
            self.marks.append(self.p)

        def release(self):
            self.p = self.marks.pop()
            S.barrier()

    A = Alloc()
    bankctr = [0]
    bankgen = [0] * 8

    class Bk(int):
        pass

    def bank():
        b = Bk(bankctr[0] % 8)
        bankctr[0] += 1
        bankgen[int(b)] = bankctr[0]
        b.gen = bankctr[0]
        return b

    def PS(b):
        return psf[int(b)][:, :]

    def PSB(b):
        return psf[int(b)][:, :].bitcast(BF16)

    def PK(b):
        assert bankgen[int(b)] == b.gen, "stale PSUM bank use"
        return 'ps%d' % int(b)

    c32 = A.f32(C32_N)
    cbt = A.bf(CB_N)
    S.dma('sp', lambda e: e.dma_start(out=c32, in_=dr['c32']), R=[], W=['c32'])
    S.dma('pool', lambda e: e.dma_start(out=cbt, in_=dr['cb']), R=[], W=['cb'])

    def c32v(name, parts=(0, 128)):
        o, n = C32_OFF[name]
        return c32[parts[0]:parts[1], o:o + n]

    def cbv(name):
        o, n = CB_OFF[name]
        return cbt[:, o:o + n]

    ident = cbv('ident')
    hmask = A.f32(2)
    S.dma('sp', lambda e: e.dma_start(out=hmask, in_=dr['hmask']), R=[], W=['hmask'])
    ptile = {}
    LN = {}
    retS = A.f32(256)
    retSb = A.bf(256)
    rwA = A.f32(256)
    rwAb = A.bf(256)
    x1T = A.bf(8 * (S_HALF + 8))
    x1Tv = x1T.rearrange("p (c t) -> p c t", c=8)
    XO = 8
    mixT = A.bf(8 * S_HALF)
    mixTv = mixT.rearrange("p (c t) -> p c t", c=8)
    x2Tv = x1Tv

    def load_ln(gn, bn):
        lng = A.f32(1024)
        lnb = A.f32(1024)
        LN['g'] = lng
        LN['b'] = lnb
        S.dma('sp', lambda e: e.dma_start(out=lng, in_=dr[gn].partition_broadcast(128)), R=[], W=['lng'])
        S.dma('sp', lambda e: e.dma_start(out=lnb, in_=dr[bn].partition_broadcast(128)), R=[], W=['lnb'])

    def load_ptiles(names):
        for n in names:
            t = A.f32(512)
            ptile[n] = t
            S.dma('sp', (lambda t, n: lambda e: e.dma_start(out=t, in_=dr[n].partition_broadcast(128)))(t, n), R=[], W=['pt_' + n])

    def transpose_to(src_bf, src_keys, n_blocks, dst_view, dst_keys, evac_eng='act'):
        b = bank()
        pk = PK(b)
        for i in range(n_blocks):
            S.op('pe', (lambda i, b: lambda e: e.transpose(out=PSB(b)[:, i * 128:(i + 1) * 128],
                                                         in_=src_bf[:, i * 128:(i + 1) * 128], identity=ident))(i, b),
                 R=list(src_keys) + ['cb'], W=[pk], inc=(i == n_blocks - 1))
        src = PSB(b)[:, 0:n_blocks * 128].rearrange("p (c t) -> p c t", c=n_blocks)
        if evac_eng == 'act':
            S.op('act', lambda e: e.activation(out=dst_view, in_=src, func=AF.Copy), R=[pk], W=list(dst_keys))
        else:
            S.op(evac_eng, lambda e: e.tensor_copy(out=dst_view, in_=src), R=[pk], W=list(dst_keys))

    def layer_norm_group(tiles, eps, junk, st):
        n_ = len(tiles)
        sl = lambda i, c: st[:, i * 8 + c:i * 8 + c + 1]
        k = lambda i, c: 'lnst%d_%d' % (i, c)
        for i, (y, yk) in enumerate(tiles):
            S.op('act', (lambda i, y: lambda e: e.activation(out=junk, in_=y, func=AF.Square, accum_out=sl(i, 0)))(i, y), R=[yk], W=['lnjunk', k(i, 0)])
            S.op('act', (lambda i, y: lambda e: e.activation(out=junk, in_=y, func=AF.Identity, accum_out=sl(i, 1)))(i, y), R=[yk], W=['lnjunk', k(i, 1)])
        for i in range(n_):
            S.op('dve', (lambda i: lambda e: e.tensor_scalar(out=sl(i, 2), in0=sl(i, 1), scalar1=1.0 / 1024, scalar2=None, op0=ALU.mult))(i), R=[k(i, 1)], W=[k(i, 2)])
        for i in range(n_):
            S.op('dve', (lambda i: lambda e: e.tensor_tensor(out=sl(i, 3), in0=sl(i, 2), in1=sl(i, 2), op=ALU.mult))(i), R=[k(i, 2)], W=[k(i, 3)])
        for i in range(n_):
            S.op('dve', (lambda i: lambda e: e.scalar_tensor_tensor(out=sl(i, 4), in0=sl(i, 0), scalar=1.0 / 1024, in1=sl(i, 3), op0=ALU.mult, op1=ALU.subtract))(i), R=[k(i, 0), k(i, 3)], W=[k(i, 4)])
        for i in range(n_):
            S.op('dve', (lambda i: lambda e: e.tensor_scalar(out=sl(i, 4), in0=sl(i, 4), scalar1=float(eps), scalar2=None, op0=ALU.add))(i), R=[k(i, 4)], W=[k(i, 4)])
        for i in range(n_):
            S.op('act', (lambda i: lambda e: e.activation(out=sl(i, 5), in_=sl(i, 4), func=AF.Sqrt))(i), R=[k(i, 4)], W=[k(i, 5)])
        for i in range(n_):
            S.op('dve', (lambda i: lambda e: e.reciprocal(out=sl(i, 6), in_=sl(i, 5)))(i), R=[k(i, 5)], W=[k(i, 6)])
        for i in range(n_):
            S.op('dve', (lambda i: lambda e: e.scalar_tensor_tensor(out=sl(i, 7), in0=sl(i, 2), scalar=-1.0, in1=sl(i, 6), op0=ALU.mult, op1=ALU.mult))(i), R=[k(i, 2), k(i, 6)], W=[k(i, 7)])
        for i, (y, yk) in enumerate(tiles):
            S.op('act', (lambda i, y: lambda e: e.activation(out=y, in_=y, func=AF.Identity, scale=sl(i, 6), bias=sl(i, 7)))(i, y), R=[yk, k(i, 6), k(i, 7)], W=[yk])
        for i, (y, yk) in enumerate(tiles):
            S.op('dve', (lambda y: lambda e: e.tensor_tensor(out=y, in0=y, in1=LN['g'], op=ALU.mult))(y), R=[yk, 'lng'], W=[yk])
            S.op('dve', (lambda y: lambda e: e.tensor_tensor(out=y, in0=y, in1=LN['b'], op=ALU.add))(y), R=[yk, 'lnb'], W=[yk])

    def layer_norm_tile(y, ykey, eps, outs, tmp_sq, st):
        S.op('act', lambda e: e.activation(out=tmp_sq, in_=y, func=AF.Square), R=[ykey], W=['lnsq'])
        S.op('dve', lambda e: e.reduce_sum(out=st[:, 0:1], in_=tmp_sq, axis=AX.X), R=['lnsq'], W=['lnst0'])
        S.op('dve', lambda e: e.reduce_sum(out=st[:, 1:2], in_=y, axis=AX.X), R=[ykey], W=['lnst1'])
        S.op('dve', lambda e: e.tensor_scalar(out=st[:, 2:3], in0=st[:, 1:2], scalar1=1.0 / 1024, scalar2=None, op0=ALU.mult), R=['lnst1'], W=['lnst2'])
        S.op('dve', lambda e: e.tensor_tensor(out=st[:, 3:4], in0=st[:, 2:3], in1=st[:, 2:3], op=ALU.mult), R=['lnst2'], W=['lnst3'])
        S.op('dve', lambda e: e.scalar_tensor_tensor(out=st[:, 4:5], in0=st[:, 0:1], scalar=1.0 / 1024, in1=st[:, 3:4], op0=ALU.mult, op1=ALU.subtract), R=['lnst0', 'lnst3'], W=['lnst4'])
        S.op('dve', lambda e: e.tensor_scalar(out=st[:, 4:5], in0=st[:, 4:5], scalar1=float(eps), scalar2=None, op0=ALU.add), R=['lnst4'], W=['lnst4'])
        S.op('act', lambda e: e.activation(out=st[:, 5:6], in_=st[:, 4:5], func=AF.Sqrt), R=['lnst4'], W=['lnst5'])
        S.op('dve', lambda e: e.reciprocal(out=st[:, 6:7], in_=st[:, 5:6]), R=['lnst5'], W=['lnst6'])
        S.op('dve', lambda e: e.scalar_tensor_tensor(out=st[:, 7:8], in0=st[:, 2:3], scalar=-1.0, in1=st[:, 6:7], op0=ALU.mult, op1=ALU.mult), R=['lnst2', 'lnst6'], W=['lnst7'])
        S.op('act', lambda e: e.activation(out=y, in_=y, func=AF.Identity, scale=st[:, 6:7], bias=st[:, 7:8]), R=[ykey, 'lnst6', 'lnst7'], W=[ykey])
        S.op('dve', lambda e: e.tensor_tensor(out=y, in0=y, in1=LN['g'], op=ALU.mult), R=[ykey, 'lng'], W=[ykey])
        S.op('dve', lambda e: e.tensor_tensor(out=y, in0=y, in1=LN['b'], op=ALU.add), R=[ykey, 'lnb'], W=[ykey])

    def ffn(xTv_src, xoff, xkey_fn, ntok, wgu, wdown, res_dram, eps, consumer, tagp):
        A.mark()
        NB = 1024
        hhT = A.bf(NJ * NB)
        hhv = hhT.rearrange("p (j t) -> p j t", j=NJ)
        wg = [A.bf(8 * 512) for _ in range(2)]
        wd = [A.bf(1024) for _ in range(6)]
        sg = [A.f32(512) for _ in range(2)]
        ybuf = [A.f32(1024) for _ in range(8)]
        tmp_sq = A.f32(1024)
        st = A.f32(32)
        wgu_v = wgu.rearrange("(c p) f -> p c f", p=128)
        deferred = []

        def run_deferred():
            for f_ in deferred:
                f_()
            del deferred[:]
        wi = 0
        di = 0
        for blk in range(ntok // NB):
            t0 = blk * NB
            for jg in range(NJ // 2):
                w = wg[wi % 2]
                wk = 'wg%d' % (wi % 2)
                wv = w.rearrange("p (c f) -> p c f", c=8)
                wi += 1
                S.dma('pool', (lambda wv, jg: lambda e: e.dma_start(out=wv[:, :, 0:256], in_=wgu_v[:, :, jg * 256:(jg + 1) * 256]))(wv, jg), R=[], W=[wk + 'g'])
                S.dma('pool', (lambda wv, jg: lambda e: e.dma_start(out=wv[:, :, 256:512], in_=wgu_v[:, :, DFF + jg * 256:DFF + (jg + 1) * 256]))(wv, jg), R=[], W=[wk + 'u'])
                for jj in range(2):
                    j = jg * 2 + jj
                    for sb in range(NB // 512):
                        bg = bank()
                        bu = bank()
                        c0 = xoff + t0 + sb * 512
                        xk = xkey_fn(t0 + sb * 512, 512)
                        for dc in range(8):
                            S.op('pe', (lambda wv, dc, jj, bg, c0: lambda e: e.matmul(PS(bg), lhsT=wv[:, dc, jj * 128:(jj + 1) * 128], rhs=xTv_src[:, dc, c0:c0 + 512], start=(dc == 0), stop=(dc == 7)))(wv, dc, jj, bg, c0),
                                 R=[wk + 'g'] + xk, W=[PK(bg)], inc=(dc == 7))
                        for dc in range(8):
                            S.op('pe', (lambda wv, dc, jj, bu, c0: lambda e: e.matmul(PS(bu), lhsT=wv[:, dc, 256 + jj * 128:256 + (jj + 1) * 128], rhs=xTv_src[:, dc, c0:c0 + 512], start=(dc == 0), stop=(dc == 7)))(wv, dc, jj, bu, c0),
                                 R=[wk + 'u'] + xk, W=[PK(bu)], inc=(dc == 7))
                        sgt = sg[(j * 2 + sb) % 2]
                        sgk = 'sg%d' % ((j * 2 + sb) % 2)
                        S.op('act', (lambda sgt, bg: lambda e: e.activation(out=sgt, in_=PS(bg), func=AF.Silu))(sgt, bg), R=[PK(bg)], W=[sgk])
                        S.op('dve', (lambda sgt, bu, j, sb: lambda e: e.tensor_tensor(out=hhv[:, j, sb * 512:(sb + 1) * 512], in0=sgt, in1=PS(bu), op=ALU.mult))(sgt, bu, j, sb),
                             R=[sgk, PK(bu)], W=['hh%d_%d' % (j, sb)])
            if blk == 0:
                dump(tagp + '_hh0', hhv[:, 0, :], ['hh0_0', 'hh0_1'])
                dump(tagp + '_hh21', hhv[:, 21, :], ['hh21_0', 'hh21_1'])
                dump(tagp + '_xT0', xTv_src[:, 0, xoff:xoff + 512], xkey_fn(0, 512))
            if os.environ.get('FFN_STOP') == 'up':
                continue
            for rnd in range(NB // 512):
                banks = [[bank(), bank()] for _ in range(4)]
                for j in range(NJ):
                    w = wd[di % 6]
                    wk = 'wd%d' % (di % 6)
                    di += 1
                    S.dma('pool', (lambda w, j: lambda e: e.dma_start(out=w, in_=wdown[j * 128:(j + 1) * 128, :]))(w, j), R=[], W=[wk])
                    for tt in range(4):
                        for dh in range(2):
                            b = banks[tt][dh]
                            S.op('pe', (lambda w, j, tt, dh, b, rnd: lambda e: e.matmul(PS(b), lhsT=hhv[:, j, rnd * 512 + tt * 128:rnd * 512 + (tt + 1) * 128], rhs=w[:, dh * 512:(dh + 1) * 512], start=(j == 0), stop=(j == NJ - 1)))(w, j, tt, dh, b, rnd),
                                 R=[wk, 'hh%d_%d' % (j, rnd)], W=[PK(b)], inc=(j == NJ - 1 or (tt == 3 and dh == 1)))
                cur = []
                for tt in range(4):
                    ti = (t0 + rnd * 512) // 128 + tt
                    y = ybuf[ti % 8]
                    yk = 'y%d' % (ti % 8)
                    S.dma('sp', (lambda y, ti: lambda e: e.dma_start(out=y, in_=res_dram[ti * 128:(ti + 1) * 128, :]))(y, ti), R=[], W=[yk])
                    for dh in range(2):
                        b = banks[tt][dh]
                        S.op('dve', (lambda y, b, dh: lambda e: e.scalar_tensor_tensor(out=y[:, dh * 512:(dh + 1) * 512], in0=PS(b), scalar=0.5 / ALPHA, in1=y[:, dh * 512:(dh + 1) * 512], op0=ALU.mult, op1=ALU.add))(y, b, dh),
                             R=[PK(b), yk], W=[yk])
                    cur.append((ti, y, yk))
                run_deferred()
                layer_norm_group([(y, yk) for (ti, y, yk) in cur], eps, tmp_sq, st)
                for (ti, y, yk) in cur:
                    later = consumer(ti, y, yk)
                    if later is not None:
                        deferred.append(later)
        run_deferred()
        A.release()

    def prep_xT(src_dram, ntok, dstv, doff, keyfn):
        A.mark()
        xb = [A.bf(1024) for _ in range(2)]
        for ti in range(ntok // 128):
            b = xb[ti % 2]
            bk = 'xb%d' % (ti % 2)
            S.dma('pool', (lambda b, ti: lambda e: e.dma_start(out=b, in_=src_dram[ti * 128:(ti + 1) * 128, :]))(b, ti), R=[], W=[bk])
            transpose_to(b, [bk], 8, dstv[:, :, doff + ti * 128:doff + (ti + 1) * 128], keyfn(ti * 128, 128))
        A.release()

    def mixer(half, full):
        A.mark()
        w_in = dr['w_in'].rearrange("(c p) f -> p c f", p=128)
        A.mark()
        wret = A.bf(8 * RETC)
        wretv = wret.rearrange("p (c f) -> p c f", c=8)
        c32r = A.f32(C32R_N)
        S.dma('sp', lambda e: e.dma_start(out=c32r, in_=dr['c32r']), R=[], W=['c32r'])

        def c32rv(name):
            o, n = C32R_OFF[name]
            return c32r[:, o:o + n]
        cos_t = A.f32(512)
        sin_t = A.f32(512)
        S.dma('sp', lambda e: e.dma_start(out=cos_t, in_=dr['cos'][:, half * 512:(half + 1) * 512]), R=[], W=['cos'])
        S.dma('sp', lambda e: e.dma_start(out=sin_t, in_=dr['sin'][:, half * 512:(half + 1) * 512]), R=[], W=['sin'])
        load_ptiles(['ret_gn_g', 'ret_gn_b'])
        for q4 in range(4):
            S.dma('pool', (lambda q4: lambda e: e.dma_start(out=wretv[:, :, q4 * 512:(q4 + 1) * 512], in_=w_in[:, :, q4 * 512:(q4 + 1) * 512]))(q4), R=[], W=['wret%d' % q4])
        qr = A.bf(512)
        kr = A.bf(512)
        vb = A.bf(512)
        vdb = A.bf(512)
        sgg = A.f32(512)
        t1 = A.f32(256)
        t2 = A.f32(256)
        t3 = A.f32(256)
        t4 = A.f32(256)
        qT = A.bf(512)
        qsTm = A.bf(1024)
        kTm = A.bf(1024)
        S.op('pool', lambda e: e.memset(qsTm, 0.0), R=[], W=['qsTm'])
        S.op('pool', lambda e: e.memset(kTm, 0.0), R=[], W=['kTm'])
        scm = A.bf(1024)
        o_sb = A.f32(512)
        o_sq = A.f32(512)
        gst = A.f32(64)
        mixr = A.bf(512)
        ret_deferred = []
        kdec = c32v('kdec')
        for ti in range(NT_HALF):
            c0 = XO + ti * 128
            gt = ti
            xk = ['x1T_%d' % ti]
            cosv = cos_t[:, gt * 32:(gt + 1) * 32].unsqueeze(1).broadcast_to([128, 8, 32])
            sinv = sin_t[:, gt * 32:(gt + 1) * 32].unsqueeze(1).broadcast_to([128, 8, 32])
            pb = {}
            which = ['q', 'k', 'v', 'g'] if full else ['k', 'v']
            RSUB = os.environ.get('RET_SUB', '')
            if RSUB == 'none':
                continue
            for nm in which:
                q4 = ['q', 'k', 'v', 'g'].index(nm)
                b = bank()
                pb[nm] = b
                for dc in range(8):
                    S.op('pe', (lambda dc, b, q4, c0: lambda e: e.matmul(PS(b), lhsT=x1Tv[:, dc, c0:c0 + 128], rhs=wretv[:, dc, q4 * 512:(q4 + 1) * 512], start=(dc == 0), stop=(dc == 7)))(dc, b, q4, c0),
                         R=xk + ['wret%d' % q4], W=[PK(b)], inc=(dc == 7))

            def rotary(b, dst, dkey):
                src = PS(b).rearrange("p (h d) -> p h d", h=8)
                dv = dst.rearrange("p (h d) -> p h d", h=8)
                a1 = t1.rearrange("p (h d) -> p h d", h=8)
                a2 = t2.rearrange("p (h d) -> p h d", h=8)
                a3 = t3.rearrange("p (h d) -> p h d", h=8)
                a4 = t4.rearrange("p (h d) -> p h d", h=8)
                pk = PK(b)
                S.op('dve', lambda e: e.tensor_tensor(out=a1, in0=src[:, :, 0:32], in1=cosv, op=ALU.mult), R=[pk, 'cos'], W=['rt1'])
                S.op('dve', lambda e: e.tensor_tensor(out=a2, in0=src[:, :, 32:64], in1=sinv, op=ALU.mult), R=[pk, 'sin'], W=['rt2'])
                S.op('dve', lambda e: e.tensor_tensor(out=a3, in0=src[:, :, 0:32], in1=sinv, op=ALU.mult), R=[pk, 'sin'], W=['rt3'])
                S.op('dve', lambda e: e.tensor_tensor(out=a4, in0=src[:, :, 32:64], in1=cosv, op=ALU.mult), R=[pk, 'cos'], W=['rt4'])
                S.op('pool', lambda e: e.tensor_tensor(out=dv[:, :, 0:32], in0=a1, in1=a2, op=ALU.subtract), R=['rt1', 'rt2'], W=[dkey])
                S.op('pool', lambda e: e.tensor_tensor(out=dv[:, :, 32:64], in0=a3, in1=a4, op=ALU.add), R=['rt3', 'rt4'], W=[dkey])

            for f_ in ret_deferred:
                f_()
            del ret_deferred[:]
            if RSUB == 'proj':
                continue
            rotary(pb['k'], kr, 'kr')
            if full:
                rotary(pb['q'], qr, 'qr')
            if RSUB == 'rot':
                continue
            bv = pb['v']
            S.op('act', (lambda bv: lambda e: e.activation(out=vb, in_=PS(bv), func=AF.Copy))(bv), R=[PK(bv)], W=['vb'])
            if RSUB == 'vb':
                continue
            S.op('dve', (lambda bv: lambda e: e.tensor_tensor(out=vdb, in0=PS(bv), in1=c32rv('kdecf'), op=ALU.mult))(bv), R=[PK(bv), 'c32r', 'vb'], W=['vdb'])
            RS = int(os.environ.get('RET_STOP', '99'))
            if RS <= 1:
                continue
            if full:
                bgg = pb['g']
                S.op('act', (lambda bgg: lambda e: e.activation(out=sgg, in_=PS(bgg), func=AF.Silu))(bgg), R=[PK(bgg)], W=['sgg'])
                b = bank()
                pk = PK(b)
                for i in range(4):
                    S.op('pe', (lambda i, b: lambda e: e.transpose(out=PSB(b)[:, i * 128:(i + 1) * 128], in_=qr[:, i * 128:(i + 1) * 128], identity=ident))(i, b), R=['qr', 'cb'], W=[pk], inc=False)
                for i in range(4):
                    S.op('pe', (lambda i, b: lambda e: e.transpose(out=PSB(b)[:, 512 + i * 128:512 + (i + 1) * 128], in_=kr[:, i * 128:(i + 1) * 128], identity=ident))(i, b), R=['kr', 'cb'], W=[pk], inc=(i == 3))
                S.op('act', (lambda b: lambda e: e.activation(out=qT, in_=PSB(b)[:, 0:512], func=AF.Copy))(b), R=[pk], W=['qT'])
                for hp in range(2):
                    rs_ = slice(hp * 64, (hp + 1) * 64)
                    S.op('dve', (lambda b, hp, rs_: lambda e: e.tensor_tensor(out=qsTm[rs_, :].rearrange("p (a c t) -> p a c t", a=4, c=2)[:, :, hp, :],
                                                                             in0=PSB(b)[rs_, 0:512].rearrange("p (a t) -> p a t", a=4),
                                                                             in1=c32rv('qdec')[rs_, :].rearrange("p (a t) -> p a t", a=4), op=ALU.mult))(b, hp, rs_), R=[pk, 'c32r'], W=['qsTm'])
                    S.op('act', (lambda b, hp, rs_: lambda e: e.activation(out=kTm[rs_, :].rearrange("p (a c t) -> p a c t", a=4, c=2)[:, :, hp, :],
                                                                          in_=PSB(b)[rs_, 512:1024].rearrange("p (a t) -> p a t", a=4), func=AF.Copy))(b, hp, rs_), R=[pk], W=['kTm'])
                if RS <= 2:
                    continue
                sb_ = [bank(), bank()]
                for h in range(8):
                    pr, hp = h // 2, h % 2
                    b = sb_[h // 4]
                    S.op('pe', (lambda h, pr, hp, b: lambda e: e.matmul(PS(b)[:, (h % 4) * 128:(h % 4 + 1) * 128], lhsT=kTm[:, h * 128:(h + 1) * 128],
                                                                       rhs=qT[:, pr * 128:(pr + 1) * 128], start=True, stop=True))(h, pr, hp, b),
                         R=['kTm', 'qT'], W=[PK(b)], inc=(h % 4 == 3))
                if RS == 24:
                    continue
                for hb in range(2):
                    b = sb_[hb]
                    S.op('dve', (lambda hb, b: lambda e: e.tensor_tensor(out=scm[:, hb * 512:(hb + 1) * 512], in0=PS(b), in1=c32rv('dmask')[:, hb * 512:(hb + 1) * 512], op=ALU.mult))(hb, b),
                         R=[PK(b), 'c32r'], W=['scm%d' % hb])
                if RS == 25:
                    continue
                bo = bank()
                for h in range(8):
                    pr, hp = h // 2, h % 2
                    S.op('pe', (lambda h, bo: lambda e: e.matmul(PS(bo)[:, h * 64:(h + 1) * 64], lhsT=scm[:, h * 128:(h + 1) * 128], rhs=vb[:, h * 64:(h + 1) * 64], start=True, stop=(RS == 26)))(h, bo),
                         R=['scm%d' % (h // 4), 'vb'], W=[PK(bo)], inc=(RS == 26 and h == 7))
                    if RS == 26:
                        continue
                    S.op('pe', (lambda h, pr, hp, bo: lambda e: e.matmul(PS(bo)[:, h * 64:(h + 1) * 64], lhsT=qsTm[:, h * 128:(h + 1) * 128],
                                                                        rhs=retSb[:, pr * 64:(pr + 1) * 64], start=False, stop=True))(h, pr, hp, bo),
                         R=['qsTm', 'retSb'], W=[PK(bo)], inc=(h == 7))
                S.op('act', (lambda bo: lambda e: e.activation(out=o_sb, in_=PS(bo), func=AF.Copy))(bo), R=[PK(bo)], W=['o_sb'])
            if RS <= 3:
                continue
            bk_ = bank()
            for pr in range(4):
                S.op('pe', (lambda pr, bk_: lambda e: e.matmul(PS(bk_)[:, pr * 128:(pr + 1) * 128], lhsT=kr[:, pr * 128:(pr + 1) * 128], rhs=vdb[:, pr * 128:(pr + 1) * 128], start=True, stop=True))(pr, bk_),
                     R=['kr', 'vdb'], W=[PK(bk_)], inc=(pr == 3))
            rs3 = retS.rearrange("p (a e) -> p a e", a=4)
            S.op('pool', lambda e: e.tensor_tensor(out=retS, in0=retS, in1=c32rv('gam128f'), op=ALU.mult), R=['retS', 'c32r', 'retSb'], W=['retS'])
            for hp in range(2):
                S.op('dve', (lambda hp, bk_: lambda e: e.tensor_tensor(out=retS[hp * 64:(hp + 1) * 64, :].rearrange("p (a e) -> p a e", a=4), in0=retS[hp * 64:(hp + 1) * 64, :].rearrange("p (a e) -> p a e", a=4),
                                                                    in1=PS(bk_)[hp * 64:(hp + 1) * 64, :].rearrange("p (a f) -> p a f", a=4)[:, :, hp * 64:(hp + 1) * 64], op=ALU.add))(hp, bk_),
                     R=[PK(bk_), 'retS'], W=['retS'])
            S.op('act', lambda e: e.activation(out=retSb, in_=retS, func=AF.Copy), R=['retS'], W=['retSb'])
            if full and ti == 0:
                dump('retS1', retS, ['retS'])
            if full and ti == 1:
                dump('retS2', retS, ['retS'])
                dump('kr1', kr, ['kr'])
            if RS <= 4:
                continue
            if full:
                group_norm_out(o_sb, 'o_sb', o_sq, 'o_sq', gst, 1e-5, ptile['ret_gn_g'], ptile['ret_gn_b'], 'pt_ret_gn_g', 'pt_ret_gn_b')
                S.op('dve', lambda e: e.tensor_tensor(out=mixr, in0=o_sb, in1=sgg, op=ALU.mult), R=['o_sb', 'sgg'], W=['mixr'])
                ret_deferred.append((lambda ti: lambda: transpose_to(mixr, ['mixr'], 4, mixTv[:, 0:4, ti * 128:(ti + 1) * 128], ['mixT_r%d' % ti]))(ti))
        for f_ in ret_deferred:
            f_()
        A.release()
        S.barrier()
        if os.environ.get('MIX_STOP') != 'ret':
            rwkv(half, full)
        if full:
            for c_ in range(8):
                dump('mixT%d' % c_, mixTv[:, c_, :], [])
            dump('retS', retS, [])
            dump('rwA', rwA, [])
        A.release()

    def group_norm_out(o, okey, sq, sqk, gst, eps, gt, bt, gk, bk):
        o3 = o.rearrange("p (h d) -> p h d", h=8)
        S.op('act', lambda e: e.activation(out=sq, in_=o, func=AF.Square), R=[okey], W=[sqk])
        S.op('dve', lambda e: e.tensor_reduce(out=gst[:, 0:8], in_=o3, axis=AX.X, op=ALU.add), R=[okey], W=['gn0'])
        S.op('dve', lambda e: e.tensor_reduce(out=gst[:, 8:16], in_=sq.rearrange("p (h d) -> p h d", h=8), axis=AX.X, op=ALU.add), R=[sqk], W=['gn1'])
        S.op('dve', lambda e: e.tensor_scalar(out=gst[:, 16:24], in0=gst[:, 0:8], scalar1=1.0 / 64, scalar2=None, op0=ALU.mult), R=['gn0'], W=['gn2'])
        S.op('dve', lambda e: e.tensor_tensor(out=gst[:, 24:32], in0=gst[:, 16:24], in1=gst[:, 16:24], op=ALU.mult), R=['gn2'], W=['gn3'])
        S.op('dve', lambda e: e.scalar_tensor_tensor(out=gst[:, 32:40], in0=gst[:, 8:16], scalar=1.0 / 64, in1=gst[:, 24:32], op0=ALU.mult, op1=ALU.subtract), R=['gn1', 'gn3'], W=['gn4'])
        S.op('dve', lambda e: e.tensor_scalar(out=gst[:, 32:40], in0=gst[:, 32:40], scalar1=float(eps), scalar2=None, op0=ALU.add), R=['gn4'], W=['gn4'])
        S.op('act', lambda e: e.activation(out=gst[:, 40:48], in_=gst[:, 32:40], func=AF.Sqrt), R=['gn4'], W=['gn5'])
        S.op('dve', lambda e: e.reciprocal(out=gst[:, 48:56], in_=gst[:, 40:48]), R=['gn5'], W=['gn6'])
        S.op('dve', lambda e: e.tensor_tensor(out=o3, in0=o3, in1=gst[:, 16:24].unsqueeze(2).broadcast_to([128, 8, 64]), op=ALU.subtract), R=[okey, 'gn2'], W=[okey])
        S.op('dve', lambda e: e.tensor_tensor(out=o3, in0=o3, in1=gst[:, 48:56].unsqueeze(2).broadcast_to([128, 8, 64]), op=ALU.mult), R=[okey, 'gn6'], W=[okey])
        S.op('pool', lambda e: e.tensor_tensor(out=o, in0=o, in1=gt, op=ALU.mult), R=[okey, gk], W=[okey])
        S.op('pool', lambda e: e.tensor_tensor(out=o, in0=o, in1=bt, op=ALU.add), R=[okey, bk], W=[okey])

    def rwkv(half, full):
        A.mark()
        w_in = dr['w_in'].rearrange("(c p) f -> p c f", p=128)
        load_ptiles(['rw_k_k', 'rw_k_a', 'rw_r_k', 'rw_gn_g', 'rw_gn_b'])
        Wa = A.bf(8 * RWC)
        Wb = A.bf(8 * RWC)
        Wav = Wa.rearrange("p (c f) -> p c f", c=8)
        Wbv = Wb.rearrange("p (c f) -> p c f", c=8)
        NW_ = 8 * RWC
        if half == 0 or 'mix_0' not in plan:
            A.mark()
            mu_t = A.f32(RWC)
            omu_t = A.f32(RWC)
            stg = [A.f32(RWC) for _ in range(2)]
            S.dma('sp', lambda e: e.dma_start(out=mu_t, in_=dr['rw_mu'].partition_broadcast(128)), R=[], W=['mu'])
            S.op('dve', lambda e: e.tensor_scalar(out=omu_t, in0=mu_t, scalar1=-1.0, scalar2=1.0, op0=ALU.mult, op1=ALU.add), R=['mu'], W=['omu'])
            for dc in range(8):
                sgb = stg[dc % 2]
                sk = 'stg%d' % (dc % 2)
                S.dma('sp', lambda e: e.dma_start(out=sgb, in_=dr['w_in'][dc * 128:(dc + 1) * 128, RETC:INC]), R=[], W=[sk])
                S.op('dve', lambda e: e.tensor_tensor(out=Wav[:, dc, :], in0=sgb, in1=omu_t, op=ALU.mult), R=[sk, 'omu'], W=['Wa'])
                S.op('pool', lambda e: e.tensor_tensor(out=Wbv[:, dc, :], in0=sgb, in1=mu_t, op=ALU.mult), R=[sk, 'mu'], W=['Wb'])
            for q_ in range(4):
                S.dma('sp', lambda e: e.dma_start(out=wab_s[:, q_ * (NW_ // 4):(q_ + 1) * (NW_ // 4)], in_=Wa[:, q_ * (NW_ // 4):(q_ + 1) * (NW_ // 4)]), R=['Wa'], W=['wabs_a%d' % q_], sk='wabs')
                S.dma('sp', lambda e: e.dma_start(out=wab_s[:, NW_ + q_ * (NW_ // 4):NW_ + (q_ + 1) * (NW_ // 4)], in_=Wb[:, q_ * (NW_ // 4):(q_ + 1) * (NW_ // 4)]), R=['Wb'], W=['wabs_b%d' % q_], sk='wabs')
            A.release()
        else:
            for q_ in range(4):
                S.dma('sp', lambda e: e.dma_start(out=Wa[:, q_ * (NW_ // 4):(q_ + 1) * (NW_ // 4)], in_=wab_s[:, q_ * (NW_ // 4):(q_ + 1) * (NW_ // 4)]), R=[], W=['Wa'], sk='wa_ld%d' % q_)
                S.dma('sp', lambda e: e.dma_start(out=Wb[:, q_ * (NW_ // 4):(q_ + 1) * (NW_ // 4)], in_=wab_s[:, NW_ + q_ * (NW_ // 4):NW_ + (q_ + 1) * (NW_ // 4)]), R=[], W=['Wb'], sk='wb_ld%d' % q_)
        wup = A.f32(512)
        aup = A.f32(512)
        gup1 = A.bf(512)
        gup2 = A.bf(512)
        w0r = A.f32(512)
        a0r = A.f32(512)
        for t_, k_ in ((wup, 'wup'), (aup, 'aup'), (gup2, 'gup2'), (w0r, 'w0r'), (a0r, 'a0r')):
            S.op('pool', (lambda t_: lambda e: e.memset(t_, 0.0))(t_), R=[], W=[k_])
        S.dma('sp', lambda e: e.dma_start(out=wup[0:64, :], in_=dr['rw_w_up']), R=[], W=['wup'])
        S.dma('sp', lambda e: e.dma_start(out=aup[64:128, :], in_=dr['rw_a_up']), R=[], W=['aup'])
        S.dma('pool', lambda e: e.dma_start(out=gup1, in_=dr['rw_g_up'][0:128, :]), R=[], W=['gup1'])
        S.dma('pool', lambda e: e.dma_start(out=gup2[96:128, :], in_=dr['rw_g_up'][128:160, :]), R=[], W=['gup2'])
        S.dma('sp', lambda e: e.dma_start(out=w0r[0:1, :], in_=dr['rw_w0']), R=[], W=['w0r'])
        S.dma('sp', lambda e: e.dma_start(out=a0r[0:1, :], in_=dr['rw_a0']), R=[], W=['a0r'])
        ones_r = c32v('ones')
        twT = A.f32(128)
        sgd1 = A.bf(128)
        sgd2 = A.bf(128)
        sw = A.f32(512)
        a_t = A.f32(512)
        gi_t = A.f32(512)
        kkr = A.f32(512)
        sq = A.f32(512)
        kp = A.f32(512)
        u1 = sq
        g_t = sw
        v32 = A.f32(512)
        st = A.f32(64)
        gst2 = A.f32(64)
        r32 = A.f32(512)
        k32 = A.f32(512)
        gp_t = k32
        gC = A.f32(4)
        rh = A.bf(512)
        kt = A.bf(512)
        bt_ = A.bf(512)
        kh = A.bf(512)
        vb = A.bf(512)
        XT = A.bf(4 * 256)
        XTv = XT.rearrange("p (a b t) -> p a b t", a=4, b=2)
        btT = A.bf(512)
        khTm = A.bf(1024)
        rhTm = A.bf(1024)
        btTm = A.bf(1024)
        ktTm = A.bf(1024)
        for t_, k_ in ((khTm, 'khTm'), (rhTm, 'rhTm'), (btTm, 'btTm'), (ktTm, 'ktTm')):
            S.op('pool', (lambda t_: lambda e: e.memset(t_, 0.0))(t_), R=[], W=[k_])
        LkT = A.bf(1024)
        MbT = A.bf(1024)
        MkT = A.bf(1024)
        Abuf = [[A.bf(256) for _ in range(2)] for _ in range(4)]
        BQbuf = [[A.bf(512) for _ in range(2)] for _ in range(4)]
        TT = A.bf(1024)
        Xb = A.bf(512)
        Ub = A.bf(512)
        mixw = A.bf(512)
        bsum = st[:, 56:64]
        rw_deferred = []

        for ti in range(NT_HALF):
            c0 = XO + ti * 128
            xk = ['x1T_%d' % ti] + (['x1T_%d' % (ti - 1)] if ti > 0 else ['x1T_prev'])
            def proj(nm):
                qi = ['r', 'k', 'v'].index(nm)
                b = bank()
                n = 0
                for (Wv, wkey, sh) in ((Wav, 'Wa', 0), (Wbv, 'Wb', 1)):
                    for dc in range(8):
                        S.op('pe', (lambda Wv, sh, dc, b, qi, n: lambda e: e.matmul(PS(b), lhsT=x1Tv[:, dc, c0 - sh:c0 - sh + 128], rhs=Wv[:, dc, qi * 512:(qi + 1) * 512], start=(n == 0), stop=(n == 15)))(Wv, sh, dc, b, qi, n),
                             R=xk + [wkey], W=[PK(b)], inc=(n == 15))
                        n += 1
                return b
            bl = bank()
            lora_groups = ((1536, 128), (1664, 128), (1696, 128)) if full else ((1536, 128),)
            for gi_, (cs, cn) in enumerate(lora_groups):
                n = 0
                for (Wv, wkey, sh) in ((Wav, 'Wa', 0), (Wbv, 'Wb', 1)):
                    for dc in range(8):
                        S.op('pe', (lambda Wv, sh, dc, cs, cn, gi_, n: lambda e: e.matmul(PS(bl)[0:cn, gi_ * 128:(gi_ + 1) * 128], lhsT=Wv[:, dc, cs:cs + cn], rhs=x1Tv[:, dc, c0 - sh:c0 - sh + 128], start=(n == 0), stop=(n == 15)))(Wv, sh, dc, cs, cn, gi_, n),
                             R=xk + [wkey], W=[PK(bl)], inc=(n == 15 and gi_ == len(lora_groups) - 1))
                        n += 1
            plk = PK(bl)
            S.op('act', lambda e: e.activation(out=twT[0:64, :], in_=PS(bl)[0:64, 0:128], func=AF.Tanh), R=[plk], W=['twT'])
            S.op('act', lambda e: e.activation(out=twT[64:128, :], in_=PS(bl)[64:128, 0:128], func=AF.Copy), R=[plk], W=['twT'])
            if full:
                S.op('act', lambda e: e.activation(out=sgd1, in_=PS(bl)[:, 128:256], func=AF.Sigmoid), R=[plk], W=['sgd1'])
                S.op('act', lambda e: e.activation(out=sgd2, in_=PS(bl)[:, 256:384], func=AF.Sigmoid), R=[plk], W=['sgd2'])
            bk2 = proj('k')
            kk_ = PK(bk2)
            S.op('act', lambda e: e.activation(out=k32, in_=PS(bk2), func=AF.Copy), R=[kk_], W=['k32'])
            bw = bank()
            S.op('pe', lambda e: e.matmul(PS(bw), lhsT=twT, rhs=wup, start=True, stop=False), R=['twT', 'wup'], W=[PK(bw)], inc=False)
            S.op('pe', lambda e: e.matmul(PS(bw), lhsT=ones_r, rhs=w0r, start=False, stop=True), R=['c32', 'w0r'], W=[PK(bw)])
            ba = bank()
            S.op('pe', lambda e: e.matmul(PS(ba), lhsT=twT, rhs=aup, start=True, stop=False), R=['twT', 'aup'], W=[PK(ba)], inc=False)
            S.op('pe', lambda e: e.matmul(PS(ba), lhsT=ones_r, rhs=a0r, start=False, stop=True), R=['c32', 'a0r'], W=[PK(ba)])
            S.op('act', lambda e: e.activation(out=sw, in_=PS(bw), func=AF.Sigmoid), R=[PK(bw)], W=['sw'])
            S.op('act', lambda e: e.activation(out=a_t, in_=PS(ba), func=AF.Sigmoid), R=[PK(ba)], W=['a_t'])
            bv = proj('v')
            vk_ = PK(bv)
            S.op('act', lambda e: e.activation(out=v32, in_=PS(bv), func=AF.Copy), R=[vk_], W=['v32'])
            S.op('dve', lambda e: e.tensor_copy(out=vb, in_=PS(bv)), R=[vk_], W=['vbw'])
            bc = bank()
            S.op('pe', lambda e: e.matmul(PS(bc), lhsT=c32v('tri_incl'), rhs=sw, start=True, stop=True), R=['c32', 'sw'], W=[PK(bc)])
            bx = bank()
            S.op('pe', lambda e: e.matmul(PS(bx), lhsT=c32v('tri_strict'), rhs=sw, start=True, stop=True), R=['c32', 'sw'], W=[PK(bx)])
            bgc = bank()
            for pr in range(4):
                S.op('pe', (lambda pr: lambda e: e.matmul(PS(bgc)[:, pr:pr + 1], lhsT=sw[:, pr * 128:(pr + 1) * 128], rhs=c32v('negcol'), start=True, stop=True))(pr), R=['sw', 'c32'], W=[PK(bgc)], inc=(pr == 3))
            if full:
                br = proj('r')
                rk_ = PK(br)
                S.op('act', lambda e: e.activation(out=r32, in_=PS(br), func=AF.Copy), R=[rk_], W=['r32'])
            for f_ in rw_deferred:
                f_()
            del rw_deferred[:]
            S.op('dve', lambda e: e.tensor_tensor(out=kkr, in0=k32, in1=ptile['rw_k_k'], op=ALU.mult), R=['k32', 'pt_rw_k_k'], W=['kkr'])
            S.op('act', lambda e: e.activation(out=sq, in_=kkr, func=AF.Square), R=['kkr'], W=['sq'])
            S.op('dve', lambda e: e.tensor_reduce(out=st[:, 0:8], in_=sq.rearrange("p (h d) -> p h d", h=8), axis=AX.X, op=ALU.add), R=['sq'], W=['st0'])
            S.op('act', lambda e: e.activation(out=st[:, 8:16], in_=st[:, 0:8], func=AF.Sqrt), R=['st0'], W=['st1'])
            S.op('dve', lambda e: e.tensor_scalar(out=st[:, 8:16], in0=st[:, 8:16], scalar1=1e-12, scalar2=None, op0=ALU.max), R=['st1'], W=['st1'])
            S.op('dve', lambda e: e.reciprocal(out=st[:, 16:24], in_=st[:, 8:16]), R=['st1'], W=['st2'])
            S.op('dve', lambda e: e.tensor_tensor(out=kkr.rearrange("p (h d) -> p h d", h=8), in0=kkr.rearrange("p (h d) -> p h d", h=8), in1=st[:, 16:24].unsqueeze(2).broadcast_to([128, 8, 64]), op=ALU.mult), R=['kkr', 'st2'], W=['kkr'])
            S.op('dve', lambda e: e.scalar_tensor_tensor(out=u1, in0=a_t, scalar=-1.0, in1=ptile['rw_k_a'], op0=ALU.add, op1=ALU.mult), R=['a_t', 'pt_rw_k_a'], W=['sq'])
            S.op('dve', lambda e: e.scalar_tensor_tensor(out=kp, in0=u1, scalar=1.0, in1=k32, op0=ALU.add, op1=ALU.mult), R=['sq', 'k32'], W=['kp'])
            if full:
                S.op('dve', lambda e: e.tensor_tensor(out=u1, in0=r32, in1=kp, op=ALU.mult), R=['r32', 'kp', 'sq'], W=['sq'])
                S.op('pool', lambda e: e.tensor_tensor(out=u1, in0=u1, in1=ptile['rw_r_k'], op=ALU.mult), R=['sq', 'pt_rw_r_k'], W=['sq'])
                S.op('dve', lambda e: e.tensor_reduce(out=bsum, in_=u1.rearrange("p (h d) -> p h d", h=8), axis=AX.X, op=ALU.add), R=['sq'], W=['bsum'])
            S.op('act', lambda e: e.activation(out=g_t, in_=PS(bc), func=AF.Exp), R=[PK(bc)], W=['sw'])
            S.op('act', lambda e: e.activation(out=gi_t, in_=PS(bc), func=AF.Exp, scale=-1.0), R=[PK(bc)], W=['gi_t'])
            S.op('act', lambda e: e.activation(out=gp_t, in_=PS(bx), func=AF.Exp), R=[PK(bx)], W=['k32'])
            S.op('act', lambda e: e.activation(out=gC, in_=PS(bgc)[:, 0:4], func=AF.Exp), R=[PK(bgc)], W=['gC'])
            if full:
                S.op('dve', lambda e: e.tensor_tensor(out=rh, in0=r32, in1=g_t, op=ALU.mult), R=['r32', 'sw'], W=['rh'])
            S.op('pool', lambda e: e.tensor_tensor(out=kt, in0=kp, in1=gi_t, op=ALU.mult), R=['kp', 'gi_t'], W=['kt'])
            S.op('pool', lambda e: e.tensor_tensor(out=sq, in0=kkr, in1=a_t, op=ALU.mult), R=['kkr', 'a_t', 'sq'], W=['sq'])
            S.op('pool', lambda e: e.tensor_tensor(out=bt_, in0=sq, in1=gi_t, op=ALU.mult), R=['sq', 'gi_t'], W=['bt'])
            S.op('pool', lambda e: e.tensor_tensor(out=kh, in0=kkr, in1=gp_t, op=ALU.mult), R=['kkr', 'k32'], W=['kh'])
            if full:
                S.op('dve', lambda e: e.tensor_tensor(out=gi_t.rearrange("p (h d) -> p h d", h=8), in0=v32.rearrange("p (h d) -> p h d", h=8), in1=bsum.unsqueeze(2).broadcast_to([128, 8, 64]), op=ALU.mult), R=['v32', 'bsum', 'gi_t'], W=['gi_t'])
            b1 = bank()
            for i in range(4):
                S.op('pe', (lambda i: lambda e: e.transpose(out=PSB(b1)[:, i * 256:i * 256 + 128], in_=kh[:, i * 128:(i + 1) * 128], identity=ident))(i), R=['kh', 'cb'], W=[PK(b1)], inc=(i == 3 and not full))
                if full:
                    S.op('pe', (lambda i: lambda e: e.transpose(out=PSB(b1)[:, i * 256 + 128:i * 256 + 256], in_=rh[:, i * 128:(i + 1) * 128], identity=ident))(i), R=['rh', 'cb'], W=[PK(b1)], inc=(i == 3))
            if full:
                S.op('act', lambda e: e.activation(out=XT, in_=PSB(b1), func=AF.Copy), R=[PK(b1)], W=['XT'])
            else:
                S.op('act', lambda e: e.activation(out=XT.rearrange("p (a c t) -> p a c t", a=4, c=2)[:, :, 0, :], in_=PSB(b1).rearrange("p (a c t) -> p a c t", a=4, c=2)[:, :, 0, :], func=AF.Copy), R=[PK(b1)], W=['XT'])
            for hp in range(2):
                rs_ = slice(hp * 64, (hp + 1) * 64)
                srcv = PSB(b1)[rs_, :].rearrange("p (a c t) -> p a c t", a=4, c=2)
                S.op('act', lambda e: e.activation(out=khTm[rs_, :].rearrange("p (a c t) -> p a c t", a=4, c=2)[:, :, hp, :], in_=srcv[:, :, 0, :], func=AF.Copy), R=[PK(b1)], W=['khTm'])
                if full:
                    S.op('dve', lambda e: e.tensor_copy(out=rhTm[rs_, :].rearrange("p (a c t) -> p a c t", a=4, c=2)[:, :, hp, :], in_=srcv[:, :, 1, :]), R=[PK(b1)], W=['rhTm'])
            b2 = bank()
            for i in range(4):
                S.op('pe', (lambda i: lambda e: e.transpose(out=PSB(b2)[:, i * 128:(i + 1) * 128], in_=bt_[:, i * 128:(i + 1) * 128], identity=ident))(i), R=['bt', 'cb'], W=[PK(b2)], inc=False)
            for i in range(4):
                S.op('pe', (lambda i: lambda e: e.transpose(out=PSB(b2)[:, 512 + i * 128:512 + (i + 1) * 128], in_=kt[:, i * 128:(i + 1) * 128], identity=ident))(i), R=['kt', 'cb'], W=[PK(b2)], inc=(i == 3))
            S.op('dve', lambda e: e.tensor_copy(out=btT, in_=PSB(b2)[:, 0:512]), R=[PK(b2)], W=['btT'])
            for hp in range(2):
                rs_ = slice(hp * 64, (hp + 1) * 64)
                S.op('act', lambda e: e.activation(out=btTm[rs_, :].rearrange("p (a c t) -> p a c t", a=4, c=2)[:, :, hp, :], in_=PSB(b2)[rs_, 0:512].rearrange("p (a t) -> p a t", a=4), func=AF.Copy), R=[PK(b2)], W=['btTm'])
                S.op('dve', lambda e: e.tensor_copy(out=ktTm[rs_, :].rearrange("p (a c t) -> p a c t", a=4, c=2)[:, :, hp, :], in_=PSB(b2)[rs_, 512:1024].rearrange("p (a t) -> p a t", a=4)), R=[PK(b2)], W=['ktTm'])
            for pr in range(4):
                bm1, bm2, bm3 = bank(), bank(), bank()
                for hp in range(2):
                    ps_ = slice(hp * 64, (hp + 1) * 64)
                    NX_ = 256 if full else 128
                    rhs_x = XT[:, pr * 256:pr * 256 + NX_]
                    hh_ = pr * 2 + hp
                    S.op('pe', (lambda hp, ps_, rhs_x, bm1: lambda e: e.matmul(PS(bm1)[:, hp * 256:hp * 256 + NX_], lhsT=btTm[:, hh_ * 128:(hh_ + 1) * 128], rhs=rhs_x, start=True, stop=True))(hp, ps_, rhs_x, bm1),
                         R=['btTm', 'XT'], W=[PK(bm1)], inc=(hp == 1))
                    S.op('pe', (lambda hp, ps_, rhs_x, bm2: lambda e: e.matmul(PS(bm2)[:, hp * 256:hp * 256 + NX_], lhsT=ktTm[:, hh_ * 128:(hh_ + 1) * 128], rhs=rhs_x, start=True, stop=True))(hp, ps_, rhs_x, bm2),
                         R=['ktTm', 'XT'], W=[PK(bm2)], inc=(hp == 1))
                    S.op('pe', (lambda hp, ps_, bm3: lambda e: e.matmul(PS(bm3)[:, hp * 128:(hp + 1) * 128], lhsT=khTm[:, hh_ * 128:(hh_ + 1) * 128], rhs=btT[:, pr * 128:(pr + 1) * 128], start=True, stop=True))(hp, ps_, bm3),
                         R=['btT', 'khTm'], W=[PK(bm3)], inc=(hp == 1))
                A0 = Abuf[pr][0]
                BQ0 = BQbuf[pr][0].rearrange("p (h x) -> p h x", h=2)
                m1v = cbv('m1').rearrange("p (h x) -> p h x", h=2)
                p1v = PS(bm1).rearrange("p (h x) -> p h x", h=2)
                S.op('dve', (lambda BQ0, p1v, m1v: lambda e: e.tensor_tensor(out=BQ0[:, :, 0:128], in0=p1v[:, :, 0:128], in1=m1v[:, :, 0:128], op=ALU.mult))(BQ0, p1v, m1v),
                     R=[PK(bm1), 'cb'], W=['BQ%d_0' % pr])
                S.op('pool', (lambda BQ0: lambda e: e.tensor_copy(out=BQ0[:, :, 128:256], in_=ident.unsqueeze(1).broadcast_to([128, 2, 128])))(BQ0), R=['cb'], W=['BQ%d_0' % pr])
                if full:
                    S.op('dve', (lambda p1v, m1v: lambda e: e.tensor_tensor(out=MbT[:, pr * 256:(pr + 1) * 256].rearrange("p (h x) -> p h x", h=2), in0=p1v[:, :, 128:256], in1=m1v[:, :, 128:256], op=ALU.mult))(p1v, m1v),
                         R=[PK(bm1), 'cb'], W=['MbT%d' % pr])
                m2v = cbv('m2').rearrange("p (h x) -> p h x", h=2)
                p2v = PS(bm2).rearrange("p (h x) -> p h x", h=2)
                S.op('dve', (lambda p2v, m2v: lambda e: e.tensor_tensor(out=LkT[:, pr * 256:(pr + 1) * 256].rearrange("p (h x) -> p h x", h=2), in0=p2v[:, :, 0:128], in1=m2v[:, :, 0:128], op=ALU.mult))(p2v, m2v),
                     R=[PK(bm2), 'cb'], W=['LkT%d' % pr])
                if full:
                    S.op('dve', (lambda p2v, m2v: lambda e: e.tensor_tensor(out=MkT[:, pr * 256:(pr + 1) * 256].rearrange("p (h x) -> p h x", h=2), in0=p2v[:, :, 128:256], in1=m2v[:, :, 128:256], op=ALU.mult))(p2v, m2v),
                         R=[PK(bm2), 'cb'], W=['MkT%d' % pr])
                S.op('dve', (lambda A0, bm3: lambda e: e.tensor_tensor(out=A0, in0=PS(bm3)[:, 0:256], in1=cbv('m3'), op=ALU.mult))(A0, bm3), R=[PK(bm3), 'cb'], W=['A%d_0' % pr])
            for lv in range(7):
                last = (lv == 6)
                for pr in range(4):
                    cur, nxt = lv % 2, (lv + 1) % 2
                    Ac = Abuf[pr][cur].rearrange("p (h x) -> p h x", h=2)
                    BQc = BQbuf[pr][cur].rearrange("p (h x) -> p h x", h=2)
                    ak, bqk = 'A%d_%d' % (pr, cur), 'BQ%d_%d' % (pr, cur)
                    if not last:
                        bA = bank()
                        for hp in range(2):
                            S.op('pe', (lambda hp, BQc, Ac, bA: lambda e: e.matmul(PS(bA)[:, hp * 128:(hp + 1) * 128], lhsT=BQc[:, hp, 0:128], rhs=Ac[:, hp, :], start=True, stop=True))(hp, BQc, Ac, bA),
                                 R=[ak, bqk], W=[PK(bA)], inc=(hp == 1))
                        bB = bank()
                        for hp in range(2):
                            S.op('pe', (lambda hp, BQc, Ac, bB: lambda e: e.matmul(PS(bB)[:, hp * 256:hp * 256 + 256], lhsT=Ac[:, hp, :], rhs=BQc[:, hp, :], start=True, stop=False, skip_group_check=True))(hp, BQc, Ac, bB),
                                 R=[ak, bqk], W=[PK(bB)], inc=False)
                            S.op('pe', (lambda hp, BQc, bB: lambda e: e.matmul(PS(bB)[:, hp * 256 + 128:hp * 256 + 256], lhsT=ident, rhs=BQc[:, hp, 128:256], start=False, stop=True, skip_group_check=True))(hp, BQc, bB),
                                 R=[bqk, 'cb'], W=[PK(bB)], inc=(hp == 1))
                        An = Abuf[pr][nxt]
                        BQn = BQbuf[pr][nxt]
                        S.op('dve' if pr % 2 == 0 else 'act', (lambda An, bA, pr: (lambda e: e.tensor_copy(out=An, in_=PS(bA)[:, 0:256])) if pr % 2 == 0 else (lambda e: e.activation(out=An, in_=PS(bA)[:, 0:256], func=AF.Copy)))(An, bA, pr),
                             R=[PK(bA)], W=['A%d_%d' % (pr, nxt)])
                        S.op('dve' if pr % 2 else 'act', (lambda BQn, bB, pr: (lambda e: e.tensor_copy(out=BQn, in_=PS(bB))) if pr % 2 else (lambda e: e.activation(out=BQn, in_=PS(bB), func=AF.Copy)))(BQn, bB, pr),
                             R=[PK(bB)], W=['BQ%d_%d' % (pr, nxt)])
                    else:
                        bB = bank()
                        for hp in range(2):
                            S.op('pe', (lambda hp, BQc, Ac, bB: lambda e: e.matmul(PS(bB)[:, hp * 128:(hp + 1) * 128], lhsT=Ac[:, hp, :], rhs=BQc[:, hp, 128:256], start=True, stop=False))(hp, BQc, Ac, bB),
                                 R=[ak, bqk], W=[PK(bB)], inc=False)
                            S.op('pe', (lambda hp, BQc, bB: lambda e: e.matmul(PS(bB)[:, hp * 128:(hp + 1) * 128], lhsT=ident, rhs=BQc[:, hp, 128:256], start=False, stop=True))(hp, BQc, bB),
                                 R=[bqk, 'cb'], W=[PK(bB)], inc=(hp == 1))
                        S.op('dve', (lambda bB, pr: lambda e: e.tensor_copy(out=TT[:, pr * 256:(pr + 1) * 256], in_=PS(bB)[:, 0:256]))(bB, pr), R=[PK(bB)], W=['TT%d' % pr])
            bX = bank()
            for h in range(8):
                pr, hp = h // 2, h % 2
                ps_ = slice(hp * 64, (hp + 1) * 64)
                S.op('pe', (lambda h, pr, ps_: lambda e: e.matmul(PS(bX)[:, h * 64:(h + 1) * 64], lhsT=khTm[:, h * 128:(h + 1) * 128], rhs=rwAb[:, pr * 64:(pr + 1) * 64], start=True, stop=False))(h, pr, ps_),
                     R=['khTm', 'rwAb'], W=[PK(bX)], inc=False)
                S.op('pe', (lambda h: lambda e: e.matmul(PS(bX)[:, h * 64:(h + 1) * 64], lhsT=LkT[:, h * 128:(h + 1) * 128], rhs=vb[:, h * 64:(h + 1) * 64], start=False, stop=True))(h),
                     R=['LkT%d' % pr, 'vbw'], W=[PK(bX)], inc=(h == 7))
            S.op('act', lambda e: e.activation(out=Xb, in_=PS(bX), func=AF.Copy, scale=-1.0), R=[PK(bX)], W=['Xb'])
            bU = bank()
            for h in range(8):
                S.op('pe', (lambda h: lambda e: e.matmul(PS(bU)[:, h * 64:(h + 1) * 64], lhsT=TT[:, h * 128:(h + 1) * 128], rhs=Xb[:, h * 64:(h + 1) * 64], start=True, stop=True))(h),
                     R=['TT%d' % (h // 2), 'Xb'], W=[PK(bU)], inc=(h == 7))
            S.op('act', lambda e: e.activation(out=Ub, in_=PS(bU), func=AF.Copy), R=[PK(bU)], W=['Ub'])
            if full:
                bY = bank()
                for h in range(8):
                    pr, hp = h // 2, h % 2
                    ps_ = slice(hp * 64, (hp + 1) * 64)
                    S.op('pe', (lambda h, pr, ps_: lambda e: e.matmul(PS(bY)[:, h * 64:(h + 1) * 64], lhsT=rhTm[:, h * 128:(h + 1) * 128], rhs=rwAb[:, pr * 64:(pr + 1) * 64], start=True, stop=False))(h, pr, ps_),
                         R=['rhTm', 'rwAb'], W=[PK(bY)], inc=False)
                    S.op('pe', (lambda h: lambda e: e.matmul(PS(bY)[:, h * 64:(h + 1) * 64], lhsT=MbT[:, h * 128:(h + 1) * 128], rhs=Ub[:, h * 64:(h + 1) * 64], start=False, stop=False))(h),
                         R=['MbT%d' % pr, 'Ub'], W=[PK(bY)], inc=False)
                    S.op('pe', (lambda h: lambda e: e.matmul(PS(bY)[:, h * 64:(h + 1) * 64], lhsT=MkT[:, h * 128:(h + 1) * 128], rhs=vb[:, h * 64:(h + 1) * 64], start=False, stop=True))(h),
                         R=['MkT%d' % pr, 'vbw'], W=[PK(bY)], inc=(h == 7))
                S.op('act', lambda e: e.activation(out=kp, in_=PS(bY), func=AF.Copy), R=[PK(bY)], W=['kp'])
            bS = bank()
            for pr in range(4):
                S.op('pe', (lambda pr: lambda e: e.matmul(PS(bS)[:, pr * 128:(pr + 1) * 128], lhsT=bt_[:, pr * 128:(pr + 1) * 128], rhs=Ub[:, pr * 128:(pr + 1) * 128], start=True, stop=False))(pr),
                     R=['bt', 'Ub'], W=[PK(bS)], inc=False)
                S.op('pe', (lambda pr: lambda e: e.matmul(PS(bS)[:, pr * 128:(pr + 1) * 128], lhsT=kt[:, pr * 128:(pr + 1) * 128], rhs=vb[:, pr * 128:(pr + 1) * 128], start=False, stop=True))(pr),
                     R=['kt', 'vbw'], W=[PK(bS)], inc=(pr == 3))
            for hp in range(2):
                S.op('dve', (lambda hp: lambda e: e.tensor_tensor(out=rwA[hp * 64:(hp + 1) * 64, :].rearrange("p (a e) -> p a e", a=4), in0=rwA[hp * 64:(hp + 1) * 64, :].rearrange("p (a e) -> p a e", a=4),
                                                                in1=PS(bS)[hp * 64:(hp + 1) * 64, :].rearrange("p (a f) -> p a f", a=4)[:, :, hp * 64:(hp + 1) * 64], op=ALU.add))(hp),
                     R=[PK(bS), 'rwA', 'rwAb'], W=['rwA'])
            S.op('dve', lambda e: e.tensor_tensor(out=rwA.rearrange("p (a e) -> p a e", a=4), in0=rwA.rearrange("p (a e) -> p a e", a=4), in1=gC.unsqueeze(2).broadcast_to([128, 4, 64]), op=ALU.mult), R=['rwA', 'gC'], W=['rwA'])
            S.op('act', lambda e: e.activation(out=rwAb, in_=rwA, func=AF.Copy), R=['rwA'], W=['rwAb'])
            if full:
                group_norm_out(kp, 'kp', sq, 'sq', gst2, 64e-5, ptile['rw_gn_g'], ptile['rw_gn_b'], 'pt_rw_gn_g', 'pt_rw_gn_b')
                S.op('pool', lambda e: e.tensor_tensor(out=kp, in0=kp, in1=gi_t, op=ALU.add), R=['kp', 'gi_t'], W=['kp'])
                bgt = bank()
                S.op('pe', lambda e: e.matmul(PS(bgt), lhsT=sgd1, rhs=gup1, start=True, stop=False), R=['sgd1', 'gup1'], W=[PK(bgt)], inc=False)
                S.op('pe', lambda e: e.matmul(PS(bgt), lhsT=sgd2, rhs=gup2, start=False, stop=True), R=['sgd2', 'gup2'], W=[PK(bgt)])
                S.op('dve', lambda e: e.tensor_tensor(out=mixw, in0=kp, in1=PS(bgt), op=ALU.mult), R=['kp', PK(bgt)], W=['mixw'])
                rw_deferred.append((lambda ti: lambda: transpose_to(mixw, ['mixw'], 4, mixTv[:, 4:8, ti * 128:(ti + 1) * 128], ['mixT_w%d' % ti]))(ti))
        for f_ in rw_deferred:
            f_()
        A.release()

    S.op('pool', lambda e: e.memset(retS, 0.0), R=[], W=['retS'])
    S.op('pool', lambda e: e.memset(retSb, 0.0), R=[], W=['retSb'])
    S.op('pool', lambda e: e.memset(rwA, 0.0), R=[], W=['rwA'])
    S.op('pool', lambda e: e.memset(rwAb, 0.0), R=[], W=['rwAb'])
    S.op('pool', lambda e: e.memset(x1T, 0.0), R=[], W=['x1T_prev'] + ['x1T_%d' % i for i in range(NT_HALF)])

    for half in range(2):
        full = (half == 1)
        A.mark()
        xTv = mixTv
        src = dr['xs'][half * S_HALF:(half + 1) * S_HALF, :]
        prep_xT(src, S_HALF, xTv, 0, lambda t0, n: ['xT_%d' % (t0 // 512)] if n == 128 else ['xT_%d' % (t0 // 512)])
        load_ln('ln1_g', 'ln1_b')
        x1b = [A.bf(1024) for _ in range(8)]
        if half == 1:
            S.op('pool', lambda e: e.tensor_copy(out=x1Tv[:, :, XO - 1:XO], in_=x1Tv[:, :, XO + S_HALF - 1:XO + S_HALF]), R=['x1T_%d' % (NT_HALF - 1)], W=['x1T_prev'])

        def cons1(ti, y, yk, half=half, full=full, x1b=x1b):
            if full:
                S.dma('sp', lambda e: e.dma_start(out=x1s[ti * 128:(ti + 1) * 128, :], in_=y), R=[yk], W=['x1s_%d' % ti], sk='x1s')
            b = x1b[ti % 8]
            bk = 'x1b%d' % (ti % 8)
            S.op('dve', lambda e: e.tensor_scalar(out=b, in0=y, scalar1=hmask[:, half:half + 1], scalar2=None, op0=ALU.mult), R=[yk, 'hmask'], W=[bk])
            return lambda: transpose_to(b, [bk], 8, x1Tv[:, :, XO + ti * 128:XO + (ti + 1) * 128], ['x1T_%d' % ti], evac_eng='dve')

        if 'ffn1_%d' % half in plan:
            ffn(xTv, 0, lambda t0, n: ['xT_%d' % (t0 // 512)], S_HALF, dr['ffn1_w_gu'], dr['ffn1_w_down'], src, LN_EPS / (ALPHA * ALPHA), cons1, 'f1')
        A.release()
        S.barrier()
        if 'mix_%d' % half in plan:
            mixer(half, full)
        S.barrier()

    A.mark()
    NTW = NT_HALF if 'wout' in plan else 0
    wo = A.bf(8 * 1024)
    wov = wo.rearrange("p (c f) -> p c f", c=8)
    w_out_v = dr['w_out'].rearrange("(c p) f -> p c f", p=128)
    for hh in range(2):
        S.dma('pool', (lambda hh: lambda e: e.dma_start(out=wov[:, :, hh * 512:(hh + 1) * 512], in_=w_out_v[:, :, hh * 512:(hh + 1) * 512]))(hh), R=[], W=['wo%d' % hh])
    load_ln('ln2_g', 'ln2_b')
    ybuf = [A.f32(1024) for _ in range(8)]
    x2b = [A.bf(1024) for _ in range(8)]
    tmp_sq = A.f32(1024)
    st = A.f32(32)
    wo_deferred = []
    for g in range(NTW // 4):
        tiles = []
        for ti in range(4 * g, 4 * g + 4):
            y, yk = ybuf[ti % 8], 'y%d' % (ti % 8)
            S.dma('sp', (lambda y, ti: lambda e: e.dma_start(out=y, in_=x1s[ti * 128:(ti + 1) * 128, :]))(y, ti), R=['x1s_%d' % ti], W=[yk])
            for dh in range(2):
                b = bank()
                for c in range(8):
                    S.op('pe', (lambda c, b, dh, ti: lambda e: e.matmul(PS(b), lhsT=mixTv[:, c, ti * 128:(ti + 1) * 128], rhs=wov[:, c, dh * 512:(dh + 1) * 512], start=(c == 0), stop=(c == 7)))(c, b, dh, ti),
                         R=['mixT_r%d' % ti, 'mixT_w%d' % ti, 'wo%d' % dh], W=[PK(b)], inc=(c == 7))
                S.op('dve', (lambda y, b, dh: lambda e: e.scalar_tensor_tensor(out=y[:, dh * 512:(dh + 1) * 512], in0=PS(b), scalar=1.0 / ALPHA, in1=y[:, dh * 512:(dh + 1) * 512], op0=ALU.mult, op1=ALU.add))(y, b, dh),
                     R=[PK(b), yk], W=[yk])
            tiles.append((ti, y, yk))
        for f_ in wo_deferred:
            f_()
        del wo_deferred[:]
        layer_norm_group([(y, yk) for (ti, y, yk) in tiles], LN_EPS / (ALPHA * ALPHA), tmp_sq, st)
        for (ti, y, yk) in tiles:
            S.dma('sp', (lambda y, ti: lambda e: e.dma_start(out=x2s[ti * 128:(ti + 1) * 128, :], in_=y))(y, ti), R=[yk], W=['x2s_%d' % ti], sk='x2s')
            b2, b2k = x2b[ti % 8], 'x2b%d' % (ti % 8)
            S.op('dve', (lambda b2, y: lambda e: e.tensor_copy(out=b2, in_=y))(b2, y), R=[yk], W=[b2k])
            wo_deferred.append((lambda b2, b2k, ti: lambda: transpose_to(b2, [b2k], 8, x2Tv[:, :, XO + ti * 128:XO + (ti + 1) * 128], ['x2T_%d' % (ti // 4)], evac_eng='dve'))(b2, b2k, ti))
    for f_ in wo_deferred:
        f_()
    A.release()
    S.barrier()

    A.mark()
    load_ln('ln3_g', 'ln3_b')
    x3b = [A.bf(1024) for _ in range(8)]

    def cons3(ti, y, yk):
        S.dma('sp', lambda e: e.dma_start(out=x3s[ti * 128:(ti + 1) * 128, :], in_=y), R=[yk], W=['x3s_%d' % ti], sk='x3s')
        b = x3b[ti % 8]
        bk = 'x3b%d' % (ti % 8)
        S.op('dve', lambda e: e.tensor_copy(out=b, in_=y), R=[yk], W=[bk])
        return lambda: transpose_to(b, [bk], 8, mixTv[:, :, ti * 128:(ti + 1) * 128], ['x3T_%d' % ti], evac_eng='dve')

    if 'ffn2' in plan:
        ffn(x2Tv, XO, lambda t0, n: ['x2T_%d' % (t0 // 512)], S_HALF, dr['ffn2_w_gu'], dr['ffn2_w_down'], x2s, LN_EPS / (ALPHA * ALPHA), cons3, 'f2')
    A.release()
    S.barrier()

    A.mark()
    wgt = A.bf(8 * 1024)
    wgv = wgt.rearrange("p (c f) -> p c f", c=8)
    wpj = A.bf(2 * 1024)
    wpv = wpj.rearrange("p (c f) -> p c f", c=2)
    bgr = A.bf(1024)
    onesb = A.bf(128)
    S.op('pool', lambda e: e.memset(bgr, 0.0), R=[], W=['bgr'])
    ple_g = dr['ple_w_gate'].rearrange("(c p) f -> p c f", p=128)
    ple_p = dr['ple_w_proj'].rearrange("(c p) f -> p c f", p=128)
    for hh in range(2):
        S.dma('pool', (lambda hh: lambda e: e.dma_start(out=wgv[:, :, hh * 512:(hh + 1) * 512], in_=ple_g[:, :, hh * 512:(hh + 1) * 512]))(hh), R=[], W=['wgt%d' % hh])
    S.dma('pool', lambda e: e.dma_start(out=wpv, in_=ple_p), R=[], W=['wpj'])
    S.dma('pool', lambda e: e.dma_start(out=bgr[0:1, :], in_=dr['ple_b_gate']), R=[], W=['bgr'])
    S.op('dve', lambda e: e.tensor_copy(out=onesb, in_=c32v('ones')), R=['c32'], W=['onesb'])
    pb_ = [A.bf(256) for _ in range(2)]
    pT = [A.bf(256) for _ in range(2)]
    x3t = [A.f32(1024) for _ in range(2)]
    gsb = [A.f32(1024) for _ in range(2)]
    for ti in range(NT_HALF if 'ple' in plan else 0):
        pbt, pbk = pb_[ti % 2], 'pb%d' % (ti % 2)
        pTt, pTk = pT[ti % 2], 'pT%d' % (ti % 2)
        x3, x3k = x3t[ti % 2], 'x3t%d' % (ti % 2)
        gs, gsk = gsb[ti % 2], 'gs%d' % (ti % 2)
        S.dma('pool', (lambda pbt, ti: lambda e: e.dma_start(out=pbt, in_=dr['p'][ti * 128:(ti + 1) * 128, :]))(pbt, ti), R=[], W=[pbk])
        S.dma('sp', (lambda x3, ti: lambda e: e.dma_start(out=x3, in_=x3s[ti * 128:(ti + 1) * 128, :]))(x3, ti), R=['x3s_%d' % ti], W=[x3k])
        transpose_to(pbt, [pbk], 2, pTt.rearrange("p (c t) -> p c t", c=2), [pTk])
        for dh in range(2):
            bg_ = bank()
            for c in range(8):
                S.op('pe', (lambda c, bg_, dh, ti: lambda e: e.matmul(PS(bg_), lhsT=mixTv[:, c, ti * 128:(ti + 1) * 128], rhs=wgv[:, c, dh * 512:(dh + 1) * 512], start=(c == 0), stop=False))(c, bg_, dh, ti),
                     R=['x3T_%d' % ti, 'wgt%d' % dh], W=[PK(bg_)], inc=False)
            S.op('pe', (lambda bg_, dh: lambda e: e.matmul(PS(bg_), lhsT=onesb, rhs=bgr[:, dh * 512:(dh + 1) * 512], start=False, stop=True))(bg_, dh), R=['onesb', 'bgr'], W=[PK(bg_)])
            bp_ = bank()
            for c in range(2):
                S.op('pe', (lambda c, bp_, dh, pTt: lambda e: e.matmul(PS(bp_), lhsT=pTt[:, c * 128:(c + 1) * 128], rhs=wpv[:, c, dh * 512:(dh + 1) * 512], start=(c == 0), stop=(c == 1)))(c, bp_, dh, pTt),
                     R=[pTk, 'wpj'], W=[PK(bp_)], inc=(c == 1))
            S.op('act', (lambda gs, bg_, dh: lambda e: e.activation(out=gs[:, dh * 512:(dh + 1) * 512], in_=PS(bg_), func=AF.Sigmoid))(gs, bg_, dh), R=[PK(bg_)], W=[gsk])
            S.op('dve', (lambda gs, bp_, dh: lambda e: e.tensor_tensor(out=gs[:, dh * 512:(dh + 1) * 512], in0=gs[:, dh * 512:(dh + 1) * 512], in1=PS(bp_), op=ALU.mult))(gs, bp_, dh), R=[PK(bp_), gsk], W=[gsk])
        S.op('dve', (lambda gs, x3: lambda e: e.tensor_tensor(out=gs, in0=gs, in1=x3, op=ALU.add))(gs, x3), R=[gsk, x3k], W=[gsk])
        S.dma('sp', (lambda gs, ti: lambda e: e.dma_start(out=out[ti * 128:(ti + 1) * 128, :], in_=gs))(gs, ti), R=[gsk], W=['out_%d' % ti], sk='out')
    A.release()
    S.barrier()

    with nc.Block() as block:
        S.emit(nc, block)
    es.close()
    print("semaphores:", len(S.semkeys), "ops:", {e: len(l) for e, l in S.lists.items()}, "arena hi:", A.hi)
    return nc


_NC_CACHE = {}


def kernel(**inputs):
    x = np.asarray(inputs['x'], np.float32)
    p = np.asarray(inputs['p'], np.float32)[0]
    if 'nc' not in _NC_CACHE:
        _NC_CACHE['nc'] = build_program()
    nc = _NC_CACHE['nc']
    c32 = np.ascontiguousarray(np.concatenate([C32[k] for k in C32], axis=1).astype(np.float32))
    cb = np.ascontiguousarray(np.concatenate([CB[k] for k in CB], axis=1).astype(np.float32))
    c32r = np.ascontiguousarray(np.concatenate([C32R[k] for k in C32R], axis=1).astype(np.float32))
    wmap = {}
    for n in WEIGHT_NAMES:
        wmap[n] = np.ascontiguousarray(np.asarray(inputs[n], np.float32)[0].reshape(WSHAPES[n]))
    in_maps = []
    for c in range(8):
        b, half = c // 2, c % 2
        m = dict(wmap)
        if half == 1:
            xs = x[b]
            pos = np.arange(4096)
            hm = np.ones((128, 2), np.float32)
        else:
            xs = np.concatenate([np.zeros((S_HALF, D), np.float32), x[b, :S_HALF]], axis=0)
            pos = np.arange(4096) - S_HALF
            hm = np.ones((128, 2), np.float32)
            hm[:, 0] = 0.0
        cos, sin = rope_tables(pos)
        m['xs'] = np.ascontiguousarray(xs)
        m['p'] = np.ascontiguousarray(p[b, half * S_HALF:(half + 1) * S_HALF])
        m['hmask'] = hm
        m['cos'] = cos
        m['sin'] = sin
        m['c32'] = c32
        m['cb'] = cb
        m['c32r'] = c32r
        in_maps.append(m)
    res = run_bass_kernel_spmd(nc, in_maps, core_ids=list(range(8)))
    outp = np.zeros((4, 4096, D), np.float32)
    for c in range(8):
        b, half = c // 2, c % 2
        outp[b, half * S_HALF:(half + 1) * S_HALF] = res.results[c]['out']
    return outp
```

```python
import os
import numpy as np
import concourse.bass as bass
import concourse.mybir as mybir
from concourse.bass_utils import run_bass_kernel_spmd

F32 = mybir.dt.float32
BF16 = mybir.dt.bfloat16
AF = mybir.ActivationFunctionType
ALU = mybir.AluOpType
AX = mybir.AxisListType

D = 1024
DFF = 2816
NJ = DFF // 128
S_HALF = 2048
NT_HALF = S_HALF // 128
RETC = 2048
RWC = 1824
INC = RETC + RWC
ALPHA = 2.0 ** 0.25
LN_EPS = 1e-5
EDEC = float(np.exp(-0.5))

STRICT = True


class _Rec:
    def __init__(self):
        self.calls = []

    def __getattr__(self, name):
        def f(*a, **k):
            self.calls.append((name, a, k))
            return self
        return f


def _capture(fn):
    r = _Rec()
    fn(r)
    assert len(r.calls) == 1, r.calls
    name, a, k = r.calls[0]
    return lambda e: getattr(e, name)(*a, **k)


class Sched:
    def __init__(self):
        self.engs = ['pe', 'act', 'dve', 'pool', 'sp']
        self.lists = {e: [] for e in self.engs}
        self.cnt = {e: 0 for e in self.engs}
        self.lastw = {}
        self.readers = {}
        self.waited = {e: {} for e in self.engs}
        self.dmacnt = {}
        self.semkeys = set(self.engs)
        self.alltok = {}

    def _deps(self, eng, R, W, is_dma):
        deps = []
        raw = set()
        for k in R:
            t = self.lastw.get(k)
            if t:
                deps.append(t)
                raw.add(t)
            if k.startswith('ps'):
                deps.extend(tk for tk in self.readers.get(k, ()) if tk[0] != eng)
        for k in W:
            t = self.lastw.get(k)
            if t:
                deps.append(t)
            deps.extend(self.readers.get(k, ()))
        waits = {}
        for (sk, v) in deps:
            if sk == eng and not is_dma and (eng == 'pe' or not STRICT or (sk, v) not in raw):
                continue
            if self.waited[eng].get(sk, 0) >= v:
                continue
            waits[sk] = max(waits.get(sk, 0), v)
        for sk, v in waits.items():
            self.waited[eng][sk] = v
        return list(waits.items())

    def _commit(self, tok, R, W):
        for k in W:
            self.lastw[k] = tok
            self.readers[k] = []
        for k in R:
            if k not in W:
                self.readers.setdefault(k, []).append(tok)
        self.alltok[tok[0]] = max(self.alltok.get(tok[0], 0), tok[1])

    def op(self, eng, fn, R=(), W=(), inc=True):
        self._clean = False
        waits = self._deps(eng, R, W, False)
        if inc:
            self.cnt[eng] += 1
            tok = (eng, self.cnt[eng])
        else:
            tok = (eng, self.cnt[eng] + 1)
        self.lists[eng].append((waits, _capture(fn), (eng, 1) if inc else None))
        self._commit(tok, R, W)

    def dma(self, eng, fn, R, W, sk=None):
        self._clean = False
        waits = self._deps(eng, R, W, True)
        sk = 'd:' + (sk or W[0])
        self.semkeys.add(sk)
        self.dmacnt[sk] = self.dmacnt.get(sk, 0) + 16
        tok = (sk, self.dmacnt[sk])
        self.lists[eng].append((waits, _capture(fn), (sk, 16)))
        self._commit(tok, R, W)

    def barrier(self):
        if getattr(self, '_clean', False):
            return
        self._clean = True
        for e in ['pe', 'act', 'dve', 'pool']:
            if self.lists[e] and self.lists[e][-1][2] is not None and self.lists[e][-1][2][0] == e:
                continue
            self.cnt[e] += 1
            self.alltok[e] = self.cnt[e]
            self.lists[e].append(([], 'nop', (e, 1)))
        for e in self.engs:
            waits = []
            for sk, v in self.alltok.items():
                if self.waited[e].get(sk, 0) >= v:
                    continue
                if sk == e:
                    continue
                waits.append((sk, v))
                self.waited[e][sk] = v
            self.lists[e].append((waits, None, None))
        self.lastw = {}
        self.readers = {}

    def emit(self, nc, block):
        sems = {sk: nc.alloc_semaphore(name=("s_" + sk.replace(':', '_').replace('.', '_'))[:40]) for sk in sorted(self.semkeys)}
        engobj = {'pe': 'tensor', 'act': 'scalar', 'dve': 'vector', 'pool': 'gpsimd', 'sp': 'sync'}

        def make(ename):
            lst = self.lists[ename]

            def body(e):
                for (waits, fn, inc) in lst:
                    for (sk, v) in waits:
                        e.wait_ge(sems[sk], v)
                    if fn is None:
                        continue
                    if fn == 'nop':
                        ins = e.nop()
                    else:
                        ins = fn(e)
                    if inc is not None:
                        ins.then_inc(sems[inc[0]], inc[1])
            return body

        for ename in self.engs:
            getattr(block, engobj[ename])(make(ename))


def gammas():
    return 1.0 - 2.0 ** (-5.0 - np.arange(8, dtype=np.float64))


def host_consts():
    g = gammas()
    i = np.arange(128)
    c = {}
    s_le_t = (i[:, None] <= i[None, :]).astype(np.float32)
    s_lt_t = (i[:, None] < i[None, :]).astype(np.float32)
    c['tri_incl'] = -EDEC * s_le_t
    c['tri_strict'] = -EDEC * s_lt_t
    c['negcol'] = np.full((128, 1), -EDEC, np.float32)
    c['ones'] = np.ones((128, 128), np.float32)
    rel = (i[None, :] - i[:, None]).astype(np.float64)
    dm = np.zeros((128, 8, 128), np.float64)
    for h in range(8):
        dm[:, h, :] = np.where(rel >= 0, 0.125 * np.exp(np.where(rel >= 0, rel, 0) * np.log(g[h])), 0.0)
    cr = {}
    cr['dmask'] = dm.reshape(128, 1024).astype(np.float32)
    kd = np.zeros((128, 8), np.float64)
    for h in range(8):
        kd[:, h] = 0.125 * g[h] ** (127.0 - i)
    c['kdec'] = kd.astype(np.float32)
    qd = np.zeros((128, 4, 128), np.float64)
    gm = np.zeros((128, 4), np.float64)
    for pr in range(4):
        for hp in range(2):
            h = 2 * pr + hp
            qd[hp * 64:(hp + 1) * 64, pr, :] = (g[h] ** (i + 1.0))[None, :]
            gm[hp * 64:(hp + 1) * 64, pr] = g[h] ** 128.0
    cr['qdec'] = qd.reshape(128, 512).astype(np.float32)
    cr['kdecf'] = np.repeat(kd, 64, axis=1).astype(np.float32)
    cr['gam128f'] = np.repeat(gm, 64, axis=1).astype(np.float32)
    c['gam128'] = gm.astype(np.float32)
    b = {}
    b['ident'] = np.eye(128, dtype=np.float32)
    m1 = np.concatenate([-s_lt_t, s_le_t], axis=1)
    m2 = np.concatenate([s_lt_t, s_le_t], axis=1)
    b['m1'] = np.concatenate([m1, m1], axis=1)
    b['m2'] = np.concatenate([m2, m2], axis=1)
    m3 = -(i[:, None] > i[None, :]).astype(np.float32)
    b['m3'] = np.concatenate([m3, m3], axis=1)
    return c, cr, b


def rope_tables(pos):
    inv = 10000.0 ** (-np.arange(0, 64, 2, dtype=np.float32) / 64.0)
    ang = pos.astype(np.float32)[:, None] * inv[None, :]
    cos = np.cos(ang).astype(np.float32)
    sin = np.sin(ang).astype(np.float32)
    n = pos.shape[0] // 128
    cos = cos.reshape(n, 128, 32).transpose(1, 0, 2).reshape(128, n * 32)
    sin = sin.reshape(n, 128, 32).transpose(1, 0, 2).reshape(128, n * 32)
    return np.ascontiguousarray(cos), np.ascontiguousarray(sin)


C32, C32R, CB = host_consts()
C32_OFF = {}
_o = 0
for _k, _v in C32.items():
    C32_OFF[_k] = (_o, _v.shape[1])
    _o += _v.shape[1]
C32_N = _o
C32R_OFF = {}
_o = 0
for _k, _v in C32R.items():
    C32R_OFF[_k] = (_o, _v.shape[1])
    _o += _v.shape[1]
C32R_N = _o
CB_OFF = {}
_o = 0
for _k, _v in CB.items():
    CB_OFF[_k] = (_o, _v.shape[1])
    _o += _v.shape[1]
CB_N = _o

WEIGHT_NAMES = ['ffn1_w_gu', 'ffn1_w_down', 'ln1_g', 'ln1_b', 'w_in', 'ret_gn_g', 'ret_gn_b', 'rw_mu',
                'rw_w0', 'rw_w_up', 'rw_a0', 'rw_a_up', 'rw_g_up', 'rw_k_k', 'rw_k_a', 'rw_r_k', 'rw_gn_g',
                'rw_gn_b', 'w_out', 'ln2_g', 'ln2_b', 'ffn2_w_gu', 'ffn2_w_down', 'ln3_g', 'ln3_b',
                'ple_w_proj', 'ple_w_gate', 'ple_b_gate']
WSHAPES = {'ffn1_w_gu': [D, 2 * DFF], 'ffn1_w_down': [DFF, D], 'ln1_g': [1, D], 'ln1_b': [1, D], 'w_in': [D, INC],
           'ret_gn_g': [1, 512], 'ret_gn_b': [1, 512], 'rw_mu': [1, RWC], 'rw_w0': [1, 512], 'rw_w_up': [64, 512],
           'rw_a0': [1, 512], 'rw_a_up': [64, 512], 'rw_g_up': [160, 512], 'rw_k_k': [1, 512], 'rw_k_a': [1, 512],
           'rw_r_k': [1, 512], 'rw_gn_g': [1, 512], 'rw_gn_b': [1, 512], 'w_out': [D, D], 'ln2_g': [1, D],
           'ln2_b': [1, D], 'ffn2_w_gu': [D, 2 * DFF], 'ffn2_w_down': [DFF, D], 'ln3_g': [1, D], 'ln3_b': [1, D],
           'ple_w_proj': [256, D], 'ple_w_gate': [D, D], 'ple_b_gate': [1, D]}


def build_program(plan=None, dbg=False):
    nc = bass.Bass("TRN2", target_bir_lowering=False)
    dr = {}
    dr['xs'] = nc.dram_tensor("xs", [2 * S_HALF, D], F32, kind="ExternalInput").ap()
    dr['p'] = nc.dram_tensor("p", [S_HALF, 256], F32, kind="ExternalInput").ap()
    dr['hmask'] = nc.dram_tensor("hmask", [128, 2], F32, kind="ExternalInput").ap()
    dr['cos'] = nc.dram_tensor("cos", [128, 1024], F32, kind="ExternalInput").ap()
    dr['sin'] = nc.dram_tensor("sin", [128, 1024], F32, kind="ExternalInput").ap()
    dr['c32'] = nc.dram_tensor("c32", [128, C32_N], F32, kind="ExternalInput").ap()
    dr['cb'] = nc.dram_tensor("cb", [128, CB_N], F32, kind="ExternalInput").ap()
    dr['c32r'] = nc.dram_tensor("c32r", [128, C32R_N], F32, kind="ExternalInput").ap()
    for n in WEIGHT_NAMES:
        dr[n] = nc.dram_tensor(n, WSHAPES[n], F32, kind="ExternalInput").ap()
    out = nc.dram_tensor("out", [S_HALF, D], F32, kind="ExternalOutput").ap()
    skind = "ExternalOutput" if dbg else "Internal"
    x1s = nc.dram_tensor("x1s", [S_HALF, D], F32, kind=skind).ap()
    x2s = nc.dram_tensor("x2s", [S_HALF, D], F32, kind=skind).ap()
    x3s = nc.dram_tensor("x3s", [S_HALF, D], F32, kind=skind).ap()
    wab_s = nc.dram_tensor("wab_s", [128, 2 * 8 * RWC], BF16, kind="Internal").ap()
    if plan is None:
        plan = ['ffn1_0', 'mix_0', 'ffn1_1', 'mix_1', 'wout', 'ffn2', 'ple']

    S = Sched()
    dumped = {}

    def dump(name, ap, keys):
        if not dbg or name in dumped:
            return
        shp = list(ap.shape)
        d_ = nc.dram_tensor("dbg_" + name, shp, F32, kind="ExternalOutput").ap()
        dumped[name] = d_
        S.dma('pool', lambda e: e.dma_start(out=d_, in_=ap), R=list(keys), W=['dbg_' + name])
    ARENA_W = int(os.environ.get('ARENA_W', '53200'))
    from contextlib import ExitStack
    es = ExitStack()
    arena = es.enter_context(nc.sbuf_tensor("arena", [128, ARENA_W], F32))
    psf = [es.enter_context(nc.psum_tensor("ps%d" % i, [128, 512], F32)) for i in range(8)]

    class Alloc:
        def __init__(self):
            self.p = 0
            self.marks = []

        def f32(self, n, parts=(0, 128)):
            if os.environ.get('DRY'):
                self.p += n
                self.hi = max(getattr(self, 'hi', 0), self.p)
                return arena[parts[0]:parts[1], 0:n]
            a = arena[parts[0]:parts[1], self.p:self.p + n]
            self.p += n
            self.hi = max(getattr(self, 'hi', 0), self.p)
            assert self.p <= ARENA_W, ("arena overflow", self.p)
            return a

        def bf(self, n, parts=(0, 128)):
            w = (n + 1) // 2
            if os.environ.get('DRY'):
                self.p += w
                self.hi = max(getattr(self, 'hi', 0), self.p)
                return arena[parts[0]:parts[1], 0:w].bitcast(BF16)
            a = arena[parts[0]:parts[1], self.p:self.p + w].bitcast(BF16)
            self.p += w
            self.hi = max(getattr(self, 'hi', 0), self.p)
            assert self.p <= ARENA_W, ("arena overflow", self.p)
            return a

        def mark(self):
            self.marks.append(self.p)

        def release(self):
            self.p = self.marks.pop()
            S.barrier()

    A = Alloc()
    bankctr = [0]
    bankgen = [0] * 8

    class Bk(int):
        pass

    def bank():
        b = Bk(bankctr[0] % 8)
        bankctr[0] += 1
        bankgen[int(b)] = bankctr[0]
        b.gen = bankctr[0]
        return b

    def PS(b):
        return psf[int(b)][:, :]

    def PSB(b):
        return psf[int(b)][:, :].bitcast(BF16)

    def PK(b):
        assert bankgen[int(b)] == b.gen, "stale PSUM bank use"
        return 'ps%d' % int(b)

    c32 = A.f32(C32_N)
    cbt = A.bf(CB_N)
    S.dma('sp', lambda e: e.dma_start(out=c32, in_=dr['c32']), R=[], W=['c32'])
    S.dma('pool', lambda e: e.dma_start(out=cbt, in_=dr['cb']), R=[], W=['cb'])

    def c32v(name, parts=(0, 128)):
        o, n = C32_OFF[name]
        return c32[parts[0]:parts[1], o:o + n]

    def cbv(name):
        o, n = CB_OFF[name]
        return cbt[:, o:o + n]

    ident = cbv('ident')
    hmask = A.f32(2)
    S.dma('sp', lambda e: e.dma_start(out=hmask, in_=dr['hmask']), R=[], W=['hmask'])
    ptile = {}
    LN = {}
    retS = A.f32(256)
    retSb = A.bf(256)
    rwA = A.f32(256)
    rwAb = A.bf(256)
    x1T = A.bf(8 * (S_HALF + 8))
    x1Tv = x1T.rearrange("p (c t) -> p c t", c=8)
    XO = 8
    mixT = A.bf(8 * S_HALF)
    mixTv = mixT.rearrange("p (c t) -> p c t", c=8)
    x2Tv = x1Tv

    def load_ln(gn, bn):
        lng = A.f32(1024)
        lnb = A.f32(1024)
        LN['g'] = lng
        LN['b'] = lnb
        S.dma('sp', lambda e: e.dma_start(out=lng, in_=dr[gn].partition_broadcast(128)), R=[], W=['lng'])
        S.dma('sp', lambda e: e.dma_start(out=lnb, in_=dr[bn].partition_broadcast(128)), R=[], W=['lnb'])

    def load_ptiles(names):
        for n in names:
            t = A.f32(512)
            ptile[n] = t
            S.dma('sp', (lambda t, n: lambda e: e.dma_start(out=t, in_=dr[n].partition_broadcast(128)))(t, n), R=[], W=['pt_' + n])

    def transpose_to(src_bf, src_keys, n_blocks, dst_view, dst_keys, evac_eng='act'):
        b = bank()
        pk = PK(b)
        for i in range(n_blocks):
            S.op('pe', (lambda i, b: lambda e: e.transpose(out=PSB(b)[:, i * 128:(i + 1) * 128],
                                                         in_=src_bf[:, i * 128:(i + 1) * 128], identity=ident))(i, b),
                 R=list(src_keys) + ['cb'], W=[pk], inc=(i == n_blocks - 1))
        src = PSB(b)[:, 0:n_blocks * 128].rearrange("p (c t) -> p c t", c=n_blocks)
        if evac_eng == 'act':
            S.op('act', lambda e: e.activation(out=dst_view, in_=src, func=AF.Copy), R=[pk], W=list(dst_keys))
        else:
            S.op(evac_eng, lambda e: e.tensor_copy(out=dst_view, in_=src), R=[pk], W=list(dst_keys))

    def layer_norm_group(tiles, eps, junk, st):
        n_ = len(tiles)
        sl = lambda i, c: st[:, i * 8 + c:i * 8 + c + 1]
        k = lambda i, c: 'lnst%d_%d' % (i, c)
        for i, (y, yk) in enumerate(tiles):
            S.op('act', (lambda i, y: lambda e: e.activation(out=junk, in_=y, func=AF.Square, accum_out=sl(i, 0)))(i, y), R=[yk], W=['lnjunk', k(i, 0)])
            S.op('act', (lambda i, y: lambda e: e.activation(out=junk, in_=y, func=AF.Identity, accum_out=sl(i, 1)))(i, y), R=[yk], W=['lnjunk', k(i, 1)])
        for i in range(n_):
            S.op('dve', (lambda i: lambda e: e.tensor_scalar(out=sl(i, 2), in0=sl(i, 1), scalar1=1.0 / 1024, scalar2=None, op0=ALU.mult))(i), R=[k(i, 1)], W=[k(i, 2)])
        for i in range(n_):
            S.op('dve', (lambda i: lambda e: e.tensor_tensor(out=sl(i, 3), in0=sl(i, 2), in1=sl(i, 2), op=ALU.mult))(i), R=[k(i, 2)], W=[k(i, 3)])
        for i in range(n_):
            S.op('dve', (lambda i: lambda e: e.scalar_tensor_tensor(out=sl(i, 4), in0=sl(i, 0), scalar=1.0 / 1024, in1=sl(i, 3), op0=ALU.mult, op1=ALU.subtract))(i), R=[k(i, 0), k(i, 3)], W=[k(i, 4)])
        for i in range(n_):
            S.op('dve', (lambda i: lambda e: e.tensor_scalar(out=sl(i, 4), in0=sl(i, 4), scalar1=float(eps), scalar2=None, op0=ALU.add))(i), R=[k(i, 4)], W=[k(i, 4)])
        for i in range(n_):
            S.op('act', (lambda i: lambda e: e.activation(out=sl(i, 5), in_=sl(i, 4), func=AF.Sqrt))(i), R=[k(i, 4)], W=[k(i, 5)])
        for i in range(n_):
            S.op('dve', (lambda i: lambda e: e.reciprocal(out=sl(i, 6), in_=sl(i, 5)))(i), R=[k(i, 5)], W=[k(i, 6)])
        for i in range(n_):
            S.op('dve', (lambda i: lambda e: e.scalar_tensor_tensor(out=sl(i, 7), in0=sl(i, 2), scalar=-1.0, in1=sl(i, 6), op0=ALU.mult, op1=ALU.mult))(i), R=[k(i, 2), k(i, 6)], W=[k(i, 7)])
        for i, (y, yk) in enumerate(tiles):
            S.op('act', (lambda i, y: lambda e: e.activation(out=y, in_=y, func=AF.Identity, scale=sl(i, 6), bias=sl(i, 7)))(i, y), R=[yk, k(i, 6), k(i, 7)], W=[yk])
        for i, (y, yk) in enumerate(tiles):
            S.op('dve', (lambda y: lambda e: e.tensor_tensor(out=y, in0=y, in1=LN['g'], op=ALU.mult))(y), R=[yk, 'lng'], W=[yk])
            S.op('dve', (lambda y: lambda e: e.tensor_tensor(out=y, in0=y, in1=LN['b'], op=ALU.add))(y), R=[yk, 'lnb'], W=[yk])

    def layer_norm_tile(y, ykey, eps, outs, tmp_sq, st):
        S.op('act', lambda e: e.activation(out=tmp_sq, in_=y, func=AF.Square), R=[ykey], W=['lnsq'])
        S.op('dve', lambda e: e.reduce_sum(out=st[:, 0:1], in_=tmp_sq, axis=AX.X), R=['lnsq'], W=['lnst0'])
        S.op('dve', lambda e: e.reduce_sum(out=st[:, 1:2], in_=y, axis=AX.X), R=[ykey], W=['lnst1'])
        S.op('dve', lambda e: e.tensor_scalar(out=st[:, 2:3], in0=st[:, 1:2], scalar1=1.0 / 1024, scalar2=None, op0=ALU.mult), R=['lnst1'], W=['lnst2'])
        S.op('dve', lambda e: e.tensor_tensor(out=st[:, 3:4], in0=st[:, 2:3], in1=st[:, 2:3], op=ALU.mult), R=['lnst2'], W=['lnst3'])
        S.op('dve', lambda e: e.scalar_tensor_tensor(out=st[:, 4:5], in0=st[:, 0:1], scalar=1.0 / 1024, in1=st[:, 3:4], op0=ALU.mult, op1=ALU.subtract), R=['lnst0', 'lnst3'], W=['lnst4'])
        S.op('dve', lambda e: e.tensor_scalar(out=st[:, 4:5], in0=st[:, 4:5], scalar1=float(eps), scalar2=None, op0=ALU.add), R=['lnst4'], W=['lnst4'])
        S.op('act', lambda e: e.activation(out=st[:, 5:6], in_=st[:, 4:5], func=AF.Sqrt), R=['lnst4'], W=['lnst5'])
        S.op('dve', lambda e: e.reciprocal(out=st[:, 6:7], in_=st[:, 5:6]), R=['lnst5'], W=['lnst6'])
        S.op('dve', lambda e: e.scalar_tensor_tensor(out=st[:, 7:8], in0=st[:, 2:3], scalar=-1.0, in1=st[:, 6:7], op0=ALU.mult, op1=ALU.mult), R=['lnst2', 'lnst6'], W=['lnst7'])
        S.op('act', lambda e: e.activation(out=y, in_=y, func=AF.Identity, scale=st[:, 6:7], bias=st[:, 7:8]), R=[ykey, 'lnst6', 'lnst7'], W=[ykey])
        S.op('dve', lambda e: e.tensor_tensor(out=y, in0=y, in1=LN['g'], op=ALU.mult), R=[ykey, 'lng'], W=[ykey])
        S.op('dve', lambda e: e.tensor_tensor(out=y, in0=y, in1=LN['b'], op=ALU.add), R=[ykey, 'lnb'], W=[ykey])

    def ffn(xTv_src, xoff, xkey_fn, ntok, wgu, wdown, res_dram, eps, consumer, tagp):
        A.mark()
        NB = 1024
        hhT = A.bf(NJ * NB)
        hhv = hhT.rearrange("p (j t) -> p j t", j=NJ)
        wg = [A.bf(8 * 512) for _ in range(2)]
        wd = [A.bf(1024) for _ in range(6)]
        sg = [A.f32(512) for _ in range(2)]
        ybuf = [A.f32(1024) for _ in range(8)]
        tmp_sq = A.f32(1024)
        st = A.f32(32)
        wgu_v = wgu.rearrange("(c p) f -> p c f", p=128)
        deferred = []

        def run_deferred():
            for f_ in deferred:
                f_()
            del deferred[:]
        wi = 0
        di = 0
        for blk in range(ntok // NB):
            t0 = blk * NB
            for jg in range(NJ // 2):
                w = wg[wi % 2]
                wk = 'wg%d' % (wi % 2)
                wv = w.rearrange("p (c f) -> p c f", c=8)
                wi += 1
                S.dma('pool', (lambda wv, jg: lambda e: e.dma_start(out=wv[:, :, 0:256], in_=wgu_v[:, :, jg * 256:(jg + 1) * 256]))(wv, jg), R=[], W=[wk + 'g'])
                S.dma('pool', (lambda wv, jg: lambda e: e.dma_start(out=wv[:, :, 256:512], in_=wgu_v[:, :, DFF + jg * 256:DFF + (jg + 1) * 256]))(wv, jg), R=[], W=[wk + 'u'])
                for jj in range(2):
                    j = jg * 2 + jj
                    for sb in range(NB // 512):
                        bg = bank()
                        bu = bank()
                        c0 = xoff + t0 + sb * 512
                        xk = xkey_fn(t0 + sb * 512, 512)
                        for dc in range(8):
                            S.op('pe', (lambda wv, dc, jj, bg, c0: lambda e: e.matmul(PS(bg), lhsT=wv[:, dc, jj * 128:(jj + 1) * 128], rhs=xTv_src[:, dc, c0:c0 + 512], start=(dc == 0), stop=(dc == 7)))(wv, dc, jj, bg, c0),
                                 R=[wk + 'g'] + xk, W=[PK(bg)], inc=(dc == 7))
                        for dc in range(8):
                            S.op('pe', (lambda wv, dc, jj, bu, c0: lambda e: e.matmul(PS(bu), lhsT=wv[:, dc, 256 + jj * 128:256 + (jj + 1) * 128], rhs=xTv_src[:, dc, c0:c0 + 512], start=(dc == 0), stop=(dc == 7)))(wv, dc, jj, bu, c0),
                                 R=[wk + 'u'] + xk, W=[PK(bu)], inc=(dc == 7))
                        sgt = sg[(j * 2 + sb) % 2]
                        sgk = 'sg%d' % ((j * 2 + sb) % 2)
                        S.op('act', (lambda sgt, bg: lambda e: e.activation(out=sgt, in_=PS(bg), func=AF.Silu))(sgt, bg), R=[PK(bg)], W=[sgk])
                        S.op('dve', (lambda sgt, bu, j, sb: lambda e: e.tensor_tensor(out=hhv[:, j, sb * 512:(sb + 1) * 512], in0=sgt, in1=PS(bu), op=ALU.mult))(sgt, bu, j, sb),
                             R=[sgk, PK(bu)], W=['hh%d_%d' % (j, sb)])
            if blk == 0:
                dump(tagp + '_hh0', hhv[:, 0, :], ['hh0_0', 'hh0_1'])
                dump(tagp + '_hh21', hhv[:, 21, :], ['hh21_0', 'hh21_1'])
                dump(tagp + '_xT0', xTv_src[:, 0, xoff:xoff + 512], xkey_fn(0, 512))
            if os.environ.get('FFN_STOP') == 'up':
                continue
            for rnd in range(NB // 512):
                banks = [[bank(), bank()] for _ in range(4)]
                for j in range(NJ):
                    w = wd[di % 6]
                    wk = 'wd%d' % (di % 6)
                    di += 1
                    S.dma('pool', (lambda w, j: lambda e: e.dma_start(out=w, in_=wdown[j * 128:(j + 1) * 128, :]))(w, j), R=[], W=[wk])
                    for tt in range(4):
                        for dh in range(2):
                            b = banks[tt][dh]
                            S.op('pe', (lambda w, j, tt, dh, b, rnd: lambda e: e.matmul(PS(b), lhsT=hhv[:, j, rnd * 512 + tt * 128:rnd * 512 + (tt + 1) * 128], rhs=w[:, dh * 512:(dh + 1) * 512], start=(j == 0), stop=(j == NJ - 1)))(w, j, tt, dh, b, rnd),
                                 R=[wk, 'hh%d_%d' % (j, rnd)], W=[PK(b)], inc=(j == NJ - 1 or (tt == 3 and dh == 1)))
                cur = []
                for tt in range(4):
                    ti = (t0 + rnd * 512) // 128 + tt
                    y = ybuf[ti % 8]
                    yk = 'y%d' % (ti % 8)
                    S.dma('sp', (lambda y, ti: lambda e: e.dma_start(out=y, in_=res_dram[ti * 128:(ti + 1) * 128, :]))(y, ti), R=[], W=[yk])
                    for dh in range(2):
                        b = banks[tt][dh]
                        S.op('dve', (lambda y, b, dh: lambda e: e.scalar_tensor_tensor(out=y[:, dh * 512:(dh + 1) * 512], in0=PS(b), scalar=0.5 / ALPHA, in1=y[:, dh * 512:(dh + 1) * 512], op0=ALU.mult, op1=ALU.add))(y, b, dh),
                             R=[PK(b), yk], W=[yk])
                    cur.append((ti, y, yk))
                run_deferred()
                layer_norm_group([(y, yk) for (ti, y, yk) in cur], eps, tmp_sq, st)
                for (ti, y, yk) in cur:
                    later = consumer(ti, y, yk)
                    if later is not None:
                        deferred.append(later)
        run_deferred()
        A.release()

    def prep_xT(src_dram, ntok, dstv, doff, keyfn):
        A.mark()
        xb = [A.bf(1024) for _ in range(4)]
        for ti in range(ntok // 128):
            b = xb[ti % 4]
            bk = 'xb%d' % (ti % 4)
            S.dma('pool', (lambda b, ti: lambda e: e.dma_start(out=b, in_=src_dram[ti * 128:(ti + 1) * 128, :]))(b, ti), R=[], W=[bk])
            transpose_to(b, [bk], 8, dstv[:, :, doff + ti * 128:doff + (ti + 1) * 128], keyfn(ti * 128, 128))
        A.release()

    def mixer(half, full):
        A.mark()
        w_in = dr['w_in'].rearrange("(c p) f -> p c f", p=128)
        A.mark()
        wret = A.bf(8 * RETC)
        wretv = wret.rearrange("p (c f) -> p c f", c=8)
        c32r = A.f32(C32R_N)
        S.dma('sp', lambda e: e.dma_start(out=c32r, in_=dr['c32r']), R=[], W=['c32r'])

        def c32rv(name):
            o, n = C32R_OFF[name]
            return c32r[:, o:o + n]
        cos_t = A.f32(512)
        sin_t = A.f32(512)
        S.dma('sp', lambda e: e.dma_start(out=cos_t, in_=dr['cos'][:, half * 512:(half + 1) * 512]), R=[], W=['cos'])
        S.dma('sp', lambda e: e.dma_start(out=sin_t, in_=dr['sin'][:, half * 512:(half + 1) * 512]), R=[], W=['sin'])
        load_ptiles(['ret_gn_g', 'ret_gn_b'])
        for q4 in ([1, 2, 0, 3] if full else [1, 2]):
            S.dma('pool', (lambda q4: lambda e: e.dma_start(out=wretv[:, :, q4 * 512:(q4 + 1) * 512], in_=w_in[:, :, q4 * 512:(q4 + 1) * 512]))(q4), R=[], W=['wret%d' % q4])
        qr = A.bf(512)
        kr = A.bf(512)
        vb = A.bf(512)
        vdb = A.bf(512)
        sgg = A.f32(512)
        t1 = A.f32(256)
        t2 = A.f32(256)
        t3 = A.f32(256)
        t4 = A.f32(256)
        qT = A.bf(512)
        qsTm = A.bf(1024)
        kTm = A.bf(1024)
        S.op('pool', lambda e: e.memset(qsTm, 0.0), R=[], W=['qsTm'])
        S.op('pool', lambda e: e.memset(kTm, 0.0), R=[], W=['kTm'])
        scm = A.bf(1024)
        o_sb = A.f32(512)
        o_sq = A.f32(512)
        gst = A.f32(64)
        mixr = A.bf(512)
        ret_deferred = []
        kdec = c32v('kdec')
        for ti in range(NT_HALF):
            c0 = XO + ti * 128
            gt = ti
            xk = ['x1T_%d' % ti]
            cosv = cos_t[:, gt * 32:(gt + 1) * 32].unsqueeze(1).broadcast_to([128, 8, 32])
            sinv = sin_t[:, gt * 32:(gt + 1) * 32].unsqueeze(1).broadcast_to([128, 8, 32])
            pb = {}
            which = ['q', 'k', 'v', 'g'] if full else ['k', 'v']
            RSUB = os.environ.get('RET_SUB', '')
            if RSUB == 'none':
                continue
            for nm in which:
                q4 = ['q', 'k', 'v', 'g'].index(nm)
                b = bank()
                pb[nm] = b
                for dc in range(8):
                    S.op('pe', (lambda dc, b, q4, c0: lambda e: e.matmul(PS(b), lhsT=x1Tv[:, dc, c0:c0 + 128], rhs=wretv[:, dc, q4 * 512:(q4 + 1) * 512], start=(dc == 0), stop=(dc == 7)))(dc, b, q4, c0),
                         R=xk + ['wret%d' % q4], W=[PK(b)], inc=(dc == 7))

            def rotary(b, dst, dkey):
                src = PS(b).rearrange("p (h d) -> p h d", h=8)
                dv = dst.rearrange("p (h d) -> p h d", h=8)
                a1 = t1.rearrange("p (h d) -> p h d", h=8)
                a2 = t2.rearrange("p (h d) -> p h d", h=8)
                a3 = t3.rearrange("p (h d) -> p h d", h=8)
                a4 = t4.rearrange("p (h d) -> p h d", h=8)
                pk = PK(b)
                S.op('dve', lambda e: e.tensor_tensor(out=a1, in0=src[:, :, 0:32], in1=cosv, op=ALU.mult), R=[pk, 'cos'], W=['rt1'])
                S.op('dve', lambda e: e.tensor_tensor(out=a2, in0=src[:, :, 32:64], in1=sinv, op=ALU.mult), R=[pk, 'sin'], W=['rt2'])
                S.op('dve', lambda e: e.tensor_tensor(out=a3, in0=src[:, :, 0:32], in1=sinv, op=ALU.mult), R=[pk, 'sin'], W=['rt3'])
                S.op('dve', lambda e: e.tensor_tensor(out=a4, in0=src[:, :, 32:64], in1=cosv, op=ALU.mult), R=[pk, 'cos'], W=['rt4'])
                S.op('pool', lambda e: e.tensor_tensor(out=dv[:, :, 0:32], in0=a1, in1=a2, op=ALU.subtract), R=['rt1', 'rt2'], W=[dkey])
                S.op('pool', lambda e: e.tensor_tensor(out=dv[:, :, 32:64], in0=a3, in1=a4, op=ALU.add), R=['rt3', 'rt4'], W=[dkey])

            for f_ in ret_deferred:
                f_()
            del ret_deferred[:]
            if RSUB == 'proj':
                continue
            rotary(pb['k'], kr, 'kr')
            if full:
                rotary(pb['q'], qr, 'qr')
            if RSUB == 'rot':
                continue
            bv = pb['v']
            S.op('act', (lambda bv: lambda e: e.activation(out=vb, in_=PS(bv), func=AF.Copy))(bv), R=[PK(bv)], W=['vb'])
            if RSUB == 'vb':
                continue
            S.op('dve', (lambda bv: lambda e: e.tensor_tensor(out=vdb, in0=PS(bv), in1=c32rv('kdecf'), op=ALU.mult))(bv), R=[PK(bv), 'c32r', 'vb'], W=['vdb'])
            RS = int(os.environ.get('RET_STOP', '99'))
            if RS <= 1:
                continue
            if full:
                bgg = pb['g']
                S.op('act', (lambda bgg: lambda e: e.activation(out=sgg, in_=PS(bgg), func=AF.Silu))(bgg), R=[PK(bgg)], W=['sgg'])
                b = bank()
                pk = PK(b)
                for i in range(4):
                    S.op('pe', (lambda i, b: lambda e: e.transpose(out=PSB(b)[:, i * 128:(i + 1) * 128], in_=qr[:, i * 128:(i + 1) * 128], identity=ident))(i, b), R=['qr', 'cb'], W=[pk], inc=False)
                for i in range(4):
                    S.op('pe', (lambda i, b: lambda e: e.transpose(out=PSB(b)[:, 512 + i * 128:512 + (i + 1) * 128], in_=kr[:, i * 128:(i + 1) * 128], identity=ident))(i, b), R=['kr', 'cb'], W=[pk], inc=(i == 3))
                S.op('act', (lambda b: lambda e: e.activation(out=qT, in_=PSB(b)[:, 0:512], func=AF.Copy))(b), R=[pk], W=['qT'])
                for hp in range(2):
                    rs_ = slice(hp * 64, (hp + 1) * 64)
                    S.op('dve', (lambda b, hp, rs_: lambda e: e.tensor_tensor(out=qsTm[rs_, :].rearrange("p (a c t) -> p a c t", a=4, c=2)[:, :, hp, :],
                                                                             in0=PSB(b)[rs_, 0:512].rearrange("p (a t) -> p a t", a=4),
                                                                             in1=c32rv('qdec')[rs_, :].rearrange("p (a t) -> p a t", a=4), op=ALU.mult))(b, hp, rs_), R=[pk, 'c32r'], W=['qsTm'])
                    S.op('act', (lambda b, hp, rs_: lambda e: e.activation(out=kTm[rs_, :].rearrange("p (a c t) -> p a c t", a=4, c=2)[:, :, hp, :],
                                                                          in_=PSB(b)[rs_, 512:1024].rearrange("p (a t) -> p a t", a=4), func=AF.Copy))(b, hp, rs_), R=[pk], W=['kTm'])
                if RS <= 2:
                    continue
                sb_ = [bank(), bank()]
                for h in range(8):
                    pr, hp = h // 2, h % 2
                    b = sb_[h // 4]
                    S.op('pe', (lambda h, pr, hp, b: lambda e: e.matmul(PS(b)[:, (h % 4) * 128:(h % 4 + 1) * 128], lhsT=kTm[:, h * 128:(h + 1) * 128],
                                                                       rhs=qT[:, pr * 128:(pr + 1) * 128], start=True, stop=True))(h, pr, hp, b),
                         R=['kTm', 'qT'], W=[PK(b)], inc=(h % 4 == 3))
                if RS == 24:
                    continue
                for hb in range(2):
                    b = sb_[hb]
                    S.op('dve', (lambda hb, b: lambda e: e.tensor_tensor(out=scm[:, hb * 512:(hb + 1) * 512], in0=PS(b), in1=c32rv('dmask')[:, hb * 512:(hb + 1) * 512], op=ALU.mult))(hb, b),
                         R=[PK(b), 'c32r'], W=['scm%d' % hb])
                if RS == 25:
                    continue
                bo = bank()
                for h in range(8):
                    pr, hp = h // 2, h % 2
                    S.op('pe', (lambda h, bo: lambda e: e.matmul(PS(bo)[:, h * 64:(h + 1) * 64], lhsT=scm[:, h * 128:(h + 1) * 128], rhs=vb[:, h * 64:(h + 1) * 64], start=True, stop=(RS == 26)))(h, bo),
                         R=['scm%d' % (h // 4), 'vb'], W=[PK(bo)], inc=(RS == 26 and h == 7))
                    if RS == 26:
                        continue
                    S.op('pe', (lambda h, pr, hp, bo: lambda e: e.matmul(PS(bo)[:, h * 64:(h + 1) * 64], lhsT=qsTm[:, h * 128:(h + 1) * 128],
                                                                        rhs=retSb[:, pr * 64:(pr + 1) * 64], start=False, stop=True))(h, pr, hp, bo),
                         R=['qsTm', 'retSb'], W=[PK(bo)], inc=(h == 7))
                S.op('act', (lambda bo: lambda e: e.activation(out=o_sb, in_=PS(bo), func=AF.Copy))(bo), R=[PK(bo)], W=['o_sb'])
            if RS <= 3:
                continue
            bk_ = bank()
            for pr in range(4):
                S.op('pe', (lambda pr, bk_: lambda e: e.matmul(PS(bk_)[:, pr * 128:(pr + 1) * 128], lhsT=kr[:, pr * 128:(pr + 1) * 128], rhs=vdb[:, pr * 128:(pr + 1) * 128], start=True, stop=True))(pr, bk_),
                     R=['kr', 'vdb'], W=[PK(bk_)], inc=(pr == 3))
            rs3 = retS.rearrange("p (a e) -> p a e", a=4)
            S.op('pool', lambda e: e.tensor_tensor(out=retS, in0=retS, in1=c32rv('gam128f'), op=ALU.mult), R=['retS', 'c32r', 'retSb'], W=['retS'])
            for hp in range(2):
                S.op('dve', (lambda hp, bk_: lambda e: e.tensor_tensor(out=retS[hp * 64:(hp + 1) * 64, :].rearrange("p (a e) -> p a e", a=4), in0=retS[hp * 64:(hp + 1) * 64, :].rearrange("p (a e) -> p a e", a=4),
                                                                    in1=PS(bk_)[hp * 64:(hp + 1) * 64, :].rearrange("p (a f) -> p a f", a=4)[:, :, hp * 64:(hp + 1) * 64], op=ALU.add))(hp, bk_),
                     R=[PK(bk_), 'retS'], W=['retS'])
            S.op('act', lambda e: e.activation(out=retSb, in_=retS, func=AF.Copy), R=['retS'], W=['retSb'])
            if full and ti == 0:
                dump('retS1', retS, ['retS'])
            if full and ti == 1:
                dump('retS2', retS, ['retS'])
                dump('kr1', kr, ['kr'])
            if RS <= 4:
                continue
            if full:
                group_norm_out(o_sb, 'o_sb', o_sq, 'o_sq', gst, 1e-5, ptile['ret_gn_g'], ptile['ret_gn_b'], 'pt_ret_gn_g', 'pt_ret_gn_b')
                S.op('dve', lambda e: e.tensor_tensor(out=mixr, in0=o_sb, in1=sgg, op=ALU.mult), R=['o_sb', 'sgg'], W=['mixr'])
                ret_deferred.append((lambda ti: lambda: transpose_to(mixr, ['mixr'], 4, mixTv[:, 0:4, ti * 128:(ti + 1) * 128], ['mixT_r%d' % ti]))(ti))
        for f_ in ret_deferred:
            f_()
        A.release()
        S.barrier()
        if os.environ.get('MIX_STOP') != 'ret':
            rwkv(half, full)
        if full:
            for c_ in range(8):
                dump('mixT%d' % c_, mixTv[:, c_, :], [])
            dump('retS', retS, [])
            dump('rwA', rwA, [])
        A.release()

    def group_norm_out(o, okey, sq, sqk, gst, eps, gt, bt, gk, bk):
        o3 = o.rearrange("p (h d) -> p h d", h=8)
        S.op('act', lambda e: e.activation(out=sq, in_=o, func=AF.Square), R=[okey], W=[sqk])
        S.op('dve', lambda e: e.tensor_reduce(out=gst[:, 0:8], in_=o3, axis=AX.X, op=ALU.add), R=[okey], W=['gn0'])
        S.op('dve', lambda e: e.tensor_reduce(out=gst[:, 8:16], in_=sq.rearrange("p (h d) -> p h d", h=8), axis=AX.X, op=ALU.add), R=[sqk], W=['gn1'])
        S.op('dve', lambda e: e.tensor_scalar(out=gst[:, 16:24], in0=gst[:, 0:8], scalar1=1.0 / 64, scalar2=None, op0=ALU.mult), R=['gn0'], W=['gn2'])
        S.op('dve', lambda e: e.tensor_tensor(out=gst[:, 24:32], in0=gst[:, 16:24], in1=gst[:, 16:24], op=ALU.mult), R=['gn2'], W=['gn3'])
        S.op('dve', lambda e: e.scalar_tensor_tensor(out=gst[:, 32:40], in0=gst[:, 8:16], scalar=1.0 / 64, in1=gst[:, 24:32], op0=ALU.mult, op1=ALU.subtract), R=['gn1', 'gn3'], W=['gn4'])
        S.op('dve', lambda e: e.tensor_scalar(out=gst[:, 32:40], in0=gst[:, 32:40], scalar1=float(eps), scalar2=None, op0=ALU.add), R=['gn4'], W=['gn4'])
        S.op('act', lambda e: e.activation(out=gst[:, 40:48], in_=gst[:, 32:40], func=AF.Sqrt), R=['gn4'], W=['gn5'])
        S.op('dve', lambda e: e.reciprocal(out=gst[:, 48:56], in_=gst[:, 40:48]), R=['gn5'], W=['gn6'])
        S.op('dve', lambda e: e.tensor_tensor(out=o3, in0=o3, in1=gst[:, 16:24].unsqueeze(2).broadcast_to([128, 8, 64]), op=ALU.subtract), R=[okey, 'gn2'], W=[okey])
        S.op('dve', lambda e: e.tensor_tensor(out=o3, in0=o3, in1=gst[:, 48:56].unsqueeze(2).broadcast_to([128, 8, 64]), op=ALU.mult), R=[okey, 'gn6'], W=[okey])
        S.op('pool', lambda e: e.tensor_tensor(out=o, in0=o, in1=gt, op=ALU.mult), R=[okey, gk], W=[okey])
        S.op('pool', lambda e: e.tensor_tensor(out=o, in0=o, in1=bt, op=ALU.add), R=[okey, bk], W=[okey])

    def rwkv(half, full):
        A.mark()
        w_in = dr['w_in'].rearrange("(c p) f -> p c f", p=128)
        load_ptiles(['rw_k_k', 'rw_k_a', 'rw_r_k', 'rw_gn_g', 'rw_gn_b'])
        Wa = A.bf(8 * RWC)
        Wb = A.bf(8 * RWC)
        Wav = Wa.rearrange("p (c f) -> p c f", c=8)
        Wbv = Wb.rearrange("p (c f) -> p c f", c=8)
        NW_ = 8 * RWC
        WK = {'Wa': ['Wa'] + ['Wa_ld%d' % q_ for q_ in range(4)], 'Wb': ['Wb'] + ['Wb_ld%d' % q_ for q_ in range(4)]}
        if half == 0 or 'mix_0' not in plan:
            A.mark()
            mu_t = A.f32(RWC)
            omu_t = A.f32(RWC)
            stg = [A.f32(RWC) for _ in range(2)]
            S.dma('sp', lambda e: e.dma_start(out=mu_t, in_=dr['rw_mu'].partition_broadcast(128)), R=[], W=['mu'])
            S.op('dve', lambda e: e.tensor_scalar(out=omu_t, in0=mu_t, scalar1=-1.0, scalar2=1.0, op0=ALU.mult, op1=ALU.add), R=['mu'], W=['omu'])
            for dc in range(8):
                sgb = stg[dc % 2]
                sk = 'stg%d' % (dc % 2)
                S.dma('sp', lambda e: e.dma_start(out=sgb, in_=dr['w_in'][dc * 128:(dc + 1) * 128, RETC:INC]), R=[], W=[sk])
                S.op('dve', lambda e: e.tensor_tensor(out=Wav[:, dc, :], in0=sgb, in1=omu_t, op=ALU.mult), R=[sk, 'omu'], W=['Wa'])
                S.op('pool', lambda e: e.tensor_tensor(out=Wbv[:, dc, :], in0=sgb, in1=mu_t, op=ALU.mult), R=[sk, 'mu'], W=['Wb'])
            for q_ in range(4):
                S.dma('sp', lambda e: e.dma_start(out=wab_s[:, q_ * (NW_ // 4):(q_ + 1) * (NW_ // 4)], in_=Wa[:, q_ * (NW_ // 4):(q_ + 1) * (NW_ // 4)]), R=['Wa'], W=['wabs_a%d' % q_], sk='wabs')
                S.dma('sp', lambda e: e.dma_start(out=wab_s[:, NW_ + q_ * (NW_ // 4):NW_ + (q_ + 1) * (NW_ // 4)], in_=Wb[:, q_ * (NW_ // 4):(q_ + 1) * (NW_ // 4)]), R=['Wb'], W=['wabs_b%d' % q_], sk='wabs')
            A.release()
        else:
            for q_ in range(4):
                S.dma('sp', lambda e: e.dma_start(out=Wa[:, q_ * (NW_ // 4):(q_ + 1) * (NW_ // 4)], in_=wab_s[:, q_ * (NW_ // 4):(q_ + 1) * (NW_ // 4)]), R=[], W=['Wa_ld%d' % q_])
                S.dma('sp', lambda e: e.dma_start(out=Wb[:, q_ * (NW_ // 4):(q_ + 1) * (NW_ // 4)], in_=wab_s[:, NW_ + q_ * (NW_ // 4):NW_ + (q_ + 1) * (NW_ // 4)]), R=[], W=['Wb_ld%d' % q_])
        wup = A.f32(512)
        aup = A.f32(512)
        gup1 = A.bf(512)
        gup2 = A.bf(512)
        w0r = A.f32(512)
        a0r = A.f32(512)
        for t_, k_ in ((wup, 'wup'), (aup, 'aup'), (gup2, 'gup2'), (w0r, 'w0r'), (a0r, 'a0r')):
            S.op('pool', (lambda t_: lambda e: e.memset(t_, 0.0))(t_), R=[], W=[k_])
        S.dma('sp', lambda e: e.dma_start(out=wup[0:64, :], in_=dr['rw_w_up']), R=[], W=['wup'])
        S.dma('sp', lambda e: e.dma_start(out=aup[64:128, :], in_=dr['rw_a_up']), R=[], W=['aup'])
        S.dma('pool', lambda e: e.dma_start(out=gup1, in_=dr['rw_g_up'][0:128, :]), R=[], W=['gup1'])
        S.dma('pool', lambda e: e.dma_start(out=gup2[96:128, :], in_=dr['rw_g_up'][128:160, :]), R=[], W=['gup2'])
        S.dma('sp', lambda e: e.dma_start(out=w0r[0:1, :], in_=dr['rw_w0']), R=[], W=['w0r'])
        S.dma('sp', lambda e: e.dma_start(out=a0r[0:1, :], in_=dr['rw_a0']), R=[], W=['a0r'])
        ones_r = c32v('ones')
        twT = A.f32(128)
        sgd1 = A.bf(128)
        sgd2 = A.bf(128)
        sw = A.f32(512)
        a_t = A.f32(512)
        gi_t = A.f32(512)
        kkr = A.f32(512)
        sq = A.f32(512)
        kp = A.f32(512)
        u1 = sq
        g_t = sw
        v32 = A.f32(512)
        st = A.f32(64)
        gst2 = A.f32(64)
        r32 = A.f32(512)
        k32 = A.f32(512)
        gp_t = k32
        gC = A.f32(4)
        rh = A.bf(512)
        kt = A.bf(512)
        bt_ = A.bf(512)
        kh = A.bf(512)
        vb = A.bf(512)
        XT = A.bf(4 * 256)
        XTv = XT.rearrange("p (a b t) -> p a b t", a=4, b=2)
        btT = A.bf(512)
        khTm = A.bf(1024)
        rhTm = A.bf(1024)
        btTm = A.bf(1024)
        ktTm = A.bf(1024)
        for t_, k_ in ((khTm, 'khTm'), (rhTm, 'rhTm'), (btTm, 'btTm'), (ktTm, 'ktTm')):
            S.op('pool', (lambda t_: lambda e: e.memset(t_, 0.0))(t_), R=[], W=[k_])
        LkT = A.bf(1024)
        MbT = A.bf(1024)
        MkT = A.bf(1024)
        Abuf = [[A.bf(256) for _ in range(2)] for _ in range(4)]
        BQbuf = [[A.bf(512) for _ in range(2)] for _ in range(4)]
        TT = A.bf(1024)
        Xb = A.bf(512)
        Ub = A.bf(512)
        mixw = A.bf(512)
        bsum = st[:, 56:64]
        rw_deferred = []

        for ti in range(NT_HALF):
            c0 = XO + ti * 128
            xk = ['x1T_%d' % ti] + (['x1T_%d' % (ti - 1)] if ti > 0 else ['x1T_prev'])
            def proj(nm):
                qi = ['r', 'k', 'v'].index(nm)
                b = bank()
                n = 0
                for (Wv, wkey, sh) in ((Wav, 'Wa', 0), (Wbv, 'Wb', 1)):
                    for dc in range(8):
                        S.op('pe', (lambda Wv, sh, dc, b, qi, n: lambda e: e.matmul(PS(b), lhsT=x1Tv[:, dc, c0 - sh:c0 - sh + 128], rhs=Wv[:, dc, qi * 512:(qi + 1) * 512], start=(n == 0), stop=(n == 15)))(Wv, sh, dc, b, qi, n),
                             R=xk + WK[wkey], W=[PK(b)], inc=(n == 15))
                        n += 1
                return b
            bl = bank()
            lora_groups = ((1536, 128), (1664, 128), (1696, 128)) if full else ((1536, 128),)
            for gi_, (cs, cn) in enumerate(lora_groups):
                n = 0
                for (Wv, wkey, sh) in ((Wav, 'Wa', 0), (Wbv, 'Wb', 1)):
                    for dc in range(8):
                        S.op('pe', (lambda Wv, sh, dc, cs, cn, gi_, n: lambda e: e.matmul(PS(bl)[0:cn, gi_ * 128:(gi_ + 1) * 128], lhsT=Wv[:, dc, cs:cs + cn], rhs=x1Tv[:, dc, c0 - sh:c0 - sh + 128], start=(n == 0), stop=(n == 15)))(Wv, sh, dc, cs, cn, gi_, n),
                             R=xk + WK[wkey], W=[PK(bl)], inc=(n == 15 and gi_ == len(lora_groups) - 1))
                        n += 1
            plk = PK(bl)
            S.op('act', lambda e: e.activation(out=twT[0:64, :], in_=PS(bl)[0:64, 0:128], func=AF.Tanh), R=[plk], W=['twT'])
            S.op('act', lambda e: e.activation(out=twT[64:128, :], in_=PS(bl)[64:128, 0:128], func=AF.Copy), R=[plk], W=['twT'])
            if full:
                S.op('act', lambda e: e.activation(out=sgd1, in_=PS(bl)[:, 128:256], func=AF.Sigmoid), R=[plk], W=['sgd1'])
                S.op('act', lambda e: e.activation(out=sgd2, in_=PS(bl)[:, 256:384], func=AF.Sigmoid), R=[plk], W=['sgd2'])
            bk2 = proj('k')
            kk_ = PK(bk2)
            S.op('act', lambda e: e.activation(out=k32, in_=PS(bk2), func=AF.Copy), R=[kk_], W=['k32'])
            bw = bank()
            S.op('pe', lambda e: e.matmul(PS(bw), lhsT=twT, rhs=wup, start=True, stop=False), R=['twT', 'wup'], W=[PK(bw)], inc=False)
            S.op('pe', lambda e: e.matmul(PS(bw), lhsT=ones_r, rhs=w0r, start=False, stop=True), R=['c32', 'w0r'], W=[PK(bw)])
            ba = bank()
            S.op('pe', lambda e: e.matmul(PS(ba), lhsT=twT, rhs=aup, start=True, stop=False), R=['twT', 'aup'], W=[PK(ba)], inc=False)
            S.op('pe', lambda e: e.matmul(PS(ba), lhsT=ones_r, rhs=a0r, start=False, stop=True), R=['c32', 'a0r'], W=[PK(ba)])
            S.op('act', lambda e: e.activation(out=sw, in_=PS(bw), func=AF.Sigmoid), R=[PK(bw)], W=['sw'])
            S.op('act', lambda e: e.activation(out=a_t, in_=PS(ba), func=AF.Sigmoid), R=[PK(ba)], W=['a_t'])
            bv = proj('v')
            vk_ = PK(bv)
            S.op('act', lambda e: e.activation(out=v32, in_=PS(bv), func=AF.Copy), R=[vk_], W=['v32'])
            S.op('dve', lambda e: e.tensor_copy(out=vb, in_=PS(bv)), R=[vk_], W=['vbw'])
            bc = bank()
            S.op('pe', lambda e: e.matmul(PS(bc), lhsT=c32v('tri_incl'), rhs=sw, start=True, stop=True), R=['c32', 'sw'], W=[PK(bc)])
            bx = bank()
            S.op('pe', lambda e: e.matmul(PS(bx), lhsT=c32v('tri_strict'), rhs=sw, start=True, stop=True), R=['c32', 'sw'], W=[PK(bx)])
            bgc = bank()
            for pr in range(4):
                S.op('pe', (lambda pr: lambda e: e.matmul(PS(bgc)[:, pr:pr + 1], lhsT=sw[:, pr * 128:(pr + 1) * 128], rhs=c32v('negcol'), start=True, stop=True))(pr), R=['sw', 'c32'], W=[PK(bgc)], inc=(pr == 3))
            if full:
                br = proj('r')
                rk_ = PK(br)
                S.op('act', lambda e: e.activation(out=r32, in_=PS(br), func=AF.Copy), R=[rk_], W=['r32'])
            for f_ in rw_deferred:
                f_()
            del rw_deferred[:]
            S.op('dve', lambda e: e.tensor_tensor(out=kkr, in0=k32, in1=ptile['rw_k_k'], op=ALU.mult), R=['k32', 'pt_rw_k_k'], W=['kkr'])
            S.op('act', lambda e: e.activation(out=sq, in_=kkr, func=AF.Square), R=['kkr'], W=['sq'])
            S.op('dve', lambda e: e.tensor_reduce(out=st[:, 0:8], in_=sq.rearrange("p (h d) -> p h d", h=8), axis=AX.X, op=ALU.add), R=['sq'], W=['st0'])
            S.op('act', lambda e: e.activation(out=st[:, 8:16], in_=st[:, 0:8], func=AF.Sqrt), R=['st0'], W=['st1'])
            S.op('dve', lambda e: e.tensor_scalar(out=st[:, 8:16], in0=st[:, 8:16], scalar1=1e-12, scalar2=None, op0=ALU.max), R=['st1'], W=['st1'])
            S.op('dve', lambda e: e.reciprocal(out=st[:, 16:24], in_=st[:, 8:16]), R=['st1'], W=['st2'])
            S.op('dve', lambda e: e.tensor_tensor(out=kkr.rearrange("p (h d) -> p h d", h=8), in0=kkr.rearrange("p (h d) -> p h d", h=8), in1=st[:, 16:24].unsqueeze(2).broadcast_to([128, 8, 64]), op=ALU.mult), R=['kkr', 'st2'], W=['kkr'])
            S.op('dve', lambda e: e.scalar_tensor_tensor(out=u1, in0=a_t, scalar=-1.0, in1=ptile['rw_k_a'], op0=ALU.add, op1=ALU.mult), R=['a_t', 'pt_rw_k_a'], W=['sq'])
            S.op('dve', lambda e: e.scalar_tensor_tensor(out=kp, in0=u1, scalar=1.0, in1=k32, op0=ALU.add, op1=ALU.mult), R=['sq', 'k32'], W=['kp'])
            if full:
                S.op('dve', lambda e: e.tensor_tensor(out=u1, in0=r32, in1=kp, op=ALU.mult), R=['r32', 'kp', 'sq'], W=['sq'])
                S.op('pool', lambda e: e.tensor_tensor(out=u1, in0=u1, in1=ptile['rw_r_k'], op=ALU.mult), R=['sq', 'pt_rw_r_k'], W=['sq'])
                S.op('dve', lambda e: e.tensor_reduce(out=bsum, in_=u1.rearrange("p (h d) -> p h d", h=8), axis=AX.X, op=ALU.add), R=['sq'], W=['bsum'])
            S.op('act', lambda e: e.activation(out=g_t, in_=PS(bc), func=AF.Exp), R=[PK(bc)], W=['sw'])
            S.op('act', lambda e: e.activation(out=gi_t, in_=PS(bc), func=AF.Exp, scale=-1.0), R=[PK(bc)], W=['gi_t'])
            S.op('act', lambda e: e.activation(out=gp_t, in_=PS(bx), func=AF.Exp), R=[PK(bx)], W=['k32'])
            S.op('act', lambda e: e.activation(out=gC, in_=PS(bgc)[:, 0:4], func=AF.Exp), R=[PK(bgc)], W=['gC'])
            if full:
                S.op('dve', lambda e: e.tensor_tensor(out=rh, in0=r32, in1=g_t, op=ALU.mult), R=['r32', 'sw'], W=['rh'])
            S.op('pool', lambda e: e.tensor_tensor(out=kt, in0=kp, in1=gi_t, op=ALU.mult), R=['kp', 'gi_t'], W=['kt'])
            S.op('pool', lambda e: e.tensor_tensor(out=sq, in0=kkr, in1=a_t, op=ALU.mult), R=['kkr', 'a_t', 'sq'], W=['sq'])
            S.op('pool', lambda e: e.tensor_tensor(out=bt_, in0=sq, in1=gi_t, op=ALU.mult), R=['sq', 'gi_t'], W=['bt'])
            S.op('pool', lambda e: e.tensor_tensor(out=kh, in0=kkr, in1=gp_t, op=ALU.mult), R=['kkr', 'k32'], W=['kh'])
            if full:
                S.op('dve', lambda e: e.tensor_tensor(out=gi_t.rearrange("p (h d) -> p h d", h=8), in0=v32.rearrange("p (h d) -> p h d", h=8), in1=bsum.unsqueeze(2).broadcast_to([128, 8, 64]), op=ALU.mult), R=['v32', 'bsum', 'gi_t'], W=['gi_t'])
            b1 = bank()
            for i in range(4):
                S.op('pe', (lambda i: lambda e: e.transpose(out=PSB(b1)[:, i * 256:i * 256 + 128], in_=kh[:, i * 128:(i + 1) * 128], identity=ident))(i), R=['kh', 'cb'], W=[PK(b1)], inc=(i == 3 and not full))
                if full:
                    S.op('pe', (lambda i: lambda e: e.transpose(out=PSB(b1)[:, i * 256 + 128:i * 256 + 256], in_=rh[:, i * 128:(i + 1) * 128], identity=ident))(i), R=['rh', 'cb'], W=[PK(b1)], inc=(i == 3))
            if full:
                S.op('act', lambda e: e.activation(out=XT, in_=PSB(b1), func=AF.Copy), R=[PK(b1)], W=['XT'])
            else:
                S.op('act', lambda e: e.activation(out=XT.rearrange("p (a c t) -> p a c t", a=4, c=2)[:, :, 0, :], in_=PSB(b1).rearrange("p (a c t) -> p a c t", a=4, c=2)[:, :, 0, :], func=AF.Copy), R=[PK(b1)], W=['XT'])
            for hp in range(2):
                rs_ = slice(hp * 64, (hp + 1) * 64)
                srcv = PSB(b1)[rs_, :].rearrange("p (a c t) -> p a c t", a=4, c=2)
                S.op('act', lambda e: e.activation(out=khTm[rs_, :].rearrange("p (a c t) -> p a c t", a=4, c=2)[:, :, hp, :], in_=srcv[:, :, 0, :], func=AF.Copy), R=[PK(b1)], W=['khTm'])
                if full:
                    S.op('dve', lambda e: e.tensor_copy(out=rhTm[rs_, :].rearrange("p (a c t) -> p a c t", a=4, c=2)[:, :, hp, :], in_=srcv[:, :, 1, :]), R=[PK(b1)], W=['rhTm'])
            b2 = bank()
            for i in range(4):
                S.op('pe', (lambda i: lambda e: e.transpose(out=PSB(b2)[:, i * 128:(i + 1) * 128], in_=bt_[:, i * 128:(i + 1) * 128], identity=ident))(i), R=['bt', 'cb'], W=[PK(b2)], inc=False)
            for i in range(4):
                S.op('pe', (lambda i: lambda e: e.transpose(out=PSB(b2)[:, 512 + i * 128:512 + (i + 1) * 128], in_=kt[:, i * 128:(i + 1) * 128], identity=ident))(i), R=['kt', 'cb'], W=[PK(b2)], inc=(i == 3))
            S.op('dve', lambda e: e.tensor_copy(out=btT, in_=PSB(b2)[:, 0:512]), R=[PK(b2)], W=['btT'])
            for hp in range(2):
                rs_ = slice(hp * 64, (hp + 1) * 64)
                S.op('act', lambda e: e.activation(out=btTm[rs_, :].rearrange("p (a c t) -> p a c t", a=4, c=2)[:, :, hp, :], in_=PSB(b2)[rs_, 0:512].rearrange("p (a t) -> p a t", a=4), func=AF.Copy), R=[PK(b2)], W=['btTm'])
                S.op('dve', lambda e: e.tensor_copy(out=ktTm[rs_, :].rearrange("p (a c t) -> p a c t", a=4, c=2)[:, :, hp, :], in_=PSB(b2)[rs_, 512:1024].rearrange("p (a t) -> p a t", a=4)), R=[PK(b2)], W=['ktTm'])
            for pr in range(4):
                bm1, bm2, bm3 = bank(), bank(), bank()
                for hp in range(2):
                    ps_ = slice(hp * 64, (hp + 1) * 64)
                    NX_ = 256 if full else 128
                    rhs_x = XT[:, pr * 256:pr * 256 + NX_]
                    hh_ = pr * 2 + hp
                    S.op('pe', (lambda hp, ps_, rhs_x, bm1: lambda e: e.matmul(PS(bm1)[:, hp * 256:hp * 256 + NX_], lhsT=btTm[:, hh_ * 128:(hh_ + 1) * 128], rhs=rhs_x, start=True, stop=True))(hp, ps_, rhs_x, bm1),
                         R=['btTm', 'XT'], W=[PK(bm1)], inc=(hp == 1))
                    S.op('pe', (lambda hp, ps_, rhs_x, bm2: lambda e: e.matmul(PS(bm2)[:, hp * 256:hp * 256 + NX_], lhsT=ktTm[:, hh_ * 128:(hh_ + 1) * 128], rhs=rhs_x, start=True, stop=True))(hp, ps_, rhs_x, bm2),
                         R=['ktTm', 'XT'], W=[PK(bm2)], inc=(hp == 1))
                    S.op('pe', (lambda hp, ps_, bm3: lambda e: e.matmul(PS(bm3)[:, hp * 128:(hp + 1) * 128], lhsT=khTm[:, hh_ * 128:(hh_ + 1) * 128], rhs=btT[:, pr * 128:(pr + 1) * 128], start=True, stop=True))(hp, ps_, bm3),
                         R=['btT', 'khTm'], W=[PK(bm3)], inc=(hp == 1))
                A0 = Abuf[pr][0]
                BQ0 = BQbuf[pr][0].rearrange("p (h x) -> p h x", h=2)
                m1v = cbv('m1').rearrange("p (h x) -> p h x", h=2)
                p1v = PS(bm1).rearrange("p (h x) -> p h x", h=2)
                S.op('dve', (lambda BQ0, p1v, m1v: lambda e: e.tensor_tensor(out=BQ0[:, :, 0:128], in0=p1v[:, :, 0:128], in1=m1v[:, :, 0:128], op=ALU.mult))(BQ0, p1v, m1v),
                     R=[PK(bm1), 'cb'], W=['BQ%d_0' % pr])
                S.op('pool', (lambda BQ0: lambda e: e.tensor_copy(out=BQ0[:, :, 128:256], in_=ident.unsqueeze(1).broadcast_to([128, 2, 128])))(BQ0), R=['cb'], W=['BQ%d_0' % pr])
                if full:
                    S.op('dve', (lambda p1v, m1v: lambda e: e.tensor_tensor(out=MbT[:, pr * 256:(pr + 1) * 256].rearrange("p (h x) -> p h x", h=2), in0=p1v[:, :, 128:256], in1=m1v[:, :, 128:256], op=ALU.mult))(p1v, m1v),
                         R=[PK(bm1), 'cb'], W=['MbT%d' % pr])
                m2v = cbv('m2').rearrange("p (h x) -> p h x", h=2)
                p2v = PS(bm2).rearrange("p (h x) -> p h x", h=2)
                S.op('dve', (lambda p2v, m2v: lambda e: e.tensor_tensor(out=LkT[:, pr * 256:(pr + 1) * 256].rearrange("p (h x) -> p h x", h=2), in0=p2v[:, :, 0:128], in1=m2v[:, :, 0:128], op=ALU.mult))(p2v, m2v),
                     R=[PK(bm2), 'cb'], W=['LkT%d' % pr])
                if full:
                    S.op('dve', (lambda p2v, m2v: lambda e: e.tensor_tensor(out=MkT[:, pr * 256:(pr + 1) * 256].rearrange("p (h x) -> p h x", h=2), in0=p2v[:, :, 128:256], in1=m2v[:, :, 128:256], op=ALU.mult))(p2v, m2v),
                         R=[PK(bm2), 'cb'], W=['MkT%d' % pr])
                S.op('dve', (lambda A0, bm3: lambda e: e.tensor_tensor(out=A0, in0=PS(bm3)[:, 0:256], in1=cbv('m3'), op=ALU.mult))(A0, bm3), R=[PK(bm3), 'cb'], W=['A%d_0' % pr])
            for lv in range(7):
                last = (lv == 6)
                for pr in range(4):
                    cur, nxt = lv % 2, (lv + 1) % 2
                    Ac = Abuf[pr][cur].rearrange("p (h x) -> p h x", h=2)
                    BQc = BQbuf[pr][cur].rearrange("p (h x) -> p h x", h=2)
                    ak, bqk = 'A%d_%d' % (pr, cur), 'BQ%d_%d' % (pr, cur)
                    if not last:
                        bA = bank()
                        for hp in range(2):
                            S.op('pe', (lambda hp, BQc, Ac, bA: lambda e: e.matmul(PS(bA)[:, hp * 128:(hp + 1) * 128], lhsT=BQc[:, hp, 0:128], rhs=Ac[:, hp, :], start=True, stop=True))(hp, BQc, Ac, bA),
                                 R=[ak, bqk], W=[PK(bA)], inc=(hp == 1))
                        bB = bank()
                        for hp in range(2):
                            S.op('pe', (lambda hp, BQc, Ac, bB: lambda e: e.matmul(PS(bB)[:, hp * 256:hp * 256 + 256], lhsT=Ac[:, hp, :], rhs=BQc[:, hp, :], start=True, stop=False, skip_group_check=True))(hp, BQc, Ac, bB),
                                 R=[ak, bqk], W=[PK(bB)], inc=False)
                            S.op('pe', (lambda hp, BQc, bB: lambda e: e.matmul(PS(bB)[:, hp * 256 + 128:hp * 256 + 256], lhsT=ident, rhs=BQc[:, hp, 128:256], start=False, stop=True, skip_group_check=True))(hp, BQc, bB),
                                 R=[bqk, 'cb'], W=[PK(bB)], inc=(hp == 1))
                        An = Abuf[pr][nxt]
                        BQn = BQbuf[pr][nxt]
                        S.op('dve' if pr % 2 == 0 else 'act', (lambda An, bA, pr: (lambda e: e.tensor_copy(out=An, in_=PS(bA)[:, 0:256])) if pr % 2 == 0 else (lambda e: e.activation(out=An, in_=PS(bA)[:, 0:256], func=AF.Copy)))(An, bA, pr),
                             R=[PK(bA)], W=['A%d_%d' % (pr, nxt)])
                        S.op('dve' if pr % 2 else 'act', (lambda BQn, bB, pr: (lambda e: e.tensor_copy(out=BQn, in_=PS(bB))) if pr % 2 else (lambda e: e.activation(out=BQn, in_=PS(bB), func=AF.Copy)))(BQn, bB, pr),
                             R=[PK(bB)], W=['BQ%d_%d' % (pr, nxt)])
                    else:
                        bB = bank()
                        for hp in range(2):
                            S.op('pe', (lambda hp, BQc, Ac, bB: lambda e: e.matmul(PS(bB)[:, hp * 128:(hp + 1) * 128], lhsT=Ac[:, hp, :], rhs=BQc[:, hp, 128:256], start=True, stop=False))(hp, BQc, Ac, bB),
                                 R=[ak, bqk], W=[PK(bB)], inc=False)
                            S.op('pe', (lambda hp, BQc, bB: lambda e: e.matmul(PS(bB)[:, hp * 128:(hp + 1) * 128], lhsT=ident, rhs=BQc[:, hp, 128:256], start=False, stop=True))(hp, BQc, bB),
                                 R=[bqk, 'cb'], W=[PK(bB)], inc=(hp == 1))
                        S.op('dve', (lambda bB, pr: lambda e: e.tensor_copy(out=TT[:, pr * 256:(pr + 1) * 256], in_=PS(bB)[:, 0:256]))(bB, pr), R=[PK(bB)], W=['TT%d' % pr])
            bX = bank()
            for h in range(8):
                pr, hp = h // 2, h % 2
                ps_ = slice(hp * 64, (hp + 1) * 64)
                S.op('pe', (lambda h, pr, ps_: lambda e: e.matmul(PS(bX)[:, h * 64:(h + 1) * 64], lhsT=khTm[:, h * 128:(h + 1) * 128], rhs=rwAb[:, pr * 64:(pr + 1) * 64], start=True, stop=False))(h, pr, ps_),
                     R=['khTm', 'rwAb'], W=[PK(bX)], inc=False)
                S.op('pe', (lambda h: lambda e: e.matmul(PS(bX)[:, h * 64:(h + 1) * 64], lhsT=LkT[:, h * 128:(h + 1) * 128], rhs=vb[:, h * 64:(h + 1) * 64], start=False, stop=True))(h),
                     R=['LkT%d' % pr, 'vbw'], W=[PK(bX)], inc=(h == 7))
            S.op('act', lambda e: e.activation(out=Xb, in_=PS(bX), func=AF.Copy, scale=-1.0), R=[PK(bX)], W=['Xb'])
            bU = bank()
            for h in range(8):
                S.op('pe', (lambda h: lambda e: e.matmul(PS(bU)[:, h * 64:(h + 1) * 64], lhsT=TT[:, h * 128:(h + 1) * 128], rhs=Xb[:, h * 64:(h + 1) * 64], start=True, stop=True))(h),
                     R=['TT%d' % (h // 2), 'Xb'], W=[PK(bU)], inc=(h == 7))
            S.op('act', lambda e: e.activation(out=Ub, in_=PS(bU), func=AF.Copy), R=[PK(bU)], W=['Ub'])
            if full:
                bY = bank()
                for h in range(8):
                    pr, hp = h // 2, h % 2
                    ps_ = slice(hp * 64, (hp + 1) * 64)
                    S.op('pe', (lambda h, pr, ps_: lambda e: e.matmul(PS(bY)[:, h * 64:(h + 1) * 64], lhsT=rhTm[:, h * 128:(h + 1) * 128], rhs=rwAb[:, pr * 64:(pr + 1) * 64], start=True, stop=False))(h, pr, ps_),
                         R=['rhTm', 'rwAb'], W=[PK(bY)], inc=False)
                    S.op('pe', (lambda h: lambda e: e.matmul(PS(bY)[:, h * 64:(h + 1) * 64], lhsT=MbT[:, h * 128:(h + 1) * 128], rhs=Ub[:, h * 64:(h + 1) * 64], start=False, stop=False))(h),
                         R=['MbT%d' % pr, 'Ub'], W=[PK(bY)], inc=False)
                    S.op('pe', (lambda h: lambda e: e.matmul(PS(bY)[:, h * 64:(h + 1) * 64], lhsT=MkT[:, h * 128:(h + 1) * 128], rhs=vb[:, h * 64:(h + 1) * 64], start=False, stop=True))(h),
                         R=['MkT%d' % pr, 'vbw'], W=[PK(bY)], inc=(h == 7))
                S.op('act', lambda e: e.activation(out=kp, in_=PS(bY), func=AF.Copy), R=[PK(bY)], W=['kp'])
            bS = bank()
            for pr in range(4):
                S.op('pe', (lambda pr: lambda e: e.matmul(PS(bS)[:, pr * 128:(pr + 1) * 128], lhsT=bt_[:, pr * 128:(pr + 1) * 128], rhs=Ub[:, pr * 128:(pr + 1) * 128], start=True, stop=False))(pr),
                     R=['bt', 'Ub'], W=[PK(bS)], inc=False)
                S.op('pe', (lambda pr: lambda e: e.matmul(PS(bS)[:, pr * 128:(pr + 1) * 128], lhsT=kt[:, pr * 128:(pr + 1) * 128], rhs=vb[:, pr * 128:(pr + 1) * 128], start=False, stop=True))(pr),
                     R=['kt', 'vbw'], W=[PK(bS)], inc=(pr == 3))
            for hp in range(2):
                S.op('dve', (lambda hp: lambda e: e.tensor_tensor(out=rwA[hp * 64:(hp + 1) * 64, :].rearrange("p (a e) -> p a e", a=4), in0=rwA[hp * 64:(hp + 1) * 64, :].rearrange("p (a e) -> p a e", a=4),
                                                                in1=PS(bS)[hp * 64:(hp + 1) * 64, :].rearrange("p (a f) -> p a f", a=4)[:, :, hp * 64:(hp + 1) * 64], op=ALU.add))(hp),
                     R=[PK(bS), 'rwA', 'rwAb'], W=['rwA'])
            S.op('dve', lambda e: e.tensor_tensor(out=rwA.rearrange("p (a e) -> p a e", a=4), in0=rwA.rearrange("p (a e) -> p a e", a=4), in1=gC.unsqueeze(2).broadcast_to([128, 4, 64]), op=ALU.mult), R=['rwA', 'gC'], W=['rwA'])
            S.op('act', lambda e: e.activation(out=rwAb, in_=rwA, func=AF.Copy), R=['rwA'], W=['rwAb'])
            if full:
                group_norm_out(kp, 'kp', sq, 'sq', gst2, 64e-5, ptile['rw_gn_g'], ptile['rw_gn_b'], 'pt_rw_gn_g', 'pt_rw_gn_b')
                S.op('pool', lambda e: e.tensor_tensor(out=kp, in0=kp, in1=gi_t, op=ALU.add), R=['kp', 'gi_t'], W=['kp'])
                bgt = bank()
                S.op('pe', lambda e: e.matmul(PS(bgt), lhsT=sgd1, rhs=gup1, start=True, stop=False), R=['sgd1', 'gup1'], W=[PK(bgt)], inc=False)
                S.op('pe', lambda e: e.matmul(PS(bgt), lhsT=sgd2, rhs=gup2, start=False, stop=True), R=['sgd2', 'gup2'], W=[PK(bgt)])
                S.op('dve', lambda e: e.tensor_tensor(out=mixw, in0=kp, in1=PS(bgt), op=ALU.mult), R=['kp', PK(bgt)], W=['mixw'])
                rw_deferred.append((lambda ti: lambda: transpose_to(mixw, ['mixw'], 4, mixTv[:, 4:8, ti * 128:(ti + 1) * 128], ['mixT_w%d' % ti]))(ti))
        for f_ in rw_deferred:
            f_()
        A.release()

    S.op('pool', lambda e: e.memset(retS, 0.0), R=[], W=['retS'])
    S.op('pool', lambda e: e.memset(retSb, 0.0), R=[], W=['retSb'])
    S.op('pool', lambda e: e.memset(rwA, 0.0), R=[], W=['rwA'])
    S.op('pool', lambda e: e.memset(rwAb, 0.0), R=[], W=['rwAb'])
    S.op('pool', lambda e: e.memset(x1T, 0.0), R=[], W=['x1T_prev'] + ['x1T_%d' % i for i in range(NT_HALF)])

    for half in range(2):
        full = (half == 1)
        A.mark()
        xTv = mixTv
        src = dr['xs'][half * S_HALF:(half + 1) * S_HALF, :]
        prep_xT(src, S_HALF, xTv, 0, lambda t0, n: ['xT_%d' % (t0 // 512)] if n == 128 else ['xT_%d' % (t0 // 512)])
        load_ln('ln1_g', 'ln1_b')
        x1b = [A.bf(1024) for _ in range(8)]
        if half == 1:
            S.op('pool', lambda e: e.tensor_copy(out=x1Tv[:, :, XO - 1:XO], in_=x1Tv[:, :, XO + S_HALF - 1:XO + S_HALF]), R=['x1T_%d' % (NT_HALF - 1)], W=['x1T_prev'])

        def cons1(ti, y, yk, half=half, full=full, x1b=x1b):
            if full:
                S.dma('sp', lambda e: e.dma_start(out=x1s[ti * 128:(ti + 1) * 128, :], in_=y), R=[yk], W=['x1s_%d' % ti], sk='x1s')
            b = x1b[ti % 8]
            bk = 'x1b%d' % (ti % 8)
            S.op('dve', lambda e: e.tensor_scalar(out=b, in0=y, scalar1=hmask[:, half:half + 1], scalar2=None, op0=ALU.mult), R=[yk, 'hmask'], W=[bk])
            return lambda: transpose_to(b, [bk], 8, x1Tv[:, :, XO + ti * 128:XO + (ti + 1) * 128], ['x1T_%d' % ti], evac_eng='dve')

        if 'ffn1_%d' % half in plan:
            ffn(xTv, 0, lambda t0, n: ['xT_%d' % (t0 // 512)], S_HALF, dr['ffn1_w_gu'], dr['ffn1_w_down'], src, LN_EPS / (ALPHA * ALPHA), cons1, 'f1')
        A.release()
        S.barrier()
        if 'mix_%d' % half in plan:
            mixer(half, full)
        S.barrier()

    A.mark()
    NTW = NT_HALF if 'wout' in plan else 0
    wo = A.bf(8 * 1024)
    wov = wo.rearrange("p (c f) -> p c f", c=8)
    w_out_v = dr['w_out'].rearrange("(c p) f -> p c f", p=128)
    for hh in range(2):
        S.dma('pool', (lambda hh: lambda e: e.dma_start(out=wov[:, :, hh * 512:(hh + 1) * 512], in_=w_out_v[:, :, hh * 512:(hh + 1) * 512]))(hh), R=[], W=['wo%d' % hh])
    load_ln('ln2_g', 'ln2_b')
    ybuf = [A.f32(1024) for _ in range(8)]
    x2b = [A.bf(1024) for _ in range(8)]
    tmp_sq = A.f32(1024)
    st = A.f32(32)
    wo_deferred = []
    for g in range(NTW // 4):
        tiles = []
        for ti in range(4 * g, 4 * g + 4):
            y, yk = ybuf[ti % 8], 'y%d' % (ti % 8)
            S.dma('sp', (lambda y, ti: lambda e: e.dma_start(out=y, in_=x1s[ti * 128:(ti + 1) * 128, :]))(y, ti), R=['x1s_%d' % ti], W=[yk])
            for dh in range(2):
                b = bank()
                for c in range(8):
                    S.op('pe', (lambda c, b, dh, ti: lambda e: e.matmul(PS(b), lhsT=mixTv[:, c, ti * 128:(ti + 1) * 128], rhs=wov[:, c, dh * 512:(dh + 1) * 512], start=(c == 0), stop=(c == 7)))(c, b, dh, ti),
                         R=['mixT_r%d' % ti, 'mixT_w%d' % ti, 'wo%d' % dh], W=[PK(b)], inc=(c == 7))
                S.op('dve', (lambda y, b, dh: lambda e: e.scalar_tensor_tensor(out=y[:, dh * 512:(dh + 1) * 512], in0=PS(b), scalar=1.0 / ALPHA, in1=y[:, dh * 512:(dh + 1) * 512], op0=ALU.mult, op1=ALU.add))(y, b, dh),
                     R=[PK(b), yk], W=[yk])
            tiles.append((ti, y, yk))
        for f_ in wo_deferred:
            f_()
        del wo_deferred[:]
        layer_norm_group([(y, yk) for (ti, y, yk) in tiles], LN_EPS / (ALPHA * ALPHA), tmp_sq, st)
        for (ti, y, yk) in tiles:
            S.dma('sp', (lambda y, ti: lambda e: e.dma_start(out=x2s[ti * 128:(ti + 1) * 128, :], in_=y))(y, ti), R=[yk], W=['x2s_%d' % ti], sk='x2s')
            b2, b2k = x2b[ti % 8], 'x2b%d' % (ti % 8)
            S.op('dve', (lambda b2, y: lambda e: e.tensor_copy(out=b2, in_=y))(b2, y), R=[yk], W=[b2k])
            wo_deferred.append((lambda b2, b2k, ti: lambda: transpose_to(b2, [b2k], 8, x2Tv[:, :, XO + ti * 128:XO + (ti + 1) * 128], ['x2T_%d' % (ti // 4)], evac_eng='dve'))(b2, b2k, ti))
    for f_ in wo_deferred:
        f_()
    A.release()
    S.barrier()

    A.mark()
    load_ln('ln3_g', 'ln3_b')
    x3b = [A.bf(1024) for _ in range(8)]

    def cons3(ti, y, yk):
        S.dma('sp', lambda e: e.dma_start(out=x3s[ti * 128:(ti + 1) * 128, :], in_=y), R=[yk], W=['x3s_%d' % ti], sk='x3s')
        b = x3b[ti % 8]
        bk = 'x3b%d' % (ti % 8)
        S.op('dve', lambda e: e.tensor_copy(out=b, in_=y), R=[yk], W=[bk])
        return lambda: transpose_to(b, [bk], 8, mixTv[:, :, ti * 128:(ti + 1) * 128], ['x3T_%d' % ti], evac_eng='dve')

    if 'ffn2' in plan:
        ffn(x2Tv, XO, lambda t0, n: ['x2T_%d' % (t0 // 512)], S_HALF, dr['ffn2_w_gu'], dr['ffn2_w_down'], x2s, LN_EPS / (ALPHA * ALPHA), cons3, 'f2')
    A.release()
    S.barrier()

    A.mark()
    wgt = A.bf(8 * 1024)
    wgv = wgt.rearrange("p (c f) -> p c f", c=8)
    wpj = A.bf(2 * 1024)
    wpv = wpj.rearrange("p (c f) -> p c f", c=2)
    bgr = A.bf(1024)
    onesb = A.bf(128)
    S.op('pool', lambda e: e.memset(bgr, 0.0), R=[], W=['bgr'])
    ple_g = dr['ple_w_gate'].rearrange("(c p) f -> p c f", p=128)
    ple_p = dr['ple_w_proj'].rearrange("(c p) f -> p c f", p=128)
    for hh in range(2):
        S.dma('pool', (lambda hh: lambda e: e.dma_start(out=wgv[:, :, hh * 512:(hh + 1) * 512], in_=ple_g[:, :, hh * 512:(hh + 1) * 512]))(hh), R=[], W=['wgt%d' % hh])
    S.dma('pool', lambda e: e.dma_start(out=wpv, in_=ple_p), R=[], W=['wpj'])
    S.dma('pool', lambda e: e.dma_start(out=bgr[0:1, :], in_=dr['ple_b_gate']), R=[], W=['bgr'])
    S.op('dve', lambda e: e.tensor_copy(out=onesb, in_=c32v('ones')), R=['c32'], W=['onesb'])
    pb_ = [A.bf(256) for _ in range(2)]
    pT = [A.bf(256) for _ in range(2)]
    x3t = [A.f32(1024) for _ in range(2)]
    gsb = [A.f32(1024) for _ in range(2)]
    for ti in range(NT_HALF if 'ple' in plan else 0):
        pbt, pbk = pb_[ti % 2], 'pb%d' % (ti % 2)
        pTt, pTk = pT[ti % 2], 'pT%d' % (ti % 2)
        x3, x3k = x3t[ti % 2], 'x3t%d' % (ti % 2)
        gs, gsk = gsb[ti % 2], 'gs%d' % (ti % 2)
        S.dma('pool', (lambda pbt, ti: lambda e: e.dma_start(out=pbt, in_=dr['p'][ti * 128:(ti + 1) * 128, :]))(pbt, ti), R=[], W=[pbk])
        S.dma('sp', (lambda x3, ti: lambda e: e.dma_start(out=x3, in_=x3s[ti * 128:(ti + 1) * 128, :]))(x3, ti), R=['x3s_%d' % ti], W=[x3k])
        transpose_to(pbt, [pbk], 2, pTt.rearrange("p (c t) -> p c t", c=2), [pTk])
        for dh in range(2):
            bg_ = bank()
            for c in range(8):
                S.op('pe', (lambda c, bg_, dh, ti: lambda e: e.matmul(PS(bg_), lhsT=mixTv[:, c, ti * 128:(ti + 1) * 128], rhs=wgv[:, c, dh * 512:(dh + 1) * 512], start=(c == 0), stop=False))(c, bg_, dh, ti),
                     R=['x3T_%d' % ti, 'wgt%d' % dh], W=[PK(bg_)], inc=False)
            S.op('pe', (lambda bg_, dh: lambda e: e.matmul(PS(bg_), lhsT=onesb, rhs=bgr[:, dh * 512:(dh + 1) * 512], start=False, stop=True))(bg_, dh), R=['onesb', 'bgr'], W=[PK(bg_)])
            bp_ = bank()
            for c in range(2):
                S.op('pe', (lambda c, bp_, dh, pTt: lambda e: e.matmul(PS(bp_), lhsT=pTt[:, c * 128:(c + 1) * 128], rhs=wpv[:, c, dh * 512:(dh + 1) * 512], start=(c == 0), stop=(c == 1)))(c, bp_, dh, pTt),
                     R=[pTk, 'wpj'], W=[PK(bp_)], inc=(c == 1))
            S.op('act', (lambda gs, bg_, dh: lambda e: e.activation(out=gs[:, dh * 512:(dh + 1) * 512], in_=PS(bg_), func=AF.Sigmoid))(gs, bg_, dh), R=[PK(bg_)], W=[gsk])
            S.op('dve', (lambda gs, bp_, dh: lambda e: e.tensor_tensor(out=gs[:, dh * 512:(dh + 1) * 512], in0=gs[:, dh * 512:(dh + 1) * 512], in1=PS(bp_), op=ALU.mult))(gs, bp_, dh), R=[PK(bp_), gsk], W=[gsk])
        S.op('dve', (lambda gs, x3: lambda e: e.tensor_tensor(out=gs, in0=gs, in1=x3, op=ALU.add))(gs, x3), R=[gsk, x3k], W=[gsk])
        S.dma('sp', (lambda gs, ti: lambda e: e.dma_start(out=out[ti * 128:(ti + 1) * 128, :], in_=gs))(gs, ti), R=[gsk], W=['out_%d' % ti], sk='out')
    A.release()
    S.barrier()

    with nc.Block() as block:
        S.emit(nc, block)
    es.close()
    print("semaphores:", len(S.semkeys), "ops:", {e: len(l) for e, l in S.lists.items()}, "arena hi:", A.hi)
    return nc


_NC_CACHE = {}


def kernel(**inputs):
    x = np.asarray(inputs['x'], np.float32)
    p = np.asarray(inputs['p'], np.float32)[0]
    if 'nc' not in _NC_CACHE:
        _NC_CACHE['nc'] = build_program()
    nc = _NC_CACHE['nc']
    c32 = np.ascontiguousarray(np.concatenate([C32[k] for k in C32], axis=1).astype(np.float32))
    cb = np.ascontiguousarray(np.concatenate([CB[k] for k in CB], axis=1).astype(np.float32))
    c32r = np.ascontiguousarray(np.concatenate([C32R[k] for k in C32R], axis=1).astype(np.float32))
    wmap = {}
    for n in WEIGHT_NAMES:
        wmap[n] = np.ascontiguousarray(np.asarray(inputs[n], np.float32)[0].reshape(WSHAPES[n]))
    in_maps = []
    for c in range(8):
        b, half = c // 2, c % 2
        m = dict(wmap)
        if half == 1:
            xs = x[b]
            pos = np.arange(4096)
            hm = np.ones((128, 2), np.float32)
        else:
            xs = np.concatenate([np.zeros((S_HALF, D), np.float32), x[b, :S_HALF]], axis=0)
            pos = np.arange(4096) - S_HALF
            hm = np.ones((128, 2), np.float32)
            hm[:, 0] = 0.0
        cos, sin = rope_tables(pos)
        m['xs'] = np.ascontiguousarray(xs)
        m['p'] = np.ascontiguousarray(p[b, half * S_HALF:(half + 1) * S_HALF])
        m['hmask'] = hm
        m['cos'] = cos
        m['sin'] = sin
        m['c32'] = c32
        m['cb'] = cb
        m['c32r'] = c32r
        in_maps.append(m)
    res = run_bass_kernel_spmd(nc, in_maps, core_ids=list(range(8)))
    outp = np.zeros((4, 4096, D), np.float32)
    for c in range(8):
        b, half = c // 2, c % 2
        outp[b, half * S_HALF:(half + 1) * S_HALF] = res.results[c]['out']
    return outp
```

```python
import os
import numpy as np
import concourse.bass as bass
import concourse.mybir as mybir
from concourse.bass_utils import run_bass_kernel_spmd

F32 = mybir.dt.float32
BF16 = mybir.dt.bfloat16
AF = mybir.ActivationFunctionType
ALU = mybir.AluOpType
AX = mybir.AxisListType

D = 1024
DFF = 2816
NJ = DFF // 128
S_HALF = 2048
NT_HALF = S_HALF // 128
RETC = 2048
RWC = 1824
INC = RETC + RWC
ALPHA = 2.0 ** 0.25
LN_EPS = 1e-5
EDEC = float(np.exp(-0.5))

STRICT = True


class _Rec:
    def __init__(self):
        self.calls = []

    def __getattr__(self, name):
        def f(*a, **k):
            self.calls.append((name, a, k))
            return self
        return f


def _capture(fn):
    r = _Rec()
    fn(r)
    assert len(r.calls) == 1, r.calls
    name, a, k = r.calls[0]
    return lambda e: getattr(e, name)(*a, **k)


class Sched:
    def __init__(self):
        self.engs = ['pe', 'act', 'dve', 'pool', 'sp']
        self.lists = {e: [] for e in self.engs}
        self.cnt = {e: 0 for e in self.engs}
        self.lastw = {}
        self.readers = {}
        self.waited = {e: {} for e in self.engs}
        self.dmacnt = {}
        self.semkeys = set(self.engs)
        self.alltok = {}

    def _deps(self, eng, R, W, is_dma):
        deps = []
        raw = set()
        for k in R:
            t = self.lastw.get(k)
            if t:
                deps.append(t)
                raw.add(t)
            if k.startswith('ps'):
                deps.extend(tk for tk in self.readers.get(k, ()) if tk[0] != eng)
        for k in W:
            t = self.lastw.get(k)
            if t:
                deps.append(t)
            deps.extend(self.readers.get(k, ()))
        waits = {}
        for (sk, v) in deps:
            if sk == eng and not is_dma and (eng == 'pe' or not STRICT or (sk, v) not in raw):
                continue
            if self.waited[eng].get(sk, 0) >= v:
                continue
            waits[sk] = max(waits.get(sk, 0), v)
        for sk, v in waits.items():
            self.waited[eng][sk] = v
        return list(waits.items())

    def _commit(self, tok, R, W):
        for k in W:
            self.lastw[k] = tok
            self.readers[k] = []
        for k in R:
            if k not in W:
                self.readers.setdefault(k, []).append(tok)
        self.alltok[tok[0]] = max(self.alltok.get(tok[0], 0), tok[1])

    def op(self, eng, fn, R=(), W=(), inc=True):
        self._clean = False
        waits = self._deps(eng, R, W, False)
        if inc:
            self.cnt[eng] += 1
            tok = (eng, self.cnt[eng])
        else:
            tok = (eng, self.cnt[eng] + 1)
        self.lists[eng].append((waits, _capture(fn), (eng, 1) if inc else None))
        self._commit(tok, R, W)

    def dma(self, eng, fn, R, W, sk=None):
        self._clean = False
        waits = self._deps(eng, R, W, True)
        sk = 'd:' + (sk or W[0])
        self.semkeys.add(sk)
        self.dmacnt[sk] = self.dmacnt.get(sk, 0) + 16
        tok = (sk, self.dmacnt[sk])
        self.lists[eng].append((waits, _capture(fn), (sk, 16)))
        self._commit(tok, R, W)

    def barrier(self):
        if getattr(self, '_clean', False):
            return
        self._clean = True
        for e in ['pe', 'act', 'dve', 'pool']:
            if self.lists[e] and self.lists[e][-1][2] is not None and self.lists[e][-1][2][0] == e:
                continue
            self.cnt[e] += 1
            self.alltok[e] = self.cnt[e]
            self.lists[e].append(([], 'nop', (e, 1)))
        for e in self.engs:
            waits = []
            for sk, v in self.alltok.items():
                if self.waited[e].get(sk, 0) >= v:
                    continue
                if sk == e:
                    continue
                waits.append((sk, v))
                self.waited[e][sk] = v
            self.lists[e].append((waits, None, None))
        self.lastw = {}
        self.readers = {}

    def emit(self, nc, block):
        sems = {sk: nc.alloc_semaphore(name=("s_" + sk.replace(':', '_').replace('.', '_'))[:40]) for sk in sorted(self.semkeys)}
        engobj = {'pe': 'tensor', 'act': 'scalar', 'dve': 'vector', 'pool': 'gpsimd', 'sp': 'sync'}

        def make(ename):
            lst = self.lists[ename]

            def body(e):
                for (waits, fn, inc) in lst:
                    for (sk, v) in waits:
                        e.wait_ge(sems[sk], v)
                    if fn is None:
                        continue
                    if fn == 'nop':
                        ins = e.nop()
                    else:
                        ins = fn(e)
                    if inc is not None:
                        ins.then_inc(sems[inc[0]], inc[1])
            return body

        for ename in self.engs:
            getattr(block, engobj[ename])(make(ename))


def gammas():
    return 1.0 - 2.0 ** (-5.0 - np.arange(8, dtype=np.float64))


def host_consts():
    g = gammas()
    i = np.arange(128)
    c = {}
    s_le_t = (i[:, None] <= i[None, :]).astype(np.float32)
    s_lt_t = (i[:, None] < i[None, :]).astype(np.float32)
    c['tri_incl'] = -EDEC * s_le_t
    c['tri_strict'] = -EDEC * s_lt_t
    c['negcol'] = np.full((128, 1), -EDEC, np.float32)
    c['ones'] = np.ones((128, 128), np.float32)
    rel = (i[None, :] - i[:, None]).astype(np.float64)
    dm = np.zeros((128, 8, 128), np.float64)
    for h in range(8):
        dm[:, h, :] = np.where(rel >= 0, 0.125 * np.exp(np.where(rel >= 0, rel, 0) * np.log(g[h])), 0.0)
    cr = {}
    cr['dmask'] = dm.reshape(128, 1024).astype(np.float32)
    kd = np.zeros((128, 8), np.float64)
    for h in range(8):
        kd[:, h] = 0.125 * g[h] ** (127.0 - i)
    c['kdec'] = kd.astype(np.float32)
    qd = np.zeros((128, 4, 128), np.float64)
    gm = np.zeros((128, 4), np.float64)
    for pr in range(4):
        for hp in range(2):
            h = 2 * pr + hp
            qd[hp * 64:(hp + 1) * 64, pr, :] = (g[h] ** (i + 1.0))[None, :]
            gm[hp * 64:(hp + 1) * 64, pr] = g[h] ** 128.0
    cr['qdec'] = qd.reshape(128, 512).astype(np.float32)
    cr['kdecf'] = np.repeat(kd, 64, axis=1).astype(np.float32)
    cr['gam128f'] = np.repeat(gm, 64, axis=1).astype(np.float32)
    c['gam128'] = gm.astype(np.float32)
    b = {}
    b['ident'] = np.eye(128, dtype=np.float32)
    m1 = np.concatenate([-s_lt_t, s_le_t], axis=1)
    m2 = np.concatenate([s_lt_t, s_le_t], axis=1)
    b['m1'] = np.concatenate([m1, m1], axis=1)
    b['m2'] = np.concatenate([m2, m2], axis=1)
    m3 = -(i[:, None] > i[None, :]).astype(np.float32)
    b['m3'] = np.concatenate([m3, m3], axis=1)
    return c, cr, b


def rope_tables(pos):
    inv = 10000.0 ** (-np.arange(0, 64, 2, dtype=np.float32) / 64.0)
    ang = pos.astype(np.float32)[:, None] * inv[None, :]
    cos = np.cos(ang).astype(np.float32)
    sin = np.sin(ang).astype(np.float32)
    n = pos.shape[0] // 128
    cos = cos.reshape(n, 128, 32).transpose(1, 0, 2).reshape(128, n * 32)
    sin = sin.reshape(n, 128, 32).transpose(1, 0, 2).reshape(128, n * 32)
    return np.ascontiguousarray(cos), np.ascontiguousarray(sin)


C32, C32R, CB = host_consts()
C32_OFF = {}
_o = 0
for _k, _v in C32.items():
    C32_OFF[_k] = (_o, _v.shape[1])
    _o += _v.shape[1]
C32_N = _o
C32R_OFF = {}
_o = 0
for _k, _v in C32R.items():
    C32R_OFF[_k] = (_o, _v.shape[1])
    _o += _v.shape[1]
C32R_N = _o
CB_OFF = {}
_o = 0
for _k, _v in CB.items():
    CB_OFF[_k] = (_o, _v.shape[1])
    _o += _v.shape[1]
CB_N = _o

WEIGHT_NAMES = ['ffn1_w_gu', 'ffn1_w_down', 'ln1_g', 'ln1_b', 'w_in', 'ret_gn_g', 'ret_gn_b', 'rw_mu',
                'rw_w0', 'rw_w_up', 'rw_a0', 'rw_a_up', 'rw_g_up', 'rw_k_k', 'rw_k_a', 'rw_r_k', 'rw_gn_g',
                'rw_gn_b', 'w_out', 'ln2_g', 'ln2_b', 'ffn2_w_gu', 'ffn2_w_down', 'ln3_g', 'ln3_b',
                'ple_w_proj', 'ple_w_gate', 'ple_b_gate']
WSHAPES = {'ffn1_w_gu': [D, 2 * DFF], 'ffn1_w_down': [DFF, D], 'ln1_g': [1, D], 'ln1_b': [1, D], 'w_in': [D, INC],
           'ret_gn_g': [1, 512], 'ret_gn_b': [1, 512], 'rw_mu': [1, RWC], 'rw_w0': [1, 512], 'rw_w_up': [64, 512],
           'rw_a0': [1, 512], 'rw_a_up': [64, 512], 'rw_g_up': [160, 512], 'rw_k_k': [1, 512], 'rw_k_a': [1, 512],
           'rw_r_k': [1, 512], 'rw_gn_g': [1, 512], 'rw_gn_b': [1, 512], 'w_out': [D, D], 'ln2_g': [1, D],
           'ln2_b': [1, D], 'ffn2_w_gu': [D, 2 * DFF], 'ffn2_w_down': [DFF, D], 'ln3_g': [1, D], 'ln3_b': [1, D],
           'ple_w_proj': [256, D], 'ple_w_gate': [D, D], 'ple_b_gate': [1, D]}


def build_program(plan=None, dbg=False):
    nc = bass.Bass("TRN2", target_bir_lowering=False)
    dr = {}
    dr['xs'] = nc.dram_tensor("xs", [2 * S_HALF, D], F32, kind="ExternalInput").ap()
    dr['p'] = nc.dram_tensor("p", [S_HALF, 256], F32, kind="ExternalInput").ap()
    dr['hmask'] = nc.dram_tensor("hmask", [128, 2], F32, kind="ExternalInput").ap()
    dr['cos'] = nc.dram_tensor("cos", [128, 1024], F32, kind="ExternalInput").ap()
    dr['sin'] = nc.dram_tensor("sin", [128, 1024], F32, kind="ExternalInput").ap()
    dr['c32'] = nc.dram_tensor("c32", [128, C32_N], F32, kind="ExternalInput").ap()
    dr['cb'] = nc.dram_tensor("cb", [128, CB_N], F32, kind="ExternalInput").ap()
    dr['c32r'] = nc.dram_tensor("c32r", [128, C32R_N], F32, kind="ExternalInput").ap()
    for n in WEIGHT_NAMES:
        dr[n] = nc.dram_tensor(n, WSHAPES[n], F32, kind="ExternalInput").ap()
    out = nc.dram_tensor("out", [S_HALF, D], F32, kind="ExternalOutput").ap()
    skind = "ExternalOutput" if dbg else "Internal"
    x1s = nc.dram_tensor("x1s", [S_HALF, D], F32, kind=skind).ap()
    x2s = nc.dram_tensor("x2s", [S_HALF, D], F32, kind=skind).ap()
    x3s = nc.dram_tensor("x3s", [S_HALF, D], F32, kind=skind).ap()
    wab_s = nc.dram_tensor("wab_s", [128, 2 * 8 * RWC], BF16, kind="Internal").ap()
    if plan is None:
        plan = ['ffn1_0', 'mix_0', 'ffn1_1', 'mix_1', 'wout', 'ffn2', 'ple']

    S = Sched()
    dumped = {}

    def dump(name, ap, keys):
        if not dbg or name in dumped:
            return
        shp = list(ap.shape)
        d_ = nc.dram_tensor("dbg_" + name, shp, F32, kind="ExternalOutput").ap()
        dumped[name] = d_
        S.dma('pool', lambda e: e.dma_start(out=d_, in_=ap), R=list(keys), W=['dbg_' + name])
    ARENA_W = int(os.environ.get('ARENA_W', '53200'))
    from contextlib import ExitStack
    es = ExitStack()
    arena = es.enter_context(nc.sbuf_tensor("arena", [128, ARENA_W], F32))
    psf = [es.enter_context(nc.psum_tensor("ps%d" % i, [128, 512], F32)) for i in range(8)]

    class Alloc:
        def __init__(self):
            self.p = 0
            self.marks = []

        def f32(self, n, parts=(0, 128)):
            if os.environ.get('DRY'):
                self.p += n
                self.hi = max(getattr(self, 'hi', 0), self.p)
                return arena[parts[0]:parts[1], 0:n]
            a = arena[parts[0]:parts[1], self.p:self.p + n]
            self.p += n
            self.hi = max(getattr(self, 'hi', 0), self.p)
            assert self.p <= ARENA_W, ("arena overflow", self.p)
            return a

        def bf(self, n, parts=(0, 128)):
            w = (n + 1) // 2
            if os.environ.get('DRY'):
                self.p += w
                self.hi = max(getattr(self, 'hi', 0), self.p)
                return arena[parts[0]:parts[1], 0:w].bitcast(BF16)
            a = arena[parts[0]:parts[1], self.p:self.p + w].bitcast(BF16)
            self.p += w
            self.hi = max(getattr(self, 'hi', 0), self.p)
            assert self.p <= ARENA_W, ("arena overflow", self.p)
            return a

        def mark(self):
            self.marks.append(self.p)

        def release(self):
            self.p = self.marks.pop()
            S.barrier()

    A = Alloc()
    bankctr = [0]
    bankgen = [0] * 8

    class Bk(int):
        pass

    def bank():
        b = Bk(bankctr[0] % 8)
        bankctr[0] += 1
        bankgen[int(b)] = bankctr[0]
        b.gen = bankctr[0]
        return b

    def PS(b):
        return psf[int(b)][:, :]

    def PSB(b):
        return psf[int(b)][:, :].bitcast(BF16)

    def PK(b):
        assert bankgen[int(b)] == b.gen, "stale PSUM bank use"
        return 'ps%d' % int(b)

    c32 = A.f32(C32_N)
    cbt = A.bf(CB_N)
    S.dma('sp', lambda e: e.dma_start(out=c32, in_=dr['c32']), R=[], W=['c32'])
    S.dma('pool', lambda e: e.dma_start(out=cbt, in_=dr['cb']), R=[], W=['cb'])

    def c32v(name, parts=(0, 128)):
        o, n = C32_OFF[name]
        return c32[parts[0]:parts[1], o:o + n]

    def cbv(name):
        o, n = CB_OFF[name]
        return cbt[:, o:o + n]

    ident = cbv('ident')
    hmask = A.f32(2)
    S.dma('sp', lambda e: e.dma_start(out=hmask, in_=dr['hmask']), R=[], W=['hmask'])
    ptile = {}
    LN = {}
    retS = A.f32(256)
    retSb = A.bf(256)
    rwA = A.f32(256)
    rwAb = A.bf(256)
    x1T = A.bf(8 * (S_HALF + 8))
    x1Tv = x1T.rearrange("p (c t) -> p c t", c=8)
    XO = 8
    mixT = A.bf(8 * S_HALF)
    mixTv = mixT.rearrange("p (c t) -> p c t", c=8)
    x2Tv = x1Tv

    def load_ln(gn, bn):
        lng = A.f32(1024)
        lnb = A.f32(1024)
        LN['g'] = lng
        LN['b'] = lnb
        S.dma('sp', lambda e: e.dma_start(out=lng, in_=dr[gn].partition_broadcast(128)), R=[], W=['lng'])
        S.dma('sp', lambda e: e.dma_start(out=lnb, in_=dr[bn].partition_broadcast(128)), R=[], W=['lnb'])

    def load_ptiles(names):
        for n in names:
            t = A.f32(512)
            ptile[n] = t
            S.dma('sp', (lambda t, n: lambda e: e.dma_start(out=t, in_=dr[n].partition_broadcast(128)))(t, n), R=[], W=['pt_' + n])

    def transpose_to(src_bf, src_keys, n_blocks, dst_view, dst_keys, evac_eng='act'):
        b = bank()
        pk = PK(b)
        for i in range(n_blocks):
            S.op('pe', (lambda i, b: lambda e: e.transpose(out=PSB(b)[:, i * 128:(i + 1) * 128],
                                                         in_=src_bf[:, i * 128:(i + 1) * 128], identity=ident))(i, b),
                 R=list(src_keys) + ['cb'], W=[pk], inc=(i == n_blocks - 1))
        src = PSB(b)[:, 0:n_blocks * 128].rearrange("p (c t) -> p c t", c=n_blocks)
        if evac_eng == 'act':
            S.op('act', lambda e: e.activation(out=dst_view, in_=src, func=AF.Copy), R=[pk], W=list(dst_keys))
        else:
            S.op(evac_eng, lambda e: e.tensor_copy(out=dst_view, in_=src), R=[pk], W=list(dst_keys))

    def layer_norm_group(tiles, eps, junk, st):
        n_ = len(tiles)
        sl = lambda i, c: st[:, i * 8 + c:i * 8 + c + 1]
        k = lambda i, c: 'lnst%d_%d' % (i, c)
        for i, (y, yk) in enumerate(tiles):
            S.op('act', (lambda i, y: lambda e: e.activation(out=junk, in_=y, func=AF.Square, accum_out=sl(i, 0)))(i, y), R=[yk], W=['lnjunk', k(i, 0)])
            S.op('act', (lambda i, y: lambda e: e.activation(out=junk, in_=y, func=AF.Identity, accum_out=sl(i, 1)))(i, y), R=[yk], W=['lnjunk', k(i, 1)])
        for i in range(n_):
            S.op('dve', (lambda i: lambda e: e.tensor_scalar(out=sl(i, 2), in0=sl(i, 1), scalar1=1.0 / 1024, scalar2=None, op0=ALU.mult))(i), R=[k(i, 1)], W=[k(i, 2)])
        for i in range(n_):
            S.op('dve', (lambda i: lambda e: e.tensor_tensor(out=sl(i, 3), in0=sl(i, 2), in1=sl(i, 2), op=ALU.mult))(i), R=[k(i, 2)], W=[k(i, 3)])
        for i in range(n_):
            S.op('dve', (lambda i: lambda e: e.scalar_tensor_tensor(out=sl(i, 4), in0=sl(i, 0), scalar=1.0 / 1024, in1=sl(i, 3), op0=ALU.mult, op1=ALU.subtract))(i), R=[k(i, 0), k(i, 3)], W=[k(i, 4)])
        for i in range(n_):
            S.op('dve', (lambda i: lambda e: e.tensor_scalar(out=sl(i, 4), in0=sl(i, 4), scalar1=float(eps), scalar2=None, op0=ALU.add))(i), R=[k(i, 4)], W=[k(i, 4)])
        for i in range(n_):
            S.op('act', (lambda i: lambda e: e.activation(out=sl(i, 5), in_=sl(i, 4), func=AF.Sqrt))(i), R=[k(i, 4)], W=[k(i, 5)])
        for i in range(n_):
            S.op('dve', (lambda i: lambda e: e.reciprocal(out=sl(i, 6), in_=sl(i, 5)))(i), R=[k(i, 5)], W=[k(i, 6)])
        for i in range(n_):
            S.op('dve', (lambda i: lambda e: e.scalar_tensor_tensor(out=sl(i, 7), in0=sl(i, 2), scalar=-1.0, in1=sl(i, 6), op0=ALU.mult, op1=ALU.mult))(i), R=[k(i, 2), k(i, 6)], W=[k(i, 7)])
        for i, (y, yk) in enumerate(tiles):
            S.op('act', (lambda i, y: lambda e: e.activation(out=y, in_=y, func=AF.Identity, scale=sl(i, 6), bias=sl(i, 7)))(i, y), R=[yk, k(i, 6), k(i, 7)], W=[yk])
        for i, (y, yk) in enumerate(tiles):
            S.op('dve', (lambda y: lambda e: e.tensor_tensor(out=y, in0=y, in1=LN['g'], op=ALU.mult))(y), R=[yk, 'lng'], W=[yk])
            S.op('dve', (lambda y: lambda e: e.tensor_tensor(out=y, in0=y, in1=LN['b'], op=ALU.add))(y), R=[yk, 'lnb'], W=[yk])

    def layer_norm_tile(y, ykey, eps, outs, tmp_sq, st):
        S.op('act', lambda e: e.activation(out=tmp_sq, in_=y, func=AF.Square), R=[ykey], W=['lnsq'])
        S.op('dve', lambda e: e.reduce_sum(out=st[:, 0:1], in_=tmp_sq, axis=AX.X), R=['lnsq'], W=['lnst0'])
        S.op('dve', lambda e: e.reduce_sum(out=st[:, 1:2], in_=y, axis=AX.X), R=[ykey], W=['lnst1'])
        S.op('dve', lambda e: e.tensor_scalar(out=st[:, 2:3], in0=st[:, 1:2], scalar1=1.0 / 1024, scalar2=None, op0=ALU.mult), R=['lnst1'], W=['lnst2'])
        S.op('dve', lambda e: e.tensor_tensor(out=st[:, 3:4], in0=st[:, 2:3], in1=st[:, 2:3], op=ALU.mult), R=['lnst2'], W=['lnst3'])
        S.op('dve', lambda e: e.scalar_tensor_tensor(out=st[:, 4:5], in0=st[:, 0:1], scalar=1.0 / 1024, in1=st[:, 3:4], op0=ALU.mult, op1=ALU.subtract), R=['lnst0', 'lnst3'], W=['lnst4'])
        S.op('dve', lambda e: e.tensor_scalar(out=st[:, 4:5], in0=st[:, 4:5], scalar1=float(eps), scalar2=None, op0=ALU.add), R=['lnst4'], W=['lnst4'])
        S.op('act', lambda e: e.activation(out=st[:, 5:6], in_=st[:, 4:5], func=AF.Sqrt), R=['lnst4'], W=['lnst5'])
        S.op('dve', lambda e: e.reciprocal(out=st[:, 6:7], in_=st[:, 5:6]), R=['lnst5'], W=['lnst6'])
        S.op('dve', lambda e: e.scalar_tensor_tensor(out=st[:, 7:8], in0=st[:, 2:3], scalar=-1.0, in1=st[:, 6:7], op0=ALU.mult, op1=ALU.mult), R=['lnst2', 'lnst6'], W=['lnst7'])
        S.op('act', lambda e: e.activation(out=y, in_=y, func=AF.Identity, scale=st[:, 6:7], bias=st[:, 7:8]), R=[ykey, 'lnst6', 'lnst7'], W=[ykey])
        S.op('dve', lambda e: e.tensor_tensor(out=y, in0=y, in1=LN['g'], op=ALU.mult), R=[ykey, 'lng'], W=[ykey])
        S.op('dve', lambda e: e.tensor_tensor(out=y, in0=y, in1=LN['b'], op=ALU.add), R=[ykey, 'lnb'], W=[ykey])

    def ffn(xTv_src, xoff, xkey_fn, ntok, wgu, wdown, res_dram, eps, consumer, tagp):
        A.mark()
        NB = 1024
        hhT = A.bf(NJ * NB)
        hhv = hhT.rearrange("p (j t) -> p j t", j=NJ)
        wg = [A.bf(8 * 512) for _ in range(2)]
        wd = [A.bf(1024) for _ in range(6)]
        sg = [A.f32(512) for _ in range(2)]
        ybuf = [A.f32(1024) for _ in range(8)]
        tmp_sq = A.f32(1024)
        st = A.f32(32)
        wgu_v = wgu.rearrange("(c p) f -> p c f", p=128)
        deferred = []

        def run_deferred():
            for f_ in deferred:
                f_()
            del deferred[:]
        wi = 0
        di = 0
        for blk in range(ntok // NB):
            t0 = blk * NB
            for jg in range(NJ // 2):
                w = wg[wi % 2]
                wk = 'wg%d' % (wi % 2)
                wv = w.rearrange("p (c f) -> p c f", c=8)
                wi += 1
                S.dma('pool', (lambda wv, jg: lambda e: e.dma_start(out=wv[:, :, 0:256], in_=wgu_v[:, :, jg * 256:(jg + 1) * 256]))(wv, jg), R=[], W=[wk + 'g'])
                S.dma('pool', (lambda wv, jg: lambda e: e.dma_start(out=wv[:, :, 256:512], in_=wgu_v[:, :, DFF + jg * 256:DFF + (jg + 1) * 256]))(wv, jg), R=[], W=[wk + 'u'])
                for jj in range(2):
                    j = jg * 2 + jj
                    for sb in range(NB // 512):
                        bg = bank()
                        bu = bank()
                        c0 = xoff + t0 + sb * 512
                        xk = xkey_fn(t0 + sb * 512, 512)
                        for dc in range(8):
                            S.op('pe', (lambda wv, dc, jj, bg, c0: lambda e: e.matmul(PS(bg), lhsT=wv[:, dc, jj * 128:(jj + 1) * 128], rhs=xTv_src[:, dc, c0:c0 + 512], start=(dc == 0), stop=(dc == 7)))(wv, dc, jj, bg, c0),
                                 R=[wk + 'g'] + xk, W=[PK(bg)], inc=(dc == 7))
                        for dc in range(8):
                            S.op('pe', (lambda wv, dc, jj, bu, c0: lambda e: e.matmul(PS(bu), lhsT=wv[:, dc, 256 + jj * 128:256 + (jj + 1) * 128], rhs=xTv_src[:, dc, c0:c0 + 512], start=(dc == 0), stop=(dc == 7)))(wv, dc, jj, bu, c0),
                                 R=[wk + 'u'] + xk, W=[PK(bu)], inc=(dc == 7))
                        sgt = sg[(j * 2 + sb) % 2]
                        sgk = 'sg%d' % ((j * 2 + sb) % 2)
                        S.op('act', (lambda sgt, bg: lambda e: e.activation(out=sgt, in_=PS(bg), func=AF.Silu))(sgt, bg), R=[PK(bg)], W=[sgk])
                        S.op('dve', (lambda sgt, bu, j, sb: lambda e: e.tensor_tensor(out=hhv[:, j, sb * 512:(sb + 1) * 512], in0=sgt, in1=PS(bu), op=ALU.mult))(sgt, bu, j, sb),
                             R=[sgk, PK(bu)], W=['hh%d_%d' % (j, sb)])
            if blk == 0:
                dump(tagp + '_hh0', hhv[:, 0, :], ['hh0_0', 'hh0_1'])
                dump(tagp + '_hh21', hhv[:, 21, :], ['hh21_0', 'hh21_1'])
                dump(tagp + '_xT0', xTv_src[:, 0, xoff:xoff + 512], xkey_fn(0, 512))
            if os.environ.get('FFN_STOP') == 'up':
                continue
            for rnd in range(NB // 512):
                banks = [[bank(), bank()] for _ in range(4)]
                for j in range(NJ):
                    w = wd[di % 6]
                    wk = 'wd%d' % (di % 6)
                    di += 1
                    S.dma('pool', (lambda w, j: lambda e: e.dma_start(out=w, in_=wdown[j * 128:(j + 1) * 128, :]))(w, j), R=[], W=[wk])
                    for tt in range(4):
                        for dh in range(2):
                            b = banks[tt][dh]
                            S.op('pe', (lambda w, j, tt, dh, b, rnd: lambda e: e.matmul(PS(b), lhsT=hhv[:, j, rnd * 512 + tt * 128:rnd * 512 + (tt + 1) * 128], rhs=w[:, dh * 512:(dh + 1) * 512], start=(j == 0), stop=(j == NJ - 1)))(w, j, tt, dh, b, rnd),
                                 R=[wk, 'hh%d_%d' % (j, rnd)], W=[PK(b)], inc=(j == NJ - 1 or (tt == 3 and dh == 1)))
                cur = []
                for tt in range(4):
                    ti = (t0 + rnd * 512) // 128 + tt
                    y = ybuf[ti % 8]
                    yk = 'y%d' % (ti % 8)
                    S.dma('sp', (lambda y, ti: lambda e: e.dma_start(out=y, in_=res_dram[ti * 128:(ti + 1) * 128, :]))(y, ti), R=[], W=[yk])
                    for dh in range(2):
                        b = banks[tt][dh]
                        S.op('dve', (lambda y, b, dh: lambda e: e.scalar_tensor_tensor(out=y[:, dh * 512:(dh + 1) * 512], in0=PS(b), scalar=0.5 / ALPHA, in1=y[:, dh * 512:(dh + 1) * 512], op0=ALU.mult, op1=ALU.add))(y, b, dh),
                             R=[PK(b), yk], W=[yk])
                    cur.append((ti, y, yk))
                run_deferred()
                layer_norm_group([(y, yk) for (ti, y, yk) in cur], eps, tmp_sq, st)
                for (ti, y, yk) in cur:
                    later = consumer(ti, y, yk)
                    if later is not None:
                        deferred.append(later)
        run_deferred()
        A.release()

    def prep_xT(src_dram, ntok, dstv, doff, keyfn):
        A.mark()
        xb = [A.bf(1024) for _ in range(4)]
        for ti in range(ntok // 128):
            b = xb[ti % 4]
            bk = 'xb%d' % (ti % 4)
            S.dma('pool', (lambda b, ti: lambda e: e.dma_start(out=b, in_=src_dram[ti * 128:(ti + 1) * 128, :]))(b, ti), R=[], W=[bk])
            transpose_to(b, [bk], 8, dstv[:, :, doff + ti * 128:doff + (ti + 1) * 128], keyfn(ti * 128, 128))
        A.release()

    def mixer(half, full):
        A.mark()
        w_in = dr['w_in'].rearrange("(c p) f -> p c f", p=128)
        A.mark()
        wret = A.bf(8 * RETC)
        wretv = wret.rearrange("p (c f) -> p c f", c=8)
        c32r = A.f32(C32R_N)
        S.dma('sp', lambda e: e.dma_start(out=c32r, in_=dr['c32r']), R=[], W=['c32r'])

        def c32rv(name):
            o, n = C32R_OFF[name]
            return c32r[:, o:o + n]
        cos_t = A.f32(512)
        sin_t = A.f32(512)
        S.dma('sp', lambda e: e.dma_start(out=cos_t, in_=dr['cos'][:, half * 512:(half + 1) * 512]), R=[], W=['cos'])
        S.dma('sp', lambda e: e.dma_start(out=sin_t, in_=dr['sin'][:, half * 512:(half + 1) * 512]), R=[], W=['sin'])
        load_ptiles(['ret_gn_g', 'ret_gn_b'])
        for q4 in ([1, 2, 0, 3] if full else [1, 2]):
            S.dma('pool', (lambda q4: lambda e: e.dma_start(out=wretv[:, :, q4 * 512:(q4 + 1) * 512], in_=w_in[:, :, q4 * 512:(q4 + 1) * 512]))(q4), R=[], W=['wret%d' % q4])
        qr = A.bf(512)
        kr = A.bf(512)
        vb = A.bf(512)
        vdb = A.bf(512)
        sgg = A.f32(512)
        t1 = A.f32(256)
        t2 = A.f32(256)
        t3 = A.f32(256)
        t4 = A.f32(256)
        qT = A.bf(512)
        qsTm = A.bf(1024)
        kTm = A.bf(1024)
        S.op('pool', lambda e: e.memset(qsTm, 0.0), R=[], W=['qsTm'])
        S.op('pool', lambda e: e.memset(kTm, 0.0), R=[], W=['kTm'])
        scm = A.bf(1024)
        o_sb = A.f32(512)
        o_sq = A.f32(512)
        gst = A.f32(64)
        mixr = A.bf(512)
        ret_deferred = []
        kdec = c32v('kdec')
        for ti in range(NT_HALF):
            c0 = XO + ti * 128
            gt = ti
            xk = ['x1T_%d' % ti]
            cosv = cos_t[:, gt * 32:(gt + 1) * 32].unsqueeze(1).broadcast_to([128, 8, 32])
            sinv = sin_t[:, gt * 32:(gt + 1) * 32].unsqueeze(1).broadcast_to([128, 8, 32])
            pb = {}
            which = ['q', 'k', 'v', 'g'] if full else ['k', 'v']
            RSUB = os.environ.get('RET_SUB', '')
            if RSUB == 'none':
                continue
            for nm in which:
                q4 = ['q', 'k', 'v', 'g'].index(nm)
                b = bank()
                pb[nm] = b
                for dc in range(8):
                    S.op('pe', (lambda dc, b, q4, c0: lambda e: e.matmul(PS(b), lhsT=x1Tv[:, dc, c0:c0 + 128], rhs=wretv[:, dc, q4 * 512:(q4 + 1) * 512], start=(dc == 0), stop=(dc == 7)))(dc, b, q4, c0),
                         R=xk + ['wret%d' % q4], W=[PK(b)], inc=(dc == 7))

            def rotary(b, dst, dkey):
                src = PS(b).rearrange("p (h d) -> p h d", h=8)
                dv = dst.rearrange("p (h d) -> p h d", h=8)
                a1 = t1.rearrange("p (h d) -> p h d", h=8)
                a2 = t2.rearrange("p (h d) -> p h d", h=8)
                a3 = t3.rearrange("p (h d) -> p h d", h=8)
                a4 = t4.rearrange("p (h d) -> p h d", h=8)
                pk = PK(b)
                S.op('dve', lambda e: e.tensor_tensor(out=a1, in0=src[:, :, 0:32], in1=cosv, op=ALU.mult), R=[pk, 'cos'], W=['rt1'])
                S.op('dve', lambda e: e.tensor_tensor(out=a2, in0=src[:, :, 32:64], in1=sinv, op=ALU.mult), R=[pk, 'sin'], W=['rt2'])
                S.op('dve', lambda e: e.tensor_tensor(out=a3, in0=src[:, :, 0:32], in1=sinv, op=ALU.mult), R=[pk, 'sin'], W=['rt3'])
                S.op('dve', lambda e: e.tensor_tensor(out=a4, in0=src[:, :, 32:64], in1=cosv, op=ALU.mult), R=[pk, 'cos'], W=['rt4'])
                S.op('pool', lambda e: e.tensor_tensor(out=dv[:, :, 0:32], in0=a1, in1=a2, op=ALU.subtract), R=['rt1', 'rt2'], W=[dkey])
                S.op('pool', lambda e: e.tensor_tensor(out=dv[:, :, 32:64], in0=a3, in1=a4, op=ALU.add), R=['rt3', 'rt4'], W=[dkey])

            for f_ in ret_deferred:
                f_()
            del ret_deferred[:]
            if RSUB == 'proj':
                continue
            rotary(pb['k'], kr, 'kr')
            if full:
                rotary(pb['q'], qr, 'qr')
            if RSUB == 'rot':
                continue
            bv = pb['v']
            S.op('act', (lambda bv: lambda e: e.activation(out=vb, in_=PS(bv), func=AF.Copy))(bv), R=[PK(bv)], W=['vb'])
            if RSUB == 'vb':
                continue
            S.op('dve', (lambda bv: lambda e: e.tensor_tensor(out=vdb, in0=PS(bv), in1=c32rv('kdecf'), op=ALU.mult))(bv), R=[PK(bv), 'c32r', 'vb'], W=['vdb'])
            RS = int(os.environ.get('RET_STOP', '99'))
            if RS <= 1:
                continue
            if full:
                bgg = pb['g']
                S.op('act', (lambda bgg: lambda e: e.activation(out=sgg, in_=PS(bgg), func=AF.Silu))(bgg), R=[PK(bgg)], W=['sgg'])
                b = bank()
                pk = PK(b)
                for i in range(4):
                    S.op('pe', (lambda i, b: lambda e: e.transpose(out=PSB(b)[:, i * 128:(i + 1) * 128], in_=qr[:, i * 128:(i + 1) * 128], identity=ident))(i, b), R=['qr', 'cb'], W=[pk], inc=False)
                for i in range(4):
                    S.op('pe', (lambda i, b: lambda e: e.transpose(out=PSB(b)[:, 512 + i * 128:512 + (i + 1) * 128], in_=kr[:, i * 128:(i + 1) * 128], identity=ident))(i, b), R=['kr', 'cb'], W=[pk], inc=(i == 3))
                S.op('act', (lambda b: lambda e: e.activation(out=qT, in_=PSB(b)[:, 0:512], func=AF.Copy))(b), R=[pk], W=['qT'])
                for hp in range(2):
                    rs_ = slice(hp * 64, (hp + 1) * 64)
                    S.op('dve', (lambda b, hp, rs_: lambda e: e.tensor_tensor(out=qsTm[rs_, :].rearrange("p (a c t) -> p a c t", a=4, c=2)[:, :, hp, :],
                                                                             in0=PSB(b)[rs_, 0:512].rearrange("p (a t) -> p a t", a=4),
                                                                             in1=c32rv('qdec')[rs_, :].rearrange("p (a t) -> p a t", a=4), op=ALU.mult))(b, hp, rs_), R=[pk, 'c32r'], W=['qsTm'])
                    S.op('act', (lambda b, hp, rs_: lambda e: e.activation(out=kTm[rs_, :].rearrange("p (a c t) -> p a c t", a=4, c=2)[:, :, hp, :],
                                                                          in_=PSB(b)[rs_, 512:1024].rearrange("p (a t) -> p a t", a=4), func=AF.Copy))(b, hp, rs_), R=[pk], W=['kTm'])
                if RS <= 2:
                    continue
                sb_ = [bank(), bank()]
                for h in range(8):
                    pr, hp = h // 2, h % 2
                    b = sb_[h // 4]
                    S.op('pe', (lambda h, pr, hp, b: lambda e: e.matmul(PS(b)[:, (h % 4) * 128:(h % 4 + 1) * 128], lhsT=kTm[:, h * 128:(h + 1) * 128],
                                                                       rhs=qT[:, pr * 128:(pr + 1) * 128], start=True, stop=True))(h, pr, hp, b),
                         R=['kTm', 'qT'], W=[PK(b)], inc=(h % 4 == 3))
                if RS == 24:
                    continue
                for hb in range(2):
                    b = sb_[hb]
                    S.op('dve', (lambda hb, b: lambda e: e.tensor_tensor(out=scm[:, hb * 512:(hb + 1) * 512], in0=PS(b), in1=c32rv('dmask')[:, hb * 512:(hb + 1) * 512], op=ALU.mult))(hb, b),
                         R=[PK(b), 'c32r'], W=['scm%d' % hb])
                if RS == 25:
                    continue
                bo = bank()
                for h in range(8):
                    pr, hp = h // 2, h % 2
                    S.op('pe', (lambda h, bo: lambda e: e.matmul(PS(bo)[:, h * 64:(h + 1) * 64], lhsT=scm[:, h * 128:(h + 1) * 128], rhs=vb[:, h * 64:(h + 1) * 64], start=True, stop=(RS == 26)))(h, bo),
                         R=['scm%d' % (h // 4), 'vb'], W=[PK(bo)], inc=(RS == 26 and h == 7))
                    if RS == 26:
                        continue
                    S.op('pe', (lambda h, pr, hp, bo: lambda e: e.matmul(PS(bo)[:, h * 64:(h + 1) * 64], lhsT=qsTm[:, h * 128:(h + 1) * 128],
                                                                        rhs=retSb[:, pr * 64:(pr + 1) * 64], start=False, stop=True))(h, pr, hp, bo),
                         R=['qsTm', 'retSb'], W=[PK(bo)], inc=(h == 7))
                S.op('act', (lambda bo: lambda e: e.activation(out=o_sb, in_=PS(bo), func=AF.Copy))(bo), R=[PK(bo)], W=['o_sb'])
            if RS <= 3:
                continue
            bk_ = bank()
            for pr in range(4):
                S.op('pe', (lambda pr, bk_: lambda e: e.matmul(PS(bk_)[:, pr * 128:(pr + 1) * 128], lhsT=kr[:, pr * 128:(pr + 1) * 128], rhs=vdb[:, pr * 128:(pr + 1) * 128], start=True, stop=True))(pr, bk_),
                     R=['kr', 'vdb'], W=[PK(bk_)], inc=(pr == 3))
            rs3 = retS.rearrange("p (a e) -> p a e", a=4)
            S.op('pool', lambda e: e.tensor_tensor(out=retS, in0=retS, in1=c32rv('gam128f'), op=ALU.mult), R=['retS', 'c32r', 'retSb'], W=['retS'])
            for hp in range(2):
                S.op('dve', (lambda hp, bk_: lambda e: e.tensor_tensor(out=retS[hp * 64:(hp + 1) * 64, :].rearrange("p (a e) -> p a e", a=4), in0=retS[hp * 64:(hp + 1) * 64, :].rearrange("p (a e) -> p a e", a=4),
                                                                    in1=PS(bk_)[hp * 64:(hp + 1) * 64, :].rearrange("p (a f) -> p a f", a=4)[:, :, hp * 64:(hp + 1) * 64], op=ALU.add))(hp, bk_),
                     R=[PK(bk_), 'retS'], W=['retS'])
            S.op('act', lambda e: e.activation(out=retSb, in_=retS, func=AF.Copy), R=['retS'], W=['retSb'])
            if full and ti == 0:
                dump('retS1', retS, ['retS'])
            if full and ti == 1:
                dump('retS2', retS, ['retS'])
                dump('kr1', kr, ['kr'])
            if RS <= 4:
                continue
            if full:
                group_norm_out(o_sb, 'o_sb', o_sq, 'o_sq', gst, 1e-5, ptile['ret_gn_g'], ptile['ret_gn_b'], 'pt_ret_gn_g', 'pt_ret_gn_b')
                S.op('dve', lambda e: e.tensor_tensor(out=mixr, in0=o_sb, in1=sgg, op=ALU.mult), R=['o_sb', 'sgg'], W=['mixr'])
                ret_deferred.append((lambda ti: lambda: transpose_to(mixr, ['mixr'], 4, mixTv[:, 0:4, ti * 128:(ti + 1) * 128], ['mixT_r%d' % ti]))(ti))
        for f_ in ret_deferred:
            f_()
        A.release()
        S.barrier()
        if os.environ.get('MIX_STOP') != 'ret':
            rwkv(half, full)
        if full:
            for c_ in range(8):
                dump('mixT%d' % c_, mixTv[:, c_, :], [])
            dump('retS', retS, [])
            dump('rwA', rwA, [])
        A.release()

    def group_norm_out(o, okey, sq, sqk, gst, eps, gt, bt, gk, bk):
        o3 = o.rearrange("p (h d) -> p h d", h=8)
        S.op('act', lambda e: e.activation(out=sq, in_=o, func=AF.Square), R=[okey], W=[sqk])
        S.op('dve', lambda e: e.tensor_reduce(out=gst[:, 0:8], in_=o3, axis=AX.X, op=ALU.add), R=[okey], W=['gn0'])
        S.op('dve', lambda e: e.tensor_reduce(out=gst[:, 8:16], in_=sq.rearrange("p (h d) -> p h d", h=8), axis=AX.X, op=ALU.add), R=[sqk], W=['gn1'])
        S.op('dve', lambda e: e.tensor_scalar(out=gst[:, 16:24], in0=gst[:, 0:8], scalar1=1.0 / 64, scalar2=None, op0=ALU.mult), R=['gn0'], W=['gn2'])
        S.op('dve', lambda e: e.tensor_tensor(out=gst[:, 24:32], in0=gst[:, 16:24], in1=gst[:, 16:24], op=ALU.mult), R=['gn2'], W=['gn3'])
        S.op('dve', lambda e: e.scalar_tensor_tensor(out=gst[:, 32:40], in0=gst[:, 8:16], scalar=1.0 / 64, in1=gst[:, 24:32], op0=ALU.mult, op1=ALU.subtract), R=['gn1', 'gn3'], W=['gn4'])
        S.op('dve', lambda e: e.tensor_scalar(out=gst[:, 32:40], in0=gst[:, 32:40], scalar1=float(eps), scalar2=None, op0=ALU.add), R=['gn4'], W=['gn4'])
        S.op('act', lambda e: e.activation(out=gst[:, 40:48], in_=gst[:, 32:40], func=AF.Sqrt), R=['gn4'], W=['gn5'])
        S.op('dve', lambda e: e.reciprocal(out=gst[:, 48:56], in_=gst[:, 40:48]), R=['gn5'], W=['gn6'])
        S.op('dve', lambda e: e.tensor_tensor(out=o3, in0=o3, in1=gst[:, 16:24].unsqueeze(2).broadcast_to([128, 8, 64]), op=ALU.subtract), R=[okey, 'gn2'], W=[okey])
        S.op('dve', lambda e: e.tensor_tensor(out=o3, in0=o3, in1=gst[:, 48:56].unsqueeze(2).broadcast_to([128, 8, 64]), op=ALU.mult), R=[okey, 'gn6'], W=[okey])
        S.op('pool', lambda e: e.tensor_tensor(out=o, in0=o, in1=gt, op=ALU.mult), R=[okey, gk], W=[okey])
        S.op('pool', lambda e: e.tensor_tensor(out=o, in0=o, in1=bt, op=ALU.add), R=[okey, bk], W=[okey])

    def rwkv(half, full):
        A.mark()
        w_in = dr['w_in'].rearrange("(c p) f -> p c f", p=128)
        load_ptiles(['rw_k_k', 'rw_k_a', 'rw_r_k', 'rw_gn_g', 'rw_gn_b'])
        Wa = A.bf(8 * RWC)
        Wb = A.bf(8 * RWC)
        Wav = Wa.rearrange("p (c f) -> p c f", c=8)
        Wbv = Wb.rearrange("p (c f) -> p c f", c=8)
        NW_ = 8 * RWC
        WK = {'Wa': ['Wa'] + ['Wa_ld%d' % q_ for q_ in range(4)], 'Wb': ['Wb'] + ['Wb_ld%d' % q_ for q_ in range(4)]}
        if half == 0 or 'mix_0' not in plan:
            A.mark()
            mu_t = A.f32(RWC)
            omu_t = A.f32(RWC)
            stg = [A.f32(RWC) for _ in range(2)]
            S.dma('sp', lambda e: e.dma_start(out=mu_t, in_=dr['rw_mu'].partition_broadcast(128)), R=[], W=['mu'])
            S.op('dve', lambda e: e.tensor_scalar(out=omu_t, in0=mu_t, scalar1=-1.0, scalar2=1.0, op0=ALU.mult, op1=ALU.add), R=['mu'], W=['omu'])
            for dc in range(8):
                sgb = stg[dc % 2]
                sk = 'stg%d' % (dc % 2)
                S.dma('sp', lambda e: e.dma_start(out=sgb, in_=dr['w_in'][dc * 128:(dc + 1) * 128, RETC:INC]), R=[], W=[sk])
                S.op('dve', lambda e: e.tensor_tensor(out=Wav[:, dc, :], in0=sgb, in1=omu_t, op=ALU.mult), R=[sk, 'omu'], W=['Wa'])
                S.op('pool', lambda e: e.tensor_tensor(out=Wbv[:, dc, :], in0=sgb, in1=mu_t, op=ALU.mult), R=[sk, 'mu'], W=['Wb'])
            for q_ in range(4):
                S.dma('sp', lambda e: e.dma_start(out=wab_s[:, q_ * (NW_ // 4):(q_ + 1) * (NW_ // 4)], in_=Wa[:, q_ * (NW_ // 4):(q_ + 1) * (NW_ // 4)]), R=['Wa'], W=['wabs_a%d' % q_], sk='wabs')
                S.dma('sp', lambda e: e.dma_start(out=wab_s[:, NW_ + q_ * (NW_ // 4):NW_ + (q_ + 1) * (NW_ // 4)], in_=Wb[:, q_ * (NW_ // 4):(q_ + 1) * (NW_ // 4)]), R=['Wb'], W=['wabs_b%d' % q_], sk='wabs')
            A.release()
        else:
            for q_ in range(4):
                S.dma('sp', lambda e: e.dma_start(out=Wa[:, q_ * (NW_ // 4):(q_ + 1) * (NW_ // 4)], in_=wab_s[:, q_ * (NW_ // 4):(q_ + 1) * (NW_ // 4)]), R=[], W=['Wa_ld%d' % q_])
                S.dma('sp', lambda e: e.dma_start(out=Wb[:, q_ * (NW_ // 4):(q_ + 1) * (NW_ // 4)], in_=wab_s[:, NW_ + q_ * (NW_ // 4):NW_ + (q_ + 1) * (NW_ // 4)]), R=[], W=['Wb_ld%d' % q_])
        wup = A.f32(512)
        aup = A.f32(512)
        gup1 = A.bf(512)
        gup2 = A.bf(512)
        w0r = A.f32(512)
        a0r = A.f32(512)
        for t_, k_ in ((wup, 'wup'), (aup, 'aup'), (gup2, 'gup2'), (w0r, 'w0r'), (a0r, 'a0r')):
            S.op('pool', (lambda t_: lambda e: e.memset(t_, 0.0))(t_), R=[], W=[k_])
        S.dma('sp', lambda e: e.dma_start(out=wup[0:64, :], in_=dr['rw_w_up']), R=[], W=['wup'])
        S.dma('sp', lambda e: e.dma_start(out=aup[64:128, :], in_=dr['rw_a_up']), R=[], W=['aup'])
        S.dma('pool', lambda e: e.dma_start(out=gup1, in_=dr['rw_g_up'][0:128, :]), R=[], W=['gup1'])
        S.dma('pool', lambda e: e.dma_start(out=gup2[96:128, :], in_=dr['rw_g_up'][128:160, :]), R=[], W=['gup2'])
        S.dma('sp', lambda e: e.dma_start(out=w0r[0:1, :], in_=dr['rw_w0']), R=[], W=['w0r'])
        S.dma('sp', lambda e: e.dma_start(out=a0r[0:1, :], in_=dr['rw_a0']), R=[], W=['a0r'])
        ones_r = c32v('ones')
        twT = A.f32(128)
        sgd1 = A.bf(128)
        sgd2 = A.bf(128)
        sw = A.f32(512)
        a_t = A.f32(512)
        gi_t = A.f32(512)
        kkr = A.f32(512)
        sq = A.f32(512)
        kp = A.f32(512)
        u1 = sq
        g_t = sw
        v32 = A.f32(512)
        st = A.f32(64)
        gst2 = A.f32(64)
        r32 = A.f32(512)
        k32 = A.f32(512)
        gp_t = k32
        gC = A.f32(4)
        rh = A.bf(512)
        kt = A.bf(512)
        bt_ = A.bf(512)
        kh = A.bf(512)
        vb = A.bf(512)
        XT = A.bf(4 * 256)
        XTv = XT.rearrange("p (a b t) -> p a b t", a=4, b=2)
        btT = A.bf(512)
        khTm = A.bf(1024)
        rhTm = A.bf(1024)
        btTm = A.bf(1024)
        ktTm = A.bf(1024)
        for t_, k_ in ((khTm, 'khTm'), (rhTm, 'rhTm'), (btTm, 'btTm'), (ktTm, 'ktTm')):
            S.op('pool', (lambda t_: lambda e: e.memset(t_, 0.0))(t_), R=[], W=[k_])
        LkT = A.bf(1024)
        MbT = A.bf(1024)
        MkT = A.bf(1024)
        Abuf = [[A.bf(256) for _ in range(2)] for _ in range(4)]
        BQbuf = [[A.bf(512) for _ in range(2)] for _ in range(4)]
        TT = A.bf(1024)
        Xb = A.bf(512)
        Ub = A.bf(512)
        mixw = A.bf(512)
        bsum = st[:, 56:64]
        rw_deferred = []

        for ti in range(NT_HALF):
            c0 = XO + ti * 128
            xk = ['x1T_%d' % ti] + (['x1T_%d' % (ti - 1)] if ti > 0 else ['x1T_prev'])
            def proj(nm):
                qi = ['r', 'k', 'v'].index(nm)
                b = bank()
                n = 0
                for (Wv, wkey, sh) in ((Wav, 'Wa', 0), (Wbv, 'Wb', 1)):
                    for dc in range(8):
                        S.op('pe', (lambda Wv, sh, dc, b, qi, n: lambda e: e.matmul(PS(b), lhsT=x1Tv[:, dc, c0 - sh:c0 - sh + 128], rhs=Wv[:, dc, qi * 512:(qi + 1) * 512], start=(n == 0), stop=(n == 15)))(Wv, sh, dc, b, qi, n),
                             R=xk + WK[wkey], W=[PK(b)], inc=(n == 15))
                        n += 1
                return b
            bl = bank()
            lora_groups = ((1536, 128), (1664, 128), (1696, 128)) if full else ((1536, 128),)
            for gi_, (cs, cn) in enumerate(lora_groups):
                n = 0
                for (Wv, wkey, sh) in ((Wav, 'Wa', 0), (Wbv, 'Wb', 1)):
                    for dc in range(8):
                        S.op('pe', (lambda Wv, sh, dc, cs, cn, gi_, n: lambda e: e.matmul(PS(bl)[0:cn, gi_ * 128:(gi_ + 1) * 128], lhsT=Wv[:, dc, cs:cs + cn], rhs=x1Tv[:, dc, c0 - sh:c0 - sh + 128], start=(n == 0), stop=(n == 15)))(Wv, sh, dc, cs, cn, gi_, n),
                             R=xk + WK[wkey], W=[PK(bl)], inc=(n == 15 and gi_ == len(lora_groups) - 1))
                        n += 1
            plk = PK(bl)
            S.op('act', lambda e: e.activation(out=twT[0:64, :], in_=PS(bl)[0:64, 0:128], func=AF.Tanh), R=[plk], W=['twT'])
            S.op('act', lambda e: e.activation(out=twT[64:128, :], in_=PS(bl)[64:128, 0:128], func=AF.Copy), R=[plk], W=['twT'])
            if full:
                S.op('act', lambda e: e.activation(out=sgd1, in_=PS(bl)[:, 128:256], func=AF.Sigmoid), R=[plk], W=['sgd1'])
                S.op('act', lambda e: e.activation(out=sgd2, in_=PS(bl)[:, 256:384], func=AF.Sigmoid), R=[plk], W=['sgd2'])
            bk2 = proj('k')
            kk_ = PK(bk2)
            S.op('act', lambda e: e.activation(out=k32, in_=PS(bk2), func=AF.Copy), R=[kk_], W=['k32'])
            bw = bank()
            S.op('pe', lambda e: e.matmul(PS(bw), lhsT=twT, rhs=wup, start=True, stop=False), R=['twT', 'wup'], W=[PK(bw)], inc=False)
            S.op('pe', lambda e: e.matmul(PS(bw), lhsT=ones_r, rhs=w0r, start=False, stop=True), R=['c32', 'w0r'], W=[PK(bw)])
            ba = bank()
            S.op('pe', lambda e: e.matmul(PS(ba), lhsT=twT, rhs=aup, start=True, stop=False), R=['twT', 'aup'], W=[PK(ba)], inc=False)
            S.op('pe', lambda e: e.matmul(PS(ba), lhsT=ones_r, rhs=a0r, start=False, stop=True), R=['c32', 'a0r'], W=[PK(ba)])
            S.op('act', lambda e: e.activation(out=sw, in_=PS(bw), func=AF.Sigmoid), R=[PK(bw)], W=['sw'])
            S.op('act', lambda e: e.activation(out=a_t, in_=PS(ba), func=AF.Sigmoid), R=[PK(ba)], W=['a_t'])
            bv = proj('v')
            vk_ = PK(bv)
            S.op('act', lambda e: e.activation(out=v32, in_=PS(bv), func=AF.Copy), R=[vk_], W=['v32'])
            S.op('dve', lambda e: e.tensor_copy(out=vb, in_=PS(bv)), R=[vk_], W=['vbw'])
            bc = bank()
            S.op('pe', lambda e: e.matmul(PS(bc), lhsT=c32v('tri_incl'), rhs=sw, start=True, stop=True), R=['c32', 'sw'], W=[PK(bc)])
            bx = bank()
            S.op('pe', lambda e: e.matmul(PS(bx), lhsT=c32v('tri_strict'), rhs=sw, start=True, stop=True), R=['c32', 'sw'], W=[PK(bx)])
            bgc = bank()
            for pr in range(4):
                S.op('pe', (lambda pr: lambda e: e.matmul(PS(bgc)[:, pr:pr + 1], lhsT=sw[:, pr * 128:(pr + 1) * 128], rhs=c32v('negcol'), start=True, stop=True))(pr), R=['sw', 'c32'], W=[PK(bgc)], inc=(pr == 3))
            if full:
                br = proj('r')
                rk_ = PK(br)
                S.op('act', lambda e: e.activation(out=r32, in_=PS(br), func=AF.Copy), R=[rk_], W=['r32'])
            for f_ in rw_deferred:
                f_()
            del rw_deferred[:]
            S.op('dve', lambda e: e.tensor_tensor(out=kkr, in0=k32, in1=ptile['rw_k_k'], op=ALU.mult), R=['k32', 'pt_rw_k_k'], W=['kkr'])
            S.op('act', lambda e: e.activation(out=sq, in_=kkr, func=AF.Square), R=['kkr'], W=['sq'])
            S.op('dve', lambda e: e.tensor_reduce(out=st[:, 0:8], in_=sq.rearrange("p (h d) -> p h d", h=8), axis=AX.X, op=ALU.add), R=['sq'], W=['st0'])
            S.op('act', lambda e: e.activation(out=st[:, 8:16], in_=st[:, 0:8], func=AF.Sqrt), R=['st0'], W=['st1'])
            S.op('dve', lambda e: e.tensor_scalar(out=st[:, 8:16], in0=st[:, 8:16], scalar1=1e-12, scalar2=None, op0=ALU.max), R=['st1'], W=['st1'])
            S.op('dve', lambda e: e.reciprocal(out=st[:, 16:24], in_=st[:, 8:16]), R=['st1'], W=['st2'])
            S.op('dve', lambda e: e.tensor_tensor(out=kkr.rearrange("p (h d) -> p h d", h=8), in0=kkr.rearrange("p (h d) -> p h d", h=8), in1=st[:, 16:24].unsqueeze(2).broadcast_to([128, 8, 64]), op=ALU.mult), R=['kkr', 'st2'], W=['kkr'])
            S.op('dve', lambda e: e.scalar_tensor_tensor(out=u1, in0=a_t, scalar=-1.0, in1=ptile['rw_k_a'], op0=ALU.add, op1=ALU.mult), R=['a_t', 'pt_rw_k_a'], W=['sq'])
            S.op('dve', lambda e: e.scalar_tensor_tensor(out=kp, in0=u1, scalar=1.0, in1=k32, op0=ALU.add, op1=ALU.mult), R=['sq', 'k32'], W=['kp'])
            if full:
                S.op('dve', lambda e: e.tensor_tensor(out=u1, in0=r32, in1=kp, op=ALU.mult), R=['r32', 'kp', 'sq'], W=['sq'])
                S.op('pool', lambda e: e.tensor_tensor(out=u1, in0=u1, in1=ptile['rw_r_k'], op=ALU.mult), R=['sq', 'pt_rw_r_k'], W=['sq'])
                S.op('dve', lambda e: e.tensor_reduce(out=bsum, in_=u1.rearrange("p (h d) -> p h d", h=8), axis=AX.X, op=ALU.add), R=['sq'], W=['bsum'])
            S.op('act', lambda e: e.activation(out=g_t, in_=PS(bc), func=AF.Exp), R=[PK(bc)], W=['sw'])
            S.op('act', lambda e: e.activation(out=gi_t, in_=PS(bc), func=AF.Exp, scale=-1.0), R=[PK(bc)], W=['gi_t'])
            S.op('act', lambda e: e.activation(out=gp_t, in_=PS(bx), func=AF.Exp), R=[PK(bx)], W=['k32'])
            S.op('act', lambda e: e.activation(out=gC, in_=PS(bgc)[:, 0:4], func=AF.Exp), R=[PK(bgc)], W=['gC'])
            if full:
                S.op('dve', lambda e: e.tensor_tensor(out=rh, in0=r32, in1=g_t, op=ALU.mult), R=['r32', 'sw'], W=['rh'])
            S.op('dve', lambda e: e.tensor_tensor(out=kt, in0=kp, in1=gi_t, op=ALU.mult), R=['kp', 'gi_t'], W=['kt'])
            S.op('pool', lambda e: e.tensor_tensor(out=sq, in0=kkr, in1=a_t, op=ALU.mult), R=['kkr', 'a_t', 'sq'], W=['sq'])
            S.op('pool', lambda e: e.tensor_tensor(out=bt_, in0=sq, in1=gi_t, op=ALU.mult), R=['sq', 'gi_t'], W=['bt'])
            S.op('dve', lambda e: e.tensor_tensor(out=kh, in0=kkr, in1=gp_t, op=ALU.mult), R=['kkr', 'k32'], W=['kh'])
            if full:
                S.op('dve', lambda e: e.tensor_tensor(out=gi_t.rearrange("p (h d) -> p h d", h=8), in0=v32.rearrange("p (h d) -> p h d", h=8), in1=bsum.unsqueeze(2).broadcast_to([128, 8, 64]), op=ALU.mult), R=['v32', 'bsum', 'gi_t'], W=['gi_t'])
            b1 = bank()
            for i in range(4):
                S.op('pe', (lambda i: lambda e: e.transpose(out=PSB(b1)[:, i * 256:i * 256 + 128], in_=kh[:, i * 128:(i + 1) * 128], identity=ident))(i), R=['kh', 'cb'], W=[PK(b1)], inc=(i == 3 and not full))
                if full:
                    S.op('pe', (lambda i: lambda e: e.transpose(out=PSB(b1)[:, i * 256 + 128:i * 256 + 256], in_=rh[:, i * 128:(i + 1) * 128], identity=ident))(i), R=['rh', 'cb'], W=[PK(b1)], inc=(i == 3))
            if full:
                S.op('act', lambda e: e.activation(out=XT, in_=PSB(b1), func=AF.Copy), R=[PK(b1)], W=['XT'])
            else:
                S.op('act', lambda e: e.activation(out=XT.rearrange("p (a c t) -> p a c t", a=4, c=2)[:, :, 0, :], in_=PSB(b1).rearrange("p (a c t) -> p a c t", a=4, c=2)[:, :, 0, :], func=AF.Copy), R=[PK(b1)], W=['XT'])
            for hp in range(2):
                rs_ = slice(hp * 64, (hp + 1) * 64)
                srcv = PSB(b1)[rs_, :].rearrange("p (a c t) -> p a c t", a=4, c=2)
                S.op('act', lambda e: e.activation(out=khTm[rs_, :].rearrange("p (a c t) -> p a c t", a=4, c=2)[:, :, hp, :], in_=srcv[:, :, 0, :], func=AF.Copy), R=[PK(b1)], W=['khTm'])
                if full:
                    S.op('dve', lambda e: e.tensor_copy(out=rhTm[rs_, :].rearrange("p (a c t) -> p a c t", a=4, c=2)[:, :, hp, :], in_=srcv[:, :, 1, :]), R=[PK(b1)], W=['rhTm'])
            b2 = bank()
            for i in range(4):
                S.op('pe', (lambda i: lambda e: e.transpose(out=PSB(b2)[:, i * 128:(i + 1) * 128], in_=bt_[:, i * 128:(i + 1) * 128], identity=ident))(i), R=['bt', 'cb'], W=[PK(b2)], inc=False)
            for i in range(4):
                S.op('pe', (lambda i: lambda e: e.transpose(out=PSB(b2)[:, 512 + i * 128:512 + (i + 1) * 128], in_=kt[:, i * 128:(i + 1) * 128], identity=ident))(i), R=['kt', 'cb'], W=[PK(b2)], inc=(i == 3))
            S.op('dve', lambda e: e.tensor_copy(out=btT, in_=PSB(b2)[:, 0:512]), R=[PK(b2)], W=['btT'])
            for hp in range(2):
                rs_ = slice(hp * 64, (hp + 1) * 64)
                S.op('act', lambda e: e.activation(out=btTm[rs_, :].rearrange("p (a c t) -> p a c t", a=4, c=2)[:, :, hp, :], in_=PSB(b2)[rs_, 0:512].rearrange("p (a t) -> p a t", a=4), func=AF.Copy), R=[PK(b2)], W=['btTm'])
                S.op('dve', lambda e: e.tensor_copy(out=ktTm[rs_, :].rearrange("p (a c t) -> p a c t", a=4, c=2)[:, :, hp, :], in_=PSB(b2)[rs_, 512:1024].rearrange("p (a t) -> p a t", a=4)), R=[PK(b2)], W=['ktTm'])
            for pr in range(4):
                bm1, bm2, bm3 = bank(), bank(), bank()
                for hp in range(2):
                    ps_ = slice(hp * 64, (hp + 1) * 64)
                    NX_ = 256 if full else 128
                    rhs_x = XT[:, pr * 256:pr * 256 + NX_]
                    hh_ = pr * 2 + hp
                    S.op('pe', (lambda hp, ps_, rhs_x, bm1: lambda e: e.matmul(PS(bm1)[:, hp * 256:hp * 256 + NX_], lhsT=btTm[:, hh_ * 128:(hh_ + 1) * 128], rhs=rhs_x, start=True, stop=True))(hp, ps_, rhs_x, bm1),
                         R=['btTm', 'XT'], W=[PK(bm1)], inc=(hp == 1))
                    S.op('pe', (lambda hp, ps_, rhs_x, bm2: lambda e: e.matmul(PS(bm2)[:, hp * 256:hp * 256 + NX_], lhsT=ktTm[:, hh_ * 128:(hh_ + 1) * 128], rhs=rhs_x, start=True, stop=True))(hp, ps_, rhs_x, bm2),
                         R=['ktTm', 'XT'], W=[PK(bm2)], inc=(hp == 1))
                    S.op('pe', (lambda hp, ps_, bm3: lambda e: e.matmul(PS(bm3)[:, hp * 128:(hp + 1) * 128], lhsT=khTm[:, hh_ * 128:(hh_ + 1) * 128], rhs=btT[:, pr * 128:(pr + 1) * 128], start=True, stop=True))(hp, ps_, bm3),
                         R=['btT', 'khTm'], W=[PK(bm3)], inc=(hp == 1))
                A0 = Abuf[pr][0]
                BQ0 = BQbuf[pr][0].rearrange("p (h x) -> p h x", h=2)
                m1v = cbv('m1').rearrange("p (h x) -> p h x", h=2)
                p1v = PS(bm1).rearrange("p (h x) -> p h x", h=2)
                S.op('dve', (lambda BQ0, p1v, m1v: lambda e: e.tensor_tensor(out=BQ0[:, :, 0:128], in0=p1v[:, :, 0:128], in1=m1v[:, :, 0:128], op=ALU.mult))(BQ0, p1v, m1v),
                     R=[PK(bm1), 'cb'], W=['BQ%d_0' % pr])
                S.op('pool', (lambda BQ0: lambda e: e.tensor_copy(out=BQ0[:, :, 128:256], in_=ident.unsqueeze(1).broadcast_to([128, 2, 128])))(BQ0), R=['cb'], W=['BQ%d_0' % pr])
                if full:
                    S.op('dve', (lambda p1v, m1v: lambda e: e.tensor_tensor(out=MbT[:, pr * 256:(pr + 1) * 256].rearrange("p (h x) -> p h x", h=2), in0=p1v[:, :, 128:256], in1=m1v[:, :, 128:256], op=ALU.mult))(p1v, m1v),
                         R=[PK(bm1), 'cb'], W=['MbT%d' % pr])
                m2v = cbv('m2').rearrange("p (h x) -> p h x", h=2)
                p2v = PS(bm2).rearrange("p (h x) -> p h x", h=2)
                S.op('dve', (lambda p2v, m2v: lambda e: e.tensor_tensor(out=LkT[:, pr * 256:(pr + 1) * 256].rearrange("p (h x) -> p h x", h=2), in0=p2v[:, :, 0:128], in1=m2v[:, :, 0:128], op=ALU.mult))(p2v, m2v),
                     R=[PK(bm2), 'cb'], W=['LkT%d' % pr])
                if full:
                    S.op('dve', (lambda p2v, m2v: lambda e: e.tensor_tensor(out=MkT[:, pr * 256:(pr + 1) * 256].rearrange("p (h x) -> p h x", h=2), in0=p2v[:, :, 128:256], in1=m2v[:, :, 128:256], op=ALU.mult))(p2v, m2v),
                         R=[PK(bm2), 'cb'], W=['MkT%d' % pr])
                S.op('dve', (lambda A0, bm3: lambda e: e.tensor_tensor(out=A0, in0=PS(bm3)[:, 0:256], in1=cbv('m3'), op=ALU.mult))(A0, bm3), R=[PK(bm3), 'cb'], W=['A%d_0' % pr])
            for lv in range(7):
                last = (lv == 6)
                for pr in range(4):
                    cur, nxt = lv % 2, (lv + 1) % 2
                    Ac = Abuf[pr][cur].rearrange("p (h x) -> p h x", h=2)
                    BQc = BQbuf[pr][cur].rearrange("p (h x) -> p h x", h=2)
                    ak, bqk = 'A%d_%d' % (pr, cur), 'BQ%d_%d' % (pr, cur)
                    if not last:
                        bA = bank()
                        for hp in range(2):
                            S.op('pe', (lambda hp, BQc, Ac, bA: lambda e: e.matmul(PS(bA)[:, hp * 128:(hp + 1) * 128], lhsT=BQc[:, hp, 0:128], rhs=Ac[:, hp, :], start=True, stop=True))(hp, BQc, Ac, bA),
                                 R=[ak, bqk], W=[PK(bA)], inc=(hp == 1))
                        bB = bank()
                        for hp in range(2):
                            S.op('pe', (lambda hp, BQc, Ac, bB: lambda e: e.matmul(PS(bB)[:, hp * 256:hp * 256 + 256], lhsT=Ac[:, hp, :], rhs=BQc[:, hp, :], start=True, stop=False, skip_group_check=True))(hp, BQc, Ac, bB),
                                 R=[ak, bqk], W=[PK(bB)], inc=False)
                            S.op('pe', (lambda hp, BQc, bB: lambda e: e.matmul(PS(bB)[:, hp * 256 + 128:hp * 256 + 256], lhsT=ident, rhs=BQc[:, hp, 128:256], start=False, stop=True, skip_group_check=True))(hp, BQc, bB),
                                 R=[bqk, 'cb'], W=[PK(bB)], inc=(hp == 1))
                        An = Abuf[pr][nxt]
                        BQn = BQbuf[pr][nxt]
                        S.op('dve' if pr % 2 == 0 else 'act', (lambda An, bA, pr: (lambda e: e.tensor_copy(out=An, in_=PS(bA)[:, 0:256])) if pr % 2 == 0 else (lambda e: e.activation(out=An, in_=PS(bA)[:, 0:256], func=AF.Copy)))(An, bA, pr),
                             R=[PK(bA)], W=['A%d_%d' % (pr, nxt)])
                        S.op('dve' if pr % 2 else 'act', (lambda BQn, bB, pr: (lambda e: e.tensor_copy(out=BQn, in_=PS(bB))) if pr % 2 else (lambda e: e.activation(out=BQn, in_=PS(bB), func=AF.Copy)))(BQn, bB, pr),
                             R=[PK(bB)], W=['BQ%d_%d' % (pr, nxt)])
                    else:
                        bB = bank()
                        for hp in range(2):
                            S.op('pe', (lambda hp, BQc, Ac, bB: lambda e: e.matmul(PS(bB)[:, hp * 128:(hp + 1) * 128], lhsT=Ac[:, hp, :], rhs=BQc[:, hp, 128:256], start=True, stop=False))(hp, BQc, Ac, bB),
                                 R=[ak, bqk], W=[PK(bB)], inc=False)
                            S.op('pe', (lambda hp, BQc, bB: lambda e: e.matmul(PS(bB)[:, hp * 128:(hp + 1) * 128], lhsT=ident, rhs=BQc[:, hp, 128:256], start=False, stop=True))(hp, BQc, bB),
                                 R=[bqk, 'cb'], W=[PK(bB)], inc=(hp == 1))
                        S.op('dve', (lambda bB, pr: lambda e: e.tensor_copy(out=TT[:, pr * 256:(pr + 1) * 256], in_=PS(bB)[:, 0:256]))(bB, pr), R=[PK(bB)], W=['TT%d' % pr])
            bX = bank()
            for h in range(8):
                pr, hp = h // 2, h % 2
                ps_ = slice(hp * 64, (hp + 1) * 64)
                S.op('pe', (lambda h, pr, ps_: lambda e: e.matmul(PS(bX)[:, h * 64:(h + 1) * 64], lhsT=khTm[:, h * 128:(h + 1) * 128], rhs=rwAb[:, pr * 64:(pr + 1) * 64], start=True, stop=False))(h, pr, ps_),
                     R=['khTm', 'rwAb'], W=[PK(bX)], inc=False)
                S.op('pe', (lambda h: lambda e: e.matmul(PS(bX)[:, h * 64:(h + 1) * 64], lhsT=LkT[:, h * 128:(h + 1) * 128], rhs=vb[:, h * 64:(h + 1) * 64], start=False, stop=True))(h),
                     R=['LkT%d' % pr, 'vbw'], W=[PK(bX)], inc=(h == 7))
            S.op('act', lambda e: e.activation(out=Xb, in_=PS(bX), func=AF.Copy, scale=-1.0), R=[PK(bX)], W=['Xb'])
            bU = bank()
            for h in range(8):
                S.op('pe', (lambda h: lambda e: e.matmul(PS(bU)[:, h * 64:(h + 1) * 64], lhsT=TT[:, h * 128:(h + 1) * 128], rhs=Xb[:, h * 64:(h + 1) * 64], start=True, stop=True))(h),
                     R=['TT%d' % (h // 2), 'Xb'], W=[PK(bU)], inc=(h == 7))
            S.op('act', lambda e: e.activation(out=Ub, in_=PS(bU), func=AF.Copy), R=[PK(bU)], W=['Ub'])
            if full:
                bY = bank()
                for h in range(8):
                    pr, hp = h // 2, h % 2
                    ps_ = slice(hp * 64, (hp + 1) * 64)
                    S.op('pe', (lambda h, pr, ps_: lambda e: e.matmul(PS(bY)[:, h * 64:(h + 1) * 64], lhsT=rhTm[:, h * 128:(h + 1) * 128], rhs=rwAb[:, pr * 64:(pr + 1) * 64], start=True, stop=False))(h, pr, ps_),
                         R=['rhTm', 'rwAb'], W=[PK(bY)], inc=False)
                    S.op('pe', (lambda h: lambda e: e.matmul(PS(bY)[:, h * 64:(h + 1) * 64], lhsT=MbT[:, h * 128:(h + 1) * 128], rhs=Ub[:, h * 64:(h + 1) * 64], start=False, stop=False))(h),
                         R=['MbT%d' % pr, 'Ub'], W=[PK(bY)], inc=False)
                    S.op('pe', (lambda h: lambda e: e.matmul(PS(bY)[:, h * 64:(h + 1) * 64], lhsT=MkT[:, h * 128:(h + 1) * 128], rhs=vb[:, h * 64:(h + 1) * 64], start=False, stop=True))(h),
                         R=['MkT%d' % pr, 'vbw'], W=[PK(bY)], inc=(h == 7))
                S.op('act', lambda e: e.activation(out=kp, in_=PS(bY), func=AF.Copy), R=[PK(bY)], W=['kp'])
            bS = bank()
            for pr in range(4):
                S.op('pe', (lambda pr: lambda e: e.matmul(PS(bS)[:, pr * 128:(pr + 1) * 128], lhsT=bt_[:, pr * 128:(pr + 1) * 128], rhs=Ub[:, pr * 128:(pr + 1) * 128], start=True, stop=False))(pr),
                     R=['bt', 'Ub'], W=[PK(bS)], inc=False)
                S.op('pe', (lambda pr: lambda e: e.matmul(PS(bS)[:, pr * 128:(pr + 1) * 128], lhsT=kt[:, pr * 128:(pr + 1) * 128], rhs=vb[:, pr * 128:(pr + 1) * 128], start=False, stop=True))(pr),
                     R=['kt', 'vbw'], W=[PK(bS)], inc=(pr == 3))
            for hp in range(2):
                S.op('dve', (lambda hp: lambda e: e.tensor_tensor(out=rwA[hp * 64:(hp + 1) * 64, :].rearrange("p (a e) -> p a e", a=4), in0=rwA[hp * 64:(hp + 1) * 64, :].rearrange("p (a e) -> p a e", a=4),
                                                                in1=PS(bS)[hp * 64:(hp + 1) * 64, :].rearrange("p (a f) -> p a f", a=4)[:, :, hp * 64:(hp + 1) * 64], op=ALU.add))(hp),
                     R=[PK(bS), 'rwA', 'rwAb'], W=['rwA'])
            S.op('dve', lambda e: e.tensor_tensor(out=rwA.rearrange("p (a e) -> p a e", a=4), in0=rwA.rearrange("p (a e) -> p a e", a=4), in1=gC.unsqueeze(2).broadcast_to([128, 4, 64]), op=ALU.mult), R=['rwA', 'gC'], W=['rwA'])
            S.op('act', lambda e: e.activation(out=rwAb, in_=rwA, func=AF.Copy), R=['rwA'], W=['rwAb'])
            if full:
                group_norm_out(kp, 'kp', sq, 'sq', gst2, 64e-5, ptile['rw_gn_g'], ptile['rw_gn_b'], 'pt_rw_gn_g', 'pt_rw_gn_b')
                S.op('pool', lambda e: e.tensor_tensor(out=kp, in0=kp, in1=gi_t, op=ALU.add), R=['kp', 'gi_t'], W=['kp'])
                bgt = bank()
                S.op('pe', lambda e: e.matmul(PS(bgt), lhsT=sgd1, rhs=gup1, start=True, stop=False), R=['sgd1', 'gup1'], W=[PK(bgt)], inc=False)
                S.op('pe', lambda e: e.matmul(PS(bgt), lhsT=sgd2, rhs=gup2, start=False, stop=True), R=['sgd2', 'gup2'], W=[PK(bgt)])
                S.op('dve', lambda e: e.tensor_tensor(out=mixw, in0=kp, in1=PS(bgt), op=ALU.mult), R=['kp', PK(bgt)], W=['mixw'])
                rw_deferred.append((lambda ti: lambda: transpose_to(mixw, ['mixw'], 4, mixTv[:, 4:8, ti * 128:(ti + 1) * 128], ['mixT_w%d' % ti]))(ti))
        for f_ in rw_deferred:
            f_()
        A.release()

    S.op('pool', lambda e: e.memset(retS, 0.0), R=[], W=['retS'])
    S.op('pool', lambda e: e.memset(retSb, 0.0), R=[], W=['retSb'])
    S.op('pool', lambda e: e.memset(rwA, 0.0), R=[], W=['rwA'])
    S.op('pool', lambda e: e.memset(rwAb, 0.0), R=[], W=['rwAb'])
    S.op('pool', lambda e: e.memset(x1T, 0.0), R=[], W=['x1T_prev'] + ['x1T_%d' % i for i in range(NT_HALF)])

    for half in range(2):
        full = (half == 1)
        A.mark()
        xTv = mixTv
        src = dr['xs'][half * S_HALF:(half + 1) * S_HALF, :]
        prep_xT(src, S_HALF, xTv, 0, lambda t0, n: ['xT_%d' % (t0 // 512)] if n == 128 else ['xT_%d' % (t0 // 512)])
        load_ln('ln1_g', 'ln1_b')
        x1b = [A.bf(1024) for _ in range(8)]
        if half == 1:
            S.op('pool', lambda e: e.tensor_copy(out=x1Tv[:, :, XO - 1:XO], in_=x1Tv[:, :, XO + S_HALF - 1:XO + S_HALF]), R=['x1T_%d' % (NT_HALF - 1)], W=['x1T_prev'])

        def cons1(ti, y, yk, half=half, full=full, x1b=x1b):
            if full:
                S.dma('sp', lambda e: e.dma_start(out=x1s[ti * 128:(ti + 1) * 128, :], in_=y), R=[yk], W=['x1s_%d' % ti], sk='x1s')
            b = x1b[ti % 8]
            bk = 'x1b%d' % (ti % 8)
            S.op('dve', lambda e: e.tensor_scalar(out=b, in0=y, scalar1=hmask[:, half:half + 1], scalar2=None, op0=ALU.mult), R=[yk, 'hmask'], W=[bk])
            return lambda: transpose_to(b, [bk], 8, x1Tv[:, :, XO + ti * 128:XO + (ti + 1) * 128], ['x1T_%d' % ti], evac_eng='dve')

        if 'ffn1_%d' % half in plan:
            ffn(xTv, 0, lambda t0, n: ['xT_%d' % (t0 // 512)], S_HALF, dr['ffn1_w_gu'], dr['ffn1_w_down'], src, LN_EPS / (ALPHA * ALPHA), cons1, 'f1')
        A.release()
        S.barrier()
        if 'mix_%d' % half in plan:
            mixer(half, full)
        S.barrier()

    A.mark()
    NTW = NT_HALF if 'wout' in plan else 0
    wo = A.bf(8 * 1024)
    wov = wo.rearrange("p (c f) -> p c f", c=8)
    w_out_v = dr['w_out'].rearrange("(c p) f -> p c f", p=128)
    for hh in range(2):
        S.dma('pool', (lambda hh: lambda e: e.dma_start(out=wov[:, :, hh * 512:(hh + 1) * 512], in_=w_out_v[:, :, hh * 512:(hh + 1) * 512]))(hh), R=[], W=['wo%d' % hh])
    load_ln('ln2_g', 'ln2_b')
    ybuf = [A.f32(1024) for _ in range(8)]
    x2b = [A.bf(1024) for _ in range(8)]
    tmp_sq = A.f32(1024)
    st = A.f32(32)
    wo_deferred = []
    for g in range(NTW // 4):
        tiles = []
        for ti in range(4 * g, 4 * g + 4):
            y, yk = ybuf[ti % 8], 'y%d' % (ti % 8)
            S.dma('sp', (lambda y, ti: lambda e: e.dma_start(out=y, in_=x1s[ti * 128:(ti + 1) * 128, :]))(y, ti), R=['x1s_%d' % ti], W=[yk])
            for dh in range(2):
                b = bank()
                for c in range(8):
                    S.op('pe', (lambda c, b, dh, ti: lambda e: e.matmul(PS(b), lhsT=mixTv[:, c, ti * 128:(ti + 1) * 128], rhs=wov[:, c, dh * 512:(dh + 1) * 512], start=(c == 0), stop=(c == 7)))(c, b, dh, ti),
                         R=['mixT_r%d' % ti, 'mixT_w%d' % ti, 'wo%d' % dh], W=[PK(b)], inc=(c == 7))
                S.op('dve', (lambda y, b, dh: lambda e: e.scalar_tensor_tensor(out=y[:, dh * 512:(dh + 1) * 512], in0=PS(b), scalar=1.0 / ALPHA, in1=y[:, dh * 512:(dh + 1) * 512], op0=ALU.mult, op1=ALU.add))(y, b, dh),
                     R=[PK(b), yk], W=[yk])
            tiles.append((ti, y, yk))
        for f_ in wo_deferred:
            f_()
        del wo_deferred[:]
        layer_norm_group([(y, yk) for (ti, y, yk) in tiles], LN_EPS / (ALPHA * ALPHA), tmp_sq, st)
        for (ti, y, yk) in tiles:
            S.dma('sp', (lambda y, ti: lambda e: e.dma_start(out=x2s[ti * 128:(ti + 1) * 128, :], in_=y))(y, ti), R=[yk], W=['x2s_%d' % ti], sk='x2s')
            b2, b2k = x2b[ti % 8], 'x2b%d' % (ti % 8)
            S.op('dve', (lambda b2, y: lambda e: e.tensor_copy(out=b2, in_=y))(b2, y), R=[yk], W=[b2k])
            wo_deferred.append((lambda b2, b2k, ti: lambda: transpose_to(b2, [b2k], 8, x2Tv[:, :, XO + ti * 128:XO + (ti + 1) * 128], ['x2T_%d' % (ti // 4)], evac_eng='dve'))(b2, b2k, ti))
    for f_ in wo_deferred:
        f_()
    A.release()
    S.barrier()

    A.mark()
    load_ln('ln3_g', 'ln3_b')
    x3b = [A.bf(1024) for _ in range(8)]

    def cons3(ti, y, yk):
        S.dma('sp', lambda e: e.dma_start(out=x3s[ti * 128:(ti + 1) * 128, :], in_=y), R=[yk], W=['x3s_%d' % ti], sk='x3s')
        b = x3b[ti % 8]
        bk = 'x3b%d' % (ti % 8)
        S.op('dve', lambda e: e.tensor_copy(out=b, in_=y), R=[yk], W=[bk])
        return lambda: transpose_to(b, [bk], 8, mixTv[:, :, ti * 128:(ti + 1) * 128], ['x3T_%d' % ti], evac_eng='dve')

    if 'ffn2' in plan:
        ffn(x2Tv, XO, lambda t0, n: ['x2T_%d' % (t0 // 512)], S_HALF, dr['ffn2_w_gu'], dr['ffn2_w_down'], x2s, LN_EPS / (ALPHA * ALPHA), cons3, 'f2')
    A.release()
    S.barrier()

    A.mark()
    wgt = A.bf(8 * 1024)
    wgv = wgt.rearrange("p (c f) -> p c f", c=8)
    wpj = A.bf(2 * 1024)
    wpv = wpj.rearrange("p (c f) -> p c f", c=2)
    bgr = A.bf(1024)
    onesb = A.bf(128)
    S.op('pool', lambda e: e.memset(bgr, 0.0), R=[], W=['bgr'])
    ple_g = dr['ple_w_gate'].rearrange("(c p) f -> p c f", p=128)
    ple_p = dr['ple_w_proj'].rearrange("(c p) f -> p c f", p=128)
    for hh in range(2):
        S.dma('pool', (lambda hh: lambda e: e.dma_start(out=wgv[:, :, hh * 512:(hh + 1) * 512], in_=ple_g[:, :, hh * 512:(hh + 1) * 512]))(hh), R=[], W=['wgt%d' % hh])
    S.dma('pool', lambda e: e.dma_start(out=wpv, in_=ple_p), R=[], W=['wpj'])
    S.dma('pool', lambda e: e.dma_start(out=bgr[0:1, :], in_=dr['ple_b_gate']), R=[], W=['bgr'])
    S.op('dve', lambda e: e.tensor_copy(out=onesb, in_=c32v('ones')), R=['c32'], W=['onesb'])
    pb_ = [A.bf(256) for _ in range(2)]
    pT = [A.bf(256) for _ in range(2)]
    x3t = [A.f32(1024) for _ in range(2)]
    gsb = [A.f32(1024) for _ in range(2)]
    for ti in range(NT_HALF if 'ple' in plan else 0):
        pbt, pbk = pb_[ti % 2], 'pb%d' % (ti % 2)
        pTt, pTk = pT[ti % 2], 'pT%d' % (ti % 2)
        x3, x3k = x3t[ti % 2], 'x3t%d' % (ti % 2)
        gs, gsk = gsb[ti % 2], 'gs%d' % (ti % 2)
        S.dma('pool', (lambda pbt, ti: lambda e: e.dma_start(out=pbt, in_=dr['p'][ti * 128:(ti + 1) * 128, :]))(pbt, ti), R=[], W=[pbk])
        S.dma('sp', (lambda x3, ti: lambda e: e.dma_start(out=x3, in_=x3s[ti * 128:(ti + 1) * 128, :]))(x3, ti), R=['x3s_%d' % ti], W=[x3k])
        transpose_to(pbt, [pbk], 2, pTt.rearrange("p (c t) -> p c t", c=2), [pTk])
        for dh in range(2):
            bg_ = bank()
            for c in range(8):
                S.op('pe', (lambda c, bg_, dh, ti: lambda e: e.matmul(PS(bg_), lhsT=mixTv[:, c, ti * 128:(ti + 1) * 128], rhs=wgv[:, c, dh * 512:(dh + 1) * 512], start=(c == 0), stop=False))(c, bg_, dh, ti),
                     R=['x3T_%d' % ti, 'wgt%d' % dh], W=[PK(bg_)], inc=False)
            S.op('pe', (lambda bg_, dh: lambda e: e.matmul(PS(bg_), lhsT=onesb, rhs=bgr[:, dh * 512:(dh + 1) * 512], start=False, stop=True))(bg_, dh), R=['onesb', 'bgr'], W=[PK(bg_)])
            bp_ = bank()
            for c in range(2):
                S.op('pe', (lambda c, bp_, dh, pTt: lambda e: e.matmul(PS(bp_), lhsT=pTt[:, c * 128:(c + 1) * 128], rhs=wpv[:, c, dh * 512:(dh + 1) * 512], start=(c == 0), stop=(c == 1)))(c, bp_, dh, pTt),
                     R=[pTk, 'wpj'], W=[PK(bp_)], inc=(c == 1))
            S.op('act', (lambda gs, bg_, dh: lambda e: e.activation(out=gs[:, dh * 512:(dh + 1) * 512], in_=PS(bg_), func=AF.Sigmoid))(gs, bg_, dh), R=[PK(bg_)], W=[gsk])
            S.op('dve', (lambda gs, bp_, dh: lambda e: e.tensor_tensor(out=gs[:, dh * 512:(dh + 1) * 512], in0=gs[:, dh * 512:(dh + 1) * 512], in1=PS(bp_), op=ALU.mult))(gs, bp_, dh), R=[PK(bp_), gsk], W=[gsk])
        S.op('dve', (lambda gs, x3: lambda e: e.tensor_tensor(out=gs, in0=gs, in1=x3, op=ALU.add))(gs, x3), R=[gsk, x3k], W=[gsk])
        S.dma('sp', (lambda gs, ti: lambda e: e.dma_start(out=out[ti * 128:(ti + 1) * 128, :], in_=gs))(gs, ti), R=[gsk], W=['out_%d' % ti], sk='out')
    A.release()
    S.barrier()

    with nc.Block() as block:
        S.emit(nc, block)
    es.close()
    print("semaphores:", len(S.semkeys), "ops:", {e: len(l) for e, l in S.lists.items()}, "arena hi:", A.hi)
    return nc


_NC_CACHE = {}


def kernel(**inputs):
    x = np.asarray(inputs['x'], np.float32)
    p = np.asarray(inputs['p'], np.float32)[0]
    if 'nc' not in _NC_CACHE:
        _NC_CACHE['nc'] = build_program()
    nc = _NC_CACHE['nc']
    c32 = np.ascontiguousarray(np.concatenate([C32[k] for k in C32], axis=1).astype(np.float32))
    cb = np.ascontiguousarray(np.concatenate([CB[k] for k in CB], axis=1).astype(np.float32))
    c32r = np.ascontiguousarray(np.concatenate([C32R[k] for k in C32R], axis=1).astype(np.float32))
    wmap = {}
    for n in WEIGHT_NAMES:
        wmap[n] = np.ascontiguousarray(np.asarray(inputs[n], np.float32)[0].reshape(WSHAPES[n]))
    in_maps = []
    for c in range(8):
        b, half = c // 2, c % 2
        m = dict(wmap)
        if half == 1:
            xs = x[b]
            pos = np.arange(4096)
            hm = np.ones((128, 2), np.float32)
        else:
            xs = np.concatenate([np.zeros((S_HALF, D), np.float32), x[b, :S_HALF]], axis=0)
            pos = np.arange(4096) - S_HALF
            hm = np.ones((128, 2), np.float32)
            hm[:, 0] = 0.0
        cos, sin = rope_tables(pos)
        m['xs'] = np.ascontiguousarray(xs)
        m['p'] = np.ascontiguousarray(p[b, half * S_HALF:(half + 1) * S_HALF])
        m['hmask'] = hm
        m['cos'] = cos
        m['sin'] = sin
        m['c32'] = c32
        m['cb'] = cb
        m['c32r'] = c32r
        in_maps.append(m)
    res = run_bass_kernel_spmd(nc, in_maps, core_ids=list(range(8)))
    outp = np.zeros((4, 4096, D), np.float32)
    for c in range(8):
        b, half = c // 2, c % 2
        outp[b, half * S_HALF:(half + 1) * S_HALF] = res.results[c]['out']
    return outp
```
